# Optimizing a Trainium2 kernel written in Bass

```python
import math, functools
import jax, jax.numpy as jnp
from jax import lax
import numpy as np

D_MODEL = 1024
BATCH = 8
SEQ = 2048
DEPTH = 2

GRID_W = 64
CTX_LEN = 256
N_EVEN = (DEPTH + 1) // 2
N_ODD = DEPTH // 2
D_FF = 2816
FFN_RES = 0.5
N_MOD = 9
EPS = 1e-6
ROPE_THETA = 10000.0
Q_BLOCK = 128
CHUNK = 64

MLA_HEADS = 8
MLA_NOPE = 64
MLA_ROPE = 32
MLA_QK = MLA_NOPE + MLA_ROPE
MLA_V = 64
MLA_Q_RANK = 384
MLA_KV_RANK = 256
MLA_VW = MLA_HEADS * MLA_V

GDN_HEADS = 8
GDN_DK = 64
GDN_DV = 64
GDN_CONV = 5
GDN_QK = GDN_HEADS * GDN_DK
GDN_VW = GDN_HEADS * GDN_DV

GLA_HEADS = 4
GLA_DK = 64
GLA_DV = 128
GLA_RANK = 16
GLA_TAU = 16.0
GLA_QK = GLA_HEADS * GLA_DK
GLA_VW = GLA_HEADS * GLA_DV

SWA_HEADS = 8
SWA_KV_HEADS = 2
SWA_DH = 64
SWA_WINDOW = 128
SWA_QW = SWA_HEADS * SWA_DH
SWA_KW = SWA_KV_HEADS * SWA_DH

EVEN_SPLITS = (MLA_Q_RANK, MLA_KV_RANK, MLA_ROPE, GDN_QK, GDN_QK, GDN_VW,
               GDN_HEADS, GDN_HEADS, GDN_HEADS, GDN_HEADS, GDN_VW)
ODD_SPLITS = (GLA_QK, GLA_QK, GLA_VW, GLA_RANK, GLA_RANK, GLA_VW, SWA_QW, SWA_KW, SWA_KW)
EVEN_IN = sum(EVEN_SPLITS)
ODD_IN = sum(ODD_SPLITS)
MIX_EVEN = MLA_VW + GDN_VW
MIX_ODD = GLA_VW + SWA_QW

F32 = jnp.float32

kernel_name = 'hybrid_mla_gdn_gla_swa_prefix_block'


def rms_norm(x, g):
    x32 = x.astype(F32)
    y = x32 * lax.rsqrt(jnp.mean(x32 * x32, axis=-1, keepdims=True) + EPS)
    return (y * g.astype(F32)).astype(x.dtype)


def l2_norm(x):
    x32 = x.astype(F32)
    return (x32 * lax.rsqrt(jnp.sum(x32 * x32, axis=-1, keepdims=True) + EPS)).astype(x.dtype)


def modulate(h, shift, scale):
    return h * (1 + scale) + shift


def split_cols(z, sizes):
    return jnp.split(z, [int(s) for s in np.cumsum(sizes)[:-1]], axis=-1)


def swiglu(h, w_gate, w_up, w_down):
    return (jax.nn.silu(h @ w_gate) * (h @ w_up)) @ w_down


def ffn_half_step(x, g, shift, scale, gate, w_gate, w_up, w_down):
    h = modulate(rms_norm(x, g), shift, scale)
    return x + FFN_RES * gate * swiglu(h, w_gate, w_up, w_down)


def axial_rope_tables(rows, rot_dim):
    t = jnp.arange(rows * GRID_W)
    row = (t // GRID_W).astype(F32)
    col = (t % GRID_W).astype(F32)
    n_freq = rot_dim // 4
    inv = ROPE_THETA ** (-jnp.arange(n_freq, dtype=F32) / n_freq)
    ang = jnp.concatenate([row[:, None] * inv, col[:, None] * inv], axis=-1)
    return jnp.cos(ang), jnp.sin(ang)


def apply_rope(x, cos, sin):
    x1, x2 = jnp.split(x.astype(F32), 2, axis=-1)
    c, s = cos[None, :, None, :], sin[None, :, None, :]
    return jnp.concatenate([x1 * c - x2 * s, x1 * s + x2 * c], axis=-1).astype(x.dtype)


def centred_depthwise_conv(x, w):
    k_w, ch = w.shape
    return lax.conv_general_dilated(x, w[:, None, :].astype(x.dtype), window_strides=(1,),
                                    padding=[(k_w // 2, k_w // 2)],
                                    dimension_numbers=('NWC', 'WIO', 'NWC'),
                                    feature_group_count=ch)


def dense_attention(q, k, v, scale):
    b, n, h, dq = q.shape
    nb = n // Q_BLOCK
    qb = jnp.moveaxis(q.reshape(b, nb, Q_BLOCK, h, dq), 1, 0)

    def block(qi):
        s = jnp.einsum('bqhd,bkhd->bhqk', qi, k, preferred_element_type=F32) * scale
        p = jax.nn.softmax(s, axis=-1).astype(v.dtype)
        return jnp.einsum('bhqk,bkhd->bqhd', p, v, preferred_element_type=F32).astype(q.dtype)

    o = lax.map(block, qb)
    return jnp.moveaxis(o, 0, 1).reshape(b, n, h, v.shape[-1])


def sink_attention(q, k, v, sink, scale):
    b, n, h, dh = q.shape
    g = k.shape[2]
    r = h // g
    qg = q.reshape(b, n, g, r, dh)
    s = jnp.einsum('bqgrd,bkgd->bgrqk', qg, k, preferred_element_type=F32) * scale
    s_sink = jnp.broadcast_to(sink.astype(F32).reshape(1, g, r, 1, 1), s.shape[:-1] + (1,))
    p = jax.nn.softmax(jnp.concatenate([s, s_sink], axis=-1), axis=-1)[..., :-1].astype(v.dtype)
    o = jnp.einsum('bgrqk,bkgd->bqgrd', p, v, preferred_element_type=F32)
    return o.reshape(b, n, h, dh).astype(q.dtype)


def banded_window_attention(q, k, v, k_ctx, v_ctx, sink, scale):
    b, n, h, dh = q.shape
    g = k.shape[2]
    r = h // g
    w = SWA_WINDOW
    nb = n // w
    qb = q.reshape(b, nb, w, g, r, dh)

    def band(t):
        tp = jnp.pad(t, ((0, 0), (w, w), (0, 0), (0, 0))).reshape(b, nb + 2, w, g, dh)
        return jnp.concatenate([tp[:, :-2], tp[:, 1:-1], tp[:, 2:]], axis=2)

    kb, vb = band(k), band(v)
    s_band = jnp.einsum('bnqgrd,bnkgd->bngrqk', qb, kb, preferred_element_type=F32) * scale
    s_ctx = jnp.einsum('bnqgrd,bcgd->bngrqc', qb, k_ctx, preferred_element_type=F32) * scale
    qi = jnp.arange(w)[:, None]
    kr = jnp.arange(3 * w)[None, :]
    kpos = jnp.arange(nb)[:, None, None] * w - w + kr
    rel = kr - qi
    valid = (rel >= 0) & (rel <= 2 * w) & (kpos >= 0) & (kpos < n)
    s_band = jnp.where(valid[None, :, None, None], s_band, -jnp.inf)
    s_sink = jnp.broadcast_to(sink.astype(F32).reshape(1, 1, g, r, 1, 1), s_band.shape[:-1] + (1,))
    p = jax.nn.softmax(jnp.concatenate([s_band, s_ctx, s_sink], axis=-1), axis=-1).astype(v.dtype)
    n_band = 3 * w
    o = (jnp.einsum('bngrqk,bnkgd->bnqgrd', p[..., :n_band], vb, preferred_element_type=F32)
         + jnp.einsum('bngrqc,bcgd->bnqgrd', p[..., n_band:n_band + k_ctx.shape[1]], v_ctx,
                      preferred_element_type=F32))
    return o.reshape(b, n, h, dh).astype(q.dtype)


def to_chunks(t):
    b, n, h = t.shape[:3]
    t = t.reshape(b, n // CHUNK, CHUNK, h, *t.shape[3:])
    return jnp.moveaxis(t, (1, 3), (0, 2))


def from_chunks(t):
    t = jnp.moveaxis(t, (0, 2), (1, 3))
    return t.reshape(t.shape[0], t.shape[1] * t.shape[2], t.shape[3], t.shape[4])


def gated_delta_chunked(q, k, v, g, beta, s0):
    dt = v.dtype
    q, k, v, g, beta = (to_chunks(t.astype(F32)) for t in (q, k, v, g, beta))
    gc = jnp.cumsum(g, axis=-1)
    idx = jnp.arange(CHUNK)
    incl = idx[:, None] >= idx[None, :]
    strict = idx[:, None] > idx[None, :]
    decay = jnp.exp(jnp.where(incl, gc[..., :, None] - gc[..., None, :], -jnp.inf))
    a = jnp.where(strict, beta[..., :, None] * jnp.einsum('nbhid,nbhjd->nbhij', k, k) * decay, 0.0)
    t_mat = a + jnp.eye(CHUNK, dtype=F32)
    solve = functools.partial(lax.linalg.triangular_solve, left_side=True, lower=True, unit_diagonal=True)
    u = solve(t_mat, v * beta[..., None])
    w = solve(t_mat, k * (beta * jnp.exp(gc))[..., None])
    qk = jnp.einsum('nbhid,nbhjd->nbhij', q, k) * decay
    q_dec = q * jnp.exp(gc)[..., None]
    k_dec = k * jnp.exp(gc[..., -1:] - gc)[..., None]
    g_last = jnp.exp(gc[..., -1])

    def step(s, xs):
        u_i, w_i, qk_i, qd_i, kd_i, gl_i = xs
        v_new = u_i - jnp.einsum('bhcd,bhde->bhce', w_i, s)
        o = jnp.einsum('bhcd,bhde->bhce', qd_i, s) + jnp.einsum('bhij,bhje->bhie', qk_i, v_new)
        s = s * gl_i[..., None, None] + jnp.einsum('bhcd,bhce->bhde', kd_i, v_new)
        return s, o

    s_fin, o = lax.scan(step, s0, (u, w, qk, q_dec, k_dec, g_last))
    return from_chunks(o).astype(dt), s_fin


def gla_chunked(q, k, v, log_a, s0):
    dt = v.dtype
    q, k, v, log_a = (to_chunks(t.astype(F32)) for t in (q, k, v, log_a))
    bc = jnp.cumsum(log_a, axis=-2)
    idx = jnp.arange(CHUNK)
    incl = idx[:, None] >= idx[None, :]
    q_dec = q * jnp.exp(bc)
    attn = jnp.where(incl, jnp.einsum('nbhid,nbhjd->nbhij', q_dec, k * jnp.exp(-bc)), 0.0)
    o_intra = jnp.einsum('nbhij,nbhje->nbhie', attn, v)
    b_last = bc[..., -1, :]
    k_dec = k * jnp.exp(b_last[..., None, :] - bc)

    def step(s, xs):
        qd_i, kd_i, v_i, bl_i = xs
        o = jnp.einsum('bhcd,bhde->bhce', qd_i, s)
        s = s * jnp.exp(bl_i)[..., :, None] + jnp.einsum('bhcd,bhce->bhde', kd_i, v_i)
        return s, o

    s_fin, o_inter = lax.scan(step, s0, (q_dec, k_dec, v, b_last))
    return from_chunks(o_intra + o_inter).astype(dt), s_fin


def bidir_prefix_scan(scan_fn, ctx_f, lat_f, ctx_b, lat_b, s0, ctx_out):
    flip = lambda t: jnp.flip(t, axis=1)
    oc_f, sc_f = scan_fn(*ctx_f, s0)
    ol_f, _ = scan_fn(*lat_f, sc_f)
    oc_b, sc_b = scan_fn(*[flip(t) for t in ctx_b], s0)
    ol_b, _ = scan_fn(*[flip(t) for t in lat_b], sc_b)
    o_lat = ol_f + flip(ol_b)
    o_ctx = oc_f + flip(oc_b) if ctx_out else None
    return o_lat, o_ctx


def mla_gdn_mixer(xn, hn, w_in, q_a_norm, w_q_up, kv_a_norm, w_kv_up, q_norm, k_norm,
                  conv_w, a_log, dt_bias, out_norm, w_out, rope, ctx_out):
    b = xn.shape[0]
    zx = split_cols(xn @ w_in, EVEN_SPLITS)
    zh = split_cols(hn @ w_in, EVEN_SPLITS)

    def mla_proj(z, rope_tab):
        cq, ckv, kr = z[0], z[1], z[2]
        n = cq.shape[1]
        q = (rms_norm(cq, q_a_norm) @ w_q_up).reshape(b, n, MLA_HEADS, MLA_QK)
        kv = (rms_norm(ckv, kv_a_norm) @ w_kv_up).reshape(b, n, MLA_HEADS, MLA_NOPE + MLA_V)
        k_nope, v = kv[..., :MLA_NOPE], kv[..., MLA_NOPE:]
        k = jnp.concatenate([k_nope, jnp.broadcast_to(kr[:, :, None, :], (b, n, MLA_HEADS, MLA_ROPE))], axis=-1)
        q, k = rms_norm(q, q_norm), rms_norm(k, k_norm)
        if rope_tab is not None:
            q = jnp.concatenate([q[..., :MLA_NOPE], apply_rope(q[..., MLA_NOPE:], *rope_tab)], axis=-1)
            k = jnp.concatenate([k[..., :MLA_NOPE], apply_rope(k[..., MLA_NOPE:], *rope_tab)], axis=-1)
        return q, k, v

    qx, kx, vx = mla_proj(zx, rope)
    qh, kh, vh = mla_proj(zh, None)
    scale = MLA_QK ** -0.5
    o_mla_x = dense_attention(qx, jnp.concatenate([kx, kh], axis=1), jnp.concatenate([vx, vh], axis=1), scale)

    def gdn_inputs(z):
        qkv = jax.nn.silu(centred_depthwise_conv(jnp.concatenate(z[3:6], axis=-1), conv_w))
        q, k, v = split_cols(qkv, (GDN_QK, GDN_QK, GDN_VW))
        n = q.shape[1]
        q = l2_norm(q.reshape(b, n, GDN_HEADS, GDN_DK)) * GDN_DK ** -0.5
        k = l2_norm(k.reshape(b, n, GDN_HEADS, GDN_DK))
        v = v.reshape(b, n, GDN_HEADS, GDN_DV)
        dirs = []
        for d in range(2):
            g = -jnp.exp(a_log[d].astype(F32)) * jax.nn.softplus(z[6 + d].astype(F32) + dt_bias[d].astype(F32))
            beta = jax.nn.sigmoid(z[8 + d].astype(F32))
            dirs.append((q, k, v, g, beta))
        return dirs

    lat_f, lat_b = gdn_inputs(zx)
    ctx_f, ctx_b = gdn_inputs(zh)
    s0 = jnp.zeros((b, GDN_HEADS, GDN_DK, GDN_DV), F32)
    o_gdn_x, o_gdn_h = bidir_prefix_scan(gated_delta_chunked, ctx_f, lat_f, ctx_b, lat_b, s0, ctx_out)

    def gdn_out(o, zg):
        n = o.shape[1]
        return (rms_norm(o, out_norm) * jax.nn.silu(zg.reshape(b, n, GDN_HEADS, GDN_DV))).reshape(b, n, GDN_VW)

    n_lat = xn.shape[1]
    ox = jnp.concatenate([o_mla_x.reshape(b, n_lat, MLA_VW), gdn_out(o_gdn_x, zx[10])], axis=-1) @ w_out
    if not ctx_out:
        return ox, None
    o_mla_h = dense_attention(qh, kh, vh, scale)
    n_ctx = hn.shape[1]
    oh = jnp.concatenate([o_mla_h.reshape(b, n_ctx, MLA_VW), gdn_out(o_gdn_h, zh[10])], axis=-1) @ w_out
    return ox, oh


def gla_swa_mixer(xn, hn, w_in, gate_w2, gate_b, gla_out_norm, q_norm, k_norm, sink, w_out, rope, ctx_out):
    b = xn.shape[0]
    zx = split_cols(xn @ w_in, ODD_SPLITS)
    zh = split_cols(hn @ w_in, ODD_SPLITS)

    def gla_inputs(z):
        n = z[0].shape[1]
        q = z[0].reshape(b, n, GLA_HEADS, GLA_DK) * GLA_DK ** -0.5
        k = z[1].reshape(b, n, GLA_HEADS, GLA_DK)
        v = z[2].reshape(b, n, GLA_HEADS, GLA_DV)
        dirs = []
        for d in range(2):
            logit = z[3 + d] @ gate_w2[d] + gate_b[d]
            log_a = jax.nn.log_sigmoid(logit.astype(F32)) / GLA_TAU
            dirs.append((q, k, v, log_a.reshape(b, n, GLA_HEADS, GLA_DK)))
        return dirs

    lat_f, lat_b = gla_inputs(zx)
    ctx_f, ctx_b = gla_inputs(zh)
    s0 = jnp.zeros((b, GLA_HEADS, GLA_DK, GLA_DV), F32)
    o_gla_x, o_gla_h = bidir_prefix_scan(gla_chunked, ctx_f, lat_f, ctx_b, lat_b, s0, ctx_out)

    def gla_out(o, r):
        n = o.shape[1]
        return (rms_norm(o, gla_out_norm) * jax.nn.silu(r.reshape(b, n, GLA_HEADS, GLA_DV))).reshape(b, n, GLA_VW)

    def swa_proj(z, rope_tab):
        n = z[6].shape[1]
        q = rms_norm(z[6].reshape(b, n, SWA_HEADS, SWA_DH), q_norm)
        k = rms_norm(z[7].reshape(b, n, SWA_KV_HEADS, SWA_DH), k_norm)
        v = z[8].reshape(b, n, SWA_KV_HEADS, SWA_DH)
        if rope_tab is not None:
            q, k = apply_rope(q, *rope_tab), apply_rope(k, *rope_tab)
        return q, k, v

    qx, kx, vx = swa_proj(zx, rope)
    qh, kh, vh = swa_proj(zh, None)
    scale = SWA_DH ** -0.5
    o_swa_x = banded_window_attention(qx, kx, vx, kh, vh, sink, scale)
    n_lat = xn.shape[1]
    ox = jnp.concatenate([gla_out(o_gla_x, zx[5]), o_swa_x.reshape(b, n_lat, SWA_QW)], axis=-1) @ w_out
    if not ctx_out:
        return ox, None
    o_swa_h = sink_attention(qh, kh, vh, sink, scale)
    n_ctx = hn.shape[1]
    oh = jnp.concatenate([gla_out(o_gla_h, zh[5]), o_swa_h.reshape(b, n_ctx, SWA_QW)], axis=-1) @ w_out
    return ox, oh


def setup_inputs(seed: int = 0) -> dict:
    key = jax.random.key(seed)
    keys = iter(jax.random.split(key, 32))

    def normal(shape, std):
        return jax.random.normal(next(keys), shape, F32) * std

    def gain(shape):
        return 1.0 + 0.05 * jax.random.normal(next(keys), shape, F32)

    d = D_MODEL
    ne, no = N_EVEN, N_ODD
    x = normal((BATCH, SEQ, d), 1.0)
    c = normal((BATCH, d), 1.0)
    ctx = normal((BATCH, CTX_LEN, d), 1.0)
    c_ctx = normal((d,), 1.0)
    ada_w = normal((DEPTH, d, N_MOD * d), 0.5 * d ** -0.5)
    ada_b = normal((DEPTH, N_MOD * d), 0.02)
    norm_g = gain((DEPTH, 3, d))
    ffn_w_gate = normal((DEPTH, 2, d, D_FF), d ** -0.5)
    ffn_w_up = normal((DEPTH, 2, d, D_FF), d ** -0.5)
    ffn_w_down = normal((DEPTH, 2, D_FF, d), D_FF ** -0.5)
    ev_w_in = normal((ne, d, EVEN_IN), d ** -0.5)
    ev_q_a_norm = gain((ne, MLA_Q_RANK))
    ev_w_q_up = normal((ne, MLA_Q_RANK, MLA_HEADS * MLA_QK), MLA_Q_RANK ** -0.5)
    ev_kv_a_norm = gain((ne, MLA_KV_RANK))
    ev_w_kv_up = normal((ne, MLA_KV_RANK, MLA_HEADS * (MLA_NOPE + MLA_V)), MLA_KV_RANK ** -0.5)
    ev_mla_q_norm = gain((ne, MLA_QK))
    ev_mla_k_norm = gain((ne, MLA_QK))
    ev_gdn_conv = normal((ne, GDN_CONV, 2 * GDN_QK + GDN_VW), GDN_CONV ** -0.5)
    ev_gdn_a_log = jnp.log(jax.random.uniform(next(keys), (ne, 2, GDN_HEADS), F32, 1.0, 16.0))
    dt = jnp.exp(jax.random.uniform(next(keys), (ne, 2, GDN_HEADS), F32, math.log(1e-3), math.log(1e-1)))
    ev_gdn_dt_bias = dt + jnp.log(-jnp.expm1(-dt))
    ev_gdn_out_norm = gain((ne, GDN_DV))
    ev_w_out = normal((ne, MIX_EVEN, d), MIX_EVEN ** -0.5)
    od_w_in = normal((no, d, ODD_IN), d ** -0.5)
    od_gla_gate_w2 = normal((no, 2, GLA_RANK, GLA_QK), GLA_RANK ** -0.5)
    od_gla_gate_b = normal((no, 2, GLA_QK), 0.1)
    od_gla_out_norm = gain((no, GLA_DV))
    od_swa_q_norm = gain((no, SWA_DH))
    od_swa_k_norm = gain((no, SWA_DH))
    od_swa_sink = normal((no, SWA_HEADS), 0.5)
    od_w_out = normal((no, MIX_ODD, d), MIX_ODD ** -0.5)
    return {
        'x': x, 'c': c, 'ctx': ctx, 'c_ctx': c_ctx,
        'ada_w': ada_w, 'ada_b': ada_b, 'norm_g': norm_g,
        'ffn_w_gate': ffn_w_gate, 'ffn_w_up': ffn_w_up, 'ffn_w_down': ffn_w_down,
        'ev_w_in': ev_w_in, 'ev_q_a_norm': ev_q_a_norm, 'ev_w_q_up': ev_w_q_up,
        'ev_kv_a_norm': ev_kv_a_norm, 'ev_w_kv_up': ev_w_kv_up,
        'ev_mla_q_norm': ev_mla_q_norm, 'ev_mla_k_norm': ev_mla_k_norm,
        'ev_gdn_conv': ev_gdn_conv, 'ev_gdn_a_log': ev_gdn_a_log, 'ev_gdn_dt_bias': ev_gdn_dt_bias,
        'ev_gdn_out_norm': ev_gdn_out_norm, 'ev_w_out': ev_w_out,
        'od_w_in': od_w_in, 'od_gla_gate_w2': od_gla_gate_w2, 'od_gla_gate_b': od_gla_gate_b,
        'od_gla_out_norm': od_gla_out_norm, 'od_swa_q_norm': od_swa_q_norm, 'od_swa_k_norm': od_swa_k_norm,
        'od_swa_sink': od_swa_sink, 'od_w_out': od_w_out,
    }


def reference(x, c, ctx, c_ctx, ada_w, ada_b, norm_g, ffn_w_gate, ffn_w_up, ffn_w_down,
              ev_w_in, ev_q_a_norm, ev_w_q_up, ev_kv_a_norm, ev_w_kv_up, ev_mla_q_norm, ev_mla_k_norm,
              ev_gdn_conv, ev_gdn_a_log, ev_gdn_dt_bias, ev_gdn_out_norm, ev_w_out,
              od_w_in, od_gla_gate_w2, od_gla_gate_b, od_gla_out_norm, od_swa_q_norm, od_swa_k_norm,
              od_swa_sink, od_w_out):
    ROWS = x.shape[1] // GRID_W
    rope_mla = axial_rope_tables(ROWS, MLA_ROPE)
    rope_swa = axial_rope_tables(ROWS, SWA_DH)
    h = ctx
    sc = jax.nn.silu(c)
    scc = jax.nn.silu(c_ctx)
    for i in range(DEPTH):
        ctx_out = i < DEPTH - 1
        mx = jnp.split((sc @ ada_w[i] + ada_b[i])[:, None, :], N_MOD, axis=-1)
        mh = jnp.split((scc @ ada_w[i] + ada_b[i])[None, None, :], N_MOD, axis=-1)
        x = ffn_half_step(x, norm_g[i, 0], mx[0], mx[1], mx[2], ffn_w_gate[i, 0], ffn_w_up[i, 0], ffn_w_down[i, 0])
        h = ffn_half_step(h, norm_g[i, 0], mh[0], mh[1], mh[2], ffn_w_gate[i, 0], ffn_w_up[i, 0], ffn_w_down[i, 0])
        xn = modulate(rms_norm(x, norm_g[i, 1]), mx[3], mx[4])
        hn = modulate(rms_norm(h, norm_g[i, 1]), mh[3], mh[4])
        j = i // 2
        if i % 2 == 0:
            ox, oh = mla_gdn_mixer(xn, hn, ev_w_in[j], ev_q_a_norm[j], ev_w_q_up[j], ev_kv_a_norm[j],
                                   ev_w_kv_up[j], ev_mla_q_norm[j], ev_mla_k_norm[j], ev_gdn_conv[j],
                                   ev_gdn_a_log[j], ev_gdn_dt_bias[j], ev_gdn_out_norm[j], ev_w_out[j],
                                   rope_mla, ctx_out)
        else:
            ox, oh = gla_swa_mixer(xn, hn, od_w_in[j], od_gla_gate_w2[j], od_gla_gate_b[j], od_gla_out_norm[j],
                                   od_swa_q_norm[j], od_swa_k_norm[j], od_swa_sink[j], od_w_out[j],
                                   rope_swa, ctx_out)
        x = x + mx[5] * ox
        x = ffn_half_step(x, norm_g[i, 2], mx[6], mx[7], mx[8], ffn_w_gate[i, 1], ffn_w_up[i, 1], ffn_w_down[i, 1])
        if ctx_out:
            h = h + mh[5] * oh
            h = ffn_half_step(h, norm_g[i, 2], mh[6], mh[7], mh[8], ffn_w_gate[i, 1], ffn_w_up[i, 1], ffn_w_down[i, 1])
    return x
```

```python
from contextlib import ExitStack
import numpy as np
import concourse.bass as bass
import concourse.mybir as mybir
from concourse.bass_utils import run_bass_kernel_spmd

F32 = mybir.dt.float32
BF16 = mybir.dt.bfloat16
AF = mybir.ActivationFunctionType
ALU = mybir.AluOpType
AX = mybir.AxisListType

D = 1024
DFF = 2816
NF = DFF // 128
SEQ = 2048
CTX = 256
T = SEQ + CTX
NT = T // 128
EPS = 1e-6
NCORES = 8

COMPUTE = ("tensor", "vector", "scalar", "gpsimd")
ENGINES = ("tensor", "vector", "scalar", "gpsimd", "sync")
N_DMA_SEMS = 24


class Buf:
    _n = 0

    def __init__(self, t, nslots=1, name=None):
        self.t = t
        self.nslots = nslots
        Buf._n += 1
        self.id = Buf._n
        self.name = name

    def __getitem__(self, idx):
        return self.t[idx]


class Sync:
    def __init__(self, nc, es):
        self.nc = nc
        self.sems = {e: es.enter_context(nc.semaphore("s_" + e)) for e in COMPUTE}
        self.dsems = [es.enter_context(nc.semaphore("d_%d" % i)) for i in range(N_DMA_SEMS)]
        self.pos = {e: 0 for e in COMPUTE}
        self.n_dma = 0
        self.dma_uses = [0] * N_DMA_SEMS


class Prog:
    def __init__(self, nc, sync, name="ph"):
        self.nc = nc
        self.sy = sync
        self.name = name
        self.es = ExitStack()
        self.ops = []
        self.state = {}
        self.dma_last = [None] * N_DMA_SEMS
        self.start_pos = dict(sync.pos)
        self.start_dma = [16 * u for u in sync.dma_uses]

    def sbuf(self, name, shape, dtype, nslots=1):
        t = self.es.enter_context(self.nc.sbuf_tensor(self.name + "_" + name, list(shape), dtype))
        return Buf(t, nslots, name)

    def psum(self, name, shape, dtype=F32, nslots=1):
        t = self.es.enter_context(self.nc.psum_tensor(self.name + "_" + name, list(shape), dtype))
        return Buf(t, nslots, name)

    def _expand(self, keys):
        out = []
        for k in keys:
            if isinstance(k, Buf):
                k = (k, None)
            b, s = k
            if s is None:
                out.extend((b.id, i) for i in range(b.nslots))
            elif isinstance(s, (list, tuple, range)):
                for i in s:
                    assert 0 <= i < b.nslots, (b.name, i, b.nslots)
                    out.append((b.id, i))
            else:
                assert 0 <= s < b.nslots, (b.name, s, b.nslots)
                out.append((b.id, s))
        return out

    def _deps(self, idx, reads, writes):
        deps = set()
        rk = self._expand(reads)
        wk = self._expand(writes)
        for k in rk:
            st = self.state.setdefault(k, [None, []])
            if st[0] is not None:
                deps.add(st[0])
        for k in wk:
            st = self.state.setdefault(k, [None, []])
            if st[0] is not None:
                deps.add(st[0])
            deps.update(st[1])
        for k in rk:
            self.state[k][1].append(idx)
        for k in wk:
            self.state[k] = [idx, []]
        deps.discard(idx)
        return deps

    def op(self, eng, fn, reads=(), writes=()):
        idx = len(self.ops)
        deps = self._deps(idx, reads, writes)
        self.ops.append(dict(eng=eng, fn=fn, deps=deps, dma=False, pos=self.sy.pos[eng]))
        self.sy.pos[eng] += 1
        return idx

    def dma(self, out, in_, reads=(), writes=(), q="sync"):
        idx = len(self.ops)
        deps = self._deps(idx, reads, writes)
        sy = self.sy
        s = sy.n_dma % N_DMA_SEMS
        sy.n_dma += 1
        prev = self.dma_last[s]
        sy.dma_uses[s] += 1
        self.dma_last[s] = idx
        self.ops.append(dict(eng=q, fn=lambda e: e.dma_start(out=out, in_=in_), deps=deps, dma=True, sem=s,
                             target=16 * sy.dma_uses[s], prev=prev))
        return idx

    def mm(self, out, lhsT, rhs, start, stop, r, w):
        return self.op("tensor", lambda e: e.matmul(out, lhsT=lhsT, rhs=rhs, start=start, stop=stop), r, w)

    def tr(self, out, in_, ident, r, w):
        return self.op("tensor", lambda e: e.transpose(out, in_, ident), r, w)

    def act(self, out, in_, func, r, w, bias=None, scale=None, accum_out=None):
        kw = {}
        if bias is not None:
            kw["bias"] = bias
        if scale is not None:
            kw["scale"] = scale
        if accum_out is not None:
            kw["accum_out"] = accum_out
        return self.op("scalar", lambda e: e.activation(out=out, in_=in_, func=func, **kw), r, w)

    def tt(self, out, in0, in1, op, r, w, eng="vector"):
        return self.op(eng, lambda e: e.tensor_tensor(out=out, in0=in0, in1=in1, op=op), r, w)

    def ts(self, out, in0, s1, op0, r, w, s2=None, op1=None, eng="vector"):
        if op1 is None:
            return self.op(eng, lambda e: e.tensor_scalar(out=out, in0=in0, scalar1=s1, scalar2=None, op0=op0), r, w)
        return self.op(eng, lambda e: e.tensor_scalar(out=out, in0=in0, scalar1=s1, scalar2=s2, op0=op0, op1=op1), r, w)

    def stt(self, out, in0, scalar, in1, op0, op1, r, w):
        return self.op("vector", lambda e: e.scalar_tensor_tensor(out=out, in0=in0, scalar=scalar, in1=in1,
                                                                  op0=op0, op1=op1), r, w)

    def copy(self, out, in_, r, w, eng="vector"):
        return self.op(eng, lambda e: e.tensor_copy(out=out, in_=in_), r, w)

    def memset(self, ap, val, w, eng="vector"):
        return self.op(eng, lambda e: e.memset(ap, val), (), w)

    def emit(self):
        nc = self.nc
        sy = self.sy
        ops = self.ops
        per_eng = {e: [] for e in ENGINES}
        for i, o in enumerate(ops):
            per_eng[o["eng"]].append(i)
        end_pos = dict(sy.pos)
        end_dma = [16 * u for u in sy.dma_uses]
        start_pos = self.start_pos
        start_dma = self.start_dma

        def replay(ename):
            def body(eng):
                waited = dict(start_pos)
                dwaited = list(start_dma)
                for i in per_eng[ename]:
                    o = ops[i]
                    need = {}
                    dneed = {}
                    deps = set(o["deps"])
                    if o["dma"] and o["prev"] is not None:
                        deps.add(o["prev"])
                    for j in deps:
                        d = ops[j]
                        if d["dma"]:
                            dneed[d["sem"]] = max(dneed.get(d["sem"], 0), d["target"])
                        else:
                            if d["eng"] == "tensor" and ename == "tensor" and not o["dma"]:
                                continue
                            need[d["eng"]] = max(need.get(d["eng"], 0), d["pos"] + 1)
                    for e, v in need.items():
                        if v > waited[e]:
                            eng.wait_ge(sy.sems[e], v)
                            waited[e] = v
                    for s, v in dneed.items():
                        if v > dwaited[s]:
                            eng.wait_ge(sy.dsems[s], v)
                            dwaited[s] = v
                    ins = o["fn"](eng)
                    if o["dma"]:
                        ins.then_inc(sy.dsems[o["sem"]], 16)
                    else:
                        ins.then_inc(sy.sems[ename], 1)
                if ename == "sync":
                    for s in range(N_DMA_SEMS):
                        if end_dma[s] > start_dma[s]:
                            eng.wait_ge(sy.dsems[s], end_dma[s])
                    for e in COMPUTE:
                        if end_pos[e] > start_pos[e]:
                            eng.wait_ge(sy.sems[e], end_pos[e])
            return body

        with nc.Block() as block:
            for e in ENGINES:
                if per_eng[e] or e == "sync":
                    getattr(block, e)(replay(e))
        self.es.close()


def segs(t0, n):
    out = []
    if t0 < CTX:
        m = min(n, CTX - t0)
        out.append((t0, m, 1))
        if n > m:
            out.append((CTX, n - m, 0))
    else:
        out.append((t0, n, 0))
    return out


def tiles_of(t0, n):
    return list(range(t0 // 128, (t0 + n + 127) // 128))


class K:
    pass


class Rot:
    def __init__(self, mk, n):
        self.t = [mk(i) for i in range(n)]
        self.i = 0

    def next(self):
        t = self.t[self.i % len(self.t)]
        self.i += 1
        return t


GROUPS5 = [(0, 256), (256, 512), (768, 512), (1280, 512), (1792, 512)]
LATG = GROUPS5[1:]
FULL = [[(0, 384), (384, 384), (768, 384)], [(1152, 384), (1536, 384), (1920, 384)]]
LAT = [[(256, 512), (768, 512)], [(1280, 512), (1792, 512)]]
C_ID, C_BD64, C_ROT64, C_UI, C_LI, C_SL, C_SU, C_ROT96 = range(8)
NCST = 8


def build(stage=99, dbg=(), layers=(0, 1), parts=None, skip_ffn1=False):
    nc = bass.Bass("TRN2", target_bir_lowering=False)
    dt = nc.dram_tensor

    def inp(name, shape, dtype=F32):
        return dt(name, list(shape), dtype, kind="ExternalInput").ap()

    def scratch(name, shape, dtype, nslots=1):
        return Buf(dt("scr_" + name, list(shape), dtype, kind="Internal").ap(), nslots, name)

    x_d = inp("x", [SEQ, D])
    ctx_d = inp("ctx", [CTX, D])
    ccol_d = inp("ccol", [128, 16])
    adaw_d = inp("ada_w", [2, D, 9 * D])
    adab_d = inp("ada_b_col", [128, 2, 72])
    ng_d = inp("norm_g_col", [128, 2, 3, 8])
    wg_d = inp("ffn_w_gate", [2, 2, D, DFF])
    wu_d = inp("ffn_w_up", [2, 2, D, DFF])
    wd_d = inp("ffn_w_down", [2, 2, DFF, D])
    cst_d = inp("cst", [128, NCST, 128])
    bdsel_d = inp("bdsel", [128, 2])
    odw_d = inp("od_w_in", [D, 2336])
    odwo_d = inp("od_w_out", [D, D])
    odcol_d = inp("od_cols", [128, 11])
    swcs_d = inp("swa_cs", [128, 2, SEQ])
    gw2b_d = inp("gla_w2b", [17, 2, 256])
    evw_d = inp("ev_w_in", [D, 2752])
    evwo_d = inp("ev_w_out", [D, D])
    evqu_d = inp("ev_w_q_up", [384, 768])
    evkv_d = inp("ev_w_kv_up", [256, 1024])
    evcol_d = inp("ev_cols", [128, 8])
    mlcs_d = inp("mla_cs", [96, 2, SEQ])
    lvm_d = inp("lvm", [128, 14, 128])
    gcv_d = inp("gdn_conv_col", [128, 12, 5])
    grow_d = inp("gdn_rows", [128, 544])
    out_d = dt("out", [SEQ, D], F32, kind="ExternalOutput").ap()
    dbg_d = {}
    for name, shape in dbg:
        dbg_d[name] = dt("dbg_" + name, list(shape), F32, kind="ExternalOutput").ap()

    xn_s = scratch("xn", [128, 8, T], BF16)
    mix_s = scratch("mix", [128, 8, T], BF16)
    swq_s = scratch("swq", [128, 4, SEQ], BF16)
    swk_s = scratch("swk", [128, T], BF16)
    swv_s = scratch("swv", [128, NT * 256], BF16)
    swr_s = scratch("swr", [128, NT * 2], F32)
    glq_s = [scratch("glq%d" % d, [128, 2, T], BF16) for d in range(2)]
    glk_s = [scratch("glk%d" % d, [128, 2, T], BF16) for d in range(2)]
    glkd_s = [scratch("glkd%d" % d, [T, 256], BF16) for d in range(2)]
    glet_s = [scratch("glet%d" % d, [128, 2, NT], F32) for d in range(2)]
    glv_s = scratch("glv", [T, 512], BF16)
    glr_s = scratch("glr", [128, 4, SEQ], BF16)

    mqf_s = scratch("mqf", [96, 8, T], BF16)
    mkf_s = scratch("mkf", [96, 8, T], BF16)
    mv_s = scratch("mv", [T, 512], BF16)
    mrk_s = scratch("mrk", [128, NT * 8], F32)
    gq_s = scratch("gq", [512, T], F32)
    gk_s = scratch("gk", [512, T], F32)
    gktm_s = scratch("gktm", [T, 512], F32)
    gvtm_s = scratch("gvtm", [T, 512], F32)
    gg_s = scratch("gg", [128, NT * 16], F32)
    gb_s = scratch("gb", [128, NT * 16], F32)
    ggate_s = scratch("ggate", [T, 512], BF16)
    og_s = [scratch("og%d" % d, [T, 512], F32) for d in range(2)]

    es = ExitStack()
    sy = Sync(nc, es)

    def persist(name, shape, dtype, nslots=1):
        return Buf(es.enter_context(nc.sbuf_tensor(name, list(shape), dtype)), nslots, name)

    XT = persist("XT", [128, 8, T], F32, NT)
    cst = persist("cst_s", [128, NCST, 128], F32)
    ident = cst[:, C_ID, :]
    ones_f = persist("ones_f", [128, 128], F32)
    ones_b = persist("ones_b", [128, 128], BF16)
    modT = persist("modT", [128, 2, 72, 2], F32)
    AG = persist("AG", [128, 2, 3, 8, 2], F32)
    GH = persist("GH", [128, 2, 3, 8, 2], F32)
    ng = persist("ng", [128, 2, 3, 8], F32)

    def xk(t0, n):
        return [(XT, i) for i in tiles_of(t0, n)]

    P = Prog(nc, sy, "p0")
    P.dma(cst[:], cst_d, writes=[cst])
    P.dma(ng[:], ng_d, writes=[ng])
    P.memset(ones_f[:], 1.0, [ones_f])
    P.memset(ones_b[:], 1.0, [ones_b])
    xin = [P.sbuf("xin%d" % i, [128, D], F32) for i in range(2)]
    ptr = [P.psum("ptr%d" % i, [128, 4, 128]) for i in range(2)]
    for tt in range(NT):
        st = xin[tt % 2]
        src = ctx_d[tt * 128:(tt + 1) * 128, :] if tt < 2 else x_d[(tt - 2) * 128:(tt - 1) * 128, :]
        P.dma(st[:], src, writes=[st])
        for half in range(2):
            ps = ptr[half]
            for c4 in range(4):
                c = half * 4 + c4
                P.tr(ps[:, c4, :], st[:, c * 128:(c + 1) * 128], ident, [st, cst], [ps])
            if half == 0:
                P.copy(XT[:, 0:4, tt * 128:(tt + 1) * 128], ps[:], [ps], [(XT, tt)])
            else:
                P.act(XT[:, 4:8, tt * 128:(tt + 1) * 128], ps[:], AF.Copy, [ps], [(XT, tt)])

    ccol = P.sbuf("ccol", [128, 16], F32)
    sc3 = P.sbuf("sc3", [128, 8, 2], F32)
    adab = P.sbuf("adab", [128, 2, 72], F32)
    P.dma(ccol[:], ccol_d, writes=[ccol])
    P.dma(adab[:], adab_d, writes=[adab])
    P.act(sc3[:, :, 0], ccol[:, 0:8], AF.Silu, [ccol], [sc3])
    P.act(sc3[:, :, 1], ccol[:, 8:16], AF.Silu, [ccol], [sc3])
    ABLK = 256
    awst = [P.sbuf("awst%d" % i, [128, 8, ABLK], F32) for i in range(2)]
    pmod = P.psum("pmod", [128, 72, 2])
    nb = 0
    for l in range(2):
        aw = adaw_d[l].rearrange("(kc p) n -> p kc n", p=128)
        for blk in range(9 * D // ABLK):
            st = awst[nb % 2]
            nb += 1
            P.dma(st[:], aw[:, :, blk * ABLK:(blk + 1) * ABLK], writes=[st])
            for m in range(ABLK // 128):
                j = blk * (ABLK // 128) + m
                for kc in range(8):
                    P.mm(pmod[:, j, :], st[:, kc, m * 128:(m + 1) * 128], sc3[:, kc, :], kc == 0, kc == 7,
                         [st, sc3], [pmod])
        for s in range(2):
            P.tt(modT[:, l, :, s], pmod[:, :, s], adab[:, l, :], ALU.add, [pmod, adab], [modT])
    for l in range(2):
        for n in range(3):
            for s in range(2):
                P.stt(AG[:, l, n, :, s], modT[:, l, (3 * n + 1) * 8:(3 * n + 2) * 8, s], 1.0, ng[:, l, n, :],
                      ALU.add, ALU.mult, [modT, ng], [AG])
                P.ts(GH[:, l, n, :, s], modT[:, l, (3 * n + 2) * 8:(3 * n + 3) * 8, s], 0.5 if n != 1 else 1.0,
                     ALU.mult, [modT], [GH])
    if "modT" in dbg_d:
        P.dma(dbg_d["modT"], modT[:].rearrange("p l j s -> p (l j s)"), reads=[modT])
    P.emit()

    def norm_mod(P, W, l, n_idx, t0, n, dst_fn, dst_keys):
        sq, lnv, rs, pss = (w.next() for w in W)
        P.act(sq[:, :, 0:n], XT[:, :, t0:t0 + n], AF.Square, xk(t0, n), [sq])
        for c in range(8):
            P.mm(pss[:, 0:n], ones_f[:], sq[:, c, 0:n], c == 0, c == 7, [sq, ones_f], [pss])
        P.act(lnv[:, 0:n], pss[:, 0:n], AF.Ln, [pss], [lnv], bias=EPS, scale=1.0 / D)
        P.act(rs[:, 0:n], lnv[:, 0:n], AF.Exp, [lnv], [rs], scale=-0.5)
        P.tt(sq[:, :, 0:n], XT[:, :, t0:t0 + n], rs[:, 0:n].unsqueeze(1).broadcast_to([128, 8, n]), ALU.mult,
             xk(t0, n) + [rs, sq], [sq])
        for (s0, sn, s) in segs(t0, n):
            for c in range(8):
                P.act(dst_fn(c, s0, sn), sq[:, c, s0 - t0:s0 - t0 + sn], AF.Identity,
                      [sq], dst_keys, bias=modT[:, l, (3 * n_idx) * 8 + c, s:s + 1],
                      scale=AG[:, l, n_idx, c, s:s + 1])

    def norm_ws(P, nmax, nb=1):
        return (Rot(lambda i: P.sbuf("sq%d" % i, [128, 8, nmax], F32), nb),
                Rot(lambda i: P.sbuf("lnv%d" % i, [128, nmax], F32), nb),
                Rot(lambda i: P.sbuf("rs%d" % i, [128, nmax], F32), nb),
                Rot(lambda i: P.psum("pss%d" % i, [128, 512]), nb))

    def ffn(l, which, halves):
        n_idx = 0 if which == 0 else 2
        wg_v = wg_d[l, which].rearrange("(kc p) n -> p kc n", p=128)
        wu_v = wu_d[l, which].rearrange("(kc p) n -> p kc n", p=128)
        wd_v = wd_d[l, which].rearrange("(f p) n -> p f n", p=128)
        for hi, groups in enumerate(halves):
            h0 = groups[0][0]
            NH = sum(g[1] for g in groups)
            ng_ = len(groups)
            tag = "f%d%d%d" % (l, which, hi)
            esA = ExitStack()
            AT = Buf(esA.enter_context(nc.sbuf_tensor(tag + "AT", [128, NF, NH], BF16)), NF * ng_, "AT")
            P = Prog(nc, sy, tag + "a")
            hT = P.sbuf("hT", [128, 8, NH], BF16, ng_)
            nmax = max(g[1] for g in groups)
            W = norm_ws(P, nmax)
            for gi, (t0, n) in enumerate(groups):
                norm_mod(P, W, l, n_idx, t0, n, lambda c, s0, sn: hT[:, c, s0 - h0:s0 - h0 + sn], [(hT, gi)])
            stg = P.sbuf("stg", [128, 8, 256], F32)
            stu = P.sbuf("stu", [128, 8, 256], F32)
            wgb = [P.sbuf("wgb%d" % i, [128, 8, 256], BF16) for i in range(2)]
            wub = [P.sbuf("wub%d" % i, [128, 8, 256], BF16) for i in range(2)]
            psg = [P.psum("psg%d" % i, [128, 512]) for i in range(2)]
            psu = [P.psum("psu%d" % i, [128, 512]) for i in range(2)]
            sga = [P.sbuf("sga%d" % i, [128, nmax], F32) for i in range(2)]
            it = 0
            NB = NF // 2

            def fetch_a(fb):
                P.dma(stg[:], wg_v[:, :, fb * 256:(fb + 1) * 256], writes=[stg])
                P.dma(stu[:], wu_v[:, :, fb * 256:(fb + 1) * 256], writes=[stu])

            def cast_a(fb):
                P.act(wgb[fb % 2][:], stg[:], AF.Copy, [stg], [wgb[fb % 2]])
                P.copy(wub[fb % 2][:], stu[:], [stu], [wub[fb % 2]])

            fetch_a(0)
            cast_a(0)
            for fb in range(NB):
                pb = fb % 2
                if fb + 1 < NB:
                    fetch_a(fb + 1)
                cnt = 0
                for f2 in range(2):
                    f = fb * 2 + f2
                    for gi, (t0, n) in enumerate(groups):
                        o0 = t0 - h0
                        pg, pu, sg = psg[it % 2], psu[it % 2], sga[it % 2]
                        it += 1
                        for kc in range(8):
                            P.mm(pg[:, 0:n], wgb[pb][:, kc, f2 * 128:(f2 + 1) * 128], hT[:, kc, o0:o0 + n], kc == 0, kc == 7,
                                 [wgb[pb], (hT, gi)], [pg])
                        for kc in range(8):
                            P.mm(pu[:, 0:n], wub[pb][:, kc, f2 * 128:(f2 + 1) * 128], hT[:, kc, o0:o0 + n], kc == 0, kc == 7,
                                 [wub[pb], (hT, gi)], [pu])
                        P.act(sg[:, 0:n], pg[:, 0:n], AF.Silu, [pg], [sg])
                        P.tt(AT[:, f, o0:o0 + n], sg[:, 0:n], pu[:, 0:n], ALU.mult, [sg, pu], [(AT, f * ng_ + gi)])
                        cnt += 1
                        if cnt == ng_ and fb + 1 < NB:
                            cast_a(fb + 1)
            P.emit()
            P = Prog(nc, sy, tag + "b")
            std = P.sbuf("std", [128, NF, 256], F32)
            wdb = [P.sbuf("wdb%d" % i, [128, NF, 256], BF16) for i in range(2)]
            psy = [P.psum("psy%d" % i, [128, 512]) for i in range(2)]
            it = 0

            def fetch_b(db):
                P.dma(std[:], wd_v[:, :, db * 256:(db + 1) * 256], writes=[std])
                P.act(wdb[db % 2][:], std[:], AF.Copy, [std], [wdb[db % 2]])

            fetch_b(0)
            for db in range(4):
                pb = db % 2
                if db + 1 < 4:
                    fetch_b(db + 1)
                for d2 in range(2):
                    d = db * 2 + d2
                    for gi, (t0, n) in enumerate(groups):
                        o0 = t0 - h0
                        py = psy[it % 2]
                        it += 1
                        for f in range(NF):
                            P.mm(py[:, 0:n], wdb[pb][:, f, d2 * 128:(d2 + 1) * 128], AT[:, f, o0:o0 + n], f == 0, f == NF - 1,
                                 [wdb[pb], (AT, f * ng_ + gi)], [py])
                        for (s0, sn, s) in segs(t0, n):
                            P.stt(XT[:, d, s0:s0 + sn], py[:, s0 - t0:s0 - t0 + sn], GH[:, l, n_idx, d, s:s + 1],
                                  XT[:, d, s0:s0 + sn], ALU.mult, ALU.add, [py] + xk(s0, sn), xk(s0, sn))
            P.emit()
            esA.close()

    def dump_xt(name):
        if name in dbg_d:
            P = Prog(nc, sy, "dbg" + name)
            P.dma(dbg_d[name], XT[:].rearrange("p c t -> p (c t)"), reads=[XT])
            P.emit()

    def make_xn(l):
        P = Prog(nc, sy, "xn%d" % l)
        W = norm_ws(P, 384, 2)
        hb = Rot(lambda i: P.sbuf("hb%d" % i, [128, 8, 384], BF16), 2)
        for (t0, n) in FULL[0] + FULL[1]:
            h = hb.next()
            norm_mod(P, W, l, 1, t0, n, lambda c, s0, sn, h=h, t0=t0: h[:, c, s0 - t0:s0 - t0 + sn], [h])
            P.dma(xn_s[:, :, t0:t0 + n], h[:, :, 0:n], reads=[h], writes=[xn_s])
        P.emit()

    def load_w(P, dst, dst_ap, src_ap, stage, stage_ap):
        P.dma(stage_ap, src_ap, writes=[stage])
        P.copy(dst_ap, stage_ap, [stage], [dst], eng="gpsimd")

    def out_proj(l, wo_d, groups):
        P = Prog(nc, sy, "op%d" % l)
        wo = P.sbuf("wo", [128, 8, D], BF16)
        stw = P.sbuf("stw", [128, 8, 256], F32)
        wv = wo_d.rearrange("(kc p) n -> p kc n", p=128)
        for q in range(4):
            load_w(P, wo, wo[:, :, q * 256:(q + 1) * 256], wv[:, :, q * 256:(q + 1) * 256], stw, stw[:])
        mb = Rot(lambda i: P.sbuf("mb%d" % i, [128, 8, 512], BF16), 2)
        pp = Rot(lambda i: P.psum("pp%d" % i, [128, 512]), 3)
        for (t0, n) in groups:
            m = mb.next()
            P.dma(m[:, :, 0:n], mix_s[:, :, t0:t0 + n], reads=[mix_s], writes=[m])
            for dc in range(8):
                ps = pp.next()
                for kc in range(8):
                    P.mm(ps[:, 0:n], wo[:, kc, dc * 128:(dc + 1) * 128], m[:, kc, 0:n], kc == 0, kc == 7, [wo, m], [ps])
                for (s0, sn, s) in segs(t0, n):
                    P.stt(XT[:, dc, s0:s0 + sn], ps[:, s0 - t0:s0 - t0 + sn], GH[:, l, 1, dc, s:s + 1],
                          XT[:, dc, s0:s0 + sn], ALU.mult, ALU.add, [ps] + xk(s0, sn), xk(s0, sn))
        P.emit()

    def swa_prep():
        P = Prog(nc, sy, "swp")
        wv_ = odw_d.rearrange("(kc p) n -> p kc n", p=128)
        odc = P.sbuf("odc", [128, 11], F32)
        P.dma(odc[:], odcol_d, writes=[odc])
        bds = P.sbuf("bds", [128, 2], F32)
        P.dma(bds[:], bdsel_d, writes=[bds])
        cs = P.sbuf("cs", [128, 2, SEQ], F32)
        P.dma(cs[:], swcs_d, writes=[cs])
        st = P.sbuf("st", [128, 8, 512], F32)
        wq = P.sbuf("wq", [128, 8, 4, 2, 64], BF16)
        wk = P.sbuf("wk", [128, 8, 128], BF16)
        wvv = P.sbuf("wvv", [128, 8, 128], BF16)
        P.dma(st[:], wv_[:, :, 1568:2080], writes=[st])
        st5 = st[:].rearrange("p k (g j d) -> p k g j d", g=2, j=4)
        for g in range(2):
            P.copy(wq[:, :, :, g, :], st5[:, :, g, :, :], [st], [wq], eng="gpsimd")
        load_w(P, wk, wk[:], wv_[:, :, 2080:2208], st, st[:, :, 0:128])
        load_w(P, wvv, wvv[:], wv_[:, :, 2208:2336], st, st[:, :, 0:128])
        xg = Rot(lambda i: P.sbuf("xg%d" % i, [128, 8, 512], BF16), 2)
        pp = Rot(lambda i: P.psum("pp%d" % i, [128, 512]), 3)
        pssR = Rot(lambda i: P.psum("pss%d" % i, [128, 512]), 2)
        prot = P.psum("prot", [128, 512])
        prk = P.psum("prk", [128, NT, 2])
        pv = P.psum("pv", [128, 128])
        sqR = Rot(lambda i: P.sbuf("sq%d" % i, [128, 512], F32), 2)
        lnvR = Rot(lambda i: P.sbuf("lnv%d" % i, [128, 512], F32), 2)
        rsR = Rot(lambda i: P.sbuf("rs%d" % i, [128, 512], F32), 2)
        qgR = Rot(lambda i: P.sbuf("qg%d" % i, [128, 512], F32), 2)
        t1R = Rot(lambda i: P.sbuf("t1%d" % i, [128, 512], F32), 2)
        t2R = Rot(lambda i: P.sbuf("t2%d" % i, [128, 512], F32), 2)
        qo = Rot(lambda i: P.sbuf("qo%d" % i, [128, 512], BF16), 2)
        vd = P.sbuf("vd", [128, NT, 2, 2, 64], BF16)
        rsk = P.sbuf("rsk", [128, NT * 2], F32)
        for (t0, n) in GROUPS5:
            x = xg.next()
            P.dma(x[:, :, 0:n], xn_s[:, :, t0:t0 + n], reads=[xn_s], writes=[x])
            lat = t0 >= CTX
            l0 = t0 - CTX
            if lat:
                for j in range(4):
                    ps = pp.next()
                    for kc in range(8):
                        P.mm(ps[:, 0:n], wq[:, kc, j].rearrange("p g d -> p (g d)"), x[:, kc, 0:n], kc == 0, kc == 7,
                             [wq, x], [ps])
                    sq, pss, lnv, rs, qg, t1, t2 = (sqR.next(), pssR.next(), lnvR.next(), rsR.next(), qgR.next(),
                                                    t1R.next(), t2R.next())
                    P.act(sq[:, 0:n], ps[:, 0:n], AF.Square, [ps], [sq])
                    P.mm(pss[:, 0:n], cst[:, C_BD64, :], sq[:, 0:n], True, True, [cst, sq], [pss])
                    P.act(lnv[:, 0:n], pss[:, 0:n], AF.Ln, [pss], [lnv], bias=EPS, scale=1.0 / 64)
                    P.act(rs[:, 0:n], lnv[:, 0:n], AF.Exp, [lnv], [rs], scale=-0.5)
                    P.act(qg[:, 0:n], ps[:, 0:n], AF.Identity, [ps], [qg], scale=odc[:, 0:1])
                    P.mm(prot[:, 0:n], cst[:, C_ROT64, :], qg[:, 0:n], True, True, [cst, qg], [prot])
                    P.tt(t1[:, 0:n], qg[:, 0:n], cs[:, 0, l0:l0 + n], ALU.mult, [qg, cs], [t1])
                    P.tt(t2[:, 0:n], prot[:, 0:n], cs[:, 1, l0:l0 + n], ALU.mult, [prot, cs], [t2])
                    P.tt(t1[:, 0:n], t1[:, 0:n], t2[:, 0:n], ALU.add, [t1, t2], [t1])
                    q = qo.next()
                    P.tt(q[:, 0:n], t1[:, 0:n], rs[:, 0:n], ALU.mult, [t1, rs], [q])
                    P.dma(swq_s[:, j, l0:l0 + n], q[:, 0:n], reads=[q], writes=[swq_s])
            ps = pp.next()
            for kc in range(8):
                P.mm(ps[:, 0:n], wk[:, kc, :], x[:, kc, 0:n], kc == 0, kc == 7, [wk, x], [ps])
            sq, qg, t1, t2 = sqR.next(), qgR.next(), t1R.next(), t2R.next()
            P.act(sq[:, 0:n], ps[:, 0:n], AF.Square, [ps], [sq])
            for ti in range(n // 128):
                tt = t0 // 128 + ti
                P.mm(prk[:, tt, :], sq[:, ti * 128:(ti + 1) * 128], bds[:], True, True, [sq, bds], [prk])
            q = qo.next()
            if lat:
                P.act(qg[:, 0:n], ps[:, 0:n], AF.Identity, [ps], [qg], scale=odc[:, 1:2])
                P.mm(prot[:, 0:n], cst[:, C_ROT64, :], qg[:, 0:n], True, True, [cst, qg], [prot])
                P.tt(t1[:, 0:n], qg[:, 0:n], cs[:, 0, l0:l0 + n], ALU.mult, [qg, cs], [t1])
                P.tt(t2[:, 0:n], prot[:, 0:n], cs[:, 1, l0:l0 + n], ALU.mult, [prot, cs], [t2])
                P.tt(q[:, 0:n], t1[:, 0:n], t2[:, 0:n], ALU.add, [t1, t2], [q])
            else:
                P.act(q[:, 0:n], ps[:, 0:n], AF.Identity, [ps], [q], scale=odc[:, 1:2])
            P.dma(swk_s[:, t0:t0 + n], q[:, 0:n], reads=[q], writes=[swk_s])
            for ti in range(n // 128):
                tt = t0 // 128 + ti
                for kc in range(8):
                    P.mm(pv[:], x[:, kc, ti * 128:(ti + 1) * 128], wvv[:, kc, :], kc == 0, kc == 7, [x, wvv], [pv])
                pv3 = pv[:].rearrange("p (g d) -> p g d", g=2)
                P.copy(vd[:, tt, :, 0, :], pv3, [pv], [vd])
                P.act(vd[:, tt, :, 1, :], pv3, AF.Copy, [pv], [vd])
        P.act(rsk[:], prk[:].rearrange("p a b -> p (a b)"), AF.Ln, [prk], [rsk], bias=EPS, scale=1.0 / 64)
        P.act(rsk[:], rsk[:], AF.Exp, [rsk], [rsk], scale=-0.5)
        P.ts(rsk[:], rsk[:], 0.125, ALU.mult, [rsk], [rsk])
        P.dma(swr_s[:], rsk[:], reads=[rsk], writes=[swr_s])
        P.dma(swv_s[:], vd[:].rearrange("p a b c d -> p (a b c d)"), reads=[vd], writes=[swv_s])
        P.emit()

    def swa_core():
        P = Prog(nc, sy, "swc")
        q = P.sbuf("q", [128, 4, SEQ], BF16)
        k = P.sbuf("k", [128, T], BF16)
        v = P.sbuf("v", [128, NT, 2, 128], BF16)
        rsk = P.sbuf("rsk", [128, NT, 2], F32)
        odc = P.sbuf("odc", [128, 11], F32)
        esk = P.sbuf("esk", [128, 8], F32)
        mlo = P.sbuf("mlo", [128, 128], BF16)
        mhi = P.sbuf("mhi", [128, 128], BF16)
        P.dma(q[:], swq_s[:], reads=[swq_s], writes=[q])
        P.dma(k[:], swk_s[:], reads=[swk_s], writes=[k])
        P.dma(v[:].rearrange("p a b c -> p (a b c)"), swv_s[:], reads=[swv_s], writes=[v])
        P.dma(rsk[:].rearrange("p a b -> p (a b)"), swr_s[:], reads=[swr_s], writes=[rsk])
        P.dma(odc[:], odcol_d, writes=[odc])
        P.act(esk[:], odc[:, 3:11], AF.Exp, [odc], [esk])
        P.copy(mlo[:], cst[:, C_LI, :], [cst], [mlo])
        P.copy(mhi[:], cst[:, C_UI, :], [cst], [mhi])
        msw = P.sbuf("msw", [128, 4, SEQ], BF16)
        pS = Rot(lambda i: P.psum("pS%d" % i, [128, 4, 128]), 4)
        pO = Rot(lambda i: P.psum("pO%d" % i, [128, 4, 128]), 2)
        pM = Rot(lambda i: P.psum("pM%d" % i, [128, 4, 128]), 2)
        pT = Rot(lambda i: P.sbuf("pT%d" % i, [128, 4, 128], BF16), 4)
        den = P.sbuf("den", [128, 4, 128], F32)
        rden = P.sbuf("rden", [128, 4, 128], F32)
        for n in range(16):
            for g in range(2):
                kts = [(0, None), (1, None)]
                if n - 1 >= 0:
                    kts.append((n + 1, mlo))
                kts.append((n + 2, None))
                if n + 1 <= 15:
                    kts.append((n + 3, mhi))
                po = pO.next()
                pm = pM.next()
                def scores(kt, mask):
                    ps = pS.next()
                    P.mm(ps[:], k[g * 64:(g + 1) * 64, kt * 128:(kt + 1) * 128],
                         q[g * 64:(g + 1) * 64, :, n * 128:(n + 1) * 128], True, True, [k, q], [ps])
                    pt = pT.next()
                    P.act(pt[:], ps[:], AF.Exp, [ps, rsk], [pt], scale=rsk[:, kt, g:g + 1])
                    if mask is not None:
                        P.tt(pt[:], pt[:], mask[:].unsqueeze(1).broadcast_to([128, 4, 128]), ALU.mult, [pt, mask], [pt])
                    return pt

                nxt = scores(*kts[0])
                for ki, (kt, mask) in enumerate(kts):
                    pt = nxt
                    if ki + 1 < len(kts):
                        nxt = scores(*kts[ki + 1])
                    P.mm(po[:], v[:, kt, g, :], pt[:], ki == 0, ki == len(kts) - 1, [v, pt], [po])
                    P.mm(pm[:], ones_b[:], pt[:], ki == 0, ki == len(kts) - 1, [ones_b, pt], [pm])
                P.tt(den[:], pm[:], esk[:, 4 * g:4 * g + 4].unsqueeze(2).broadcast_to([128, 4, 128]), ALU.add,
                     [pm, esk], [den])
                P.act(rden[:], den[:], AF.Ln, [den], [rden])
                P.act(rden[:], rden[:], AF.Exp, [rden], [rden], scale=-1.0)
                for par in range(2):
                    sl = slice(par * 64, (par + 1) * 64)
                    P.tt(msw[sl, 2 * g:2 * g + 2, n * 128:(n + 1) * 128], po[sl, par::2, :], rden[sl, par::2, :],
                         ALU.mult, [po, rden], [msw])
        P.dma(mix_s[:, 4:8, CTX:T], msw[:], reads=[msw], writes=[mix_s])
        P.emit()

    def gla_prep():
        P = Prog(nc, sy, "glp")
        wv_ = odw_d.rearrange("(kc p) n -> p kc n", p=128)
        stR = Rot(lambda i: P.sbuf("st%d" % i, [128, 8, 256], F32), 1)
        wqk = P.sbuf("wqk", [128, 8, 512], BF16)
        wvv = P.sbuf("wvv", [128, 8, 512], BF16)
        wgz = P.sbuf("wgz", [128, 8, 32], BF16)
        wr = P.sbuf("wr", [128, 8, 512], BF16)
        for (wt_, c0_) in ((wqk, 0), (wvv, 512), (wr, 1056)):
            for hf in range(2):
                st = stR.next()
                load_w(P, wt_, wt_[:, :, hf * 256:(hf + 1) * 256], wv_[:, :, c0_ + hf * 256:c0_ + (hf + 1) * 256], st, st[:])
        st = stR.next()
        load_w(P, wgz, wgz[:], wv_[:, :, 1024:1056], st, st[:, :, 0:32])
        g2f = P.sbuf("g2f", [17, 2, 256], F32)
        g2b = P.sbuf("g2b", [17, 2, 256], BF16)
        P.dma(g2f[:], gw2b_d, writes=[g2f])
        P.copy(g2b[:], g2f[:], [g2f], [g2b])
        msk = P.sbuf("msk", [128, 512], F32)
        P.memset(msk[:], 1.0, [msk])
        for i in range(4):
            P.memset(msk[:, i * 128:i * 128 + 1], 0.0, [msk])
        gz = [P.sbuf("gz%d" % d, [17, 512], BF16) for d in range(2)]
        for d in range(2):
            P.memset(gz[d][:], 1.0, [gz[d]])
        xg = Rot(lambda i: P.sbuf("xg%d" % i, [128, 8, 512], BF16), 2)
        pp = Rot(lambda i: P.psum("pp%d" % i, [128, 512]), 4)
        ptr_ = P.psum("ptr", [128, 2, 128], BF16)
        pgz = P.psum("pgz", [16, 512])
        qkraw = P.sbuf("qkraw", [128, 4, 512], F32)
        spD = [P.sbuf("sp%d" % i, [128, 2, 512], F32) for i in range(2)]
        preD = [P.sbuf("pre%d" % i, [128, 2, 512], F32) for i in range(2)]
        bcD = [P.sbuf("bc%d" % i, [128, 2, 512], F32) for i in range(2)]
        remD = [P.sbuf("rem%d" % i, [128, 2, 512], F32) for i in range(2)]
        eeR = Rot(lambda i: P.sbuf("ee%d" % i, [128, 2, 512], F32), 2)
        etD = [P.sbuf("et%d" % i, [128, 2, 4], F32) for i in range(2)]
        ob = Rot(lambda i: P.sbuf("ob%d" % i, [128, 2, 512], BF16), 2)
        kdt = Rot(lambda i: P.sbuf("kdt%d" % i, [128, 4, 256], BF16), 2)
        vtm = Rot(lambda i: P.sbuf("vtm%d" % i, [128, 4, 512], BF16), 2)
        rg = Rot(lambda i: P.sbuf("rg%d" % i, [128, 4, 512], BF16), 2)
        idb = P.sbuf("idb", [128, 128], BF16)
        P.copy(idb[:], cst[:, C_ID, :], [cst], [idb])
        for (t0, n) in GROUPS5:
            nt = n // 128
            x = xg.next()
            P.dma(x[:, :, 0:n], xn_s[:, :, t0:t0 + n], reads=[xn_s], writes=[x])
            for cq in range(4):
                ps = pp.next()
                for kc in range(8):
                    P.mm(ps[:, 0:n], wqk[:, kc, cq * 128:(cq + 1) * 128], x[:, kc, 0:n], kc == 0, kc == 7, [wqk, x], [ps])
                P.act(qkraw[:, cq, 0:n], ps[:, 0:n], AF.Copy, [ps], [qkraw])
            vt = vtm.next()
            for ti in range(nt):
                ps = pp.next()
                for kc in range(8):
                    P.mm(ps[:], x[:, kc, ti * 128:(ti + 1) * 128], wvv[:, kc, :], kc == 0, kc == 7, [x, wvv], [ps])
                P.act(vt[:, ti, :], ps[:], AF.Copy, [ps], [vt])
            P.dma(glv_s[t0:t0 + n, :].rearrange("(a p) f -> p a f", p=128), vt[:, 0:nt, :], reads=[vt], writes=[glv_s])
            if t0 >= CTX:
                r = rg.next()
                for h in range(4):
                    ps = pp.next()
                    for kc in range(8):
                        P.mm(ps[:, 0:n], wr[:, kc, h * 128:(h + 1) * 128], x[:, kc, 0:n], kc == 0, kc == 7, [wr, x], [ps])
                    P.act(r[:, h, 0:n], ps[:, 0:n], AF.Silu, [ps], [r])
                P.dma(glr_s[:, :, t0 - CTX:t0 - CTX + n], r[:, :, 0:n], reads=[r], writes=[glr_s])
            for d in range(2):
                sp, pre, bc, rem, et = spD[d], preD[d], bcD[d], remD[d], etD[d]
                for kc in range(8):
                    P.mm(pgz[:, 0:n], wgz[:, kc, 16 * d:16 * d + 16], x[:, kc, 0:n], kc == 0, kc == 7, [wgz, x], [pgz])
                P.act(gz[d][0:16, 0:n], pgz[:, 0:n], AF.Copy, [pgz], [gz[d]])
                for c in range(2):
                    ps = pp.next()
                    P.mm(ps[:, 0:n], g2b[:, d, c * 128:(c + 1) * 128], gz[d][:, 0:n], True, True, [g2b, gz[d]], [ps])
                    P.act(sp[:, c, 0:n], ps[:, 0:n], AF.Exp, [ps], [sp], scale=-1.0)
                    P.act(sp[:, c, 0:n], sp[:, c, 0:n], AF.Ln, [sp], [sp], bias=1.0)
                    P.op("vector", lambda e, c=c, n=n, pre=pre, sp=sp: e.tensor_tensor_scan(
                        out=pre[:, c, 0:n], data0=msk[:, 0:n], data1=sp[:, c, 0:n], initial=0.0,
                        op0=ALU.mult, op1=ALU.add), [msk, sp], [pre])
                pre4 = pre[:, :, 0:n].rearrange("p c (a b) -> p c a b", b=128)
                totb = pre4[:, :, :, 127:128].broadcast_to([128, 2, nt, 128])
                bc4 = bc[:, :, 0:n].rearrange("p c (a b) -> p c a b", b=128)
                rem4 = rem[:, :, 0:n].rearrange("p c (a b) -> p c a b", b=128)
                if d == 0:
                    P.copy(bc[:, :, 0:n], pre[:, :, 0:n], [pre], [bc], eng="gpsimd")
                    P.tt(rem4, totb, pre4, ALU.subtract, [pre], [rem])
                else:
                    P.tt(rem[:, :, 0:n], pre[:, :, 0:n], sp[:, :, 0:n], ALU.subtract, [pre, sp], [rem])
                    P.tt(bc4, totb, rem4, ALU.subtract, [pre, rem], [bc])
                P.act(et[:, :, 0:nt], pre4[:, :, :, 127], AF.Exp, [pre], [et], scale=-1.0 / 16)
                P.dma(glet_s[d][:, :, t0 // 128:t0 // 128 + nt], et[:, :, 0:nt], reads=[et], writes=[glet_s[d]])
                ee = eeR.next()
                P.act(ee[:, :, 0:n], bc[:, :, 0:n], AF.Exp, [bc], [ee], scale=-1.0 / 16)
                o = ob.next()
                P.stt(o[:, :, 0:n], qkraw[:, 0:2, 0:n], 0.125, ee[:, :, 0:n], ALU.mult, ALU.mult, [qkraw, ee], [o])
                P.dma(glq_s[d][:, :, t0:t0 + n], o[:, :, 0:n], reads=[o], writes=[glq_s[d]])
                ee = eeR.next()
                P.act(ee[:, :, 0:n], bc[:, :, 0:n], AF.Exp, [bc], [ee], scale=1.0 / 16)
                o = ob.next()
                P.tt(o[:, :, 0:n], qkraw[:, 2:4, 0:n], ee[:, :, 0:n], ALU.mult, [qkraw, ee], [o])
                P.dma(glk_s[d][:, :, t0:t0 + n], o[:, :, 0:n], reads=[o], writes=[glk_s[d]])
                ee = eeR.next()
                P.act(ee[:, :, 0:n], rem[:, :, 0:n], AF.Exp, [rem], [ee], scale=-1.0 / 16)
                o = ob.next()
                P.tt(o[:, :, 0:n], qkraw[:, 2:4, 0:n], ee[:, :, 0:n], ALU.mult, [qkraw, ee], [o])
                kd = kdt.next()
                for ti in range(nt):
                    for c in range(2):
                        P.tr(ptr_[:, c, :], o[:, c, ti * 128:(ti + 1) * 128], idb[:], [o, idb], [ptr_])
                    P.copy(kd[:, ti, :], ptr_[:].rearrange("p a b -> p (a b)"), [ptr_], [kd])
                P.dma(glkd_s[d][t0:t0 + n, :].rearrange("(a p) f -> p a f", p=128), kd[:, 0:nt, :], reads=[kd],
                      writes=[glkd_s[d]])
        P.emit()

    def gla_core():
        P = Prog(nc, sy, "glc")
        odc = P.sbuf("odc", [128, 11], F32)
        P.dma(odc[:], odcol_d, writes=[odc])
        ogla = P.sbuf("ogla", [128, 4, SEQ], F32, 16)
        for i in range(4):
            P.memset(ogla[:, i, :], 0.0, [ogla], eng="gpsimd" if i % 2 else "vector")
        S = [[P.sbuf("S%d%d" % (d, c), [128, 128], F32) for c in range(2)] for d in range(2)]
        Sb = [[P.sbuf("Sb%d%d" % (d, h), [128, 128], BF16) for h in range(4)] for d in range(2)]
        qd = Rot(lambda i: P.sbuf("qd%d" % i, [128, 2, 128], BF16), 4)
        kp = Rot(lambda i: P.sbuf("kp%d" % i, [128, 2, 128], BF16), 4)
        kd = Rot(lambda i: P.sbuf("kd%d" % i, [128, 256], BF16), 4)
        vt = Rot(lambda i: P.sbuf("vt%d" % i, [128, 512], BF16), 4)
        et = [P.sbuf("et%d" % d, [128, 2, NT], F32) for d in range(2)]
        aT = Rot(lambda i: P.sbuf("aT%d" % i, [128, 4, 128], BF16), 2)
        pA0 = Rot(lambda i: P.psum("pA0%d" % i, [128, 2, 128]), 1)
        pA1 = Rot(lambda i: P.psum("pA1%d" % i, [128, 2, 128]), 1)
        pO = Rot(lambda i: P.psum("pO%d" % i, [128, 4, 128]), 2)
        pSt = Rot(lambda i: P.psum("pSt%d" % i, [128, 128]), 3)
        orders = [list(range(NT)), [1, 0] + list(range(NT - 1, 1, -1))]
        masks = [cst[:, C_UI, :], cst[:, C_LI, :]]
        for d in range(2):
            P.dma(et[d][:], glet_s[d][:], reads=[glet_s[d]], writes=[et[d]])
            for c in range(2):
                P.memset(S[d][c][:], 0.0, [S[d][c]])
            for h in range(4):
                P.memset(Sb[d][h][:], 0.0, [Sb[d][h]])
        for s_ in range(NT):
            for d in range(2):
                tt = orders[d][s_]
                mask = masks[d]
                q_, k_, kd_, v_ = qd.next(), kp.next(), kd.next(), vt.next()
                tsl = slice(tt * 128, (tt + 1) * 128)
                P.dma(q_[:], glq_s[d][:, :, tsl], reads=[glq_s[d]], writes=[q_])
                P.dma(k_[:], glk_s[d][:, :, tsl], reads=[glk_s[d]], writes=[k_])
                P.dma(kd_[:], glkd_s[d][tsl, :], reads=[glkd_s[d]], writes=[kd_])
                P.dma(v_[:], glv_s[tsl, :], reads=[glv_s], writes=[v_])
                if tt >= 2:
                    a_ = aT.next()
                    for par, pa_ in enumerate((pA0, pA1)):
                        pa = pa_.next()
                        hs = slice(par * 64, par * 64 + 64)
                        for c in range(2):
                            P.mm(pa[:, c, :], k_[hs, c, :], q_[hs, c, :], True, True, [k_, q_], [pa])
                        P.tt(a_[:, par::2, :], pa[:], mask.unsqueeze(1).broadcast_to([128, 2, 128]), ALU.mult,
                             [pa, cst], [a_])
                    po = pO.next()
                    for h in range(4):
                        P.mm(po[:, h, :], v_[:, h * 128:(h + 1) * 128], a_[:, h, :], True, False, [v_, a_], [po])
                        P.mm(po[:, h, :], Sb[d][h][:], q_[:, h // 2, :], False, True, [Sb[d][h], q_], [po])
                    lt = tt - 2
                    osl = ogla[:, :, lt * 128:(lt + 1) * 128]
                    P.tt(osl, osl, po[:], ALU.add, [po, (ogla, lt)], [(ogla, lt)])
                for c in range(2):
                    ps = pSt.next()
                    for hh in range(2):
                        h = 2 * c + hh
                        P.mm(ps[hh * 64:(hh + 1) * 64, :], kd_[:, h * 64:(h + 1) * 64], v_[:, h * 128:(h + 1) * 128],
                             True, True, [kd_, v_], [ps])
                    P.stt(S[d][c][:], S[d][c][:], et[d][:, c, tt:tt + 1], ps[:], ALU.mult, ALU.add,
                          [S[d][c], et[d], ps], [S[d][c]])
                    for hh in range(2):
                        hs = slice(hh * 64, hh * 64 + 64)
                        P.act(Sb[d][2 * c + hh][hs, :], S[d][c][hs, :], AF.Copy, [S[d][c]], [Sb[d][2 * c + hh]])
        sqR = Rot(lambda i: P.sbuf("sq%d" % i, [128, 512], F32), 2)
        lnvR = Rot(lambda i: P.sbuf("lnv%d" % i, [128, 512], F32), 2)
        rsR = Rot(lambda i: P.sbuf("rs%d" % i, [128, 512], F32), 2)
        tmpR = Rot(lambda i: P.sbuf("tmp%d" % i, [128, 512], F32), 2)
        pss = P.psum("pss", [128, 512])
        rg = Rot(lambda i: P.sbuf("rg%d" % i, [128, 4, 512], BF16), 2)
        mo = Rot(lambda i: P.sbuf("mo%d" % i, [128, 4, 512], BF16), 2)
        for gi in range(4):
            r = rg.next()
            m = mo.next()
            gsl = slice(gi * 512, (gi + 1) * 512)
            keys = [(ogla, i) for i in range(gi * 4, gi * 4 + 4)]
            P.dma(r[:], glr_s[:, :, gsl], reads=[glr_s], writes=[r])
            for h in range(4):
                sq, lnv, rs, tmp = sqR.next(), lnvR.next(), rsR.next(), tmpR.next()
                P.act(sq[:], ogla[:, h, gsl], AF.Square, keys, [sq])
                P.mm(pss[:], ones_f[:], sq[:], True, True, [ones_f, sq], [pss])
                P.act(lnv[:], pss[:], AF.Ln, [pss], [lnv], bias=EPS, scale=1.0 / 128)
                P.act(rs[:], lnv[:], AF.Exp, [lnv], [rs], scale=-0.5)
                P.tt(tmp[:], ogla[:, h, gsl], rs[:], ALU.mult, keys + [rs], [tmp])
                P.stt(m[:, h, :], tmp[:], odc[:, 2:3], r[:, h, :], ALU.mult, ALU.mult, [tmp, odc, r], [m])
            P.dma(mix_s[:, 0:4, CTX + gi * 512:CTX + (gi + 1) * 512], m[:], reads=[m], writes=[mix_s])
        P.emit()

    def mla_prep():
        P = Prog(nc, sy, "mlp")
        wv_ = evw_d.rearrange("(kc p) n -> p kc n", p=128)
        evc = P.sbuf("evc", [128, 8], F32)
        P.dma(evc[:], evcol_d, writes=[evc])
        cs = P.sbuf("cs", [96, 2, SEQ], F32)
        P.dma(cs[:], mlcs_d, writes=[cs])
        st = P.sbuf("st", [128, 8, 384], F32)
        wcq = P.sbuf("wcq", [128, 8, 384], BF16)
        wckv = P.sbuf("wckv", [128, 8, 256], BF16)
        wkr = P.sbuf("wkr", [128, 8, 32], BF16)
        wqu = P.sbuf("wqu", [128, 3, 768], BF16)
        wkK = P.sbuf("wkK", [128, 2, 8, 64], BF16)
        wkV = P.sbuf("wkV", [128, 2, 8, 64], BF16)
        load_w(P, wcq, wcq[:], wv_[:, :, 0:384], st, st[:])
        load_w(P, wckv, wckv[:], wv_[:, :, 384:640], st, st[:, :, 0:256])
        load_w(P, wkr, wkr[:], wv_[:, :, 640:672], st, st[:, :, 0:32])
        stf = st[:].rearrange("p a b -> p (a b)")
        qu_v = evqu_d.rearrange("(kc p) n -> p kc n", p=128)
        st_qu = stf[:, 0:3 * 768].rearrange("p (a b) -> p a b", a=3)
        load_w(P, wqu, wqu[:], qu_v, st, st_qu)
        kv_v = evkv_d.rearrange("(kc p) n -> p kc n", p=128)
        st_kv = stf[:, 0:2048].rearrange("p (a b) -> p a b", a=2)
        P.dma(st_kv, kv_v, writes=[st])
        st_kv5 = stf[:, 0:2048].rearrange("p (a h t d) -> p a h t d", a=2, h=8, t=2)
        P.copy(wkK[:], st_kv5[:, :, :, 0, :], [st], [wkK], eng="gpsimd")
        P.copy(wkV[:], st_kv5[:, :, :, 1, :], [st], [wkV], eng="gpsimd")
        xg = Rot(lambda i: P.sbuf("xg%d" % i, [128, 8, 512], BF16), 2)
        pp = Rot(lambda i: P.psum("pp%d" % i, [128, 512]), 3)
        pssR = Rot(lambda i: P.psum("pss%d" % i, [128, 512]), 2)
        protR = Rot(lambda i: P.psum("prot%d" % i, [128, 512]), 1)
        prk = P.psum("prk", [128, NT, 8])
        prkr = P.psum("prkr", [128, NT])
        raw = P.sbuf("raw", [128, 3, 512], F32)
        sq3 = P.sbuf("sq3", [128, 3, 512], F32)
        lnvR = Rot(lambda i: P.sbuf("lnv%d" % i, [128, 512], F32), 2)
        rsR = Rot(lambda i: P.sbuf("rs%d" % i, [128, 512], F32), 2)
        cqn = P.sbuf("cqn", [128, 3, 512], BF16)
        ckvn = P.sbuf("ckvn", [128, 2, 512], BF16)
        sqR = Rot(lambda i: P.sbuf("sq%d" % i, [128, 512], F32), 2)
        qgR = Rot(lambda i: P.sbuf("qg%d" % i, [128, 512], F32), 2)
        t1R = Rot(lambda i: P.sbuf("t1%d" % i, [128, 512], F32), 2)
        t2R = Rot(lambda i: P.sbuf("t2%d" % i, [128, 512], F32), 2)
        qo = Rot(lambda i: P.sbuf("qo%d" % i, [128, 512], BF16), 3)
        vtm = Rot(lambda i: P.sbuf("vtm%d" % i, [128, 4, 512], BF16), 2)
        rsk = P.sbuf("rsk", [128, NT * 8], F32)
        R96 = cst[0:96, C_ROT96, 0:96]

        def rope_out(q, src, rs, n, lat, l0, with_rs):
            if lat:
                prot, t1, t2 = protR.next(), t1R.next(), t2R.next()
                P.mm(prot[0:96, 0:n], R96, src[0:96, 0:n], True, True, [cst, src], [prot])
                P.tt(t1[0:96, 0:n], src[0:96, 0:n], cs[:, 0, l0:l0 + n], ALU.mult, [src, cs], [t1])
                P.tt(t2[0:96, 0:n], prot[0:96, 0:n], cs[:, 1, l0:l0 + n], ALU.mult, [prot, cs], [t2])
                if with_rs:
                    P.tt(t1[0:96, 0:n], t1[0:96, 0:n], t2[0:96, 0:n], ALU.add, [t1, t2], [t1])
                    P.tt(q[0:96, 0:n], t1[0:96, 0:n], rs[0:96, 0:n], ALU.mult, [t1, rs], [q])
                else:
                    P.tt(q[0:96, 0:n], t1[0:96, 0:n], t2[0:96, 0:n], ALU.add, [t1, t2], [q])
            else:
                if with_rs:
                    P.tt(q[0:96, 0:n], src[0:96, 0:n], rs[0:96, 0:n], ALU.mult, [src, rs], [q])
                else:
                    P.copy(q[0:96, 0:n], src[0:96, 0:n], [src], [q])

        for (t0, n) in GROUPS5:
            nt = n // 128
            lat = t0 >= CTX
            l0 = t0 - CTX
            x = xg.next()
            P.dma(x[:, :, 0:n], xn_s[:, :, t0:t0 + n], reads=[xn_s], writes=[x])
            for (wt, nch, off, dstn, dim) in ((wcq, 3, 0, cqn, 384), (wckv, 2, 3, ckvn, 256)):
                for c in range(nch):
                    ps = pp.next()
                    for kc in range(8):
                        P.mm(ps[:, 0:n], wt[:, kc, c * 128:(c + 1) * 128], x[:, kc, 0:n], kc == 0, kc == 7, [wt, x], [ps])
                    P.act(raw[:, c, 0:n], ps[:, 0:n], AF.Copy, [ps], [raw])
                    P.act(sq3[:, c, 0:n], ps[:, 0:n], AF.Square, [ps], [sq3])
                pss, lnv, rs = pssR.next(), lnvR.next(), rsR.next()
                for c in range(nch):
                    P.mm(pss[:, 0:n], ones_f[:], sq3[:, c, 0:n], c == 0, c == nch - 1, [ones_f, sq3], [pss])
                P.act(lnv[:, 0:n], pss[:, 0:n], AF.Ln, [pss], [lnv], bias=EPS, scale=1.0 / dim)
                P.act(rs[:, 0:n], lnv[:, 0:n], AF.Exp, [lnv], [rs], scale=-0.5)
                for c in range(nch):
                    P.stt(dstn[:, c, 0:n], raw[:, c, 0:n], evc[:, off + c:off + c + 1], rs[:, 0:n], ALU.mult, ALU.mult,
                          [raw, evc, rs], [dstn])
            vt = vtm.next()
            for ti in range(nt):
                ps = pp.next()
                for c in range(2):
                    P.mm(ps[:], ckvn[:, c, ti * 128:(ti + 1) * 128], wkV[:, c].rearrange("p h d -> p (h d)"),
                         c == 0, c == 1, [ckvn, wkV], [ps])
                P.act(vt[:, ti, :], ps[:], AF.Copy, [ps], [vt])
            P.dma(mv_s[t0:t0 + n, :].rearrange("(a p) f -> p a f", p=128), vt[:, 0:nt, :], reads=[vt], writes=[mv_s])
            R = slice(64, 96)
            psr = pp.next()
            for kc in range(8):
                P.mm(psr[R, 0:n], wkr[:, kc, :], x[:, kc, 0:n], kc == 0, kc == 7, [wkr, x], [psr])
            sqr, krg = sqR.next(), qgR.next()
            P.act(sqr[R, 0:n], psr[R, 0:n], AF.Square, [psr], [sqr])
            for ti in range(nt):
                tt = t0 // 128 + ti
                P.mm(prkr[:, tt:tt + 1], sqr[R, ti * 128:(ti + 1) * 128], ones_f[R, 0:1], True, True, [sqr, ones_f], [prkr])
            P.act(krg[R, 0:n], psr[R, 0:n], AF.Identity, [psr], [krg], scale=evc[R, 6:7])
            krf = qo.next()
            if lat:
                prot, t1, t2 = protR.next(), t1R.next(), t2R.next()
                P.mm(prot[R, 0:n], cst[R, C_ROT96, 64:96], krg[R, 0:n], True, True, [cst, krg], [prot])
                P.tt(t1[R, 0:n], krg[R, 0:n], cs[R, 0, l0:l0 + n], ALU.mult, [krg, cs], [t1])
                P.tt(t2[R, 0:n], prot[R, 0:n], cs[R, 1, l0:l0 + n], ALU.mult, [prot, cs], [t2])
                P.tt(krf[R, 0:n], t1[R, 0:n], t2[R, 0:n], ALU.add, [t1, t2], [krf])
            else:
                P.copy(krf[R, 0:n], krg[R, 0:n], [krg], [krf])
            for h in range(8):
                P.dma(mkf_s[R, h, t0:t0 + n], krf[R, 0:n], reads=[krf], writes=[mkf_s])
            for h in range(8):
                ps = pp.next()
                for c in range(3):
                    P.mm(ps[0:96, 0:n], wqu[:, c, h * 96:(h + 1) * 96], cqn[:, c, 0:n], c == 0, c == 2, [wqu, cqn], [ps])
                sq, pss, lnv, rs, qg = sqR.next(), pssR.next(), lnvR.next(), rsR.next(), qgR.next()
                P.act(sq[0:96, 0:n], ps[0:96, 0:n], AF.Square, [ps], [sq])
                P.mm(pss[0:96, 0:n], ones_f[0:96, 0:96], sq[0:96, 0:n], True, True, [ones_f, sq], [pss])
                P.act(lnv[0:96, 0:n], pss[0:96, 0:n], AF.Ln, [pss], [lnv], bias=EPS, scale=1.0 / 96)
                P.act(rs[0:96, 0:n], lnv[0:96, 0:n], AF.Exp, [lnv], [rs], scale=-0.5)
                P.act(qg[0:96, 0:n], ps[0:96, 0:n], AF.Identity, [ps], [qg], scale=evc[0:96, 5:6])
                q = qo.next()
                rope_out(q, qg, rs, n, lat, l0, True)
                P.dma(mqf_s[:, h, t0:t0 + n], q[0:96, 0:n], reads=[q], writes=[mqf_s])
                ps = pp.next()
                for c in range(2):
                    P.mm(ps[0:64, 0:n], wkK[:, c, h, :], ckvn[:, c, 0:n], c == 0, c == 1, [wkK, ckvn], [ps])
                sq = sqR.next()
                P.act(sq[0:64, 0:n], ps[0:64, 0:n], AF.Square, [ps], [sq])
                for ti in range(nt):
                    tt = t0 // 128 + ti
                    P.mm(prk[:, tt, h:h + 1], sq[0:64, ti * 128:(ti + 1) * 128], ones_f[0:64, 0:1], True, True,
                         [sq, ones_f], [prk])
                q = qo.next()
                P.act(q[0:64, 0:n], ps[0:64, 0:n], AF.Identity, [ps], [q], scale=evc[0:64, 6:7])
                P.dma(mkf_s[0:64, h, t0:t0 + n], q[0:64, 0:n], reads=[q], writes=[mkf_s])
        rk3 = P.sbuf("rk3", [128, NT, 8], F32)
        rkr = P.sbuf("rkr", [128, NT], F32)
        P.act(rkr[:], prkr[:], AF.Copy, [prkr], [rkr])
        P.tt(rk3[:], prk[:], rkr[:].unsqueeze(2).broadcast_to([128, NT, 8]), ALU.add, [prk, rkr], [rk3])
        P.act(rsk[:], rk3[:].rearrange("p a b -> p (a b)"), AF.Ln, [rk3], [rsk], bias=EPS, scale=1.0 / 96)
        P.act(rsk[:], rsk[:], AF.Exp, [rsk], [rsk], scale=-0.5)
        P.ts(rsk[:], rsk[:], float(96 ** -0.5), ALU.mult, [rsk], [rsk])
        P.dma(mrk_s[:], rsk[:], reads=[rsk], writes=[mrk_s])
        P.emit()

    def mla_core():
        P = Prog(nc, sy, "mlc")
        rsk = P.sbuf("rsk", [128, NT, 8], F32)
        P.dma(rsk[:].rearrange("p a b -> p (a b)"), mrk_s[:], reads=[mrk_s], writes=[rsk])
        kf = Rot(lambda i: P.sbuf("kf%d" % i, [96, 2, T], BF16), 2)
        qf = Rot(lambda i: P.sbuf("qf%d" % i, [96, 2, T], BF16), 2)
        vv = Rot(lambda i: P.sbuf("vv%d" % i, [128, NT, 128], BF16), 2)
        pS = Rot(lambda i: P.psum("pS%d" % i, [128, 512]), 4)
        pO = Rot(lambda i: P.psum("pO%d" % i, [128, 512]), 2)
        pM = Rot(lambda i: P.psum("pM%d" % i, [128, 512]), 2)
        pT = Rot(lambda i: P.sbuf("pT%d" % i, [128, 512], BF16), 4)
        rden = P.sbuf("rden", [128, 512], F32)
        mo = Rot(lambda i: P.sbuf("mo%d" % i, [128, 512], BF16), 2)
        for j in range(4):
            k_, q_, v_ = kf.next(), qf.next(), vv.next()
            P.dma(k_[:], mkf_s[:, 2 * j:2 * j + 2, :], reads=[mkf_s], writes=[k_])
            P.dma(q_[:], mqf_s[:, 2 * j:2 * j + 2, :], reads=[mqf_s], writes=[q_])
            P.dma(v_[:], mv_s[:, j * 128:(j + 1) * 128].rearrange("(a p) f -> p a f", p=128), reads=[mv_s], writes=[v_])
            for (t0, n) in GROUPS5:
                kts = [0, 1] if t0 < CTX else list(range(NT))
                po, pm = pO.next(), pM.next()

                def scores(kt):
                    pts = []
                    for hh in range(2):
                        ps = pS.next()
                        P.mm(ps[:, 0:n], k_[:, hh, kt * 128:(kt + 1) * 128], q_[:, hh, t0:t0 + n], True, True, [k_, q_], [ps])
                        pt = pT.next()
                        P.act(pt[:, 0:n], ps[:, 0:n], AF.Exp, [ps, rsk], [pt], scale=rsk[:, kt, 2 * j + hh:2 * j + hh + 1])
                        pts.append(pt)
                    return pts

                nxt = scores(kts[0])
                for ki, kt in enumerate(kts):
                    first, last = ki == 0, ki == len(kts) - 1
                    pts = nxt
                    if not last:
                        nxt = scores(kts[ki + 1])
                    for hh in range(2):
                        sl = slice(hh * 64, (hh + 1) * 64)
                        P.mm(po[sl, 0:n], v_[:, kt, hh * 64:(hh + 1) * 64], pts[hh][:, 0:n], first, last, [v_, pts[hh]], [po])
                    for hh in range(2):
                        sl = slice(hh * 64, (hh + 1) * 64)
                        P.mm(pm[sl, 0:n], ones_b[:, 0:64], pts[hh][:, 0:n], first, last, [ones_b, pts[hh]], [pm])
                P.act(rden[:, 0:n], pm[:, 0:n], AF.Ln, [pm], [rden])
                P.act(rden[:, 0:n], rden[:, 0:n], AF.Exp, [rden], [rden], scale=-1.0)
                m = mo.next()
                P.tt(m[:, 0:n], po[:, 0:n], rden[:, 0:n], ALU.mult, [po, rden], [m])
                P.dma(mix_s[:, j, t0:t0 + n], m[:, 0:n], reads=[m], writes=[mix_s])
        P.emit()

    def gdn_prep():
        P = Prog(nc, sy, "gdp")
        wv_ = evw_d.rearrange("(kc p) n -> p kc n", p=128)
        xn = P.sbuf("xn", [128, 8, T], BF16)
        P.dma(xn[:], xn_s[:], reads=[xn_s], writes=[xn])
        cvw = P.sbuf("cvw", [128, 12, 5], F32)
        P.dma(cvw[:], gcv_d, writes=[cvw])
        grow = P.sbuf("grow", [128, 544], F32)
        P.dma(grow[:], grow_d, writes=[grow])
        st = P.sbuf("st", [128, 8, 512], F32)
        wch = Rot(lambda i: P.sbuf("wch%d" % i, [128, 8, 128], BF16), 2)
        wab = P.sbuf("wab", [128, 8, 32], BF16)
        wgt = P.sbuf("wgt", [128, 8, 512], BF16)
        load_w(P, wab, wab[:], wv_[:, :, 2208:2240], st, st[:, :, 0:32])
        load_w(P, wgt, wgt[:], wv_[:, :, 2240:2752], st, st[:])
        pp = Rot(lambda i: P.psum("pp%d" % i, [128, 512]), 3)
        pssR = Rot(lambda i: P.psum("pss%d" % i, [128, 512]), 2)
        ptr_ = Rot(lambda i: P.psum("ptr%d" % i, [128, 4, 128]), 2)
        zcR = Rot(lambda i: P.sbuf("zc%d" % i, [128, T], F32), 2)
        accR = Rot(lambda i: P.sbuf("acc%d" % i, [128, T], F32), 1)
        sqR = Rot(lambda i: P.sbuf("sq%d" % i, [128, 512], F32), 2)
        lnvR = Rot(lambda i: P.sbuf("lnv%d" % i, [128, 512], F32), 2)
        tmst = P.sbuf("tmst", [128, NT, 128], F32)
        gall = P.sbuf("gall", [128, NT, 16], F32)
        ball = P.sbuf("ball", [128, NT, 16], F32)
        negA = P.sbuf("negA", [128, 16], F32)
        P.act(negA[:], grow[:, 0:16], AF.Exp, [grow], [negA])
        P.ts(negA[:], negA[:], -1.0, ALU.mult, [negA], [negA])
        tg = P.sbuf("tg", [128, 16], F32)
        gt = Rot(lambda i: P.sbuf("gt%d" % i, [128, 512], BF16), 2)
        for tt in range(NT):
            tsl = slice(tt * 128, (tt + 1) * 128)
            ps = pp.next()
            for kc in range(8):
                P.mm(ps[:, 0:32], xn[:, kc, tsl], wab[:, kc, :], kc == 0, kc == 7, [xn, wab], [ps])
            P.tt(tg[:], ps[:, 0:16], grow[:, 16:32], ALU.add, [ps, grow], [tg])
            P.act(tg[:], tg[:], AF.Exp, [tg], [tg])
            P.act(tg[:], tg[:], AF.Ln, [tg], [tg], bias=1.0)
            P.tt(gall[:, tt, :], tg[:], negA[:], ALU.mult, [tg, negA], [gall])
            P.act(ball[:, tt, :], ps[:, 16:32], AF.Exp, [ps], [ball], scale=-1.0)
            P.ts(ball[:, tt, :], ball[:, tt, :], 1.0, ALU.add, [ball], [ball])
            P.op("vector", lambda e, tt=tt: e.reciprocal(out=ball[:, tt, :], in_=ball[:, tt, :]), [ball], [ball])
            ps = pp.next()
            for kc in range(8):
                P.mm(ps[:], xn[:, kc, tsl], wgt[:, kc, :], kc == 0, kc == 7, [xn, wgt], [ps])
            g_ = gt.next()
            P.act(g_[:], ps[:], AF.Silu, [ps], [g_])
            P.dma(ggate_s[tsl, :], g_[:], reads=[g_], writes=[ggate_s])
        P.dma(gg_s[:], gall[:].rearrange("p a b -> p (a b)"), reads=[gall], writes=[gg_s])
        P.dma(gb_s[:], ball[:].rearrange("p a b -> p (a b)"), reads=[ball], writes=[gb_s])
        ZW = T + 8

        def zoff(t):
            return t + 2 if t < CTX else t + 6

        zpR = Rot(lambda i: P.sbuf("zp%d" % i, [128, ZW], BF16), 2)
        for z_ in zpR.t:
            P.memset(z_[:], 0.0, [z_])
        dgR = Rot(lambda i: P.sbuf("dgw%d" % i, [128, 5, 128], BF16), 2)
        for cc in range(12):
            zc, acc = zcR.next(), accR.next()
            zp, dgw = zpR.next(), dgR.next()
            w = wch.next()
            load_w(P, w, w[:], wv_[:, :, 672 + cc * 128:672 + (cc + 1) * 128], st, st[:, :, 0:128])
            for j in range(5):
                P.ts(dgw[:, j, :], cst[:, C_ID, :], cvw[:, cc, j:j + 1], ALU.mult, [cst, cvw], [dgw], eng="gpsimd")
            for (t0, n) in GROUPS5:
                ps = pp.next()
                for kc in range(8):
                    P.mm(ps[:, 0:n], w[:, kc, :], xn[:, kc, t0:t0 + n], kc == 0, kc == 7, [w, xn], [ps])
                P.act(zp[:, zoff(t0):zoff(t0) + n], ps[:, 0:n], AF.Copy, [ps], [zp])
            for (t0, n) in GROUPS5:
                ps = pp.next()
                for j in range(5):
                    o_ = zoff(t0) + j - 2
                    P.mm(ps[:, 0:n], dgw[:, j, :], zp[:, o_:o_ + n], j == 0, j == 4, [dgw, zp], [ps])
                P.act(zc[:, t0:t0 + n], ps[:, 0:n], AF.Silu, [ps], [zc])
            if cc < 8:
                for (t0, n) in GROUPS5:
                    sq, lnv = sqR.next(), lnvR.next()
                    P.act(sq[:, 0:n], zc[:, t0:t0 + n], AF.Square, [zc], [sq])
                    pss = pssR.next()
                    P.mm(pss[:, 0:n], cst[:, C_BD64, :], sq[:, 0:n], True, True, [cst, sq], [pss])
                    P.act(lnv[:, 0:n], pss[:, 0:n], AF.Ln, [pss], [lnv], bias=EPS)
                    P.act(lnv[:, 0:n], lnv[:, 0:n], AF.Exp, [lnv], [lnv], scale=-0.5)
                    if cc < 4:
                        P.stt(acc[:, t0:t0 + n], zc[:, t0:t0 + n], 0.125, lnv[:, 0:n], ALU.mult, ALU.mult, [zc, lnv], [acc])
                    else:
                        P.tt(acc[:, t0:t0 + n], zc[:, t0:t0 + n], lnv[:, 0:n], ALU.mult, [zc, lnv], [acc])
                dst = gq_s if cc < 4 else gk_s
                r0 = (cc % 4) * 128
                P.dma(dst[r0:r0 + 128, :], acc[:], reads=[acc], writes=[dst])
                src = acc
            else:
                src = zc
            if cc >= 4:
                for t4 in range(0, NT, 4):
                    nn = min(4, NT - t4)
                    pt = ptr_.next()
                    for i in range(nn):
                        tt = t4 + i
                        P.tr(pt[:, i, :], src[:, tt * 128:(tt + 1) * 128], ident, [src, cst], [pt])
                    P.copy(tmst[:, t4:t4 + nn, :], pt[:, 0:nn, :], [pt], [tmst])
                dtm = gktm_s if cc < 8 else gvtm_s
                c0 = (cc % 4) * 128
                P.dma(dtm[:, c0:c0 + 128].rearrange("(a p) f -> p a f", p=128), tmst[:], reads=[tmst], writes=[dtm])
        P.emit()

    def gdn_core():
        P = Prog(nc, sy, "gdc")
        lvm = P.sbuf("lvm", [128, 7, 128], F32)
        nlv = P.sbuf("nlv", [128, 14, 128], BF16)
        for hf in range(2):
            P.dma(lvm[:], lvm_d[:, 7 * hf:7 * hf + 7, :], writes=[lvm])
            P.ts(nlv[:, 7 * hf:7 * hf + 7, :], lvm[:], -1.0, ALU.mult, [lvm], [nlv])
        idb = P.sbuf("idb", [128, 128], BF16)
        P.copy(idb[:], cst[:, C_ID, :], [cst], [idb])
        gall = P.sbuf("gall", [128, NT, 16], F32)
        ball = P.sbuf("ball", [128, NT, 16], F32)
        P.dma(gall[:].rearrange("p a b -> p (a b)"), gg_s[:], reads=[gg_s], writes=[gall])
        P.dma(ball[:].rearrange("p a b -> p (a b)"), gb_s[:], reads=[gb_s], writes=[ball])
        PS = Rot(lambda i: P.psum("ps%d" % i, [128, 512]), 6)
        PB = Rot(lambda i: P.psum("pb%d" % i, [128, 4, 128], BF16), 2)

        def b4(ap):
            return ap.unsqueeze(1).broadcast_to([128, 4, 128])

        def bl(ap, w):
            return ap.unsqueeze(2).broadcast_to([128, ap.shape[1], w])

        Ibc = b4(idb[:])
        D_ = []
        for d in range(2):
            n_ = lambda s, d=d: "%s_%d" % (s, d)
            o = K()
            o.S = P.sbuf(n_("S"), [128, 8, 64], F32)
            o.q = P.sbuf(n_("q"), [64, 8, 128], F32)
            o.k = P.sbuf(n_("k"), [64, 8, 128], F32)
            o.kt = P.sbuf(n_("kt"), [128, 8, 64], F32)
            o.vt = P.sbuf(n_("vt"), [128, 8, 64], F32)
            o.eg = P.sbuf(n_("eg"), [128, 24], F32)
            o.cfk = P.sbuf(n_("cfk"), [128, 8], F32)
            o.GU = P.sbuf(n_("GU"), [128, 4, 128], F32)
            o.eD = P.sbuf(n_("eD"), [128, 4, 128], F32)
            o.eDT = P.sbuf(n_("eDT"), [128, 4, 128], F32)
            o.A = P.sbuf(n_("A"), [128, 4, 128], BF16)
            o.AT = P.sbuf(n_("AT"), [128, 4, 128], BF16)
            o.X = [P.sbuf(n_("X%d" % i), [128, 4, 128], BF16) for i in range(2)]
            o.XT = [P.sbuf(n_("XT%d" % i), [128, 4, 128], BF16) for i in range(2)]
            o.Ym = P.sbuf(n_("Ym"), [128, 4, 128], BF16)
            o.Ymt = P.sbuf(n_("Ymt"), [128, 4, 128], BF16)
            o.qkd = P.sbuf(n_("qkd"), [128, 8, 128], F32)
            o.Rv = P.sbuf(n_("Rv"), [128, 4, 64], BF16)
            o.Rk = P.sbuf(n_("Rk"), [128, 4, 64], BF16)
            o.U = P.sbuf(n_("U"), [128, 8, 64], F32)
            o.wT = P.sbuf(n_("wT"), [64, 8, 128], F32)
            o.dg = P.sbuf(n_("dg"), [128, 4, 128], F32)
            o.qdT = P.sbuf(n_("qdT"), [128, 8, 128], F32)
            o.kdec = P.sbuf(n_("kdec"), [128, 8, 64], F32)
            o.vnew = P.sbuf(n_("vnew"), [128, 8, 64], F32)
            o.ost = P.sbuf(n_("ost"), [128, 512], F32)
            P.memset(o.qdT[:], 0.0, [o.qdT])
            P.memset(o.S[:], 0.0, [o.S])
            o.order = list(range(NT)) if d == 0 else [1, 0] + list(range(NT - 1, 1, -1))
            o.Uinc = cst[:, C_UI, :] if d == 0 else cst[:, C_LI, :]
            o.Bst = cst[:, C_SL, :] if d == 0 else cst[:, C_SU, :]
            o.mX = (lambda lv: nlv[:, lv, :]) if d == 0 else (lambda lv: nlv[:, 7 + lv, :])
            o.mXT = (lambda lv: nlv[:, 7 + lv, :]) if d == 0 else (lambda lv: nlv[:, lv, :])
            o.d = d
            D_.append(o)

        def v4(ps):
            return ps[:].rearrange("p (h i) -> p h i", h=4)

        for s in range(NT):
            for o in D_:
                d = o.d
                tt = o.order[s]
                o.tt = tt
                tsl = slice(tt * 128, (tt + 1) * 128)
                P.dma(o.q[:], gq_s[:, tsl].rearrange("(h d) t -> d h t", d=64), reads=[gq_s], writes=[o.q])
                P.dma(o.k[:], gk_s[:, tsl].rearrange("(h d) t -> d h t", d=64), reads=[gk_s], writes=[o.k])
                P.dma(o.kt[:].rearrange("p h d -> p (h d)"), gktm_s[tsl, :], reads=[gktm_s], writes=[o.kt])
                P.dma(o.vt[:].rearrange("p h d -> p (h d)"), gvtm_s[tsl, :], reads=[gvtm_s], writes=[o.vt])
                gd = gall[:, tt, d * 8:(d + 1) * 8]
                pg = PS.next()
                P.mm(pg[:, 0:8], o.Uinc, gd, True, True, [cst, gall], [pg])
                P.mm(pg[:, 8:16], o.Bst, gd, True, True, [cst, gall], [pg])
                P.mm(pg[:, 16:24], ones_f[:], gd, True, True, [ones_f, gall], [pg])
                P.act(o.eg[:], pg[:, 0:24], AF.Exp, [pg], [o.eg])
                P.tt(o.cfk[:], ball[:, tt, d * 8:(d + 1) * 8], o.eg[:, 0:8], ALU.mult, [ball, o.eg], [o.cfk])
            for hg in range(2):
                h0 = 4 * hg
                hsl = slice(h0, h0 + 4)
                for o in D_:
                    d, tt = o.d, o.tt
                    P.tt(o.GU[:], b4(o.Uinc), bl(gall[:, tt, d * 8 + h0:d * 8 + h0 + 4], 128), ALU.mult, [cst, gall], [o.GU],
                         eng="gpsimd")
                for o in D_:
                    o.pD, o.pDT = PS.next(), PS.next()
                    for h in range(4):
                        P.mm(v4(o.pD)[:, h, :], o.GU[:, h, :], o.Bst, True, True, [o.GU, cst], [o.pD])
                    P.mm(o.pDT[:], o.Bst, o.GU[:].rearrange("p h i -> p (h i)"), True, True, [o.GU, cst], [o.pDT])
                for o in D_:
                    P.act(o.eD[:], v4(o.pD), AF.Exp, [o.pD], [o.eD])
                    P.act(o.eDT[:], v4(o.pDT), AF.Exp, [o.pDT], [o.eDT])
                for o in D_:
                    d, tt = o.d, o.tt
                    pKK = PS.next()
                    for h in range(4):
                        P.mm(v4(pKK)[:, h, :], o.k[:, h0 + h, :], o.k[:, h0 + h, :], True, True, [o.k], [pKK])
                    P.tt(o.eD[:], v4(pKK), o.eD[:], ALU.mult, [pKK, o.eD], [o.eD])
                    P.tt(o.eD[:], o.eD[:], b4(o.Bst), ALU.mult, [o.eD, cst], [o.eD], eng="gpsimd")
                    P.tt(o.A[:], o.eD[:], bl(ball[:, tt, d * 8 + h0:d * 8 + h0 + 4], 128), ALU.mult, [o.eD, ball], [o.A],
                         eng="gpsimd")
                for o in D_:
                    pQK = PS.next()
                    for h in range(4):
                        P.mm(v4(pQK)[:, h, :], o.k[:, h0 + h, :], o.q[:, h0 + h, :], True, True, [o.k, o.q], [pQK])
                    P.tt(o.eDT[:], v4(pQK), o.eDT[:], ALU.mult, [pQK, o.eDT], [o.eDT])
                    P.tt(o.qkd[:, hsl, :], o.eDT[:], b4(o.Uinc), ALU.mult, [o.eDT, cst], [o.qkd], eng="gpsimd")
                for o in D_:
                    pT_ = PB.next()
                    for h in range(4):
                        P.tr(pT_[:, h, :], o.A[:, h, :], idb[:], [o.A, idb], [pT_])
                    P.act(o.AT[:], pT_[:], AF.Copy, [pT_], [o.AT])
                for o in D_:
                    P.tt(o.Ym[:], o.A[:], b4(o.mX(0)), ALU.mult, [o.A, nlv], [o.Ym], eng="gpsimd")
                    P.tt(o.X[0][:], o.Ym[:], Ibc, ALU.add, [o.Ym, idb], [o.X[0]], eng="gpsimd")
                    P.tt(o.Ymt[:], o.AT[:], b4(o.mXT(0)), ALU.mult, [o.AT, nlv], [o.Ymt])
                    P.tt(o.XT[0][:], o.Ymt[:], Ibc, ALU.add, [o.Ymt, idb], [o.XT[0]])
                cur = 0
                for lv in range(1, 7):
                    last = lv == 6
                    if not last:
                        for o in D_:
                            o.pY = PS.next()
                            for h in range(4):
                                P.mm(v4(o.pY)[:, h, :], o.AT[:, h, :], o.X[cur][:, h, :], True, True, [o.AT, o.X[cur]], [o.pY])
                    for o in D_:
                        o.pY2 = PS.next()
                        for h in range(4):
                            P.mm(v4(o.pY2)[:, h, :], o.A[:, h, :], o.XT[cur][:, h, :], True, True, [o.A, o.XT[cur]], [o.pY2])
                    if not last:
                        for o in D_:
                            P.tt(o.Ym[:], v4(o.pY), b4(o.mX(lv)), ALU.mult, [o.pY, nlv], [o.Ym])
                    for o in D_:
                        P.tt(o.Ymt[:], v4(o.pY2), b4(o.mXT(lv)), ALU.mult, [o.pY2, nlv], [o.Ymt])
                    if not last:
                        for o in D_:
                            pZ = PS.next()
                            P.mm(pZ[:], idb[:], o.X[cur][:].rearrange("p h i -> p (h i)"), True, False, [idb, o.X[cur]], [pZ])
                            for h in range(4):
                                P.mm(v4(pZ)[:, h, :], o.XT[cur][:, h, :], o.Ym[:, h, :], False, True, [o.XT[cur], o.Ym], [pZ])
                            P.act(o.X[1 - cur][:], v4(pZ), AF.Copy, [pZ], [o.X[1 - cur]])
                    for o in D_:
                        pZ = PS.next()
                        P.mm(pZ[:], idb[:], o.XT[cur][:].rearrange("p h i -> p (h i)"), True, False, [idb, o.XT[cur]], [pZ])
                        for h in range(4):
                            P.mm(v4(pZ)[:, h, :], o.X[cur][:, h, :], o.Ymt[:, h, :], False, True, [o.X[cur], o.Ymt], [pZ])
                        P.act(o.XT[1 - cur][:], v4(pZ), AF.Copy, [pZ], [o.XT[1 - cur]])
                    cur = 1 - cur
                for o in D_:
                    d, tt = o.d, o.tt
                    XTf = o.XT[cur]
                    P.tt(o.Rv[:], o.vt[:, hsl, :], bl(ball[:, tt, d * 8 + h0:d * 8 + h0 + 4], 64), ALU.mult, [o.vt, ball], [o.Rv],
                         eng="gpsimd")
                    P.tt(o.Rk[:], o.kt[:, hsl, :], bl(o.cfk[:, hsl], 64), ALU.mult, [o.kt, o.cfk], [o.Rk], eng="gpsimd")
                    pU, pW = PS.next(), PS.next()
                    pU3 = pU[:, 0:256].rearrange("p (h i) -> p h i", h=4)
                    pW3 = pW[0:64, :].rearrange("p (h i) -> p h i", h=4)
                    for h in range(4):
                        P.mm(pU3[:, h, :], XTf[:, h, :], o.Rv[:, h, :], True, True, [XTf, o.Rv], [pU])
                    for h in range(4):
                        P.mm(pW3[:, h, :], o.Rk[:, h, :], XTf[:, h, :], True, True, [XTf, o.Rk], [pW])
                    P.act(o.U[:, hsl, :], pU3, AF.Copy, [pU], [o.U])
                    P.act(o.wT[:, hsl, :], pW3, AF.Copy, [pW], [o.wT])
            for o in D_:
                for half in range(2):
                    P.tt(o.dg[:], b4(cst[:, C_ID, :]), bl(o.eg[:, 4 * half:4 * half + 4], 128), ALU.mult, [cst, o.eg], [o.dg],
                         eng="gpsimd")
                    pE = PS.next()
                    P.mm(pE[0:64, :], ones_f[:, 0:64], o.dg[:].rearrange("p h i -> p (h i)"), True, True, [ones_f, o.dg], [pE])
                    P.tt(o.qdT[0:64, 4 * half:4 * half + 4, :], o.q[:, 4 * half:4 * half + 4, :],
                         pE[0:64, :].rearrange("p (h i) -> p h i", h=4), ALU.mult, [o.q, pE], [o.qdT])
                P.tt(o.kdec[:], o.kt[:], bl(o.eg[:, 8:16], 64), ALU.mult, [o.kt, o.eg], [o.kdec], eng="gpsimd")
            for o in D_:
                pWS = PS.next()
                pWS3 = pWS[:].rearrange("p (h e) -> p h e", h=8)
                for h in range(8):
                    P.mm(pWS3[:, h, :], o.wT[:, h, :], o.S[0:64, h, :], True, True, [o.wT, o.S], [pWS])
                P.tt(o.vnew[:], o.U[:], pWS3, ALU.subtract, [o.U, pWS], [o.vnew])
            for o in D_:
                tt = o.tt
                tsl = slice(tt * 128, (tt + 1) * 128)
                pOO = PS.next()
                pOO3 = pOO[:].rearrange("p (h e) -> p h e", h=8)
                for h in range(8):
                    P.mm(pOO3[:, h, :], o.qdT[:, h, :], o.S[:, h, :], True, False, [o.qdT, o.S], [pOO])
                    P.mm(pOO3[:, h, :], o.qkd[:, h, :], o.vnew[:, h, :], False, True, [o.qkd, o.vnew], [pOO])
                P.act(o.ost[:], pOO[:], AF.Copy, [pOO], [o.ost])
                P.dma(og_s[o.d][tsl, :], o.ost[:], reads=[o.ost], writes=[og_s[o.d]])
                pSn = PS.next()
                pSn3 = pSn[0:64, :].rearrange("p (h e) -> p h e", h=8)
                for h in range(8):
                    P.mm(pSn3[:, h, :], o.kdec[:, h, :], o.vnew[:, h, :], True, True, [o.kdec, o.vnew], [pSn])
                P.tt(o.S[0:64], o.S[0:64], o.eg[0:64, 16:24].unsqueeze(2).broadcast_to([64, 8, 64]), ALU.mult, [o.S, o.eg], [o.S])
                P.tt(o.S[0:64], o.S[0:64], pSn3, ALU.add, [o.S, pSn], [o.S])
        P.emit()
        P = Prog(nc, sy, "gdf")
        grow = P.sbuf("grow", [128, 544], F32)
        P.dma(grow[:], grow_d, writes=[grow])
        sqoR = Rot(lambda i: P.sbuf("sqo%d" % i, [128, 8, 64], F32), 2)
        ssR = Rot(lambda i: P.sbuf("ss%d" % i, [128, 8], F32), 2)
        toR = Rot(lambda i: P.sbuf("to%d" % i, [128, 8, 64], F32), 2)
        tobR = Rot(lambda i: P.sbuf("tob%d" % i, [128, 512], BF16), 2)
        gtt = Rot(lambda i: P.sbuf("gtt%d" % i, [128, 512], BF16), 2)
        oa = Rot(lambda i: P.sbuf("oa%d" % i, [128, 512], F32), 2)
        ob_ = Rot(lambda i: P.sbuf("ob%d" % i, [128, 512], F32), 2)
        idb = P.sbuf("idb", [128, 128], BF16)
        P.copy(idb[:], cst[:, C_ID, :], [cst], [idb])
        mixo = Rot(lambda i: P.sbuf("mixo%d" % i, [128, 4, 128], BF16), 2)
        ptbR = Rot(lambda i: P.psum("ptb%d" % i, [128, 4, 128], BF16), 2)
        for tt in range(NT):
            sqo, ss, to, tob, ptb = sqoR.next(), ssR.next(), toR.next(), tobR.next(), ptbR.next()
            tsl = slice(tt * 128, (tt + 1) * 128)
            g_, a_, b_ = gtt.next(), oa.next(), ob_.next()
            P.dma(g_[:], ggate_s[tsl, :], reads=[ggate_s], writes=[g_])
            P.dma(a_[:], og_s[0][tsl, :], reads=[og_s[0]], writes=[a_])
            P.dma(b_[:], og_s[1][tsl, :], reads=[og_s[1]], writes=[b_])
            P.tt(a_[:], a_[:], b_[:], ALU.add, [a_, b_], [a_], eng="gpsimd")
            o3 = a_[:].rearrange("p (h e) -> p h e", h=8)
            P.act(sqo[:], o3, AF.Square, [a_], [sqo])
            P.op("vector", lambda e, ss=ss, sqo=sqo: e.tensor_reduce(out=ss[:], in_=sqo[:], axis=AX.X, op=ALU.add), [sqo], [ss])
            P.act(ss[:], ss[:], AF.Ln, [ss], [ss], bias=EPS, scale=1.0 / 64)
            P.act(ss[:], ss[:], AF.Exp, [ss], [ss], scale=-0.5)
            P.tt(to[:], o3, ss[:].unsqueeze(2).broadcast_to([128, 8, 64]), ALU.mult, [a_, ss], [to])
            P.tt(to[:].rearrange("p h e -> p (h e)"), to[:].rearrange("p h e -> p (h e)"), grow[:, 32:544], ALU.mult,
                 [to, grow], [to], eng="gpsimd")
            P.tt(tob[:], to[:].rearrange("p h e -> p (h e)"), g_[:], ALU.mult, [to, g_], [tob])
            for c in range(4):
                P.tr(ptb[:, c, :], tob[:, c * 128:(c + 1) * 128], idb[:], [tob, idb], [ptb])
            mo_ = mixo.next()
            P.act(mo_[:], ptb[:], AF.Copy, [ptb], [mo_])
            P.dma(mix_s[:, 4:8, tsl], mo_[:], reads=[mo_], writes=[mix_s])
        P.emit()


    for l in layers:
        if stage >= 1 and not skip_ffn1:
            ffn(l, 0, FULL)
        dump_xt("xt_ffn1_%d" % l)
        if stage >= 2:
            make_xn(l)
            if l == 0:
                if parts is None or "mlp" in parts:
                    mla_prep()
                if parts is None or "mlc" in parts:
                    mla_core()
                if parts is None or "gdp" in parts:
                    gdn_prep()
                if parts is None or "gdc" in parts:
                    gdn_core()
                if parts is None or "op" in parts:
                    out_proj(l, evwo_d, GROUPS5)
            if l == 1:
                if parts is None or "swp" in parts:
                    swa_prep()
                if parts is None or "swc" in parts:
                    swa_core()
                if parts is None or "glp" in parts:
                    gla_prep()
                if parts is None or "glc" in parts:
                    gla_core()
                if parts is None or "op" in parts:
                    out_proj(l, odwo_d, LATG)
        if "mix" in dbg_d:
            P = Prog(nc, sy, "dbgmix%d" % l)
            mb_ = Rot(lambda i: P.sbuf("mb%d" % i, [128, 8, 128], BF16), 2)
            mf_ = Rot(lambda i: P.sbuf("mf%d" % i, [128, 8, 128], F32), 2)
            dv = dbg_d["mix"].rearrange("p (c t) -> p c t", c=8)
            for tt in range(NT):
                a_, f_ = mb_.next(), mf_.next()
                P.dma(a_[:], mix_s[:, :, tt * 128:(tt + 1) * 128], reads=[mix_s], writes=[a_])
                P.act(f_[:], a_[:], AF.Copy, [a_], [f_])
                P.dma(dv[:, :, tt * 128:(tt + 1) * 128], f_[:], reads=[f_])
            P.emit()
        dump_xt("xt_mix_%d" % l)
        if stage >= 3:
            ffn(l, 1, FULL if l == 0 else LAT)

    P = Prog(nc, sy, "pout")
    ost = [P.sbuf("ost%d" % i, [128, D], F32) for i in range(2)]
    pot = [P.psum("pot%d" % i, [128, 4, 128]) for i in range(2)]
    for tt in range(2, NT):
        o = ost[tt % 2]
        for half in range(2):
            ps = pot[half]
            for c4 in range(4):
                c = half * 4 + c4
                P.tr(ps[:, c4, :], XT[:, c, tt * 128:(tt + 1) * 128], ident, [(XT, tt), cst], [ps])
            if half == 0:
                P.copy(o[:, 0:512], ps[:].rearrange("p a b -> p (a b)"), [ps], [o])
            else:
                P.act(o[:, 512:1024], ps[:].rearrange("p a b -> p (a b)"), AF.Copy, [ps], [o])
        P.dma(out_d[(tt - 2) * 128:(tt - 1) * 128, :], o[:], reads=[o])
    P.emit()
    es.close()
    return nc


def rope_tables(rot_dim):
    f = np.float32
    t = np.arange(SEQ)
    row = (t // 64).astype(f)
    col = (t % 64).astype(f)
    nf = rot_dim // 4
    inv = (f(10000.0) ** (-np.arange(nf, dtype=f) / f(nf))).astype(f)
    ang = np.concatenate([row[:, None] * inv, col[:, None] * inv], axis=-1).astype(f)
    return np.cos(ang).astype(f), np.sin(ang).astype(f)


def host_consts():
    f = np.float32
    idx = np.arange(128)
    ident = np.eye(128)
    bd64 = (idx[:, None] // 64 == idx[None, :] // 64)
    Rm = np.zeros((64, 64))
    for i in range(32):
        Rm[i, i + 32] = -1.0
        Rm[i + 32, i] = 1.0
    rot = np.zeros((128, 128))
    rot[0:64, 0:64] = Rm.T
    rot[64:128, 64:128] = Rm.T
    UI = idx[:, None] <= idx[None, :]
    LI = idx[:, None] >= idx[None, :]
    SL = idx[:, None] > idx[None, :]
    SU = idx[:, None] < idx[None, :]
    rot96 = np.zeros((128, 128))
    for i in range(16):
        rot96[80 + i, 64 + i] = -1.0
        rot96[64 + i, 80 + i] = 1.0
    cst = np.stack([ident, bd64, rot, UI, LI, SL, SU, rot96], axis=1).astype(f)
    lv = []
    for k in range(7):
        bsz = 1 << k
        same = (idx[:, None] // (2 * bsz)) == (idx[None, :] // (2 * bsz))
        m = same & ((idx[:, None] % (2 * bsz)) >= bsz) & ((idx[None, :] % (2 * bsz)) < bsz)
        lv.append(m)
    lvm = np.stack(lv + [m.T for m in lv], axis=1).astype(f)
    c8, s8 = rope_tables(32)
    mcs = np.zeros((96, 2, SEQ), f)
    mcs[0:64, 0, :] = 1.0
    for dd in range(32):
        mcs[64 + dd, 0, :] = c8[:, dd % 16]
        mcs[64 + dd, 1, :] = s8[:, dd % 16]
    bdsel = np.stack([(idx // 64 == 0), (idx // 64 == 1)], axis=1).astype(f)
    c, s = rope_tables(64)
    d = idx % 64
    swcs = np.stack([c[:, d % 32].T, s[:, d % 32].T], axis=1).astype(f)
    return {"cst": np.ascontiguousarray(cst), "bdsel": np.ascontiguousarray(bdsel), "swa_cs": np.ascontiguousarray(swcs),
            "lvm": np.ascontiguousarray(lvm), "mla_cs": np.ascontiguousarray(mcs)}


_CONSTS = None


def host_inputs(inputs, b):
    global _CONSTS
    f = np.float32
    if _CONSTS is None:
        _CONSTS = host_consts()
    c = np.asarray(inputs["c"][b], f)
    cc = np.asarray(inputs["c_ctx"], f)
    ccol = np.concatenate([c.reshape(8, 128).T, cc.reshape(8, 128).T], axis=1)
    idx = np.arange(128)
    odc = np.zeros((128, 11), f)
    odc[:, 0] = np.asarray(inputs["od_swa_q_norm"], f)[0][idx % 64]
    odc[:, 1] = np.asarray(inputs["od_swa_k_norm"], f)[0][idx % 64]
    odc[:, 2] = np.asarray(inputs["od_gla_out_norm"], f)[0]
    odc[:, 3:11] = np.asarray(inputs["od_swa_sink"], f)[0][None, :]
    w2 = np.asarray(inputs["od_gla_gate_w2"], f)[0]
    gb = np.asarray(inputs["od_gla_gate_b"], f)[0]
    gw2b = np.concatenate([w2, gb[:, None, :]], axis=1).transpose(1, 0, 2)
    evc = np.zeros((128, 8), f)
    evc[:, 0:3] = np.asarray(inputs["ev_q_a_norm"], f)[0].reshape(3, 128).T
    evc[:, 3:5] = np.asarray(inputs["ev_kv_a_norm"], f)[0].reshape(2, 128).T
    evc[0:96, 5] = np.asarray(inputs["ev_mla_q_norm"], f)[0]
    evc[0:96, 6] = np.asarray(inputs["ev_mla_k_norm"], f)[0]
    gcv = np.asarray(inputs["ev_gdn_conv"], f)[0].reshape(5, 12, 128).transpose(2, 1, 0)
    grow = np.zeros((128, 544), f)
    grow[:, 0:16] = np.asarray(inputs["ev_gdn_a_log"], f)[0].reshape(16)[None, :]
    grow[:, 16:32] = np.asarray(inputs["ev_gdn_dt_bias"], f)[0].reshape(16)[None, :]
    grow[:, 32:544] = np.tile(np.asarray(inputs["ev_gdn_out_norm"], f)[0], 8)[None, :]
    m = {
        "ev_w_in": np.ascontiguousarray(np.asarray(inputs["ev_w_in"], f)[0]),
        "ev_w_out": np.ascontiguousarray(np.asarray(inputs["ev_w_out"], f)[0]),
        "ev_w_q_up": np.ascontiguousarray(np.asarray(inputs["ev_w_q_up"], f)[0]),
        "ev_w_kv_up": np.ascontiguousarray(np.asarray(inputs["ev_w_kv_up"], f)[0]),
        "ev_cols": evc,
        "gdn_conv_col": np.ascontiguousarray(gcv),
        "gdn_rows": grow,
        "x": np.ascontiguousarray(inputs["x"][b], dtype=f),
        "ctx": np.ascontiguousarray(inputs["ctx"][b], dtype=f),
        "ccol": np.ascontiguousarray(ccol),
        "ada_w": np.ascontiguousarray(inputs["ada_w"], dtype=f),
        "ada_b_col": np.ascontiguousarray(np.asarray(inputs["ada_b"], f).reshape(2, 72, 128).transpose(2, 0, 1)),
        "norm_g_col": np.ascontiguousarray(np.asarray(inputs["norm_g"], f).reshape(2, 3, 8, 128).transpose(3, 0, 1, 2)),
        "ffn_w_gate": np.ascontiguousarray(inputs["ffn_w_gate"], dtype=f),
        "ffn_w_up": np.ascontiguousarray(inputs["ffn_w_up"], dtype=f),
        "ffn_w_down": np.ascontiguousarray(inputs["ffn_w_down"], dtype=f),
        "od_w_in": np.ascontiguousarray(np.asarray(inputs["od_w_in"], f)[0]),
        "od_w_out": np.ascontiguousarray(np.asarray(inputs["od_w_out"], f)[0]),
        "od_cols": odc,
        "gla_w2b": np.ascontiguousarray(gw2b),
    }
    m.update(_CONSTS)
    return m


def kernel(**inputs):
    nc = build()
    in_maps = [host_inputs(inputs, b) for b in range(NCORES)]
    res = run_bass_kernel_spmd(nc, in_maps, core_ids=list(range(NCORES)))
    return np.stack([np.asarray(r["out"], np.float32) for r in res.results], axis=0)
```

```python
from contextlib import ExitStack
import numpy as np
import concourse.bass as bass
import concourse.mybir as mybir
from concourse.bass_utils import run_bass_kernel_spmd

F32 = mybir.dt.float32
BF16 = mybir.dt.bfloat16
AF = mybir.ActivationFunctionType
ALU = mybir.AluOpType
AX = mybir.AxisListType

D = 1024
DFF = 2816
NF = DFF // 128
SEQ = 2048
CTX = 256
T = SEQ + CTX
NT = T // 128
EPS = 1e-6
NCORES = 8

COMPUTE = ("tensor", "vector", "scalar", "gpsimd")
ENGINES = ("tensor", "vector", "scalar", "gpsimd", "sync")
N_DMA_SEMS = 24


class Buf:
    _n = 0

    def __init__(self, t, nslots=1, name=None):
        self.t = t
        self.nslots = nslots
        Buf._n += 1
        self.id = Buf._n
        self.name = name

    def __getitem__(self, idx):
        return self.t[idx]


class Sync:
    def __init__(self, nc, es):
        self.nc = nc
        self.sems = {e: es.enter_context(nc.semaphore("s_" + e)) for e in COMPUTE}
        self.dsems = [es.enter_context(nc.semaphore("d_%d" % i)) for i in range(N_DMA_SEMS)]
        self.pos = {e: 0 for e in COMPUTE}
        self.n_dma = 0
        self.dma_uses = [0] * N_DMA_SEMS


class Prog:
    def __init__(self, nc, sync, name="ph"):
        self.nc = nc
        self.sy = sync
        self.name = name
        self.es = ExitStack()
        self.ops = []
        self.state = {}
        self.dma_last = [None] * N_DMA_SEMS
        self.start_pos = dict(sync.pos)
        self.start_dma = [16 * u for u in sync.dma_uses]

    def sbuf(self, name, shape, dtype, nslots=1):
        t = self.es.enter_context(self.nc.sbuf_tensor(self.name + "_" + name, list(shape), dtype))
        return Buf(t, nslots, name)

    def psum(self, name, shape, dtype=F32, nslots=1):
        t = self.es.enter_context(self.nc.psum_tensor(self.name + "_" + name, list(shape), dtype))
        return Buf(t, nslots, name)

    def _expand(self, keys):
        out = []
        for k in keys:
            if isinstance(k, Buf):
                k = (k, None)
            b, s = k
            if s is None:
                out.extend((b.id, i) for i in range(b.nslots))
            elif isinstance(s, (list, tuple, range)):
                for i in s:
                    assert 0 <= i < b.nslots, (b.name, i, b.nslots)
                    out.append((b.id, i))
            else:
                assert 0 <= s < b.nslots, (b.name, s, b.nslots)
                out.append((b.id, s))
        return out

    def _deps(self, idx, reads, writes):
        deps = set()
        rk = self._expand(reads)
        wk = self._expand(writes)
        for k in rk:
            st = self.state.setdefault(k, [None, []])
            if st[0] is not None:
                deps.add(st[0])
        for k in wk:
            st = self.state.setdefault(k, [None, []])
            if st[0] is not None:
                deps.add(st[0])
            deps.update(st[1])
        for k in rk:
            self.state[k][1].append(idx)
        for k in wk:
            self.state[k] = [idx, []]
        deps.discard(idx)
        return deps

    def op(self, eng, fn, reads=(), writes=()):
        idx = len(self.ops)
        deps = self._deps(idx, reads, writes)
        self.ops.append(dict(eng=eng, fn=fn, deps=deps, dma=False, pos=self.sy.pos[eng]))
        self.sy.pos[eng] += 1
        return idx

    def dma(self, out, in_, reads=(), writes=(), q="sync"):
        idx = len(self.ops)
        deps = self._deps(idx, reads, writes)
        sy = self.sy
        s = sy.n_dma % N_DMA_SEMS
        sy.n_dma += 1
        prev = self.dma_last[s]
        sy.dma_uses[s] += 1
        self.dma_last[s] = idx
        self.ops.append(dict(eng=q, fn=lambda e: e.dma_start(out=out, in_=in_), deps=deps, dma=True, sem=s,
                             target=16 * sy.dma_uses[s], prev=prev))
        return idx

    def mm(self, out, lhsT, rhs, start, stop, r, w):
        return self.op("tensor", lambda e: e.matmul(out, lhsT=lhsT, rhs=rhs, start=start, stop=stop), r, w)

    def tr(self, out, in_, ident, r, w):
        return self.op("tensor", lambda e: e.transpose(out, in_, ident), r, w)

    def act(self, out, in_, func, r, w, bias=None, scale=None, accum_out=None):
        kw = {}
        if bias is not None:
            kw["bias"] = bias
        if scale is not None:
            kw["scale"] = scale
        if accum_out is not None:
            kw["accum_out"] = accum_out
        return self.op("scalar", lambda e: e.activation(out=out, in_=in_, func=func, **kw), r, w)

    def tt(self, out, in0, in1, op, r, w, eng="vector"):
        return self.op(eng, lambda e: e.tensor_tensor(out=out, in0=in0, in1=in1, op=op), r, w)

    def ts(self, out, in0, s1, op0, r, w, s2=None, op1=None, eng="vector"):
        if op1 is None:
            return self.op(eng, lambda e: e.tensor_scalar(out=out, in0=in0, scalar1=s1, scalar2=None, op0=op0), r, w)
        return self.op(eng, lambda e: e.tensor_scalar(out=out, in0=in0, scalar1=s1, scalar2=s2, op0=op0, op1=op1), r, w)

    def stt(self, out, in0, scalar, in1, op0, op1, r, w):
        return self.op("vector", lambda e: e.scalar_tensor_tensor(out=out, in0=in0, scalar=scalar, in1=in1,
                                                                  op0=op0, op1=op1), r, w)

    def copy(self, out, in_, r, w, eng="vector"):
        return self.op(eng, lambda e: e.tensor_copy(out=out, in_=in_), r, w)

    def memset(self, ap, val, w, eng="vector"):
        return self.op(eng, lambda e: e.memset(ap, val), (), w)

    def emit(self):
        nc = self.nc
        sy = self.sy
        ops = self.ops
        per_eng = {e: [] for e in ENGINES}
        for i, o in enumerate(ops):
            per_eng[o["eng"]].append(i)
        end_pos = dict(sy.pos)
        end_dma = [16 * u for u in sy.dma_uses]
        start_pos = self.start_pos
        start_dma = self.start_dma

        def replay(ename):
            def body(eng):
                waited = dict(start_pos)
                dwaited = list(start_dma)
                for i in per_eng[ename]:
                    o = ops[i]
                    need = {}
                    dneed = {}
                    deps = set(o["deps"])
                    if o["dma"] and o["prev"] is not None:
                        deps.add(o["prev"])
                    for j in deps:
                        d = ops[j]
                        if d["dma"]:
                            dneed[d["sem"]] = max(dneed.get(d["sem"], 0), d["target"])
                        else:
                            if d["eng"] == "tensor" and ename == "tensor" and not o["dma"]:
                                continue
                            need[d["eng"]] = max(need.get(d["eng"], 0), d["pos"] + 1)
                    for e, v in need.items():
                        if v > waited[e]:
                            eng.wait_ge(sy.sems[e], v)
                            waited[e] = v
                    for s, v in dneed.items():
                        if v > dwaited[s]:
                            eng.wait_ge(sy.dsems[s], v)
                            dwaited[s] = v
                    ins = o["fn"](eng)
                    if o["dma"]:
                        ins.then_inc(sy.dsems[o["sem"]], 16)
                    else:
                        ins.then_inc(sy.sems[ename], 1)
                if ename == "sync":
                    for s in range(N_DMA_SEMS):
                        if end_dma[s] > start_dma[s]:
                            eng.wait_ge(sy.dsems[s], end_dma[s])
                    for e in COMPUTE:
                        if end_pos[e] > start_pos[e]:
                            eng.wait_ge(sy.sems[e], end_pos[e])
            return body

        with nc.Block() as block:
            for e in ENGINES:
                if per_eng[e] or e == "sync":
                    getattr(block, e)(replay(e))
        self.es.close()


def segs(t0, n):
    out = []
    if t0 < CTX:
        m = min(n, CTX - t0)
        out.append((t0, m, 1))
        if n > m:
            out.append((CTX, n - m, 0))
    else:
        out.append((t0, n, 0))
    return out


def tiles_of(t0, n):
    return list(range(t0 // 128, (t0 + n + 127) // 128))


class K:
    pass


class Rot:
    def __init__(self, mk, n):
        self.t = [mk(i) for i in range(n)]
        self.i = 0

    def next(self):
        t = self.t[self.i % len(self.t)]
        self.i += 1
        return t


GROUPS5 = [(0, 256), (256, 512), (768, 512), (1280, 512), (1792, 512)]
LATG = GROUPS5[1:]
FULL = [[(0, 384), (384, 384), (768, 384)], [(1152, 384), (1536, 384), (1920, 384)]]
LAT = [[(256, 512), (768, 512)], [(1280, 512), (1792, 512)]]
C_ID, C_BD64, C_ROT64, C_UI, C_LI, C_SL, C_SU, C_ROT96 = range(8)
NCST = 8


def build(stage=99, dbg=(), layers=(0, 1), parts=None, skip_ffn1=False):
    nc = bass.Bass("TRN2", target_bir_lowering=False)
    dt = nc.dram_tensor

    def inp(name, shape, dtype=F32):
        return dt(name, list(shape), dtype, kind="ExternalInput").ap()

    def scratch(name, shape, dtype, nslots=1):
        return Buf(dt("scr_" + name, list(shape), dtype, kind="Internal").ap(), nslots, name)

    x_d = inp("x", [SEQ, D])
    ctx_d = inp("ctx", [CTX, D])
    ccol_d = inp("ccol", [128, 16])
    adaw_d = inp("ada_w", [2, D, 9 * D])
    adab_d = inp("ada_b_col", [128, 2, 72])
    ng_d = inp("norm_g_col", [128, 2, 3, 8])
    wg_d = inp("ffn_w_gate", [2, 2, D, DFF])
    wu_d = inp("ffn_w_up", [2, 2, D, DFF])
    wd_d = inp("ffn_w_down", [2, 2, DFF, D])
    cst_d = inp("cst", [128, NCST, 128])
    bdsel_d = inp("bdsel", [128, 2])
    odw_d = inp("od_w_in", [D, 2336])
    odwo_d = inp("od_w_out", [D, D])
    odcol_d = inp("od_cols", [128, 11])
    swcs_d = inp("swa_cs", [128, 2, SEQ])
    gw2b_d = inp("gla_w2b", [17, 2, 256])
    evw_d = inp("ev_w_in", [D, 2752])
    evwo_d = inp("ev_w_out", [D, D])
    evqu_d = inp("ev_w_q_up", [384, 768])
    evkv_d = inp("ev_w_kv_up", [256, 1024])
    evcol_d = inp("ev_cols", [128, 8])
    mlcs_d = inp("mla_cs", [96, 2, SEQ])
    lvm_d = inp("lvm", [128, 14, 128])
    gcv_d = inp("gdn_conv_col", [128, 12, 5])
    grow_d = inp("gdn_rows", [128, 544])
    out_d = dt("out", [SEQ, D], F32, kind="ExternalOutput").ap()
    dbg_d = {}
    for name, shape in dbg:
        dbg_d[name] = dt("dbg_" + name, list(shape), F32, kind="ExternalOutput").ap()

    xn_s = scratch("xn", [128, 8, T], BF16)
    mix_s = scratch("mix", [128, 8, T], BF16)
    swq_s = scratch("swq", [128, 4, SEQ], BF16)
    swk_s = scratch("swk", [128, T], BF16)
    swv_s = scratch("swv", [128, NT * 256], BF16)
    swr_s = scratch("swr", [128, NT * 2], F32)
    glq_s = [scratch("glq%d" % d, [128, 2, T], BF16) for d in range(2)]
    glk_s = [scratch("glk%d" % d, [128, 2, T], BF16) for d in range(2)]
    glkd_s = [scratch("glkd%d" % d, [T, 256], BF16) for d in range(2)]
    glet_s = [scratch("glet%d" % d, [128, 2, NT], F32) for d in range(2)]
    glv_s = scratch("glv", [T, 512], BF16)
    glr_s = scratch("glr", [128, 4, SEQ], BF16)

    mqf_s = scratch("mqf", [96, 8, T], BF16)
    mkf_s = scratch("mkf", [96, 8, T], BF16)
    mv_s = scratch("mv", [T, 512], BF16)
    mrk_s = scratch("mrk", [128, NT * 8], F32)
    gq_s = scratch("gq", [512, T], F32)
    gk_s = scratch("gk", [512, T], F32)
    gktm_s = scratch("gktm", [T, 512], F32)
    gvtm_s = scratch("gvtm", [T, 512], F32)
    gg_s = scratch("gg", [128, NT * 16], F32)
    gb_s = scratch("gb", [128, NT * 16], F32)
    ggate_s = scratch("ggate", [T, 512], BF16)
    og_s = [scratch("og%d" % d, [T, 512], F32) for d in range(2)]
    xt_s = scratch("xtpark", [128, 4 * T], F32)

    es = ExitStack()
    sy = Sync(nc, es)

    def persist(name, shape, dtype, nslots=1):
        return Buf(es.enter_context(nc.sbuf_tensor(name, list(shape), dtype)), nslots, name)

    XT = persist("XT", [128, 8, T], F32, NT)
    cst = persist("cst_s", [128, NCST, 128], F32)
    ident = cst[:, C_ID, :]
    ones_f = persist("ones_f", [128, 128], F32)
    ones_b = persist("ones_b", [128, 128], BF16)
    modT = persist("modT", [128, 2, 72, 2], F32)
    AG = persist("AG", [128, 2, 3, 8, 2], F32)
    GH = persist("GH", [128, 2, 3, 8, 2], F32)
    ng = persist("ng", [128, 2, 3, 8], F32)

    def xk(t0, n):
        return [(XT, i) for i in tiles_of(t0, n)]

    P = Prog(nc, sy, "p0")
    P.dma(cst[:], cst_d, writes=[cst])
    P.dma(ng[:], ng_d, writes=[ng])
    P.memset(ones_f[:], 1.0, [ones_f])
    P.memset(ones_b[:], 1.0, [ones_b])
    xin = [P.sbuf("xin%d" % i, [128, D], F32) for i in range(2)]
    ptr = [P.psum("ptr%d" % i, [128, 4, 128]) for i in range(2)]
    for tt in range(NT):
        st = xin[tt % 2]
        src = ctx_d[tt * 128:(tt + 1) * 128, :] if tt < 2 else x_d[(tt - 2) * 128:(tt - 1) * 128, :]
        P.dma(st[:], src, writes=[st])
        for half in range(2):
            ps = ptr[half]
            for c4 in range(4):
                c = half * 4 + c4
                P.tr(ps[:, c4, :], st[:, c * 128:(c + 1) * 128], ident, [st, cst], [ps])
            if half == 0:
                P.copy(XT[:, 0:4, tt * 128:(tt + 1) * 128], ps[:], [ps], [(XT, tt)])
            else:
                P.act(XT[:, 4:8, tt * 128:(tt + 1) * 128], ps[:], AF.Copy, [ps], [(XT, tt)])

    ccol = P.sbuf("ccol", [128, 16], F32)
    sc3 = P.sbuf("sc3", [128, 8, 2], F32)
    adab = P.sbuf("adab", [128, 2, 72], F32)
    P.dma(ccol[:], ccol_d, writes=[ccol])
    P.dma(adab[:], adab_d, writes=[adab])
    P.act(sc3[:, :, 0], ccol[:, 0:8], AF.Silu, [ccol], [sc3])
    P.act(sc3[:, :, 1], ccol[:, 8:16], AF.Silu, [ccol], [sc3])
    ABLK = 256
    awst = [P.sbuf("awst%d" % i, [128, 8, ABLK], F32) for i in range(2)]
    pmod = P.psum("pmod", [128, 72, 2])
    nb = 0
    for l in range(2):
        aw = adaw_d[l].rearrange("(kc p) n -> p kc n", p=128)
        for blk in range(9 * D // ABLK):
            st = awst[nb % 2]
            nb += 1
            P.dma(st[:], aw[:, :, blk * ABLK:(blk + 1) * ABLK], writes=[st])
            for m in range(ABLK // 128):
                j = blk * (ABLK // 128) + m
                for kc in range(8):
                    P.mm(pmod[:, j, :], st[:, kc, m * 128:(m + 1) * 128], sc3[:, kc, :], kc == 0, kc == 7,
                         [st, sc3], [pmod])
        for s in range(2):
            P.tt(modT[:, l, :, s], pmod[:, :, s], adab[:, l, :], ALU.add, [pmod, adab], [modT])
    for l in range(2):
        for n in range(3):
            for s in range(2):
                P.stt(AG[:, l, n, :, s], modT[:, l, (3 * n + 1) * 8:(3 * n + 2) * 8, s], 1.0, ng[:, l, n, :],
                      ALU.add, ALU.mult, [modT, ng], [AG])
                P.ts(GH[:, l, n, :, s], modT[:, l, (3 * n + 2) * 8:(3 * n + 3) * 8, s], 0.5 if n != 1 else 1.0,
                     ALU.mult, [modT], [GH])
    if "modT" in dbg_d:
        P.dma(dbg_d["modT"], modT[:].rearrange("p l j s -> p (l j s)"), reads=[modT])
    P.emit()

    def norm_mod(P, W, l, n_idx, t0, n, dst_fn, dst_keys):
        sq, lnv, rs, pss = (w.next() for w in W)
        P.act(sq[:, :, 0:n], XT[:, :, t0:t0 + n], AF.Square, xk(t0, n), [sq])
        for c in range(8):
            P.mm(pss[:, 0:n], ones_f[:], sq[:, c, 0:n], c == 0, c == 7, [sq, ones_f], [pss])
        P.act(lnv[:, 0:n], pss[:, 0:n], AF.Ln, [pss], [lnv], bias=EPS, scale=1.0 / D)
        P.act(rs[:, 0:n], lnv[:, 0:n], AF.Exp, [lnv], [rs], scale=-0.5)
        P.tt(sq[:, :, 0:n], XT[:, :, t0:t0 + n], rs[:, 0:n].unsqueeze(1).broadcast_to([128, 8, n]), ALU.mult,
             xk(t0, n) + [rs, sq], [sq])
        for (s0, sn, s) in segs(t0, n):
            for c in range(8):
                P.act(dst_fn(c, s0, sn), sq[:, c, s0 - t0:s0 - t0 + sn], AF.Identity,
                      [sq], dst_keys, bias=modT[:, l, (3 * n_idx) * 8 + c, s:s + 1],
                      scale=AG[:, l, n_idx, c, s:s + 1])

    def norm_ws(P, nmax, nb=1):
        return (Rot(lambda i: P.sbuf("sq%d" % i, [128, 8, nmax], F32), nb),
                Rot(lambda i: P.sbuf("lnv%d" % i, [128, nmax], F32), nb),
                Rot(lambda i: P.sbuf("rs%d" % i, [128, nmax], F32), nb),
                Rot(lambda i: P.psum("pss%d" % i, [128, 512]), nb))

    def ffn(l, which, halves):
        n_idx = 0 if which == 0 else 2
        wg_v = wg_d[l, which].rearrange("(kc p) n -> p kc n", p=128)
        wu_v = wu_d[l, which].rearrange("(kc p) n -> p kc n", p=128)
        wd_v = wd_d[l, which].rearrange("(f p) n -> p f n", p=128)
        for hi, groups in enumerate(halves):
            h0 = groups[0][0]
            NH = sum(g[1] for g in groups)
            ng_ = len(groups)
            tag = "f%d%d%d" % (l, which, hi)
            esA = ExitStack()
            AT = Buf(esA.enter_context(nc.sbuf_tensor(tag + "AT", [128, NF, NH], BF16)), NF * ng_, "AT")
            P = Prog(nc, sy, tag + "a")
            hT = P.sbuf("hT", [128, 8, NH], BF16, ng_)
            nmax = max(g[1] for g in groups)
            W = norm_ws(P, nmax)
            for gi, (t0, n) in enumerate(groups):
                norm_mod(P, W, l, n_idx, t0, n, lambda c, s0, sn: hT[:, c, s0 - h0:s0 - h0 + sn], [(hT, gi)])
            stg = P.sbuf("stg", [128, 8, 256], F32)
            stu = P.sbuf("stu", [128, 8, 256], F32)
            wgb = [P.sbuf("wgb%d" % i, [128, 8, 256], BF16) for i in range(2)]
            wub = [P.sbuf("wub%d" % i, [128, 8, 256], BF16) for i in range(2)]
            psg = [P.psum("psg%d" % i, [128, 512]) for i in range(2)]
            psu = [P.psum("psu%d" % i, [128, 512]) for i in range(2)]
            sga = [P.sbuf("sga%d" % i, [128, nmax], F32) for i in range(2)]
            it = 0
            NB = NF // 2

            def fetch_a(fb):
                P.dma(stg[:], wg_v[:, :, fb * 256:(fb + 1) * 256], writes=[stg])
                P.dma(stu[:], wu_v[:, :, fb * 256:(fb + 1) * 256], writes=[stu])

            def cast_a(fb):
                P.act(wgb[fb % 2][:], stg[:], AF.Copy, [stg], [wgb[fb % 2]])
                P.copy(wub[fb % 2][:], stu[:], [stu], [wub[fb % 2]])

            fetch_a(0)
            cast_a(0)
            for fb in range(NB):
                pb = fb % 2
                if fb + 1 < NB:
                    fetch_a(fb + 1)
                cnt = 0
                for f2 in range(2):
                    f = fb * 2 + f2
                    for gi, (t0, n) in enumerate(groups):
                        o0 = t0 - h0
                        pg, pu, sg = psg[it % 2], psu[it % 2], sga[it % 2]
                        it += 1
                        for kc in range(8):
                            P.mm(pg[:, 0:n], wgb[pb][:, kc, f2 * 128:(f2 + 1) * 128], hT[:, kc, o0:o0 + n], kc == 0, kc == 7,
                                 [wgb[pb], (hT, gi)], [pg])
                        for kc in range(8):
                            P.mm(pu[:, 0:n], wub[pb][:, kc, f2 * 128:(f2 + 1) * 128], hT[:, kc, o0:o0 + n], kc == 0, kc == 7,
                                 [wub[pb], (hT, gi)], [pu])
                        P.act(sg[:, 0:n], pg[:, 0:n], AF.Silu, [pg], [sg])
                        P.tt(AT[:, f, o0:o0 + n], sg[:, 0:n], pu[:, 0:n], ALU.mult, [sg, pu], [(AT, f * ng_ + gi)])
                        cnt += 1
                        if cnt == ng_ and fb + 1 < NB:
                            cast_a(fb + 1)
            P.emit()
            P = Prog(nc, sy, tag + "b")
            std = P.sbuf("std", [128, NF, 256], F32)
            wdb = [P.sbuf("wdb%d" % i, [128, NF, 256], BF16) for i in range(2)]
            psy = [P.psum("psy%d" % i, [128, 512]) for i in range(2)]
            it = 0

            def fetch_b(db):
                P.dma(std[:], wd_v[:, :, db * 256:(db + 1) * 256], writes=[std])
                P.act(wdb[db % 2][:], std[:], AF.Copy, [std], [wdb[db % 2]])

            fetch_b(0)
            for db in range(4):
                pb = db % 2
                if db + 1 < 4:
                    fetch_b(db + 1)
                for d2 in range(2):
                    d = db * 2 + d2
                    for gi, (t0, n) in enumerate(groups):
                        o0 = t0 - h0
                        py = psy[it % 2]
                        it += 1
                        for f in range(NF):
                            P.mm(py[:, 0:n], wdb[pb][:, f, d2 * 128:(d2 + 1) * 128], AT[:, f, o0:o0 + n], f == 0, f == NF - 1,
                                 [wdb[pb], (AT, f * ng_ + gi)], [py])
                        for (s0, sn, s) in segs(t0, n):
                            P.stt(XT[:, d, s0:s0 + sn], py[:, s0 - t0:s0 - t0 + sn], GH[:, l, n_idx, d, s:s + 1],
                                  XT[:, d, s0:s0 + sn], ALU.mult, ALU.add, [py] + xk(s0, sn), xk(s0, sn))
            P.emit()
            esA.close()

    def dump_xt(name):
        if name in dbg_d:
            P = Prog(nc, sy, "dbg" + name)
            P.dma(dbg_d[name], XT[:].rearrange("p c t -> p (c t)"), reads=[XT])
            P.emit()

    def make_xn(l):
        P = Prog(nc, sy, "xn%d" % l)
        W = norm_ws(P, 384, 2)
        hb = Rot(lambda i: P.sbuf("hb%d" % i, [128, 8, 384], BF16), 2)
        for (t0, n) in FULL[0] + FULL[1]:
            h = hb.next()
            norm_mod(P, W, l, 1, t0, n, lambda c, s0, sn, h=h, t0=t0: h[:, c, s0 - t0:s0 - t0 + sn], [h])
            P.dma(xn_s[:, :, t0:t0 + n], h[:, :, 0:n], reads=[h], writes=[xn_s])
        P.emit()

    def load_w(P, dst, dst_ap, src_ap, stage, stage_ap):
        P.dma(stage_ap, src_ap, writes=[stage])
        P.copy(dst_ap, stage_ap, [stage], [dst], eng="gpsimd")

    def out_proj(l, wo_d, groups):
        P = Prog(nc, sy, "op%d" % l)
        wo = P.sbuf("wo", [128, 8, D], BF16)
        stw = P.sbuf("stw", [128, 8, 256], F32)
        wv = wo_d.rearrange("(kc p) n -> p kc n", p=128)
        for q in range(4):
            load_w(P, wo, wo[:, :, q * 256:(q + 1) * 256], wv[:, :, q * 256:(q + 1) * 256], stw, stw[:])
        mb = Rot(lambda i: P.sbuf("mb%d" % i, [128, 8, 512], BF16), 2)
        pp = Rot(lambda i: P.psum("pp%d" % i, [128, 512]), 3)
        for (t0, n) in groups:
            m = mb.next()
            P.dma(m[:, :, 0:n], mix_s[:, :, t0:t0 + n], reads=[mix_s], writes=[m])
            for dc in range(8):
                ps = pp.next()
                for kc in range(8):
                    P.mm(ps[:, 0:n], wo[:, kc, dc * 128:(dc + 1) * 128], m[:, kc, 0:n], kc == 0, kc == 7, [wo, m], [ps])
                for (s0, sn, s) in segs(t0, n):
                    P.stt(XT[:, dc, s0:s0 + sn], ps[:, s0 - t0:s0 - t0 + sn], GH[:, l, 1, dc, s:s + 1],
                          XT[:, dc, s0:s0 + sn], ALU.mult, ALU.add, [ps] + xk(s0, sn), xk(s0, sn))
        P.emit()

    def swa_prep():
        P = Prog(nc, sy, "swp")
        wv_ = odw_d.rearrange("(kc p) n -> p kc n", p=128)
        odc = P.sbuf("odc", [128, 11], F32)
        P.dma(odc[:], odcol_d, writes=[odc])
        bds = P.sbuf("bds", [128, 2], F32)
        P.dma(bds[:], bdsel_d, writes=[bds])
        cs = P.sbuf("cs", [128, 2, SEQ], F32)
        P.dma(cs[:], swcs_d, writes=[cs])
        st = P.sbuf("st", [128, 8, 512], F32)
        wq = P.sbuf("wq", [128, 8, 4, 2, 64], BF16)
        wk = P.sbuf("wk", [128, 8, 128], BF16)
        wvv = P.sbuf("wvv", [128, 8, 128], BF16)
        P.dma(st[:], wv_[:, :, 1568:2080], writes=[st])
        st5 = st[:].rearrange("p k (g j d) -> p k g j d", g=2, j=4)
        for g in range(2):
            P.copy(wq[:, :, :, g, :], st5[:, :, g, :, :], [st], [wq], eng="gpsimd")
        load_w(P, wk, wk[:], wv_[:, :, 2080:2208], st, st[:, :, 0:128])
        load_w(P, wvv, wvv[:], wv_[:, :, 2208:2336], st, st[:, :, 0:128])
        xg = Rot(lambda i: P.sbuf("xg%d" % i, [128, 8, 512], BF16), 2)
        pp = Rot(lambda i: P.psum("pp%d" % i, [128, 512]), 3)
        pssR = Rot(lambda i: P.psum("pss%d" % i, [128, 512]), 2)
        prot = P.psum("prot", [128, 512])
        prk = P.psum("prk", [128, NT, 2])
        pv = P.psum("pv", [128, 128])
        sqR = Rot(lambda i: P.sbuf("sq%d" % i, [128, 512], F32), 2)
        lnvR = Rot(lambda i: P.sbuf("lnv%d" % i, [128, 512], F32), 2)
        rsR = Rot(lambda i: P.sbuf("rs%d" % i, [128, 512], F32), 2)
        qgR = Rot(lambda i: P.sbuf("qg%d" % i, [128, 512], F32), 2)
        t1R = Rot(lambda i: P.sbuf("t1%d" % i, [128, 512], F32), 2)
        t2R = Rot(lambda i: P.sbuf("t2%d" % i, [128, 512], F32), 2)
        qo = Rot(lambda i: P.sbuf("qo%d" % i, [128, 512], BF16), 2)
        vd = P.sbuf("vd", [128, NT, 2, 2, 64], BF16)
        rsk = P.sbuf("rsk", [128, NT * 2], F32)
        for (t0, n) in GROUPS5:
            x = xg.next()
            P.dma(x[:, :, 0:n], xn_s[:, :, t0:t0 + n], reads=[xn_s], writes=[x])
            lat = t0 >= CTX
            l0 = t0 - CTX
            if lat:
                for j in range(4):
                    ps = pp.next()
                    for kc in range(8):
                        P.mm(ps[:, 0:n], wq[:, kc, j].rearrange("p g d -> p (g d)"), x[:, kc, 0:n], kc == 0, kc == 7,
                             [wq, x], [ps])
                    sq, pss, lnv, rs, qg, t1, t2 = (sqR.next(), pssR.next(), lnvR.next(), rsR.next(), qgR.next(),
                                                    t1R.next(), t2R.next())
                    P.act(sq[:, 0:n], ps[:, 0:n], AF.Square, [ps], [sq])
                    P.mm(pss[:, 0:n], cst[:, C_BD64, :], sq[:, 0:n], True, True, [cst, sq], [pss])
                    P.act(lnv[:, 0:n], pss[:, 0:n], AF.Ln, [pss], [lnv], bias=EPS, scale=1.0 / 64)
                    P.act(rs[:, 0:n], lnv[:, 0:n], AF.Exp, [lnv], [rs], scale=-0.5)
                    P.act(qg[:, 0:n], ps[:, 0:n], AF.Identity, [ps], [qg], scale=odc[:, 0:1])
                    P.mm(prot[:, 0:n], cst[:, C_ROT64, :], qg[:, 0:n], True, True, [cst, qg], [prot])
                    P.tt(t1[:, 0:n], qg[:, 0:n], cs[:, 0, l0:l0 + n], ALU.mult, [qg, cs], [t1])
                    P.tt(t2[:, 0:n], prot[:, 0:n], cs[:, 1, l0:l0 + n], ALU.mult, [prot, cs], [t2])
                    P.tt(t1[:, 0:n], t1[:, 0:n], t2[:, 0:n], ALU.add, [t1, t2], [t1])
                    q = qo.next()
                    P.tt(q[:, 0:n], t1[:, 0:n], rs[:, 0:n], ALU.mult, [t1, rs], [q])
                    P.dma(swq_s[:, j, l0:l0 + n], q[:, 0:n], reads=[q], writes=[swq_s])
            ps = pp.next()
            for kc in range(8):
                P.mm(ps[:, 0:n], wk[:, kc, :], x[:, kc, 0:n], kc == 0, kc == 7, [wk, x], [ps])
            sq, qg, t1, t2 = sqR.next(), qgR.next(), t1R.next(), t2R.next()
            P.act(sq[:, 0:n], ps[:, 0:n], AF.Square, [ps], [sq])
            for ti in range(n // 128):
                tt = t0 // 128 + ti
                P.mm(prk[:, tt, :], sq[:, ti * 128:(ti + 1) * 128], bds[:], True, True, [sq, bds], [prk])
            q = qo.next()
            if lat:
                P.act(qg[:, 0:n], ps[:, 0:n], AF.Identity, [ps], [qg], scale=odc[:, 1:2])
                P.mm(prot[:, 0:n], cst[:, C_ROT64, :], qg[:, 0:n], True, True, [cst, qg], [prot])
                P.tt(t1[:, 0:n], qg[:, 0:n], cs[:, 0, l0:l0 + n], ALU.mult, [qg, cs], [t1])
                P.tt(t2[:, 0:n], prot[:, 0:n], cs[:, 1, l0:l0 + n], ALU.mult, [prot, cs], [t2])
                P.tt(q[:, 0:n], t1[:, 0:n], t2[:, 0:n], ALU.add, [t1, t2], [q])
            else:
                P.act(q[:, 0:n], ps[:, 0:n], AF.Identity, [ps], [q], scale=odc[:, 1:2])
            P.dma(swk_s[:, t0:t0 + n], q[:, 0:n], reads=[q], writes=[swk_s])
            for ti in range(n // 128):
                tt = t0 // 128 + ti
                for kc in range(8):
                    P.mm(pv[:], x[:, kc, ti * 128:(ti + 1) * 128], wvv[:, kc, :], kc == 0, kc == 7, [x, wvv], [pv])
                pv3 = pv[:].rearrange("p (g d) -> p g d", g=2)
                P.copy(vd[:, tt, :, 0, :], pv3, [pv], [vd])
                P.act(vd[:, tt, :, 1, :], pv3, AF.Copy, [pv], [vd])
        P.act(rsk[:], prk[:].rearrange("p a b -> p (a b)"), AF.Ln, [prk], [rsk], bias=EPS, scale=1.0 / 64)
        P.act(rsk[:], rsk[:], AF.Exp, [rsk], [rsk], scale=-0.5)
        P.ts(rsk[:], rsk[:], 0.125, ALU.mult, [rsk], [rsk])
        P.dma(swr_s[:], rsk[:], reads=[rsk], writes=[swr_s])
        P.dma(swv_s[:], vd[:].rearrange("p a b c d -> p (a b c d)"), reads=[vd], writes=[swv_s])
        P.emit()

    def swa_core():
        P = Prog(nc, sy, "swc")
        q = P.sbuf("q", [128, 4, SEQ], BF16)
        k = P.sbuf("k", [128, T], BF16)
        v = P.sbuf("v", [128, NT, 2, 128], BF16)
        rsk = P.sbuf("rsk", [128, NT, 2], F32)
        odc = P.sbuf("odc", [128, 11], F32)
        esk = P.sbuf("esk", [128, 8], F32)
        mlo = P.sbuf("mlo", [128, 128], BF16)
        mhi = P.sbuf("mhi", [128, 128], BF16)
        P.dma(q[:], swq_s[:], reads=[swq_s], writes=[q])
        P.dma(k[:], swk_s[:], reads=[swk_s], writes=[k])
        P.dma(v[:].rearrange("p a b c -> p (a b c)"), swv_s[:], reads=[swv_s], writes=[v])
        P.dma(rsk[:].rearrange("p a b -> p (a b)"), swr_s[:], reads=[swr_s], writes=[rsk])
        P.dma(odc[:], odcol_d, writes=[odc])
        P.act(esk[:], odc[:, 3:11], AF.Exp, [odc], [esk])
        P.copy(mlo[:], cst[:, C_LI, :], [cst], [mlo])
        P.copy(mhi[:], cst[:, C_UI, :], [cst], [mhi])
        msw = P.sbuf("msw", [128, 4, SEQ], BF16)
        pS = Rot(lambda i: P.psum("pS%d" % i, [128, 4, 128]), 4)
        pO = Rot(lambda i: P.psum("pO%d" % i, [128, 4, 128]), 2)
        pM = Rot(lambda i: P.psum("pM%d" % i, [128, 4, 128]), 2)
        pT = Rot(lambda i: P.sbuf("pT%d" % i, [128, 4, 128], BF16), 4)
        den = P.sbuf("den", [128, 4, 128], F32)
        rden = P.sbuf("rden", [128, 4, 128], F32)
        for n in range(16):
            for g in range(2):
                kts = [(0, None), (1, None)]
                if n - 1 >= 0:
                    kts.append((n + 1, mlo))
                kts.append((n + 2, None))
                if n + 1 <= 15:
                    kts.append((n + 3, mhi))
                po = pO.next()
                pm = pM.next()
                def scores(kt, mask):
                    ps = pS.next()
                    P.mm(ps[:], k[g * 64:(g + 1) * 64, kt * 128:(kt + 1) * 128],
                         q[g * 64:(g + 1) * 64, :, n * 128:(n + 1) * 128], True, True, [k, q], [ps])
                    pt = pT.next()
                    P.act(pt[:], ps[:], AF.Exp, [ps, rsk], [pt], scale=rsk[:, kt, g:g + 1])
                    if mask is not None:
                        P.tt(pt[:], pt[:], mask[:].unsqueeze(1).broadcast_to([128, 4, 128]), ALU.mult, [pt, mask], [pt])
                    return pt

                nxt = scores(*kts[0])
                for ki, (kt, mask) in enumerate(kts):
                    pt = nxt
                    if ki + 1 < len(kts):
                        nxt = scores(*kts[ki + 1])
                    P.mm(po[:], v[:, kt, g, :], pt[:], ki == 0, ki == len(kts) - 1, [v, pt], [po])
                    P.mm(pm[:], ones_b[:], pt[:], ki == 0, ki == len(kts) - 1, [ones_b, pt], [pm])
                P.tt(den[:], pm[:], esk[:, 4 * g:4 * g + 4].unsqueeze(2).broadcast_to([128, 4, 128]), ALU.add,
                     [pm, esk], [den])
                P.act(rden[:], den[:], AF.Ln, [den], [rden])
                P.act(rden[:], rden[:], AF.Exp, [rden], [rden], scale=-1.0)
                for par in range(2):
                    sl = slice(par * 64, (par + 1) * 64)
                    P.tt(msw[sl, 2 * g:2 * g + 2, n * 128:(n + 1) * 128], po[sl, par::2, :], rden[sl, par::2, :],
                         ALU.mult, [po, rden], [msw])
        P.dma(mix_s[:, 4:8, CTX:T], msw[:], reads=[msw], writes=[mix_s])
        P.emit()

    def gla_prep():
        P = Prog(nc, sy, "glp")
        wv_ = odw_d.rearrange("(kc p) n -> p kc n", p=128)
        stR = Rot(lambda i: P.sbuf("st%d" % i, [128, 8, 256], F32), 1)
        wqk = P.sbuf("wqk", [128, 8, 512], BF16)
        wvv = P.sbuf("wvv", [128, 8, 512], BF16)
        wgz = P.sbuf("wgz", [128, 8, 32], BF16)
        wr = P.sbuf("wr", [128, 8, 512], BF16)
        for (wt_, c0_) in ((wqk, 0), (wvv, 512), (wr, 1056)):
            for hf in range(2):
                st = stR.next()
                load_w(P, wt_, wt_[:, :, hf * 256:(hf + 1) * 256], wv_[:, :, c0_ + hf * 256:c0_ + (hf + 1) * 256], st, st[:])
        st = stR.next()
        load_w(P, wgz, wgz[:], wv_[:, :, 1024:1056], st, st[:, :, 0:32])
        g2f = P.sbuf("g2f", [17, 2, 256], F32)
        g2b = P.sbuf("g2b", [17, 2, 256], BF16)
        P.dma(g2f[:], gw2b_d, writes=[g2f])
        P.copy(g2b[:], g2f[:], [g2f], [g2b])
        msk = P.sbuf("msk", [128, 512], F32)
        P.memset(msk[:], 1.0, [msk])
        for i in range(4):
            P.memset(msk[:, i * 128:i * 128 + 1], 0.0, [msk])
        gz = [P.sbuf("gz%d" % d, [17, 512], BF16) for d in range(2)]
        for d in range(2):
            P.memset(gz[d][:], 1.0, [gz[d]])
        xg = Rot(lambda i: P.sbuf("xg%d" % i, [128, 8, 512], BF16), 2)
        pp = Rot(lambda i: P.psum("pp%d" % i, [128, 512]), 4)
        ptr_ = P.psum("ptr", [128, 2, 128], BF16)
        pgz = P.psum("pgz", [16, 512])
        qkraw = P.sbuf("qkraw", [128, 4, 512], F32)
        spD = [P.sbuf("sp%d" % i, [128, 2, 512], F32) for i in range(2)]
        preD = [P.sbuf("pre%d" % i, [128, 2, 512], F32) for i in range(2)]
        bcD = [P.sbuf("bc%d" % i, [128, 2, 512], F32) for i in range(2)]
        remD = [P.sbuf("rem%d" % i, [128, 2, 512], F32) for i in range(2)]
        eeR = Rot(lambda i: P.sbuf("ee%d" % i, [128, 2, 512], F32), 2)
        etD = [P.sbuf("et%d" % i, [128, 2, 4], F32) for i in range(2)]
        ob = Rot(lambda i: P.sbuf("ob%d" % i, [128, 2, 512], BF16), 2)
        kdt = Rot(lambda i: P.sbuf("kdt%d" % i, [128, 4, 256], BF16), 2)
        vtm = Rot(lambda i: P.sbuf("vtm%d" % i, [128, 4, 512], BF16), 2)
        rg = Rot(lambda i: P.sbuf("rg%d" % i, [128, 4, 512], BF16), 2)
        idb = P.sbuf("idb", [128, 128], BF16)
        P.copy(idb[:], cst[:, C_ID, :], [cst], [idb])
        for (t0, n) in GROUPS5:
            nt = n // 128
            x = xg.next()
            P.dma(x[:, :, 0:n], xn_s[:, :, t0:t0 + n], reads=[xn_s], writes=[x])
            for cq in range(4):
                ps = pp.next()
                for kc in range(8):
                    P.mm(ps[:, 0:n], wqk[:, kc, cq * 128:(cq + 1) * 128], x[:, kc, 0:n], kc == 0, kc == 7, [wqk, x], [ps])
                P.act(qkraw[:, cq, 0:n], ps[:, 0:n], AF.Copy, [ps], [qkraw])
            vt = vtm.next()
            for ti in range(nt):
                ps = pp.next()
                for kc in range(8):
                    P.mm(ps[:], x[:, kc, ti * 128:(ti + 1) * 128], wvv[:, kc, :], kc == 0, kc == 7, [x, wvv], [ps])
                P.act(vt[:, ti, :], ps[:], AF.Copy, [ps], [vt])
            P.dma(glv_s[t0:t0 + n, :].rearrange("(a p) f -> p a f", p=128), vt[:, 0:nt, :], reads=[vt], writes=[glv_s])
            if t0 >= CTX:
                r = rg.next()
                for h in range(4):
                    ps = pp.next()
                    for kc in range(8):
                        P.mm(ps[:, 0:n], wr[:, kc, h * 128:(h + 1) * 128], x[:, kc, 0:n], kc == 0, kc == 7, [wr, x], [ps])
                    P.act(r[:, h, 0:n], ps[:, 0:n], AF.Silu, [ps], [r])
                P.dma(glr_s[:, :, t0 - CTX:t0 - CTX + n], r[:, :, 0:n], reads=[r], writes=[glr_s])
            for d in range(2):
                sp, pre, bc, rem, et = spD[d], preD[d], bcD[d], remD[d], etD[d]
                for kc in range(8):
                    P.mm(pgz[:, 0:n], wgz[:, kc, 16 * d:16 * d + 16], x[:, kc, 0:n], kc == 0, kc == 7, [wgz, x], [pgz])
                P.act(gz[d][0:16, 0:n], pgz[:, 0:n], AF.Copy, [pgz], [gz[d]])
                for c in range(2):
                    ps = pp.next()
                    P.mm(ps[:, 0:n], g2b[:, d, c * 128:(c + 1) * 128], gz[d][:, 0:n], True, True, [g2b, gz[d]], [ps])
                    P.act(sp[:, c, 0:n], ps[:, 0:n], AF.Exp, [ps], [sp], scale=-1.0)
                    P.act(sp[:, c, 0:n], sp[:, c, 0:n], AF.Ln, [sp], [sp], bias=1.0)
                    P.op("vector", lambda e, c=c, n=n, pre=pre, sp=sp: e.tensor_tensor_scan(
                        out=pre[:, c, 0:n], data0=msk[:, 0:n], data1=sp[:, c, 0:n], initial=0.0,
                        op0=ALU.mult, op1=ALU.add), [msk, sp], [pre])
                pre4 = pre[:, :, 0:n].rearrange("p c (a b) -> p c a b", b=128)
                totb = pre4[:, :, :, 127:128].broadcast_to([128, 2, nt, 128])
                bc4 = bc[:, :, 0:n].rearrange("p c (a b) -> p c a b", b=128)
                rem4 = rem[:, :, 0:n].rearrange("p c (a b) -> p c a b", b=128)
                if d == 0:
                    P.copy(bc[:, :, 0:n], pre[:, :, 0:n], [pre], [bc], eng="gpsimd")
                    P.tt(rem4, totb, pre4, ALU.subtract, [pre], [rem])
                else:
                    P.tt(rem[:, :, 0:n], pre[:, :, 0:n], sp[:, :, 0:n], ALU.subtract, [pre, sp], [rem])
                    P.tt(bc4, totb, rem4, ALU.subtract, [pre, rem], [bc])
                P.act(et[:, :, 0:nt], pre4[:, :, :, 127], AF.Exp, [pre], [et], scale=-1.0 / 16)
                P.dma(glet_s[d][:, :, t0 // 128:t0 // 128 + nt], et[:, :, 0:nt], reads=[et], writes=[glet_s[d]])
                ee = eeR.next()
                P.act(ee[:, :, 0:n], bc[:, :, 0:n], AF.Exp, [bc], [ee], scale=-1.0 / 16)
                o = ob.next()
                P.stt(o[:, :, 0:n], qkraw[:, 0:2, 0:n], 0.125, ee[:, :, 0:n], ALU.mult, ALU.mult, [qkraw, ee], [o])
                P.dma(glq_s[d][:, :, t0:t0 + n], o[:, :, 0:n], reads=[o], writes=[glq_s[d]])
                ee = eeR.next()
                P.act(ee[:, :, 0:n], bc[:, :, 0:n], AF.Exp, [bc], [ee], scale=1.0 / 16)
                o = ob.next()
                P.tt(o[:, :, 0:n], qkraw[:, 2:4, 0:n], ee[:, :, 0:n], ALU.mult, [qkraw, ee], [o])
                P.dma(glk_s[d][:, :, t0:t0 + n], o[:, :, 0:n], reads=[o], writes=[glk_s[d]])
                ee = eeR.next()
                P.act(ee[:, :, 0:n], rem[:, :, 0:n], AF.Exp, [rem], [ee], scale=-1.0 / 16)
                o = ob.next()
                P.tt(o[:, :, 0:n], qkraw[:, 2:4, 0:n], ee[:, :, 0:n], ALU.mult, [qkraw, ee], [o])
                kd = kdt.next()
                for ti in range(nt):
                    for c in range(2):
                        P.tr(ptr_[:, c, :], o[:, c, ti * 128:(ti + 1) * 128], idb[:], [o, idb], [ptr_])
                    P.copy(kd[:, ti, :], ptr_[:].rearrange("p a b -> p (a b)"), [ptr_], [kd])
                P.dma(glkd_s[d][t0:t0 + n, :].rearrange("(a p) f -> p a f", p=128), kd[:, 0:nt, :], reads=[kd],
                      writes=[glkd_s[d]])
        P.emit()

    def gla_core():
        P = Prog(nc, sy, "glc")
        odc = P.sbuf("odc", [128, 11], F32)
        P.dma(odc[:], odcol_d, writes=[odc])
        ogla = P.sbuf("ogla", [128, 4, SEQ], F32, 16)
        for i in range(4):
            P.memset(ogla[:, i, :], 0.0, [ogla], eng="gpsimd" if i % 2 else "vector")
        S = [[P.sbuf("S%d%d" % (d, c), [128, 128], F32) for c in range(2)] for d in range(2)]
        Sb = [[P.sbuf("Sb%d%d" % (d, h), [128, 128], BF16) for h in range(4)] for d in range(2)]
        qd = Rot(lambda i: P.sbuf("qd%d" % i, [128, 2, 128], BF16), 4)
        kp = Rot(lambda i: P.sbuf("kp%d" % i, [128, 2, 128], BF16), 4)
        kd = Rot(lambda i: P.sbuf("kd%d" % i, [128, 256], BF16), 4)
        vt = Rot(lambda i: P.sbuf("vt%d" % i, [128, 512], BF16), 4)
        et = [P.sbuf("et%d" % d, [128, 2, NT], F32) for d in range(2)]
        aT = Rot(lambda i: P.sbuf("aT%d" % i, [128, 4, 128], BF16), 2)
        pA0 = Rot(lambda i: P.psum("pA0%d" % i, [128, 2, 128]), 1)
        pA1 = Rot(lambda i: P.psum("pA1%d" % i, [128, 2, 128]), 1)
        pO = Rot(lambda i: P.psum("pO%d" % i, [128, 4, 128]), 2)
        pSt = Rot(lambda i: P.psum("pSt%d" % i, [128, 128]), 3)
        orders = [list(range(NT)), [1, 0] + list(range(NT - 1, 1, -1))]
        masks = [cst[:, C_UI, :], cst[:, C_LI, :]]
        for d in range(2):
            P.dma(et[d][:], glet_s[d][:], reads=[glet_s[d]], writes=[et[d]])
            for c in range(2):
                P.memset(S[d][c][:], 0.0, [S[d][c]])
            for h in range(4):
                P.memset(Sb[d][h][:], 0.0, [Sb[d][h]])
        for s_ in range(NT):
            for d in range(2):
                tt = orders[d][s_]
                mask = masks[d]
                q_, k_, kd_, v_ = qd.next(), kp.next(), kd.next(), vt.next()
                tsl = slice(tt * 128, (tt + 1) * 128)
                P.dma(q_[:], glq_s[d][:, :, tsl], reads=[glq_s[d]], writes=[q_])
                P.dma(k_[:], glk_s[d][:, :, tsl], reads=[glk_s[d]], writes=[k_])
                P.dma(kd_[:], glkd_s[d][tsl, :], reads=[glkd_s[d]], writes=[kd_])
                P.dma(v_[:], glv_s[tsl, :], reads=[glv_s], writes=[v_])
                if tt >= 2:
                    a_ = aT.next()
                    for par, pa_ in enumerate((pA0, pA1)):
                        pa = pa_.next()
                        hs = slice(par * 64, par * 64 + 64)
                        for c in range(2):
                            P.mm(pa[:, c, :], k_[hs, c, :], q_[hs, c, :], True, True, [k_, q_], [pa])
                        P.tt(a_[:, par::2, :], pa[:], mask.unsqueeze(1).broadcast_to([128, 2, 128]), ALU.mult,
                             [pa, cst], [a_])
                    po = pO.next()
                    for h in range(4):
                        P.mm(po[:, h, :], v_[:, h * 128:(h + 1) * 128], a_[:, h, :], True, False, [v_, a_], [po])
                        P.mm(po[:, h, :], Sb[d][h][:], q_[:, h // 2, :], False, True, [Sb[d][h], q_], [po])
                    lt = tt - 2
                    osl = ogla[:, :, lt * 128:(lt + 1) * 128]
                    P.tt(osl, osl, po[:], ALU.add, [po, (ogla, lt)], [(ogla, lt)])
                for c in range(2):
                    ps = pSt.next()
                    for hh in range(2):
                        h = 2 * c + hh
                        P.mm(ps[hh * 64:(hh + 1) * 64, :], kd_[:, h * 64:(h + 1) * 64], v_[:, h * 128:(h + 1) * 128],
                             True, True, [kd_, v_], [ps])
                    P.stt(S[d][c][:], S[d][c][:], et[d][:, c, tt:tt + 1], ps[:], ALU.mult, ALU.add,
                          [S[d][c], et[d], ps], [S[d][c]])
                    for hh in range(2):
                        hs = slice(hh * 64, hh * 64 + 64)
                        P.act(Sb[d][2 * c + hh][hs, :], S[d][c][hs, :], AF.Copy, [S[d][c]], [Sb[d][2 * c + hh]])
        sqR = Rot(lambda i: P.sbuf("sq%d" % i, [128, 512], F32), 2)
        lnvR = Rot(lambda i: P.sbuf("lnv%d" % i, [128, 512], F32), 2)
        rsR = Rot(lambda i: P.sbuf("rs%d" % i, [128, 512], F32), 2)
        tmpR = Rot(lambda i: P.sbuf("tmp%d" % i, [128, 512], F32), 2)
        pss = P.psum("pss", [128, 512])
        rg = Rot(lambda i: P.sbuf("rg%d" % i, [128, 4, 512], BF16), 2)
        mo = Rot(lambda i: P.sbuf("mo%d" % i, [128, 4, 512], BF16), 2)
        for gi in range(4):
            r = rg.next()
            m = mo.next()
            gsl = slice(gi * 512, (gi + 1) * 512)
            keys = [(ogla, i) for i in range(gi * 4, gi * 4 + 4)]
            P.dma(r[:], glr_s[:, :, gsl], reads=[glr_s], writes=[r])
            for h in range(4):
                sq, lnv, rs, tmp = sqR.next(), lnvR.next(), rsR.next(), tmpR.next()
                P.act(sq[:], ogla[:, h, gsl], AF.Square, keys, [sq])
                P.mm(pss[:], ones_f[:], sq[:], True, True, [ones_f, sq], [pss])
                P.act(lnv[:], pss[:], AF.Ln, [pss], [lnv], bias=EPS, scale=1.0 / 128)
                P.act(rs[:], lnv[:], AF.Exp, [lnv], [rs], scale=-0.5)
                P.tt(tmp[:], ogla[:, h, gsl], rs[:], ALU.mult, keys + [rs], [tmp])
                P.stt(m[:, h, :], tmp[:], odc[:, 2:3], r[:, h, :], ALU.mult, ALU.mult, [tmp, odc, r], [m])
            P.dma(mix_s[:, 0:4, CTX + gi * 512:CTX + (gi + 1) * 512], m[:], reads=[m], writes=[mix_s])
        P.emit()

    def mla_prep():
        P = Prog(nc, sy, "mlp")
        wv_ = evw_d.rearrange("(kc p) n -> p kc n", p=128)
        evc = P.sbuf("evc", [128, 8], F32)
        P.dma(evc[:], evcol_d, writes=[evc])
        cs = P.sbuf("cs", [96, 2, SEQ], F32)
        P.dma(cs[:], mlcs_d, writes=[cs])
        st = P.sbuf("st", [128, 8, 384], F32)
        wcq = P.sbuf("wcq", [128, 8, 384], BF16)
        wckv = P.sbuf("wckv", [128, 8, 256], BF16)
        wkr = P.sbuf("wkr", [128, 8, 32], BF16)
        wqu = P.sbuf("wqu", [128, 3, 768], BF16)
        wkK = P.sbuf("wkK", [128, 2, 8, 64], BF16)
        wkV = P.sbuf("wkV", [128, 2, 8, 64], BF16)
        load_w(P, wcq, wcq[:], wv_[:, :, 0:384], st, st[:])
        load_w(P, wckv, wckv[:], wv_[:, :, 384:640], st, st[:, :, 0:256])
        load_w(P, wkr, wkr[:], wv_[:, :, 640:672], st, st[:, :, 0:32])
        stf = st[:].rearrange("p a b -> p (a b)")
        qu_v = evqu_d.rearrange("(kc p) n -> p kc n", p=128)
        st_qu = stf[:, 0:3 * 768].rearrange("p (a b) -> p a b", a=3)
        load_w(P, wqu, wqu[:], qu_v, st, st_qu)
        kv_v = evkv_d.rearrange("(kc p) n -> p kc n", p=128)
        st_kv = stf[:, 0:2048].rearrange("p (a b) -> p a b", a=2)
        P.dma(st_kv, kv_v, writes=[st])
        st_kv5 = stf[:, 0:2048].rearrange("p (a h t d) -> p a h t d", a=2, h=8, t=2)
        P.copy(wkK[:], st_kv5[:, :, :, 0, :], [st], [wkK], eng="gpsimd")
        P.copy(wkV[:], st_kv5[:, :, :, 1, :], [st], [wkV], eng="gpsimd")
        xg = Rot(lambda i: P.sbuf("xg%d" % i, [128, 8, 512], BF16), 2)
        pp = Rot(lambda i: P.psum("pp%d" % i, [128, 512]), 3)
        pssR = Rot(lambda i: P.psum("pss%d" % i, [128, 512]), 2)
        protR = Rot(lambda i: P.psum("prot%d" % i, [128, 512]), 1)
        prk = P.psum("prk", [128, NT, 8])
        prkr = P.psum("prkr", [128, NT])
        raw = P.sbuf("raw", [128, 3, 512], F32)
        sq3 = P.sbuf("sq3", [128, 3, 512], F32)
        lnvR = Rot(lambda i: P.sbuf("lnv%d" % i, [128, 512], F32), 2)
        rsR = Rot(lambda i: P.sbuf("rs%d" % i, [128, 512], F32), 2)
        cqn = P.sbuf("cqn", [128, 3, 512], BF16)
        ckvn = P.sbuf("ckvn", [128, 2, 512], BF16)
        sqR = Rot(lambda i: P.sbuf("sq%d" % i, [128, 512], F32), 2)
        qgR = Rot(lambda i: P.sbuf("qg%d" % i, [128, 512], F32), 2)
        t1R = Rot(lambda i: P.sbuf("t1%d" % i, [128, 512], F32), 2)
        t2R = Rot(lambda i: P.sbuf("t2%d" % i, [128, 512], F32), 2)
        qo = Rot(lambda i: P.sbuf("qo%d" % i, [128, 512], BF16), 3)
        vtm = Rot(lambda i: P.sbuf("vtm%d" % i, [128, 4, 512], BF16), 2)
        rsk = P.sbuf("rsk", [128, NT * 8], F32)
        R96 = cst[0:96, C_ROT96, 0:96]

        def rope_out(q, src, rs, n, lat, l0, with_rs):
            if lat:
                prot, t1, t2 = protR.next(), t1R.next(), t2R.next()
                P.mm(prot[0:96, 0:n], R96, src[0:96, 0:n], True, True, [cst, src], [prot])
                P.tt(t1[0:96, 0:n], src[0:96, 0:n], cs[:, 0, l0:l0 + n], ALU.mult, [src, cs], [t1])
                P.tt(t2[0:96, 0:n], prot[0:96, 0:n], cs[:, 1, l0:l0 + n], ALU.mult, [prot, cs], [t2])
                if with_rs:
                    P.tt(t1[0:96, 0:n], t1[0:96, 0:n], t2[0:96, 0:n], ALU.add, [t1, t2], [t1])
                    P.tt(q[0:96, 0:n], t1[0:96, 0:n], rs[0:96, 0:n], ALU.mult, [t1, rs], [q])
                else:
                    P.tt(q[0:96, 0:n], t1[0:96, 0:n], t2[0:96, 0:n], ALU.add, [t1, t2], [q])
            else:
                if with_rs:
                    P.tt(q[0:96, 0:n], src[0:96, 0:n], rs[0:96, 0:n], ALU.mult, [src, rs], [q])
                else:
                    P.copy(q[0:96, 0:n], src[0:96, 0:n], [src], [q])

        for (t0, n) in GROUPS5:
            nt = n // 128
            lat = t0 >= CTX
            l0 = t0 - CTX
            x = xg.next()
            P.dma(x[:, :, 0:n], xn_s[:, :, t0:t0 + n], reads=[xn_s], writes=[x])
            for (wt, nch, off, dstn, dim) in ((wcq, 3, 0, cqn, 384), (wckv, 2, 3, ckvn, 256)):
                for c in range(nch):
                    ps = pp.next()
                    for kc in range(8):
                        P.mm(ps[:, 0:n], wt[:, kc, c * 128:(c + 1) * 128], x[:, kc, 0:n], kc == 0, kc == 7, [wt, x], [ps])
                    P.act(raw[:, c, 0:n], ps[:, 0:n], AF.Copy, [ps], [raw])
                    P.act(sq3[:, c, 0:n], ps[:, 0:n], AF.Square, [ps], [sq3])
                pss, lnv, rs = pssR.next(), lnvR.next(), rsR.next()
                for c in range(nch):
                    P.mm(pss[:, 0:n], ones_f[:], sq3[:, c, 0:n], c == 0, c == nch - 1, [ones_f, sq3], [pss])
                P.act(lnv[:, 0:n], pss[:, 0:n], AF.Ln, [pss], [lnv], bias=EPS, scale=1.0 / dim)
                P.act(rs[:, 0:n], lnv[:, 0:n], AF.Exp, [lnv], [rs], scale=-0.5)
                for c in range(nch):
                    P.stt(dstn[:, c, 0:n], raw[:, c, 0:n], evc[:, off + c:off + c + 1], rs[:, 0:n], ALU.mult, ALU.mult,
                          [raw, evc, rs], [dstn])
            vt = vtm.next()
            for ti in range(nt):
                ps = pp.next()
                for c in range(2):
                    P.mm(ps[:], ckvn[:, c, ti * 128:(ti + 1) * 128], wkV[:, c].rearrange("p h d -> p (h d)"),
                         c == 0, c == 1, [ckvn, wkV], [ps])
                P.act(vt[:, ti, :], ps[:], AF.Copy, [ps], [vt])
            P.dma(mv_s[t0:t0 + n, :].rearrange("(a p) f -> p a f", p=128), vt[:, 0:nt, :], reads=[vt], writes=[mv_s])
            R = slice(64, 96)
            psr = pp.next()
            for kc in range(8):
                P.mm(psr[R, 0:n], wkr[:, kc, :], x[:, kc, 0:n], kc == 0, kc == 7, [wkr, x], [psr])
            sqr, krg = sqR.next(), qgR.next()
            P.act(sqr[R, 0:n], psr[R, 0:n], AF.Square, [psr], [sqr])
            for ti in range(nt):
                tt = t0 // 128 + ti
                P.mm(prkr[:, tt:tt + 1], sqr[R, ti * 128:(ti + 1) * 128], ones_f[R, 0:1], True, True, [sqr, ones_f], [prkr])
            P.act(krg[R, 0:n], psr[R, 0:n], AF.Identity, [psr], [krg], scale=evc[R, 6:7])
            krf = qo.next()
            if lat:
                prot, t1, t2 = protR.next(), t1R.next(), t2R.next()
                P.mm(prot[R, 0:n], cst[R, C_ROT96, 64:96], krg[R, 0:n], True, True, [cst, krg], [prot])
                P.tt(t1[R, 0:n], krg[R, 0:n], cs[R, 0, l0:l0 + n], ALU.mult, [krg, cs], [t1])
                P.tt(t2[R, 0:n], prot[R, 0:n], cs[R, 1, l0:l0 + n], ALU.mult, [prot, cs], [t2])
                P.tt(krf[R, 0:n], t1[R, 0:n], t2[R, 0:n], ALU.add, [t1, t2], [krf])
            else:
                P.copy(krf[R, 0:n], krg[R, 0:n], [krg], [krf])
            for h in range(8):
                P.dma(mkf_s[R, h, t0:t0 + n], krf[R, 0:n], reads=[krf], writes=[mkf_s])
            for h in range(8):
                ps = pp.next()
                for c in range(3):
                    P.mm(ps[0:96, 0:n], wqu[:, c, h * 96:(h + 1) * 96], cqn[:, c, 0:n], c == 0, c == 2, [wqu, cqn], [ps])
                sq, pss, lnv, rs, qg = sqR.next(), pssR.next(), lnvR.next(), rsR.next(), qgR.next()
                P.act(sq[0:96, 0:n], ps[0:96, 0:n], AF.Square, [ps], [sq])
                P.mm(pss[0:96, 0:n], ones_f[0:96, 0:96], sq[0:96, 0:n], True, True, [ones_f, sq], [pss])
                P.act(lnv[0:96, 0:n], pss[0:96, 0:n], AF.Ln, [pss], [lnv], bias=EPS, scale=1.0 / 96)
                P.act(rs[0:96, 0:n], lnv[0:96, 0:n], AF.Exp, [lnv], [rs], scale=-0.5)
                P.act(qg[0:96, 0:n], ps[0:96, 0:n], AF.Identity, [ps], [qg], scale=evc[0:96, 5:6])
                q = qo.next()
                rope_out(q, qg, rs, n, lat, l0, True)
                P.dma(mqf_s[:, h, t0:t0 + n], q[0:96, 0:n], reads=[q], writes=[mqf_s])
                ps = pp.next()
                for c in range(2):
                    P.mm(ps[0:64, 0:n], wkK[:, c, h, :], ckvn[:, c, 0:n], c == 0, c == 1, [wkK, ckvn], [ps])
                sq = sqR.next()
                P.act(sq[0:64, 0:n], ps[0:64, 0:n], AF.Square, [ps], [sq])
                for ti in range(nt):
                    tt = t0 // 128 + ti
                    P.mm(prk[:, tt, h:h + 1], sq[0:64, ti * 128:(ti + 1) * 128], ones_f[0:64, 0:1], True, True,
                         [sq, ones_f], [prk])
                q = qo.next()
                P.act(q[0:64, 0:n], ps[0:64, 0:n], AF.Identity, [ps], [q], scale=evc[0:64, 6:7])
                P.dma(mkf_s[0:64, h, t0:t0 + n], q[0:64, 0:n], reads=[q], writes=[mkf_s])
        rk3 = P.sbuf("rk3", [128, NT, 8], F32)
        rkr = P.sbuf("rkr", [128, NT], F32)
        P.act(rkr[:], prkr[:], AF.Copy, [prkr], [rkr])
        P.tt(rk3[:], prk[:], rkr[:].unsqueeze(2).broadcast_to([128, NT, 8]), ALU.add, [prk, rkr], [rk3])
        P.act(rsk[:], rk3[:].rearrange("p a b -> p (a b)"), AF.Ln, [rk3], [rsk], bias=EPS, scale=1.0 / 96)
        P.act(rsk[:], rsk[:], AF.Exp, [rsk], [rsk], scale=-0.5)
        P.ts(rsk[:], rsk[:], float(96 ** -0.5), ALU.mult, [rsk], [rsk])
        P.dma(mrk_s[:], rsk[:], reads=[rsk], writes=[mrk_s])
        P.emit()

    def mla_core():
        P = Prog(nc, sy, "mlc")
        rsk = P.sbuf("rsk", [128, NT, 8], F32)
        P.dma(rsk[:].rearrange("p a b -> p (a b)"), mrk_s[:], reads=[mrk_s], writes=[rsk])
        kf = Rot(lambda i: P.sbuf("kf%d" % i, [96, 2, T], BF16), 2)
        qf = Rot(lambda i: P.sbuf("qf%d" % i, [96, 2, T], BF16), 2)
        vv = Rot(lambda i: P.sbuf("vv%d" % i, [128, NT, 128], BF16), 2)
        pS = Rot(lambda i: P.psum("pS%d" % i, [128, 512]), 4)
        pO = Rot(lambda i: P.psum("pO%d" % i, [128, 512]), 2)
        pM = Rot(lambda i: P.psum("pM%d" % i, [128, 512]), 2)
        pT = Rot(lambda i: P.sbuf("pT%d" % i, [128, 512], BF16), 4)
        rden = P.sbuf("rden", [128, 512], F32)
        mo = Rot(lambda i: P.sbuf("mo%d" % i, [128, 512], BF16), 2)
        for j in range(4):
            k_, q_, v_ = kf.next(), qf.next(), vv.next()
            P.dma(k_[:], mkf_s[:, 2 * j:2 * j + 2, :], reads=[mkf_s], writes=[k_])
            P.dma(q_[:], mqf_s[:, 2 * j:2 * j + 2, :], reads=[mqf_s], writes=[q_])
            P.dma(v_[:], mv_s[:, j * 128:(j + 1) * 128].rearrange("(a p) f -> p a f", p=128), reads=[mv_s], writes=[v_])
            for (t0, n) in GROUPS5:
                kts = [0, 1] if t0 < CTX else list(range(NT))
                po, pm = pO.next(), pM.next()

                def scores(kt):
                    pts = []
                    for hh in range(2):
                        ps = pS.next()
                        P.mm(ps[:, 0:n], k_[:, hh, kt * 128:(kt + 1) * 128], q_[:, hh, t0:t0 + n], True, True, [k_, q_], [ps])
                        pt = pT.next()
                        P.act(pt[:, 0:n], ps[:, 0:n], AF.Exp, [ps, rsk], [pt], scale=rsk[:, kt, 2 * j + hh:2 * j + hh + 1])
                        pts.append(pt)
                    return pts

                nxt = scores(kts[0])
                for ki, kt in enumerate(kts):
                    first, last = ki == 0, ki == len(kts) - 1
                    pts = nxt
                    if not last:
                        nxt = scores(kts[ki + 1])
                    for hh in range(2):
                        sl = slice(hh * 64, (hh + 1) * 64)
                        P.mm(po[sl, 0:n], v_[:, kt, hh * 64:(hh + 1) * 64], pts[hh][:, 0:n], first, last, [v_, pts[hh]], [po])
                    for hh in range(2):
                        sl = slice(hh * 64, (hh + 1) * 64)
                        P.mm(pm[sl, 0:n], ones_b[:, 0:64], pts[hh][:, 0:n], first, last, [ones_b, pts[hh]], [pm])
                P.act(rden[:, 0:n], pm[:, 0:n], AF.Ln, [pm], [rden])
                P.act(rden[:, 0:n], rden[:, 0:n], AF.Exp, [rden], [rden], scale=-1.0)
                m = mo.next()
                P.tt(m[:, 0:n], po[:, 0:n], rden[:, 0:n], ALU.mult, [po, rden], [m])
                P.dma(mix_s[:, j, t0:t0 + n], m[:, 0:n], reads=[m], writes=[mix_s])
        P.emit()

    def gdn_prep():
        P = Prog(nc, sy, "gdp")
        wv_ = evw_d.rearrange("(kc p) n -> p kc n", p=128)
        xn = P.sbuf("xn", [128, 8, T], BF16)
        P.dma(xn[:], xn_s[:], reads=[xn_s], writes=[xn])
        cvw = P.sbuf("cvw", [128, 12, 5], F32)
        P.dma(cvw[:], gcv_d, writes=[cvw])
        grow = P.sbuf("grow", [128, 544], F32)
        P.dma(grow[:], grow_d, writes=[grow])
        st = P.sbuf("st", [128, 8, 512], F32)
        wch = Rot(lambda i: P.sbuf("wch%d" % i, [128, 8, 128], BF16), 2)
        wab = P.sbuf("wab", [128, 8, 32], BF16)
        wgt = P.sbuf("wgt", [128, 8, 512], BF16)
        load_w(P, wab, wab[:], wv_[:, :, 2208:2240], st, st[:, :, 0:32])
        load_w(P, wgt, wgt[:], wv_[:, :, 2240:2752], st, st[:])
        pp = Rot(lambda i: P.psum("pp%d" % i, [128, 512]), 3)
        pssR = Rot(lambda i: P.psum("pss%d" % i, [128, 512]), 2)
        ptr_ = Rot(lambda i: P.psum("ptr%d" % i, [128, 4, 128]), 2)
        zcR = Rot(lambda i: P.sbuf("zc%d" % i, [128, T], F32), 2)
        accR = Rot(lambda i: P.sbuf("acc%d" % i, [128, T], F32), 1)
        sqR = Rot(lambda i: P.sbuf("sq%d" % i, [128, 512], F32), 2)
        lnvR = Rot(lambda i: P.sbuf("lnv%d" % i, [128, 512], F32), 2)
        tmst = P.sbuf("tmst", [128, NT, 128], F32)
        gall = P.sbuf("gall", [128, NT, 16], F32)
        ball = P.sbuf("ball", [128, NT, 16], F32)
        negA = P.sbuf("negA", [128, 16], F32)
        P.act(negA[:], grow[:, 0:16], AF.Exp, [grow], [negA])
        P.ts(negA[:], negA[:], -1.0, ALU.mult, [negA], [negA])
        tg = P.sbuf("tg", [128, 16], F32)
        gt = Rot(lambda i: P.sbuf("gt%d" % i, [128, 512], BF16), 2)
        for tt in range(NT):
            tsl = slice(tt * 128, (tt + 1) * 128)
            ps = pp.next()
            for kc in range(8):
                P.mm(ps[:, 0:32], xn[:, kc, tsl], wab[:, kc, :], kc == 0, kc == 7, [xn, wab], [ps])
            P.tt(tg[:], ps[:, 0:16], grow[:, 16:32], ALU.add, [ps, grow], [tg])
            P.act(tg[:], tg[:], AF.Exp, [tg], [tg])
            P.act(tg[:], tg[:], AF.Ln, [tg], [tg], bias=1.0)
            P.tt(gall[:, tt, :], tg[:], negA[:], ALU.mult, [tg, negA], [gall])
            P.act(ball[:, tt, :], ps[:, 16:32], AF.Exp, [ps], [ball], scale=-1.0)
            P.ts(ball[:, tt, :], ball[:, tt, :], 1.0, ALU.add, [ball], [ball])
            P.op("vector", lambda e, tt=tt: e.reciprocal(out=ball[:, tt, :], in_=ball[:, tt, :]), [ball], [ball])
            ps = pp.next()
            for kc in range(8):
                P.mm(ps[:], xn[:, kc, tsl], wgt[:, kc, :], kc == 0, kc == 7, [xn, wgt], [ps])
            g_ = gt.next()
            P.act(g_[:], ps[:], AF.Silu, [ps], [g_])
            P.dma(ggate_s[tsl, :], g_[:], reads=[g_], writes=[ggate_s])
        P.dma(gg_s[:], gall[:].rearrange("p a b -> p (a b)"), reads=[gall], writes=[gg_s])
        P.dma(gb_s[:], ball[:].rearrange("p a b -> p (a b)"), reads=[ball], writes=[gb_s])
        ZW = T + 8

        def zoff(t):
            return t + 2 if t < CTX else t + 6

        zpR = Rot(lambda i: P.sbuf("zp%d" % i, [128, ZW], BF16), 2)
        for z_ in zpR.t:
            P.memset(z_[:], 0.0, [z_])
        dgR = Rot(lambda i: P.sbuf("dgw%d" % i, [128, 5, 128], BF16), 2)
        for cc in range(12):
            zc, acc = zcR.next(), accR.next()
            zp, dgw = zpR.next(), dgR.next()
            w = wch.next()
            load_w(P, w, w[:], wv_[:, :, 672 + cc * 128:672 + (cc + 1) * 128], st, st[:, :, 0:128])
            for j in range(5):
                P.ts(dgw[:, j, :], cst[:, C_ID, :], cvw[:, cc, j:j + 1], ALU.mult, [cst, cvw], [dgw], eng="gpsimd")
            for (t0, n) in GROUPS5:
                ps = pp.next()
                for kc in range(8):
                    P.mm(ps[:, 0:n], w[:, kc, :], xn[:, kc, t0:t0 + n], kc == 0, kc == 7, [w, xn], [ps])
                P.act(zp[:, zoff(t0):zoff(t0) + n], ps[:, 0:n], AF.Copy, [ps], [zp])
            for (t0, n) in GROUPS5:
                ps = pp.next()
                for j in range(5):
                    o_ = zoff(t0) + j - 2
                    P.mm(ps[:, 0:n], dgw[:, j, :], zp[:, o_:o_ + n], j == 0, j == 4, [dgw, zp], [ps])
                P.act(zc[:, t0:t0 + n], ps[:, 0:n], AF.Silu, [ps], [zc])
            if cc < 8:
                for (t0, n) in GROUPS5:
                    sq, lnv = sqR.next(), lnvR.next()
                    P.act(sq[:, 0:n], zc[:, t0:t0 + n], AF.Square, [zc], [sq])
                    pss = pssR.next()
                    P.mm(pss[:, 0:n], cst[:, C_BD64, :], sq[:, 0:n], True, True, [cst, sq], [pss])
                    P.act(lnv[:, 0:n], pss[:, 0:n], AF.Ln, [pss], [lnv], bias=EPS)
                    P.act(lnv[:, 0:n], lnv[:, 0:n], AF.Exp, [lnv], [lnv], scale=-0.5)
                    if cc < 4:
                        P.stt(acc[:, t0:t0 + n], zc[:, t0:t0 + n], 0.125, lnv[:, 0:n], ALU.mult, ALU.mult, [zc, lnv], [acc])
                    else:
                        P.tt(acc[:, t0:t0 + n], zc[:, t0:t0 + n], lnv[:, 0:n], ALU.mult, [zc, lnv], [acc])
                dst = gq_s if cc < 4 else gk_s
                r0 = (cc % 4) * 128
                P.dma(dst[r0:r0 + 128, :], acc[:], reads=[acc], writes=[dst])
                src = acc
            else:
                src = zc
            if cc >= 4:
                for t4 in range(0, NT, 4):
                    nn = min(4, NT - t4)
                    pt = ptr_.next()
                    for i in range(nn):
                        tt = t4 + i
                        P.tr(pt[:, i, :], src[:, tt * 128:(tt + 1) * 128], ident, [src, cst], [pt])
                    P.copy(tmst[:, t4:t4 + nn, :], pt[:, 0:nn, :], [pt], [tmst])
                dtm = gktm_s if cc < 8 else gvtm_s
                c0 = (cc % 4) * 128
                P.dma(dtm[:, c0:c0 + 128].rearrange("(a p) f -> p a f", p=128), tmst[:], reads=[tmst], writes=[dtm])
        P.emit()

    def gdn_core():
        P = Prog(nc, sy, "gdsp")
        P.dma(xt_s[:], XT[:, 4:8, :].rearrange("p c t -> p (c t)"), reads=[XT], writes=[xt_s])
        P.emit()
        P = Prog(nc, sy, "gdc")
        XTf = XT[:, 4:8, :].rearrange("p c t -> p (c t)")
        carve_off = [0]

        def carve(name, dtype):
            words = 512 if dtype == F32 else 256
            a = XTf[:, carve_off[0]:carve_off[0] + words]
            carve_off[0] += words
            if dtype != F32:
                a = a.bitcast(dtype)
            return Buf(a.rearrange("p (h i) -> p h i", h=4), 1, name)

        lvm = P.sbuf("lvm", [128, 7, 128], F32)
        nlv = P.sbuf("nlv", [128, 14, 128], BF16)
        for hf in range(2):
            P.dma(lvm[:], lvm_d[:, 7 * hf:7 * hf + 7, :], writes=[lvm])
            P.ts(nlv[:, 7 * hf:7 * hf + 7, :], lvm[:], -1.0, ALU.mult, [lvm], [nlv])
        idb = P.sbuf("idb", [128, 128], BF16)
        P.copy(idb[:], cst[:, C_ID, :], [cst], [idb])
        gall = P.sbuf("gall", [128, NT, 16], F32)
        ball = P.sbuf("ball", [128, NT, 16], F32)
        P.dma(gall[:].rearrange("p a b -> p (a b)"), gg_s[:], reads=[gg_s], writes=[gall])
        P.dma(ball[:].rearrange("p a b -> p (a b)"), gb_s[:], reads=[gb_s], writes=[ball])
        PS = Rot(lambda i: P.psum("ps%d" % i, [128, 512]), 6)
        PB = Rot(lambda i: P.psum("pb%d" % i, [128, 4, 128], BF16), 2)

        def b4(ap):
            return ap.unsqueeze(1).broadcast_to([128, 4, 128])

        def bl(ap, w):
            return ap.unsqueeze(2).broadcast_to([128, ap.shape[1], w])

        Ibc = b4(idb[:])
        D_ = []
        for d in range(2):
            n_ = lambda s, d=d: "%s_%d" % (s, d)
            o = K()
            o.S = P.sbuf(n_("S"), [128, 8, 64], F32)
            o.q = P.sbuf(n_("q"), [64, 8, 128], F32)
            o.k = P.sbuf(n_("k"), [64, 8, 128], F32)
            o.kt = P.sbuf(n_("kt"), [128, 8, 64], F32)
            o.vt = P.sbuf(n_("vt"), [128, 8, 64], F32)
            o.eg = P.sbuf(n_("eg"), [128, 24], F32)
            o.cfk = P.sbuf(n_("cfk"), [128, 8], F32)
            o.qkd = P.sbuf(n_("qkd"), [128, 8, 128], F32)
            o.U = P.sbuf(n_("U"), [128, 8, 64], F32)
            o.wT = P.sbuf(n_("wT"), [64, 8, 128], F32)
            o.dg = P.sbuf(n_("dg"), [128, 4, 128], F32)
            o.qdT = P.sbuf(n_("qdT"), [128, 8, 128], F32)
            o.kdec = P.sbuf(n_("kdec"), [128, 8, 64], F32)
            o.vnew = P.sbuf(n_("vnew"), [128, 8, 64], F32)
            o.ost = P.sbuf(n_("ost"), [128, 512], F32)
            P.memset(o.qdT[:], 0.0, [o.qdT])
            P.memset(o.S[:], 0.0, [o.S])
            o.order = list(range(NT)) if d == 0 else [1, 0] + list(range(NT - 1, 1, -1))
            o.Uinc = cst[:, C_UI, :] if d == 0 else cst[:, C_LI, :]
            o.Bst = cst[:, C_SL, :] if d == 0 else cst[:, C_SU, :]
            o.mX = (lambda lv: nlv[:, lv, :]) if d == 0 else (lambda lv: nlv[:, 7 + lv, :])
            o.mXT = (lambda lv: nlv[:, 7 + lv, :]) if d == 0 else (lambda lv: nlv[:, lv, :])
            o.d = d
            D_.append(o)
        CH = []
        for o in D_:
            for hg in range(2):
                c = K()
                c.o, c.hg, c.h0, c.hsl = o, hg, 4 * hg, slice(4 * hg, 4 * hg + 4)
                nm = lambda s, o=o, hg=hg: "%s_%d%d" % (s, o.d, hg)
                if hg == 0:
                    mk = lambda s, dt_: P.sbuf(nm(s), [128, 4, 128], dt_)
                else:
                    mk = lambda s, dt_: carve(nm(s), dt_)
                c.GU, c.eD, c.eDT = mk("GU", F32), mk("eD", F32), mk("eDT", F32)
                c.A, c.AT, c.Ym, c.Ymt = mk("A", BF16), mk("AT", BF16), mk("Ym", BF16), mk("Ymt", BF16)
                c.X = [mk("X%d" % i, BF16) for i in range(2)]
                c.XT = [mk("XT%d" % i, BF16) for i in range(2)]
                c.Rv = P.sbuf(nm("Rv"), [128, 4, 64], BF16)
                c.Rk = P.sbuf(nm("Rk"), [128, 4, 64], BF16)
                CH.append(c)

        def v4(ps):
            return ps[:].rearrange("p (h i) -> p h i", h=4)

        for s in range(NT):
            for o in D_:
                d = o.d
                tt = o.order[s]
                o.tt = tt
                tsl = slice(tt * 128, (tt + 1) * 128)
                P.dma(o.q[:], gq_s[:, tsl].rearrange("(h d) t -> d h t", d=64), reads=[gq_s], writes=[o.q])
                P.dma(o.k[:], gk_s[:, tsl].rearrange("(h d) t -> d h t", d=64), reads=[gk_s], writes=[o.k])
                P.dma(o.kt[:].rearrange("p h d -> p (h d)"), gktm_s[tsl, :], reads=[gktm_s], writes=[o.kt])
                P.dma(o.vt[:].rearrange("p h d -> p (h d)"), gvtm_s[tsl, :], reads=[gvtm_s], writes=[o.vt])
                gd = gall[:, tt, d * 8:(d + 1) * 8]
                pg = PS.next()
                P.mm(pg[:, 0:8], o.Uinc, gd, True, True, [cst, gall], [pg])
                P.mm(pg[:, 8:16], o.Bst, gd, True, True, [cst, gall], [pg])
                P.mm(pg[:, 16:24], ones_f[:], gd, True, True, [ones_f, gall], [pg])
                P.act(o.eg[:], pg[:, 0:24], AF.Exp, [pg], [o.eg])
                P.tt(o.cfk[:], ball[:, tt, d * 8:(d + 1) * 8], o.eg[:, 0:8], ALU.mult, [ball, o.eg], [o.cfk])
            for c in CH:
                o, h0 = c.o, c.h0
                d, tt = o.d, o.tt
                P.tt(c.GU[:], b4(o.Uinc), bl(gall[:, tt, d * 8 + h0:d * 8 + h0 + 4], 128), ALU.mult, [cst, gall], [c.GU],
                     eng="gpsimd")
            for c in CH:
                o = c.o
                c.pD, c.pDT = PS.next(), PS.next()
                for h in range(4):
                    P.mm(v4(c.pD)[:, h, :], c.GU[:, h, :], o.Bst, True, True, [c.GU, cst], [c.pD])
                P.mm(c.pDT[:], o.Bst, c.GU[:].rearrange("p h i -> p (h i)"), True, True, [c.GU, cst], [c.pDT])
                P.act(c.eD[:], v4(c.pD), AF.Exp, [c.pD], [c.eD])
                P.act(c.eDT[:], v4(c.pDT), AF.Exp, [c.pDT], [c.eDT])
            for c in CH:
                o, h0 = c.o, c.h0
                d, tt = o.d, o.tt
                pKK = PS.next()
                for h in range(4):
                    P.mm(v4(pKK)[:, h, :], o.k[:, h0 + h, :], o.k[:, h0 + h, :], True, True, [o.k], [pKK])
                P.tt(c.eD[:], v4(pKK), c.eD[:], ALU.mult, [pKK, c.eD], [c.eD])
                P.tt(c.eD[:], c.eD[:], b4(o.Bst), ALU.mult, [c.eD, cst], [c.eD], eng="gpsimd")
                P.tt(c.A[:], c.eD[:], bl(ball[:, tt, d * 8 + h0:d * 8 + h0 + 4], 128), ALU.mult, [c.eD, ball], [c.A],
                     eng="gpsimd")
            for c in CH:
                o, h0 = c.o, c.h0
                pQK = PS.next()
                for h in range(4):
                    P.mm(v4(pQK)[:, h, :], o.k[:, h0 + h, :], o.q[:, h0 + h, :], True, True, [o.k, o.q], [pQK])
                P.tt(c.eDT[:], v4(pQK), c.eDT[:], ALU.mult, [pQK, c.eDT], [c.eDT])
                P.tt(o.qkd[:, c.hsl, :], c.eDT[:], b4(o.Uinc), ALU.mult, [c.eDT, cst], [o.qkd], eng="gpsimd")
            for c in CH:
                pT_ = PB.next()
                for h in range(4):
                    P.tr(pT_[:, h, :], c.A[:, h, :], idb[:], [c.A, idb], [pT_])
                P.act(c.AT[:], pT_[:], AF.Copy, [pT_], [c.AT])
            for c in CH:
                o = c.o
                P.tt(c.Ym[:], c.A[:], b4(o.mX(0)), ALU.mult, [c.A, nlv], [c.Ym], eng="gpsimd")
                P.tt(c.X[0][:], c.Ym[:], Ibc, ALU.add, [c.Ym, idb], [c.X[0]], eng="gpsimd")
                P.tt(c.Ymt[:], c.AT[:], b4(o.mXT(0)), ALU.mult, [c.AT, nlv], [c.Ymt])
                P.tt(c.XT[0][:], c.Ymt[:], Ibc, ALU.add, [c.Ymt, idb], [c.XT[0]])
            cur = 0
            for lv in range(1, 7):
                last = lv == 6
                for c in CH:
                    o = c.o
                    if not last:
                        c.pY = PS.next()
                        for h in range(4):
                            P.mm(v4(c.pY)[:, h, :], c.AT[:, h, :], c.X[cur][:, h, :], True, True, [c.AT, c.X[cur]], [c.pY])
                        P.tt(c.Ym[:], v4(c.pY), b4(o.mX(lv)), ALU.mult, [c.pY, nlv], [c.Ym])
                    c.pY2 = PS.next()
                    for h in range(4):
                        P.mm(v4(c.pY2)[:, h, :], c.A[:, h, :], c.XT[cur][:, h, :], True, True, [c.A, c.XT[cur]], [c.pY2])
                    P.tt(c.Ymt[:], v4(c.pY2), b4(o.mXT(lv)), ALU.mult, [c.pY2, nlv], [c.Ymt])
                for c in CH:
                    if not last:
                        pZ = PS.next()
                        P.mm(pZ[:], idb[:], c.X[cur][:].rearrange("p h i -> p (h i)"), True, False, [idb, c.X[cur]], [pZ])
                        for h in range(4):
                            P.mm(v4(pZ)[:, h, :], c.XT[cur][:, h, :], c.Ym[:, h, :], False, True, [c.XT[cur], c.Ym], [pZ])
                        P.act(c.X[1 - cur][:], v4(pZ), AF.Copy, [pZ], [c.X[1 - cur]])
                    pZ = PS.next()
                    P.mm(pZ[:], idb[:], c.XT[cur][:].rearrange("p h i -> p (h i)"), True, False, [idb, c.XT[cur]], [pZ])
                    for h in range(4):
                        P.mm(v4(pZ)[:, h, :], c.X[cur][:, h, :], c.Ymt[:, h, :], False, True, [c.X[cur], c.Ymt], [pZ])
                    P.act(c.XT[1 - cur][:], v4(pZ), AF.Copy, [pZ], [c.XT[1 - cur]])
                cur = 1 - cur
            for c in CH:
                o, h0, hsl = c.o, c.h0, c.hsl
                d, tt = o.d, o.tt
                XTf_ = c.XT[cur]
                P.tt(c.Rv[:], o.vt[:, hsl, :], bl(ball[:, tt, d * 8 + h0:d * 8 + h0 + 4], 64), ALU.mult, [o.vt, ball], [c.Rv],
                     eng="gpsimd")
                P.tt(c.Rk[:], o.kt[:, hsl, :], bl(o.cfk[:, hsl], 64), ALU.mult, [o.kt, o.cfk], [c.Rk], eng="gpsimd")
                pU, pW = PS.next(), PS.next()
                pU3 = pU[:, 0:256].rearrange("p (h i) -> p h i", h=4)
                pW3 = pW[0:64, :].rearrange("p (h i) -> p h i", h=4)
                for h in range(4):
                    P.mm(pU3[:, h, :], XTf_[:, h, :], c.Rv[:, h, :], True, True, [XTf_, c.Rv], [pU])
                for h in range(4):
                    P.mm(pW3[:, h, :], c.Rk[:, h, :], XTf_[:, h, :], True, True, [XTf_, c.Rk], [pW])
                P.act(o.U[:, hsl, :], pU3, AF.Copy, [pU], [o.U])
                P.act(o.wT[:, hsl, :], pW3, AF.Copy, [pW], [o.wT])
            for o in D_:
                for half in range(2):
                    P.tt(o.dg[:], b4(cst[:, C_ID, :]), bl(o.eg[:, 4 * half:4 * half + 4], 128), ALU.mult, [cst, o.eg], [o.dg],
                         eng="gpsimd")
                    pE = PS.next()
                    P.mm(pE[0:64, :], ones_f[:, 0:64], o.dg[:].rearrange("p h i -> p (h i)"), True, True, [ones_f, o.dg], [pE])
                    P.tt(o.qdT[0:64, 4 * half:4 * half + 4, :], o.q[:, 4 * half:4 * half + 4, :],
                         pE[0:64, :].rearrange("p (h i) -> p h i", h=4), ALU.mult, [o.q, pE], [o.qdT])
                P.tt(o.kdec[:], o.kt[:], bl(o.eg[:, 8:16], 64), ALU.mult, [o.kt, o.eg], [o.kdec], eng="gpsimd")
            for o in D_:
                pWS = PS.next()
                pWS3 = pWS[:].rearrange("p (h e) -> p h e", h=8)
                for h in range(8):
                    P.mm(pWS3[:, h, :], o.wT[:, h, :], o.S[0:64, h, :], True, True, [o.wT, o.S], [pWS])
                P.tt(o.vnew[:], o.U[:], pWS3, ALU.subtract, [o.U, pWS], [o.vnew])
            for o in D_:
                tt = o.tt
                tsl = slice(tt * 128, (tt + 1) * 128)
                pOO = PS.next()
                pOO3 = pOO[:].rearrange("p (h e) -> p h e", h=8)
                for h in range(8):
                    P.mm(pOO3[:, h, :], o.qdT[:, h, :], o.S[:, h, :], True, False, [o.qdT, o.S], [pOO])
                    P.mm(pOO3[:, h, :], o.qkd[:, h, :], o.vnew[:, h, :], False, True, [o.qkd, o.vnew], [pOO])
                P.act(o.ost[:], pOO[:], AF.Copy, [pOO], [o.ost])
                P.dma(og_s[o.d][tsl, :], o.ost[:], reads=[o.ost], writes=[og_s[o.d]])
                pSn = PS.next()
                pSn3 = pSn[0:64, :].rearrange("p (h e) -> p h e", h=8)
                for h in range(8):
                    P.mm(pSn3[:, h, :], o.kdec[:, h, :], o.vnew[:, h, :], True, True, [o.kdec, o.vnew], [pSn])
                P.tt(o.S[0:64], o.S[0:64], o.eg[0:64, 16:24].unsqueeze(2).broadcast_to([64, 8, 64]), ALU.mult, [o.S, o.eg], [o.S])
                P.tt(o.S[0:64], o.S[0:64], pSn3, ALU.add, [o.S, pSn], [o.S])
        P.emit()
        P = Prog(nc, sy, "gdf")
        grow = P.sbuf("grow", [128, 544], F32)
        P.dma(grow[:], grow_d, writes=[grow])
        sqoR = Rot(lambda i: P.sbuf("sqo%d" % i, [128, 8, 64], F32), 2)
        ssR = Rot(lambda i: P.sbuf("ss%d" % i, [128, 8], F32), 2)
        toR = Rot(lambda i: P.sbuf("to%d" % i, [128, 8, 64], F32), 2)
        tobR = Rot(lambda i: P.sbuf("tob%d" % i, [128, 512], BF16), 2)
        gtt = Rot(lambda i: P.sbuf("gtt%d" % i, [128, 512], BF16), 2)
        oa = Rot(lambda i: P.sbuf("oa%d" % i, [128, 512], F32), 2)
        ob_ = Rot(lambda i: P.sbuf("ob%d" % i, [128, 512], F32), 2)
        idb = P.sbuf("idb", [128, 128], BF16)
        P.copy(idb[:], cst[:, C_ID, :], [cst], [idb])
        mixo = Rot(lambda i: P.sbuf("mixo%d" % i, [128, 4, 128], BF16), 2)
        ptbR = Rot(lambda i: P.psum("ptb%d" % i, [128, 4, 128], BF16), 2)
        for tt in range(NT):
            sqo, ss, to, tob, ptb = sqoR.next(), ssR.next(), toR.next(), tobR.next(), ptbR.next()
            tsl = slice(tt * 128, (tt + 1) * 128)
            g_, a_, b_ = gtt.next(), oa.next(), ob_.next()
            P.dma(g_[:], ggate_s[tsl, :], reads=[ggate_s], writes=[g_])
            P.dma(a_[:], og_s[0][tsl, :], reads=[og_s[0]], writes=[a_])
            P.dma(b_[:], og_s[1][tsl, :], reads=[og_s[1]], writes=[b_])
            P.tt(a_[:], a_[:], b_[:], ALU.add, [a_, b_], [a_], eng="gpsimd")
            o3 = a_[:].rearrange("p (h e) -> p h e", h=8)
            P.act(sqo[:], o3, AF.Square, [a_], [sqo])
            P.op("vector", lambda e, ss=ss, sqo=sqo: e.tensor_reduce(out=ss[:], in_=sqo[:], axis=AX.X, op=ALU.add), [sqo], [ss])
            P.act(ss[:], ss[:], AF.Ln, [ss], [ss], bias=EPS, scale=1.0 / 64)
            P.act(ss[:], ss[:], AF.Exp, [ss], [ss], scale=-0.5)
            P.tt(to[:], o3, ss[:].unsqueeze(2).broadcast_to([128, 8, 64]), ALU.mult, [a_, ss], [to])
            P.tt(to[:].rearrange("p h e -> p (h e)"), to[:].rearrange("p h e -> p (h e)"), grow[:, 32:544], ALU.mult,
                 [to, grow], [to], eng="gpsimd")
            P.tt(tob[:], to[:].rearrange("p h e -> p (h e)"), g_[:], ALU.mult, [to, g_], [tob])
            for c in range(4):
                P.tr(ptb[:, c, :], tob[:, c * 128:(c + 1) * 128], idb[:], [tob, idb], [ptb])
            mo_ = mixo.next()
            P.act(mo_[:], ptb[:], AF.Copy, [ptb], [mo_])
            P.dma(mix_s[:, 4:8, tsl], mo_[:], reads=[mo_], writes=[mix_s])
        P.dma(XT[:, 4:8, :].rearrange("p c t -> p (c t)"), xt_s[:], reads=[xt_s], writes=[XT])
        P.emit()


    for l in layers:
        if stage >= 1 and not skip_ffn1:
            ffn(l, 0, FULL)
        dump_xt("xt_ffn1_%d" % l)
        if stage >= 2:
            make_xn(l)
            if l == 0:
                if parts is None or "mlp" in parts:
                    mla_prep()
                if parts is None or "mlc" in parts:
                    mla_core()
                if parts is None or "gdp" in parts:
                    gdn_prep()
                if parts is None or "gdc" in parts:
                    gdn_core()
                if parts is None or "op" in parts:
                    out_proj(l, evwo_d, GROUPS5)
            if l == 1:
                if parts is None or "swp" in parts:
                    swa_prep()
                if parts is None or "swc" in parts:
                    swa_core()
                if parts is None or "glp" in parts:
                    gla_prep()
                if parts is None or "glc" in parts:
                    gla_core()
                if parts is None or "op" in parts:
                    out_proj(l, odwo_d, LATG)
        if "mix" in dbg_d:
            P = Prog(nc, sy, "dbgmix%d" % l)
            mb_ = Rot(lambda i: P.sbuf("mb%d" % i, [128, 8, 128], BF16), 2)
            mf_ = Rot(lambda i: P.sbuf("mf%d" % i, [128, 8, 128], F32), 2)
            dv = dbg_d["mix"].rearrange("p (c t) -> p c t", c=8)
            for tt in range(NT):
                a_, f_ = mb_.next(), mf_.next()
                P.dma(a_[:], mix_s[:, :, tt * 128:(tt + 1) * 128], reads=[mix_s], writes=[a_])
                P.act(f_[:], a_[:], AF.Copy, [a_], [f_])
                P.dma(dv[:, :, tt * 128:(tt + 1) * 128], f_[:], reads=[f_])
            P.emit()
        dump_xt("xt_mix_%d" % l)
        if stage >= 3:
            ffn(l, 1, FULL if l == 0 else LAT)

    P = Prog(nc, sy, "pout")
    ost = [P.sbuf("ost%d" % i, [128, D], F32) for i in range(2)]
    pot = [P.psum("pot%d" % i, [128, 4, 128]) for i in range(2)]
    for tt in range(2, NT):
        o = ost[tt % 2]
        for half in range(2):
            ps = pot[half]
            for c4 in range(4):
                c = half * 4 + c4
                P.tr(ps[:, c4, :], XT[:, c, tt * 128:(tt + 1) * 128], ident, [(XT, tt), cst], [ps])
            if half == 0:
                P.copy(o[:, 0:512], ps[:].rearrange("p a b -> p (a b)"), [ps], [o])
            else:
                P.act(o[:, 512:1024], ps[:].rearrange("p a b -> p (a b)"), AF.Copy, [ps], [o])
        P.dma(out_d[(tt - 2) * 128:(tt - 1) * 128, :], o[:], reads=[o])
    P.emit()
    es.close()
    return nc


def rope_tables(rot_dim):
    f = np.float32
    t = np.arange(SEQ)
    row = (t // 64).astype(f)
    col = (t % 64).astype(f)
    nf = rot_dim // 4
    inv = (f(10000.0) ** (-np.arange(nf, dtype=f) / f(nf))).astype(f)
    ang = np.concatenate([row[:, None] * inv, col[:, None] * inv], axis=-1).astype(f)
    return np.cos(ang).astype(f), np.sin(ang).astype(f)


def host_consts():
    f = np.float32
    idx = np.arange(128)
    ident = np.eye(128)
    bd64 = (idx[:, None] // 64 == idx[None, :] // 64)
    Rm = np.zeros((64, 64))
    for i in range(32):
        Rm[i, i + 32] = -1.0
        Rm[i + 32, i] = 1.0
    rot = np.zeros((128, 128))
    rot[0:64, 0:64] = Rm.T
    rot[64:128, 64:128] = Rm.T
    UI = idx[:, None] <= idx[None, :]
    LI = idx[:, None] >= idx[None, :]
    SL = idx[:, None] > idx[None, :]
    SU = idx[:, None] < idx[None, :]
    rot96 = np.zeros((128, 128))
    for i in range(16):
        rot96[80 + i, 64 + i] = -1.0
        rot96[64 + i, 80 + i] = 1.0
    cst = np.stack([ident, bd64, rot, UI, LI, SL, SU, rot96], axis=1).astype(f)
    lv = []
    for k in range(7):
        bsz = 1 << k
        same = (idx[:, None] // (2 * bsz)) == (idx[None, :] // (2 * bsz))
        m = same & ((idx[:, None] % (2 * bsz)) >= bsz) & ((idx[None, :] % (2 * bsz)) < bsz)
        lv.append(m)
    lvm = np.stack(lv + [m.T for m in lv], axis=1).astype(f)
    c8, s8 = rope_tables(32)
    mcs = np.zeros((96, 2, SEQ), f)
    mcs[0:64, 0, :] = 1.0
    for dd in range(32):
        mcs[64 + dd, 0, :] = c8[:, dd % 16]
        mcs[64 + dd, 1, :] = s8[:, dd % 16]
    bdsel = np.stack([(idx // 64 == 0), (idx // 64 == 1)], axis=1).astype(f)
    c, s = rope_tables(64)
    d = idx % 64
    swcs = np.stack([c[:, d % 32].T, s[:, d % 32].T], axis=1).astype(f)
    return {"cst": np.ascontiguousarray(cst), "bdsel": np.ascontiguousarray(bdsel), "swa_cs": np.ascontiguousarray(swcs),
            "lvm": np.ascontiguousarray(lvm), "mla_cs": np.ascontiguousarray(mcs)}


_CONSTS = None


def host_inputs(inputs, b):
    global _CONSTS
    f = np.float32
    if _CONSTS is None:
        _CONSTS = host_consts()
    c = np.asarray(inputs["c"][b], f)
    cc = np.asarray(inputs["c_ctx"], f)
    ccol = np.concatenate([c.reshape(8, 128).T, cc.reshape(8, 128).T], axis=1)
    idx = np.arange(128)
    odc = np.zeros((128, 11), f)
    odc[:, 0] = np.asarray(inputs["od_swa_q_norm"], f)[0][idx % 64]
    odc[:, 1] = np.asarray(inputs["od_swa_k_norm"], f)[0][idx % 64]
    odc[:, 2] = np.asarray(inputs["od_gla_out_norm"], f)[0]
    odc[:, 3:11] = np.asarray(inputs["od_swa_sink"], f)[0][None, :]
    w2 = np.asarray(inputs["od_gla_gate_w2"], f)[0]
    gb = np.asarray(inputs["od_gla_gate_b"], f)[0]
    gw2b = np.concatenate([w2, gb[:, None, :]], axis=1).transpose(1, 0, 2)
    evc = np.zeros((128, 8), f)
    evc[:, 0:3] = np.asarray(inputs["ev_q_a_norm"], f)[0].reshape(3, 128).T
    evc[:, 3:5] = np.asarray(inputs["ev_kv_a_norm"], f)[0].reshape(2, 128).T
    evc[0:96, 5] = np.asarray(inputs["ev_mla_q_norm"], f)[0]
    evc[0:96, 6] = np.asarray(inputs["ev_mla_k_norm"], f)[0]
    gcv = np.asarray(inputs["ev_gdn_conv"], f)[0].reshape(5, 12, 128).transpose(2, 1, 0)
    grow = np.zeros((128, 544), f)
    grow[:, 0:16] = np.asarray(inputs["ev_gdn_a_log"], f)[0].reshape(16)[None, :]
    grow[:, 16:32] = np.asarray(inputs["ev_gdn_dt_bias"], f)[0].reshape(16)[None, :]
    grow[:, 32:544] = np.tile(np.asarray(inputs["ev_gdn_out_norm"], f)[0], 8)[None, :]
    m = {
        "ev_w_in": np.ascontiguousarray(np.asarray(inputs["ev_w_in"], f)[0]),
        "ev_w_out": np.ascontiguousarray(np.asarray(inputs["ev_w_out"], f)[0]),
        "ev_w_q_up": np.ascontiguousarray(np.asarray(inputs["ev_w_q_up"], f)[0]),
        "ev_w_kv_up": np.ascontiguousarray(np.asarray(inputs["ev_w_kv_up"], f)[0]),
        "ev_cols": evc,
        "gdn_conv_col": np.ascontiguousarray(gcv),
        "gdn_rows": grow,
        "x": np.ascontiguousarray(inputs["x"][b], dtype=f),
        "ctx": np.ascontiguousarray(inputs["ctx"][b], dtype=f),
        "ccol": np.ascontiguousarray(ccol),
        "ada_w": np.ascontiguousarray(inputs["ada_w"], dtype=f),
        "ada_b_col": np.ascontiguousarray(np.asarray(inputs["ada_b"], f).reshape(2, 72, 128).transpose(2, 0, 1)),
        "norm_g_col": np.ascontiguousarray(np.asarray(inputs["norm_g"], f).reshape(2, 3, 8, 128).transpose(3, 0, 1, 2)),
        "ffn_w_gate": np.ascontiguousarray(inputs["ffn_w_gate"], dtype=f),
        "ffn_w_up": np.ascontiguousarray(inputs["ffn_w_up"], dtype=f),
        "ffn_w_down": np.ascontiguousarray(inputs["ffn_w_down"], dtype=f),
        "od_w_in": np.ascontiguousarray(np.asarray(inputs["od_w_in"], f)[0]),
        "od_w_out": np.ascontiguousarray(np.asarray(inputs["od_w_out"], f)[0]),
        "od_cols": odc,
        "gla_w2b": np.ascontiguousarray(gw2b),
    }
    m.update(_CONSTS)
    return m


def kernel(**inputs):
    nc = build()
    in_maps = [host_inputs(inputs, b) for b in range(NCORES)]
    res = run_bass_kernel_spmd(nc, in_maps, core_ids=list(range(NCORES)))
    return np.stack([np.asarray(r["out"], np.float32) for r in res.results], axis=0)
```

```python
from contextlib import ExitStack
import numpy as np
import concourse.bass as bass
import concourse.mybir as mybir
from concourse.bass_utils import run_bass_kernel_spmd

F32 = mybir.dt.float32
BF16 = mybir.dt.bfloat16
AF = mybir.ActivationFunctionType
ALU = mybir.AluOpType
AX = mybir.AxisListType

D = 1024
DFF = 2816
NF = DFF // 128
SEQ = 2048
CTX = 256
T = SEQ + CTX
NT = T // 128
EPS = 1e-6
NCORES = 8

COMPUTE = ("tensor", "vector", "scalar", "gpsimd")
ENGINES = ("tensor", "vector", "scalar", "gpsimd", "sync")
N_DMA_SEMS = 24


class Buf:
    _n = 0

    def __init__(self, t, nslots=1, name=None):
        self.t = t
        self.nslots = nslots
        Buf._n += 1
        self.id = Buf._n
        self.name = name

    def __getitem__(self, idx):
        return self.t[idx]


class Sync:
    def __init__(self, nc, es):
        self.nc = nc
        self.sems = {e: es.enter_context(nc.semaphore("s_" + e)) for e in COMPUTE}
        self.dsems = [es.enter_context(nc.semaphore("d_%d" % i)) for i in range(N_DMA_SEMS)]
        self.pos = {e: 0 for e in COMPUTE}
        self.n_dma = 0
        self.dma_uses = [0] * N_DMA_SEMS


class Prog:
    def __init__(self, nc, sync, name="ph"):
        self.nc = nc
        self.sy = sync
        self.name = name
        self.es = ExitStack()
        self.ops = []
        self.state = {}
        self.dma_last = [None] * N_DMA_SEMS
        self.start_pos = dict(sync.pos)
        self.start_dma = [16 * u for u in sync.dma_uses]

    def sbuf(self, name, shape, dtype, nslots=1):
        t = self.es.enter_context(self.nc.sbuf_tensor(self.name + "_" + name, list(shape), dtype))
        return Buf(t, nslots, name)

    def psum(self, name, shape, dtype=F32, nslots=1):
        t = self.es.enter_context(self.nc.psum_tensor(self.name + "_" + name, list(shape), dtype))
        return Buf(t, nslots, name)

    def _expand(self, keys):
        out = []
        for k in keys:
            if isinstance(k, Buf):
                k = (k, None)
            b, s = k
            if s is None:
                out.extend((b.id, i) for i in range(b.nslots))
            elif isinstance(s, (list, tuple, range)):
                for i in s:
                    assert 0 <= i < b.nslots, (b.name, i, b.nslots)
                    out.append((b.id, i))
            else:
                assert 0 <= s < b.nslots, (b.name, s, b.nslots)
                out.append((b.id, s))
        return out

    def _deps(self, idx, reads, writes):
        deps = set()
        rk = self._expand(reads)
        wk = self._expand(writes)
        for k in rk:
            st = self.state.setdefault(k, [None, []])
            if st[0] is not None:
                deps.add(st[0])
        for k in wk:
            st = self.state.setdefault(k, [None, []])
            if st[0] is not None:
                deps.add(st[0])
            deps.update(st[1])
        for k in rk:
            self.state[k][1].append(idx)
        for k in wk:
            self.state[k] = [idx, []]
        deps.discard(idx)
        return deps

    def op(self, eng, fn, reads=(), writes=()):
        idx = len(self.ops)
        deps = self._deps(idx, reads, writes)
        self.ops.append(dict(eng=eng, fn=fn, deps=deps, dma=False, pos=self.sy.pos[eng]))
        self.sy.pos[eng] += 1
        return idx

    def dma(self, out, in_, reads=(), writes=(), q="sync"):
        idx = len(self.ops)
        deps = self._deps(idx, reads, writes)
        sy = self.sy
        s = sy.n_dma % N_DMA_SEMS
        sy.n_dma += 1
        prev = self.dma_last[s]
        sy.dma_uses[s] += 1
        self.dma_last[s] = idx
        self.ops.append(dict(eng=q, fn=lambda e: e.dma_start(out=out, in_=in_), deps=deps, dma=True, sem=s,
                             target=16 * sy.dma_uses[s], prev=prev))
        return idx

    def mm(self, out, lhsT, rhs, start, stop, r, w):
        return self.op("tensor", lambda e: e.matmul(out, lhsT=lhsT, rhs=rhs, start=start, stop=stop), r, w)

    def tr(self, out, in_, ident, r, w):
        return self.op("tensor", lambda e: e.transpose(out, in_, ident), r, w)

    def act(self, out, in_, func, r, w, bias=None, scale=None, accum_out=None):
        kw = {}
        if bias is not None:
            kw["bias"] = bias
        if scale is not None:
            kw["scale"] = scale
        if accum_out is not None:
            kw["accum_out"] = accum_out
        return self.op("scalar", lambda e: e.activation(out=out, in_=in_, func=func, **kw), r, w)

    def tt(self, out, in0, in1, op, r, w, eng="vector"):
        return self.op(eng, lambda e: e.tensor_tensor(out=out, in0=in0, in1=in1, op=op), r, w)

    def ts(self, out, in0, s1, op0, r, w, s2=None, op1=None, eng="vector"):
        if op1 is None:
            return self.op(eng, lambda e: e.tensor_scalar(out=out, in0=in0, scalar1=s1, scalar2=None, op0=op0), r, w)
        return self.op(eng, lambda e: e.tensor_scalar(out=out, in0=in0, scalar1=s1, scalar2=s2, op0=op0, op1=op1), r, w)

    def stt(self, out, in0, scalar, in1, op0, op1, r, w):
        return self.op("vector", lambda e: e.scalar_tensor_tensor(out=out, in0=in0, scalar=scalar, in1=in1,
                                                                  op0=op0, op1=op1), r, w)

    def copy(self, out, in_, r, w, eng="vector"):
        return self.op(eng, lambda e: e.tensor_copy(out=out, in_=in_), r, w)

    def memset(self, ap, val, w, eng="vector"):
        return self.op(eng, lambda e: e.memset(ap, val), (), w)

    def emit(self):
        nc = self.nc
        sy = self.sy
        ops = self.ops
        per_eng = {e: [] for e in ENGINES}
        for i, o in enumerate(ops):
            per_eng[o["eng"]].append(i)
        end_pos = dict(sy.pos)
        end_dma = [16 * u for u in sy.dma_uses]
        start_pos = self.start_pos
        start_dma = self.start_dma

        def replay(ename):
            def body(eng):
                waited = dict(start_pos)
                dwaited = list(start_dma)
                for i in per_eng[ename]:
                    o = ops[i]
                    need = {}
                    dneed = {}
                    deps = set(o["deps"])
                    if o["dma"] and o["prev"] is not None:
                        deps.add(o["prev"])
                    for j in deps:
                        d = ops[j]
                        if d["dma"]:
                            dneed[d["sem"]] = max(dneed.get(d["sem"], 0), d["target"])
                        else:
                            if d["eng"] == "tensor" and ename == "tensor" and not o["dma"]:
                                continue
                            need[d["eng"]] = max(need.get(d["eng"], 0), d["pos"] + 1)
                    for e, v in need.items():
                        if v > waited[e]:
                            eng.wait_ge(sy.sems[e], v)
                            waited[e] = v
                    for s, v in dneed.items():
                        if v > dwaited[s]:
                            eng.wait_ge(sy.dsems[s], v)
                            dwaited[s] = v
                    ins = o["fn"](eng)
                    if o["dma"]:
                        ins.then_inc(sy.dsems[o["sem"]], 16)
                    else:
                        ins.then_inc(sy.sems[ename], 1)
                if ename == "sync":
                    for s in range(N_DMA_SEMS):
                        if end_dma[s] > start_dma[s]:
                            eng.wait_ge(sy.dsems[s], end_dma[s])
                    for e in COMPUTE:
                        if end_pos[e] > start_pos[e]:
                            eng.wait_ge(sy.sems[e], end_pos[e])
            return body

        with nc.Block() as block:
            for e in ENGINES:
                if per_eng[e] or e == "sync":
                    getattr(block, e)(replay(e))
        self.es.close()


def segs(t0, n):
    out = []
    if t0 < CTX:
        m = min(n, CTX - t0)
        out.append((t0, m, 1))
        if n > m:
            out.append((CTX, n - m, 0))
    else:
        out.append((t0, n, 0))
    return out


def tiles_of(t0, n):
    return list(range(t0 // 128, (t0 + n + 127) // 128))


class K:
    pass


class Rot:
    def __init__(self, mk, n):
        self.t = [mk(i) for i in range(n)]
        self.i = 0

    def next(self):
        t = self.t[self.i % len(self.t)]
        self.i += 1
        return t


GROUPS5 = [(0, 256), (256, 512), (768, 512), (1280, 512), (1792, 512)]
LATG = GROUPS5[1:]
FULL = [[(0, 384), (384, 384), (768, 384)], [(1152, 384), (1536, 384), (1920, 384)]]
LAT = [[(256, 512), (768, 512)], [(1280, 512), (1792, 512)]]
C_ID, C_BD64, C_ROT64, C_UI, C_LI, C_SL, C_SU, C_ROT96 = range(8)
NCST = 8


def build(stage=99, dbg=(), layers=(0, 1), parts=None, skip_ffn1=False):
    nc = bass.Bass("TRN2", target_bir_lowering=False)
    dt = nc.dram_tensor

    def inp(name, shape, dtype=F32):
        return dt(name, list(shape), dtype, kind="ExternalInput").ap()

    def scratch(name, shape, dtype, nslots=1):
        return Buf(dt("scr_" + name, list(shape), dtype, kind="Internal").ap(), nslots, name)

    x_d = inp("x", [SEQ, D])
    ctx_d = inp("ctx", [CTX, D])
    ccol_d = inp("ccol", [128, 16])
    adaw_d = inp("ada_w", [2, D, 9 * D])
    adab_d = inp("ada_b_col", [128, 2, 72])
    ng_d = inp("norm_g_col", [128, 2, 3, 8])
    wg_d = inp("ffn_w_gate", [2, 2, D, DFF])
    wu_d = inp("ffn_w_up", [2, 2, D, DFF])
    wd_d = inp("ffn_w_down", [2, 2, DFF, D])
    cst_d = inp("cst", [128, NCST, 128])
    bdsel_d = inp("bdsel", [128, 2])
    odw_d = inp("od_w_in", [D, 2336])
    odwo_d = inp("od_w_out", [D, D])
    odcol_d = inp("od_cols", [128, 11])
    swcs_d = inp("swa_cs", [128, 2, SEQ])
    gw2b_d = inp("gla_w2b", [17, 2, 256])
    evw_d = inp("ev_w_in", [D, 2752])
    evwo_d = inp("ev_w_out", [D, D])
    evqu_d = inp("ev_w_q_up", [384, 768])
    evkv_d = inp("ev_w_kv_up", [256, 1024])
    evcol_d = inp("ev_cols", [128, 8])
    mlcs_d = inp("mla_cs", [96, 2, SEQ])
    lvm_d = inp("lvm", [128, 14, 128])
    gcv_d = inp("gdn_conv_col", [128, 12, 5])
    grow_d = inp("gdn_rows", [128, 544])
    out_d = dt("out", [SEQ, D], F32, kind="ExternalOutput").ap()
    dbg_d = {}
    for name, shape in dbg:
        dbg_d[name] = dt("dbg_" + name, list(shape), F32, kind="ExternalOutput").ap()

    xn_s = scratch("xn", [128, 8, T], BF16)
    mix_s = scratch("mix", [128, 8, T], BF16)
    swq_s = scratch("swq", [128, 4, SEQ], BF16)
    swk_s = scratch("swk", [128, T], BF16)
    swv_s = scratch("swv", [128, NT * 256], BF16)
    swr_s = scratch("swr", [128, NT * 2], F32)
    glq_s = [scratch("glq%d" % d, [128, 2, T], BF16) for d in range(2)]
    glk_s = [scratch("glk%d" % d, [128, 2, T], BF16) for d in range(2)]
    glkd_s = [scratch("glkd%d" % d, [T, 256], BF16) for d in range(2)]
    glet_s = [scratch("glet%d" % d, [128, 2, NT], F32) for d in range(2)]
    glv_s = scratch("glv", [T, 512], BF16)
    glr_s = scratch("glr", [128, 4, SEQ], BF16)

    mqf_s = scratch("mqf", [96, 8, T], BF16)
    mkf_s = scratch("mkf", [96, 8, T], BF16)
    mv_s = scratch("mv", [T, 512], BF16)
    mrk_s = scratch("mrk", [128, NT * 8], F32)
    gq_s = scratch("gq", [512, T], F32)
    gk_s = scratch("gk", [512, T], F32)
    gktm_s = scratch("gktm", [T, 512], F32)
    gvtm_s = scratch("gvtm", [T, 512], F32)
    gg_s = scratch("gg", [128, NT * 16], F32)
    gb_s = scratch("gb", [128, NT * 16], F32)
    ggate_s = scratch("ggate", [T, 512], BF16)
    og_s = [scratch("og%d" % d, [T, 512], F32) for d in range(2)]
    xt_s = scratch("xtpark", [128, 4 * T], F32)

    es = ExitStack()
    sy = Sync(nc, es)

    def persist(name, shape, dtype, nslots=1):
        return Buf(es.enter_context(nc.sbuf_tensor(name, list(shape), dtype)), nslots, name)

    XT = persist("XT", [128, 8, T], F32, NT)
    cst = persist("cst_s", [128, NCST, 128], F32)
    ident = cst[:, C_ID, :]
    ones_f = persist("ones_f", [128, 128], F32)
    ones_b = persist("ones_b", [128, 128], BF16)
    modT = persist("modT", [128, 2, 72, 2], F32)
    AG = persist("AG", [128, 2, 3, 8, 2], F32)
    GH = persist("GH", [128, 2, 3, 8, 2], F32)
    ng = persist("ng", [128, 2, 3, 8], F32)

    def xk(t0, n):
        return [(XT, i) for i in tiles_of(t0, n)]

    P = Prog(nc, sy, "p0")
    P.dma(cst[:], cst_d, writes=[cst])
    P.dma(ng[:], ng_d, writes=[ng])
    P.memset(ones_f[:], 1.0, [ones_f])
    P.memset(ones_b[:], 1.0, [ones_b])
    xin = [P.sbuf("xin%d" % i, [128, D], F32) for i in range(2)]
    ptr = [P.psum("ptr%d" % i, [128, 4, 128]) for i in range(2)]
    for tt in range(NT):
        st = xin[tt % 2]
        src = ctx_d[tt * 128:(tt + 1) * 128, :] if tt < 2 else x_d[(tt - 2) * 128:(tt - 1) * 128, :]
        P.dma(st[:], src, writes=[st])
        for half in range(2):
            ps = ptr[half]
            for c4 in range(4):
                c = half * 4 + c4
                P.tr(ps[:, c4, :], st[:, c * 128:(c + 1) * 128], ident, [st, cst], [ps])
            if half == 0:
                P.copy(XT[:, 0:4, tt * 128:(tt + 1) * 128], ps[:], [ps], [(XT, tt)])
            else:
                P.act(XT[:, 4:8, tt * 128:(tt + 1) * 128], ps[:], AF.Copy, [ps], [(XT, tt)])

    ccol = P.sbuf("ccol", [128, 16], F32)
    sc3 = P.sbuf("sc3", [128, 8, 2], F32)
    adab = P.sbuf("adab", [128, 2, 72], F32)
    P.dma(ccol[:], ccol_d, writes=[ccol])
    P.dma(adab[:], adab_d, writes=[adab])
    P.act(sc3[:, :, 0], ccol[:, 0:8], AF.Silu, [ccol], [sc3])
    P.act(sc3[:, :, 1], ccol[:, 8:16], AF.Silu, [ccol], [sc3])
    ABLK = 256
    awst = [P.sbuf("awst%d" % i, [128, 8, ABLK], F32) for i in range(2)]
    pmod = P.psum("pmod", [128, 72, 2])
    nb = 0
    for l in range(2):
        aw = adaw_d[l].rearrange("(kc p) n -> p kc n", p=128)
        for blk in range(9 * D // ABLK):
            st = awst[nb % 2]
            nb += 1
            P.dma(st[:], aw[:, :, blk * ABLK:(blk + 1) * ABLK], writes=[st])
            for m in range(ABLK // 128):
                j = blk * (ABLK // 128) + m
                for kc in range(8):
                    P.mm(pmod[:, j, :], st[:, kc, m * 128:(m + 1) * 128], sc3[:, kc, :], kc == 0, kc == 7,
                         [st, sc3], [pmod])
        for s in range(2):
            P.tt(modT[:, l, :, s], pmod[:, :, s], adab[:, l, :], ALU.add, [pmod, adab], [modT])
    for l in range(2):
        for n in range(3):
            for s in range(2):
                P.stt(AG[:, l, n, :, s], modT[:, l, (3 * n + 1) * 8:(3 * n + 2) * 8, s], 1.0, ng[:, l, n, :],
                      ALU.add, ALU.mult, [modT, ng], [AG])
                P.ts(GH[:, l, n, :, s], modT[:, l, (3 * n + 2) * 8:(3 * n + 3) * 8, s], 0.5 if n != 1 else 1.0,
                     ALU.mult, [modT], [GH])
    if "modT" in dbg_d:
        P.dma(dbg_d["modT"], modT[:].rearrange("p l j s -> p (l j s)"), reads=[modT])
    P.emit()

    def norm_mod(P, W, l, n_idx, t0, n, dst_fn, dst_keys):
        sq, lnv, rs, pss = (w.next() for w in W)
        P.act(sq[:, :, 0:n], XT[:, :, t0:t0 + n], AF.Square, xk(t0, n), [sq])
        for c in range(8):
            P.mm(pss[:, 0:n], ones_f[:], sq[:, c, 0:n], c == 0, c == 7, [sq, ones_f], [pss])
        P.act(lnv[:, 0:n], pss[:, 0:n], AF.Ln, [pss], [lnv], bias=EPS, scale=1.0 / D)
        P.act(rs[:, 0:n], lnv[:, 0:n], AF.Exp, [lnv], [rs], scale=-0.5)
        P.tt(sq[:, :, 0:n], XT[:, :, t0:t0 + n], rs[:, 0:n].unsqueeze(1).broadcast_to([128, 8, n]), ALU.mult,
             xk(t0, n) + [rs, sq], [sq])
        for (s0, sn, s) in segs(t0, n):
            for c in range(8):
                P.act(dst_fn(c, s0, sn), sq[:, c, s0 - t0:s0 - t0 + sn], AF.Identity,
                      [sq], dst_keys, bias=modT[:, l, (3 * n_idx) * 8 + c, s:s + 1],
                      scale=AG[:, l, n_idx, c, s:s + 1])

    def norm_ws(P, nmax, nb=1):
        return (Rot(lambda i: P.sbuf("sq%d" % i, [128, 8, nmax], F32), nb),
                Rot(lambda i: P.sbuf("lnv%d" % i, [128, nmax], F32), nb),
                Rot(lambda i: P.sbuf("rs%d" % i, [128, nmax], F32), nb),
                Rot(lambda i: P.psum("pss%d" % i, [128, 512]), nb))

    def ffn(l, which, halves):
        n_idx = 0 if which == 0 else 2
        wg_v = wg_d[l, which].rearrange("(kc p) n -> p kc n", p=128)
        wu_v = wu_d[l, which].rearrange("(kc p) n -> p kc n", p=128)
        wd_v = wd_d[l, which].rearrange("(f p) n -> p f n", p=128)
        for hi, groups in enumerate(halves):
            h0 = groups[0][0]
            NH = sum(g[1] for g in groups)
            ng_ = len(groups)
            tag = "f%d%d%d" % (l, which, hi)
            esA = ExitStack()
            AT = Buf(esA.enter_context(nc.sbuf_tensor(tag + "AT", [128, NF, NH], BF16)), NF * ng_, "AT")
            P = Prog(nc, sy, tag + "a")
            hT = P.sbuf("hT", [128, 8, NH], BF16, ng_)
            nmax = max(g[1] for g in groups)
            W = norm_ws(P, nmax)
            for gi, (t0, n) in enumerate(groups):
                norm_mod(P, W, l, n_idx, t0, n, lambda c, s0, sn: hT[:, c, s0 - h0:s0 - h0 + sn], [(hT, gi)])
            stg = P.sbuf("stg", [128, 8, 256], F32)
            stu = P.sbuf("stu", [128, 8, 256], F32)
            wgb = [P.sbuf("wgb%d" % i, [128, 8, 256], BF16) for i in range(2)]
            wub = [P.sbuf("wub%d" % i, [128, 8, 256], BF16) for i in range(2)]
            psg = [P.psum("psg%d" % i, [128, 512]) for i in range(2)]
            psu = [P.psum("psu%d" % i, [128, 512]) for i in range(2)]
            sga = [P.sbuf("sga%d" % i, [128, nmax], F32) for i in range(2)]
            it = 0
            NB = NF // 2

            def fetch_a(fb):
                P.dma(stg[:], wg_v[:, :, fb * 256:(fb + 1) * 256], writes=[stg])
                P.dma(stu[:], wu_v[:, :, fb * 256:(fb + 1) * 256], writes=[stu])

            def cast_a(fb):
                P.act(wgb[fb % 2][:], stg[:], AF.Copy, [stg], [wgb[fb % 2]])
                P.copy(wub[fb % 2][:], stu[:], [stu], [wub[fb % 2]])

            fetch_a(0)
            cast_a(0)
            for fb in range(NB):
                pb = fb % 2
                if fb + 1 < NB:
                    fetch_a(fb + 1)
                cnt = 0
                for f2 in range(2):
                    f = fb * 2 + f2
                    for gi, (t0, n) in enumerate(groups):
                        o0 = t0 - h0
                        pg, pu, sg = psg[it % 2], psu[it % 2], sga[it % 2]
                        it += 1
                        for kc in range(8):
                            P.mm(pg[:, 0:n], wgb[pb][:, kc, f2 * 128:(f2 + 1) * 128], hT[:, kc, o0:o0 + n], kc == 0, kc == 7,
                                 [wgb[pb], (hT, gi)], [pg])
                        for kc in range(8):
                            P.mm(pu[:, 0:n], wub[pb][:, kc, f2 * 128:(f2 + 1) * 128], hT[:, kc, o0:o0 + n], kc == 0, kc == 7,
                                 [wub[pb], (hT, gi)], [pu])
                        P.act(sg[:, 0:n], pg[:, 0:n], AF.Silu, [pg], [sg])
                        P.tt(AT[:, f, o0:o0 + n], sg[:, 0:n], pu[:, 0:n], ALU.mult, [sg, pu], [(AT, f * ng_ + gi)])
                        cnt += 1
                        if cnt == ng_ and fb + 1 < NB:
                            cast_a(fb + 1)
            P.emit()
            P = Prog(nc, sy, tag + "b")
            std = P.sbuf("std", [128, NF, 256], F32)
            wdb = [P.sbuf("wdb%d" % i, [128, NF, 256], BF16) for i in range(2)]
            psy = [P.psum("psy%d" % i, [128, 512]) for i in range(2)]
            it = 0

            def fetch_b(db):
                P.dma(std[:], wd_v[:, :, db * 256:(db + 1) * 256], writes=[std])
                P.act(wdb[db % 2][:], std[:], AF.Copy, [std], [wdb[db % 2]])

            fetch_b(0)
            for db in range(4):
                pb = db % 2
                if db + 1 < 4:
                    fetch_b(db + 1)
                for d2 in range(2):
                    d = db * 2 + d2
                    for gi, (t0, n) in enumerate(groups):
                        o0 = t0 - h0
                        py = psy[it % 2]
                        it += 1
                        for f in range(NF):
                            P.mm(py[:, 0:n], wdb[pb][:, f, d2 * 128:(d2 + 1) * 128], AT[:, f, o0:o0 + n], f == 0, f == NF - 1,
                                 [wdb[pb], (AT, f * ng_ + gi)], [py])
                        for (s0, sn, s) in segs(t0, n):
                            P.stt(XT[:, d, s0:s0 + sn], py[:, s0 - t0:s0 - t0 + sn], GH[:, l, n_idx, d, s:s + 1],
                                  XT[:, d, s0:s0 + sn], ALU.mult, ALU.add, [py] + xk(s0, sn), xk(s0, sn))
            P.emit()
            esA.close()

    def dump_xt(name):
        if name in dbg_d:
            P = Prog(nc, sy, "dbg" + name)
            P.dma(dbg_d[name], XT[:].rearrange("p c t -> p (c t)"), reads=[XT])
            P.emit()

    def make_xn(l):
        P = Prog(nc, sy, "xn%d" % l)
        W = norm_ws(P, 384, 2)
        hb = Rot(lambda i: P.sbuf("hb%d" % i, [128, 8, 384], BF16), 2)
        for (t0, n) in FULL[0] + FULL[1]:
            h = hb.next()
            norm_mod(P, W, l, 1, t0, n, lambda c, s0, sn, h=h, t0=t0: h[:, c, s0 - t0:s0 - t0 + sn], [h])
            P.dma(xn_s[:, :, t0:t0 + n], h[:, :, 0:n], reads=[h], writes=[xn_s])
        P.emit()

    def load_w(P, dst, dst_ap, src_ap, stage, stage_ap):
        P.dma(stage_ap, src_ap, writes=[stage])
        P.copy(dst_ap, stage_ap, [stage], [dst], eng="gpsimd")

    def out_proj(l, wo_d, groups):
        P = Prog(nc, sy, "op%d" % l)
        wo = P.sbuf("wo", [128, 8, D], BF16)
        stw = P.sbuf("stw", [128, 8, 256], F32)
        wv = wo_d.rearrange("(kc p) n -> p kc n", p=128)
        for q in range(4):
            load_w(P, wo, wo[:, :, q * 256:(q + 1) * 256], wv[:, :, q * 256:(q + 1) * 256], stw, stw[:])
        mb = Rot(lambda i: P.sbuf("mb%d" % i, [128, 8, 512], BF16), 2)
        pp = Rot(lambda i: P.psum("pp%d" % i, [128, 512]), 3)
        for (t0, n) in groups:
            m = mb.next()
            P.dma(m[:, :, 0:n], mix_s[:, :, t0:t0 + n], reads=[mix_s], writes=[m])
            for dc in range(8):
                ps = pp.next()
                for kc in range(8):
                    P.mm(ps[:, 0:n], wo[:, kc, dc * 128:(dc + 1) * 128], m[:, kc, 0:n], kc == 0, kc == 7, [wo, m], [ps])
                for (s0, sn, s) in segs(t0, n):
                    P.stt(XT[:, dc, s0:s0 + sn], ps[:, s0 - t0:s0 - t0 + sn], GH[:, l, 1, dc, s:s + 1],
                          XT[:, dc, s0:s0 + sn], ALU.mult, ALU.add, [ps] + xk(s0, sn), xk(s0, sn))
        P.emit()

    def swa_prep():
        P = Prog(nc, sy, "swp")
        wv_ = odw_d.rearrange("(kc p) n -> p kc n", p=128)
        odc = P.sbuf("odc", [128, 11], F32)
        P.dma(odc[:], odcol_d, writes=[odc])
        bds = P.sbuf("bds", [128, 2], F32)
        P.dma(bds[:], bdsel_d, writes=[bds])
        cs = P.sbuf("cs", [128, 2, SEQ], F32)
        P.dma(cs[:], swcs_d, writes=[cs])
        st = P.sbuf("st", [128, 8, 512], F32)
        wq = P.sbuf("wq", [128, 8, 4, 2, 64], BF16)
        wk = P.sbuf("wk", [128, 8, 128], BF16)
        wvv = P.sbuf("wvv", [128, 8, 128], BF16)
        P.dma(st[:], wv_[:, :, 1568:2080], writes=[st])
        st5 = st[:].rearrange("p k (g j d) -> p k g j d", g=2, j=4)
        for g in range(2):
            P.copy(wq[:, :, :, g, :], st5[:, :, g, :, :], [st], [wq], eng="gpsimd")
        load_w(P, wk, wk[:], wv_[:, :, 2080:2208], st, st[:, :, 0:128])
        load_w(P, wvv, wvv[:], wv_[:, :, 2208:2336], st, st[:, :, 0:128])
        xg = Rot(lambda i: P.sbuf("xg%d" % i, [128, 8, 512], BF16), 2)
        pp = Rot(lambda i: P.psum("pp%d" % i, [128, 512]), 3)
        pssR = Rot(lambda i: P.psum("pss%d" % i, [128, 512]), 2)
        prot = P.psum("prot", [128, 512])
        prk = P.psum("prk", [128, NT, 2])
        pv = P.psum("pv", [128, 128])
        sqR = Rot(lambda i: P.sbuf("sq%d" % i, [128, 512], F32), 2)
        lnvR = Rot(lambda i: P.sbuf("lnv%d" % i, [128, 512], F32), 2)
        rsR = Rot(lambda i: P.sbuf("rs%d" % i, [128, 512], F32), 2)
        qgR = Rot(lambda i: P.sbuf("qg%d" % i, [128, 512], F32), 2)
        t1R = Rot(lambda i: P.sbuf("t1%d" % i, [128, 512], F32), 2)
        t2R = Rot(lambda i: P.sbuf("t2%d" % i, [128, 512], F32), 2)
        qo = Rot(lambda i: P.sbuf("qo%d" % i, [128, 512], BF16), 2)
        vd = P.sbuf("vd", [128, NT, 2, 2, 64], BF16)
        rsk = P.sbuf("rsk", [128, NT * 2], F32)
        for (t0, n) in GROUPS5:
            x = xg.next()
            P.dma(x[:, :, 0:n], xn_s[:, :, t0:t0 + n], reads=[xn_s], writes=[x])
            lat = t0 >= CTX
            l0 = t0 - CTX
            if lat:
                for j in range(4):
                    ps = pp.next()
                    for kc in range(8):
                        P.mm(ps[:, 0:n], wq[:, kc, j].rearrange("p g d -> p (g d)"), x[:, kc, 0:n], kc == 0, kc == 7,
                             [wq, x], [ps])
                    sq, pss, lnv, rs, qg, t1, t2 = (sqR.next(), pssR.next(), lnvR.next(), rsR.next(), qgR.next(),
                                                    t1R.next(), t2R.next())
                    P.act(sq[:, 0:n], ps[:, 0:n], AF.Square, [ps], [sq])
                    P.mm(pss[:, 0:n], cst[:, C_BD64, :], sq[:, 0:n], True, True, [cst, sq], [pss])
                    P.act(lnv[:, 0:n], pss[:, 0:n], AF.Ln, [pss], [lnv], bias=EPS, scale=1.0 / 64)
                    P.act(rs[:, 0:n], lnv[:, 0:n], AF.Exp, [lnv], [rs], scale=-0.5)
                    P.act(qg[:, 0:n], ps[:, 0:n], AF.Identity, [ps], [qg], scale=odc[:, 0:1])
                    P.mm(prot[:, 0:n], cst[:, C_ROT64, :], qg[:, 0:n], True, True, [cst, qg], [prot])
                    P.tt(t1[:, 0:n], qg[:, 0:n], cs[:, 0, l0:l0 + n], ALU.mult, [qg, cs], [t1])
                    P.tt(t2[:, 0:n], prot[:, 0:n], cs[:, 1, l0:l0 + n], ALU.mult, [prot, cs], [t2])
                    P.tt(t1[:, 0:n], t1[:, 0:n], t2[:, 0:n], ALU.add, [t1, t2], [t1])
                    q = qo.next()
                    P.tt(q[:, 0:n], t1[:, 0:n], rs[:, 0:n], ALU.mult, [t1, rs], [q])
                    P.dma(swq_s[:, j, l0:l0 + n], q[:, 0:n], reads=[q], writes=[swq_s])
            ps = pp.next()
            for kc in range(8):
                P.mm(ps[:, 0:n], wk[:, kc, :], x[:, kc, 0:n], kc == 0, kc == 7, [wk, x], [ps])
            sq, qg, t1, t2 = sqR.next(), qgR.next(), t1R.next(), t2R.next()
            P.act(sq[:, 0:n], ps[:, 0:n], AF.Square, [ps], [sq])
            for ti in range(n // 128):
                tt = t0 // 128 + ti
                P.mm(prk[:, tt, :], sq[:, ti * 128:(ti + 1) * 128], bds[:], True, True, [sq, bds], [prk])
            q = qo.next()
            if lat:
                P.act(qg[:, 0:n], ps[:, 0:n], AF.Identity, [ps], [qg], scale=odc[:, 1:2])
                P.mm(prot[:, 0:n], cst[:, C_ROT64, :], qg[:, 0:n], True, True, [cst, qg], [prot])
                P.tt(t1[:, 0:n], qg[:, 0:n], cs[:, 0, l0:l0 + n], ALU.mult, [qg, cs], [t1])
                P.tt(t2[:, 0:n], prot[:, 0:n], cs[:, 1, l0:l0 + n], ALU.mult, [prot, cs], [t2])
                P.tt(q[:, 0:n], t1[:, 0:n], t2[:, 0:n], ALU.add, [t1, t2], [q])
            else:
                P.act(q[:, 0:n], ps[:, 0:n], AF.Identity, [ps], [q], scale=odc[:, 1:2])
            P.dma(swk_s[:, t0:t0 + n], q[:, 0:n], reads=[q], writes=[swk_s])
            for ti in range(n // 128):
                tt = t0 // 128 + ti
                for kc in range(8):
                    P.mm(pv[:], x[:, kc, ti * 128:(ti + 1) * 128], wvv[:, kc, :], kc == 0, kc == 7, [x, wvv], [pv])
                pv3 = pv[:].rearrange("p (g d) -> p g d", g=2)
                P.copy(vd[:, tt, :, 0, :], pv3, [pv], [vd])
                P.act(vd[:, tt, :, 1, :], pv3, AF.Copy, [pv], [vd])
        P.act(rsk[:], prk[:].rearrange("p a b -> p (a b)"), AF.Ln, [prk], [rsk], bias=EPS, scale=1.0 / 64)
        P.act(rsk[:], rsk[:], AF.Exp, [rsk], [rsk], scale=-0.5)
        P.ts(rsk[:], rsk[:], 0.125, ALU.mult, [rsk], [rsk])
        P.dma(swr_s[:], rsk[:], reads=[rsk], writes=[swr_s])
        P.dma(swv_s[:], vd[:].rearrange("p a b c d -> p (a b c d)"), reads=[vd], writes=[swv_s])
        P.emit()

    def swa_core():
        P = Prog(nc, sy, "swc")
        q = P.sbuf("q", [128, 4, SEQ], BF16)
        k = P.sbuf("k", [128, T], BF16)
        v = P.sbuf("v", [128, NT, 2, 128], BF16)
        rsk = P.sbuf("rsk", [128, NT, 2], F32)
        odc = P.sbuf("odc", [128, 11], F32)
        esk = P.sbuf("esk", [128, 8], F32)
        mlo = P.sbuf("mlo", [128, 128], BF16)
        mhi = P.sbuf("mhi", [128, 128], BF16)
        P.dma(q[:], swq_s[:], reads=[swq_s], writes=[q])
        P.dma(k[:], swk_s[:], reads=[swk_s], writes=[k])
        P.dma(v[:].rearrange("p a b c -> p (a b c)"), swv_s[:], reads=[swv_s], writes=[v])
        P.dma(rsk[:].rearrange("p a b -> p (a b)"), swr_s[:], reads=[swr_s], writes=[rsk])
        P.dma(odc[:], odcol_d, writes=[odc])
        P.act(esk[:], odc[:, 3:11], AF.Exp, [odc], [esk])
        P.copy(mlo[:], cst[:, C_LI, :], [cst], [mlo])
        P.copy(mhi[:], cst[:, C_UI, :], [cst], [mhi])
        msw = P.sbuf("msw", [128, 4, SEQ], BF16)
        pS = Rot(lambda i: P.psum("pS%d" % i, [128, 4, 128]), 4)
        pO = Rot(lambda i: P.psum("pO%d" % i, [128, 4, 128]), 2)
        pM = Rot(lambda i: P.psum("pM%d" % i, [128, 4, 128]), 2)
        pT = Rot(lambda i: P.sbuf("pT%d" % i, [128, 4, 128], BF16), 4)
        den = P.sbuf("den", [128, 4, 128], F32)
        rden = P.sbuf("rden", [128, 4, 128], F32)
        for n in range(16):
            for g in range(2):
                kts = [(0, None), (1, None)]
                if n - 1 >= 0:
                    kts.append((n + 1, mlo))
                kts.append((n + 2, None))
                if n + 1 <= 15:
                    kts.append((n + 3, mhi))
                po = pO.next()
                pm = pM.next()
                def scores(kt, mask):
                    ps = pS.next()
                    P.mm(ps[:], k[g * 64:(g + 1) * 64, kt * 128:(kt + 1) * 128],
                         q[g * 64:(g + 1) * 64, :, n * 128:(n + 1) * 128], True, True, [k, q], [ps])
                    pt = pT.next()
                    P.act(pt[:], ps[:], AF.Exp, [ps, rsk], [pt], scale=rsk[:, kt, g:g + 1])
                    if mask is not None:
                        P.tt(pt[:], pt[:], mask[:].unsqueeze(1).broadcast_to([128, 4, 128]), ALU.mult, [pt, mask], [pt])
                    return pt

                nxt = scores(*kts[0])
                for ki, (kt, mask) in enumerate(kts):
                    pt = nxt
                    if ki + 1 < len(kts):
                        nxt = scores(*kts[ki + 1])
                    P.mm(po[:], v[:, kt, g, :], pt[:], ki == 0, ki == len(kts) - 1, [v, pt], [po])
                    P.mm(pm[:], ones_b[:], pt[:], ki == 0, ki == len(kts) - 1, [ones_b, pt], [pm])
                P.tt(den[:], pm[:], esk[:, 4 * g:4 * g + 4].unsqueeze(2).broadcast_to([128, 4, 128]), ALU.add,
                     [pm, esk], [den])
                P.act(rden[:], den[:], AF.Ln, [den], [rden])
                P.act(rden[:], rden[:], AF.Exp, [rden], [rden], scale=-1.0)
                for par in range(2):
                    sl = slice(par * 64, (par + 1) * 64)
                    P.tt(msw[sl, 2 * g:2 * g + 2, n * 128:(n + 1) * 128], po[sl, par::2, :], rden[sl, par::2, :],
                         ALU.mult, [po, rden], [msw])
        P.dma(mix_s[:, 4:8, CTX:T], msw[:], reads=[msw], writes=[mix_s])
        P.emit()

    def gla_prep():
        P = Prog(nc, sy, "glp")
        wv_ = odw_d.rearrange("(kc p) n -> p kc n", p=128)
        stR = Rot(lambda i: P.sbuf("st%d" % i, [128, 8, 256], F32), 1)
        wqk = P.sbuf("wqk", [128, 8, 512], BF16)
        wvv = P.sbuf("wvv", [128, 8, 512], BF16)
        wgz = P.sbuf("wgz", [128, 8, 32], BF16)
        wr = P.sbuf("wr", [128, 8, 512], BF16)
        for (wt_, c0_) in ((wqk, 0), (wvv, 512), (wr, 1056)):
            for hf in range(2):
                st = stR.next()
                load_w(P, wt_, wt_[:, :, hf * 256:(hf + 1) * 256], wv_[:, :, c0_ + hf * 256:c0_ + (hf + 1) * 256], st, st[:])
        st = stR.next()
        load_w(P, wgz, wgz[:], wv_[:, :, 1024:1056], st, st[:, :, 0:32])
        g2f = P.sbuf("g2f", [17, 2, 256], F32)
        g2b = P.sbuf("g2b", [17, 2, 256], BF16)
        P.dma(g2f[:], gw2b_d, writes=[g2f])
        P.copy(g2b[:], g2f[:], [g2f], [g2b])
        msk = P.sbuf("msk", [128, 512], F32)
        P.memset(msk[:], 1.0, [msk])
        for i in range(4):
            P.memset(msk[:, i * 128:i * 128 + 1], 0.0, [msk])
        gz = [P.sbuf("gz%d" % d, [17, 512], BF16) for d in range(2)]
        for d in range(2):
            P.memset(gz[d][:], 1.0, [gz[d]])
        xg = Rot(lambda i: P.sbuf("xg%d" % i, [128, 8, 512], BF16), 2)
        pp = Rot(lambda i: P.psum("pp%d" % i, [128, 512]), 4)
        ptr_ = P.psum("ptr", [128, 2, 128], BF16)
        pgz = P.psum("pgz", [16, 512])
        qkraw = P.sbuf("qkraw", [128, 4, 512], F32)
        spD = [P.sbuf("sp%d" % i, [128, 2, 512], F32) for i in range(2)]
        preD = [P.sbuf("pre%d" % i, [128, 2, 512], F32) for i in range(2)]
        bcD = [P.sbuf("bc%d" % i, [128, 2, 512], F32) for i in range(2)]
        remD = [P.sbuf("rem%d" % i, [128, 2, 512], F32) for i in range(2)]
        eeR = Rot(lambda i: P.sbuf("ee%d" % i, [128, 2, 512], F32), 2)
        etD = [P.sbuf("et%d" % i, [128, 2, 4], F32) for i in range(2)]
        ob = Rot(lambda i: P.sbuf("ob%d" % i, [128, 2, 512], BF16), 2)
        kdt = Rot(lambda i: P.sbuf("kdt%d" % i, [128, 4, 256], BF16), 2)
        vtm = Rot(lambda i: P.sbuf("vtm%d" % i, [128, 4, 512], BF16), 2)
        rg = Rot(lambda i: P.sbuf("rg%d" % i, [128, 4, 512], BF16), 2)
        idb = P.sbuf("idb", [128, 128], BF16)
        P.copy(idb[:], cst[:, C_ID, :], [cst], [idb])
        for (t0, n) in GROUPS5:
            nt = n // 128
            x = xg.next()
            P.dma(x[:, :, 0:n], xn_s[:, :, t0:t0 + n], reads=[xn_s], writes=[x])
            for cq in range(4):
                ps = pp.next()
                for kc in range(8):
                    P.mm(ps[:, 0:n], wqk[:, kc, cq * 128:(cq + 1) * 128], x[:, kc, 0:n], kc == 0, kc == 7, [wqk, x], [ps])
                P.act(qkraw[:, cq, 0:n], ps[:, 0:n], AF.Copy, [ps], [qkraw])
            vt = vtm.next()
            for ti in range(nt):
                ps = pp.next()
                for kc in range(8):
                    P.mm(ps[:], x[:, kc, ti * 128:(ti + 1) * 128], wvv[:, kc, :], kc == 0, kc == 7, [x, wvv], [ps])
                P.act(vt[:, ti, :], ps[:], AF.Copy, [ps], [vt])
            P.dma(glv_s[t0:t0 + n, :].rearrange("(a p) f -> p a f", p=128), vt[:, 0:nt, :], reads=[vt], writes=[glv_s])
            if t0 >= CTX:
                r = rg.next()
                for h in range(4):
                    ps = pp.next()
                    for kc in range(8):
                        P.mm(ps[:, 0:n], wr[:, kc, h * 128:(h + 1) * 128], x[:, kc, 0:n], kc == 0, kc == 7, [wr, x], [ps])
                    P.act(r[:, h, 0:n], ps[:, 0:n], AF.Silu, [ps], [r])
                P.dma(glr_s[:, :, t0 - CTX:t0 - CTX + n], r[:, :, 0:n], reads=[r], writes=[glr_s])
            for d in range(2):
                sp, pre, bc, rem, et = spD[d], preD[d], bcD[d], remD[d], etD[d]
                for kc in range(8):
                    P.mm(pgz[:, 0:n], wgz[:, kc, 16 * d:16 * d + 16], x[:, kc, 0:n], kc == 0, kc == 7, [wgz, x], [pgz])
                P.act(gz[d][0:16, 0:n], pgz[:, 0:n], AF.Copy, [pgz], [gz[d]])
                for c in range(2):
                    ps = pp.next()
                    P.mm(ps[:, 0:n], g2b[:, d, c * 128:(c + 1) * 128], gz[d][:, 0:n], True, True, [g2b, gz[d]], [ps])
                    P.act(sp[:, c, 0:n], ps[:, 0:n], AF.Exp, [ps], [sp], scale=-1.0)
                    P.act(sp[:, c, 0:n], sp[:, c, 0:n], AF.Ln, [sp], [sp], bias=1.0)
                    P.op("vector", lambda e, c=c, n=n, pre=pre, sp=sp: e.tensor_tensor_scan(
                        out=pre[:, c, 0:n], data0=msk[:, 0:n], data1=sp[:, c, 0:n], initial=0.0,
                        op0=ALU.mult, op1=ALU.add), [msk, sp], [pre])
                pre4 = pre[:, :, 0:n].rearrange("p c (a b) -> p c a b", b=128)
                totb = pre4[:, :, :, 127:128].broadcast_to([128, 2, nt, 128])
                bc4 = bc[:, :, 0:n].rearrange("p c (a b) -> p c a b", b=128)
                rem4 = rem[:, :, 0:n].rearrange("p c (a b) -> p c a b", b=128)
                if d == 0:
                    P.copy(bc[:, :, 0:n], pre[:, :, 0:n], [pre], [bc], eng="gpsimd")
                    P.tt(rem4, totb, pre4, ALU.subtract, [pre], [rem])
                else:
                    P.tt(rem[:, :, 0:n], pre[:, :, 0:n], sp[:, :, 0:n], ALU.subtract, [pre, sp], [rem])
                    P.tt(bc4, totb, rem4, ALU.subtract, [pre, rem], [bc])
                P.act(et[:, :, 0:nt], pre4[:, :, :, 127], AF.Exp, [pre], [et], scale=-1.0 / 16)
                P.dma(glet_s[d][:, :, t0 // 128:t0 // 128 + nt], et[:, :, 0:nt], reads=[et], writes=[glet_s[d]])
                ee = eeR.next()
                P.act(ee[:, :, 0:n], bc[:, :, 0:n], AF.Exp, [bc], [ee], scale=-1.0 / 16)
                o = ob.next()
                P.stt(o[:, :, 0:n], qkraw[:, 0:2, 0:n], 0.125, ee[:, :, 0:n], ALU.mult, ALU.mult, [qkraw, ee], [o])
                P.dma(glq_s[d][:, :, t0:t0 + n], o[:, :, 0:n], reads=[o], writes=[glq_s[d]])
                ee = eeR.next()
                P.act(ee[:, :, 0:n], bc[:, :, 0:n], AF.Exp, [bc], [ee], scale=1.0 / 16)
                o = ob.next()
                P.tt(o[:, :, 0:n], qkraw[:, 2:4, 0:n], ee[:, :, 0:n], ALU.mult, [qkraw, ee], [o])
                P.dma(glk_s[d][:, :, t0:t0 + n], o[:, :, 0:n], reads=[o], writes=[glk_s[d]])
                ee = eeR.next()
                P.act(ee[:, :, 0:n], rem[:, :, 0:n], AF.Exp, [rem], [ee], scale=-1.0 / 16)
                o = ob.next()
                P.tt(o[:, :, 0:n], qkraw[:, 2:4, 0:n], ee[:, :, 0:n], ALU.mult, [qkraw, ee], [o])
                kd = kdt.next()
                for ti in range(nt):
                    for c in range(2):
                        P.tr(ptr_[:, c, :], o[:, c, ti * 128:(ti + 1) * 128], idb[:], [o, idb], [ptr_])
                    P.copy(kd[:, ti, :], ptr_[:].rearrange("p a b -> p (a b)"), [ptr_], [kd])
                P.dma(glkd_s[d][t0:t0 + n, :].rearrange("(a p) f -> p a f", p=128), kd[:, 0:nt, :], reads=[kd],
                      writes=[glkd_s[d]])
        P.emit()

    def gla_core():
        P = Prog(nc, sy, "glc")
        odc = P.sbuf("odc", [128, 11], F32)
        P.dma(odc[:], odcol_d, writes=[odc])
        ogla = P.sbuf("ogla", [128, 4, SEQ], F32, 16)
        for i in range(4):
            P.memset(ogla[:, i, :], 0.0, [ogla], eng="gpsimd" if i % 2 else "vector")
        S = [[P.sbuf("S%d%d" % (d, c), [128, 128], F32) for c in range(2)] for d in range(2)]
        Sb = [[P.sbuf("Sb%d%d" % (d, h), [128, 128], BF16) for h in range(4)] for d in range(2)]
        qd = Rot(lambda i: P.sbuf("qd%d" % i, [128, 2, 128], BF16), 4)
        kp = Rot(lambda i: P.sbuf("kp%d" % i, [128, 2, 128], BF16), 4)
        kd = Rot(lambda i: P.sbuf("kd%d" % i, [128, 256], BF16), 4)
        vt = Rot(lambda i: P.sbuf("vt%d" % i, [128, 512], BF16), 4)
        et = [P.sbuf("et%d" % d, [128, 2, NT], F32) for d in range(2)]
        aT = Rot(lambda i: P.sbuf("aT%d" % i, [128, 4, 128], BF16), 2)
        pA0 = Rot(lambda i: P.psum("pA0%d" % i, [128, 2, 128]), 1)
        pA1 = Rot(lambda i: P.psum("pA1%d" % i, [128, 2, 128]), 1)
        pO = Rot(lambda i: P.psum("pO%d" % i, [128, 4, 128]), 2)
        pSt = Rot(lambda i: P.psum("pSt%d" % i, [128, 128]), 3)
        orders = [list(range(NT)), [1, 0] + list(range(NT - 1, 1, -1))]
        masks = [cst[:, C_UI, :], cst[:, C_LI, :]]
        for d in range(2):
            P.dma(et[d][:], glet_s[d][:], reads=[glet_s[d]], writes=[et[d]])
            for c in range(2):
                P.memset(S[d][c][:], 0.0, [S[d][c]])
            for h in range(4):
                P.memset(Sb[d][h][:], 0.0, [Sb[d][h]])
        for s_ in range(NT):
            for d in range(2):
                tt = orders[d][s_]
                mask = masks[d]
                q_, k_, kd_, v_ = qd.next(), kp.next(), kd.next(), vt.next()
                tsl = slice(tt * 128, (tt + 1) * 128)
                P.dma(q_[:], glq_s[d][:, :, tsl], reads=[glq_s[d]], writes=[q_])
                P.dma(k_[:], glk_s[d][:, :, tsl], reads=[glk_s[d]], writes=[k_])
                P.dma(kd_[:], glkd_s[d][tsl, :], reads=[glkd_s[d]], writes=[kd_])
                P.dma(v_[:], glv_s[tsl, :], reads=[glv_s], writes=[v_])
                if tt >= 2:
                    a_ = aT.next()
                    for par, pa_ in enumerate((pA0, pA1)):
                        pa = pa_.next()
                        hs = slice(par * 64, par * 64 + 64)
                        for c in range(2):
                            P.mm(pa[:, c, :], k_[hs, c, :], q_[hs, c, :], True, True, [k_, q_], [pa])
                        P.tt(a_[:, par::2, :], pa[:], mask.unsqueeze(1).broadcast_to([128, 2, 128]), ALU.mult,
                             [pa, cst], [a_])
                    po = pO.next()
                    for h in range(4):
                        P.mm(po[:, h, :], v_[:, h * 128:(h + 1) * 128], a_[:, h, :], True, False, [v_, a_], [po])
                        P.mm(po[:, h, :], Sb[d][h][:], q_[:, h // 2, :], False, True, [Sb[d][h], q_], [po])
                    lt = tt - 2
                    osl = ogla[:, :, lt * 128:(lt + 1) * 128]
                    P.tt(osl, osl, po[:], ALU.add, [po, (ogla, lt)], [(ogla, lt)])
                for c in range(2):
                    ps = pSt.next()
                    for hh in range(2):
                        h = 2 * c + hh
                        P.mm(ps[hh * 64:(hh + 1) * 64, :], kd_[:, h * 64:(h + 1) * 64], v_[:, h * 128:(h + 1) * 128],
                             True, True, [kd_, v_], [ps])
                    P.stt(S[d][c][:], S[d][c][:], et[d][:, c, tt:tt + 1], ps[:], ALU.mult, ALU.add,
                          [S[d][c], et[d], ps], [S[d][c]])
                    for hh in range(2):
                        hs = slice(hh * 64, hh * 64 + 64)
                        P.act(Sb[d][2 * c + hh][hs, :], S[d][c][hs, :], AF.Copy, [S[d][c]], [Sb[d][2 * c + hh]])
        sqR = Rot(lambda i: P.sbuf("sq%d" % i, [128, 512], F32), 2)
        lnvR = Rot(lambda i: P.sbuf("lnv%d" % i, [128, 512], F32), 2)
        rsR = Rot(lambda i: P.sbuf("rs%d" % i, [128, 512], F32), 2)
        tmpR = Rot(lambda i: P.sbuf("tmp%d" % i, [128, 512], F32), 2)
        pss = P.psum("pss", [128, 512])
        rg = Rot(lambda i: P.sbuf("rg%d" % i, [128, 4, 512], BF16), 2)
        mo = Rot(lambda i: P.sbuf("mo%d" % i, [128, 4, 512], BF16), 2)
        for gi in range(4):
            r = rg.next()
            m = mo.next()
            gsl = slice(gi * 512, (gi + 1) * 512)
            keys = [(ogla, i) for i in range(gi * 4, gi * 4 + 4)]
            P.dma(r[:], glr_s[:, :, gsl], reads=[glr_s], writes=[r])
            for h in range(4):
                sq, lnv, rs, tmp = sqR.next(), lnvR.next(), rsR.next(), tmpR.next()
                P.act(sq[:], ogla[:, h, gsl], AF.Square, keys, [sq])
                P.mm(pss[:], ones_f[:], sq[:], True, True, [ones_f, sq], [pss])
                P.act(lnv[:], pss[:], AF.Ln, [pss], [lnv], bias=EPS, scale=1.0 / 128)
                P.act(rs[:], lnv[:], AF.Exp, [lnv], [rs], scale=-0.5)
                P.tt(tmp[:], ogla[:, h, gsl], rs[:], ALU.mult, keys + [rs], [tmp])
                P.stt(m[:, h, :], tmp[:], odc[:, 2:3], r[:, h, :], ALU.mult, ALU.mult, [tmp, odc, r], [m])
            P.dma(mix_s[:, 0:4, CTX + gi * 512:CTX + (gi + 1) * 512], m[:], reads=[m], writes=[mix_s])
        P.emit()

    def mla_prep():
        P = Prog(nc, sy, "mlp")
        wv_ = evw_d.rearrange("(kc p) n -> p kc n", p=128)
        evc = P.sbuf("evc", [128, 8], F32)
        P.dma(evc[:], evcol_d, writes=[evc])
        cs = P.sbuf("cs", [96, 2, SEQ], F32)
        P.dma(cs[:], mlcs_d, writes=[cs])
        st = P.sbuf("st", [128, 8, 384], F32)
        wcq = P.sbuf("wcq", [128, 8, 384], BF16)
        wckv = P.sbuf("wckv", [128, 8, 256], BF16)
        wkr = P.sbuf("wkr", [128, 8, 32], BF16)
        wqu = P.sbuf("wqu", [128, 3, 768], BF16)
        wkK = P.sbuf("wkK", [128, 2, 8, 64], BF16)
        wkV = P.sbuf("wkV", [128, 2, 8, 64], BF16)
        load_w(P, wcq, wcq[:], wv_[:, :, 0:384], st, st[:])
        load_w(P, wckv, wckv[:], wv_[:, :, 384:640], st, st[:, :, 0:256])
        load_w(P, wkr, wkr[:], wv_[:, :, 640:672], st, st[:, :, 0:32])
        stf = st[:].rearrange("p a b -> p (a b)")
        qu_v = evqu_d.rearrange("(kc p) n -> p kc n", p=128)
        st_qu = stf[:, 0:3 * 768].rearrange("p (a b) -> p a b", a=3)
        load_w(P, wqu, wqu[:], qu_v, st, st_qu)
        kv_v = evkv_d.rearrange("(kc p) n -> p kc n", p=128)
        st_kv = stf[:, 0:2048].rearrange("p (a b) -> p a b", a=2)
        P.dma(st_kv, kv_v, writes=[st])
        st_kv5 = stf[:, 0:2048].rearrange("p (a h t d) -> p a h t d", a=2, h=8, t=2)
        P.copy(wkK[:], st_kv5[:, :, :, 0, :], [st], [wkK], eng="gpsimd")
        P.copy(wkV[:], st_kv5[:, :, :, 1, :], [st], [wkV], eng="gpsimd")
        xg = Rot(lambda i: P.sbuf("xg%d" % i, [128, 8, 512], BF16), 2)
        pp = Rot(lambda i: P.psum("pp%d" % i, [128, 512]), 3)
        pssR = Rot(lambda i: P.psum("pss%d" % i, [128, 512]), 2)
        protR = Rot(lambda i: P.psum("prot%d" % i, [128, 512]), 1)
        prk = P.psum("prk", [128, NT, 8])
        prkr = P.psum("prkr", [128, NT])
        raw = P.sbuf("raw", [128, 3, 512], F32)
        sq3 = P.sbuf("sq3", [128, 3, 512], F32)
        lnvR = Rot(lambda i: P.sbuf("lnv%d" % i, [128, 512], F32), 2)
        rsR = Rot(lambda i: P.sbuf("rs%d" % i, [128, 512], F32), 2)
        cqn = P.sbuf("cqn", [128, 3, 512], BF16)
        ckvn = P.sbuf("ckvn", [128, 2, 512], BF16)
        sqR = Rot(lambda i: P.sbuf("sq%d" % i, [128, 512], F32), 2)
        qgR = Rot(lambda i: P.sbuf("qg%d" % i, [128, 512], F32), 2)
        t1R = Rot(lambda i: P.sbuf("t1%d" % i, [128, 512], F32), 2)
        t2R = Rot(lambda i: P.sbuf("t2%d" % i, [128, 512], F32), 2)
        qo = Rot(lambda i: P.sbuf("qo%d" % i, [128, 512], BF16), 3)
        vtm = Rot(lambda i: P.sbuf("vtm%d" % i, [128, 4, 512], BF16), 2)
        rsk = P.sbuf("rsk", [128, NT * 8], F32)
        R96 = cst[0:96, C_ROT96, 0:96]

        def rope_out(q, src, rs, n, lat, l0, with_rs):
            if lat:
                prot, t1, t2 = protR.next(), t1R.next(), t2R.next()
                P.mm(prot[0:96, 0:n], R96, src[0:96, 0:n], True, True, [cst, src], [prot])
                P.tt(t1[0:96, 0:n], src[0:96, 0:n], cs[:, 0, l0:l0 + n], ALU.mult, [src, cs], [t1])
                P.tt(t2[0:96, 0:n], prot[0:96, 0:n], cs[:, 1, l0:l0 + n], ALU.mult, [prot, cs], [t2])
                if with_rs:
                    P.tt(t1[0:96, 0:n], t1[0:96, 0:n], t2[0:96, 0:n], ALU.add, [t1, t2], [t1])
                    P.tt(q[0:96, 0:n], t1[0:96, 0:n], rs[0:96, 0:n], ALU.mult, [t1, rs], [q])
                else:
                    P.tt(q[0:96, 0:n], t1[0:96, 0:n], t2[0:96, 0:n], ALU.add, [t1, t2], [q])
            else:
                if with_rs:
                    P.tt(q[0:96, 0:n], src[0:96, 0:n], rs[0:96, 0:n], ALU.mult, [src, rs], [q])
                else:
                    P.copy(q[0:96, 0:n], src[0:96, 0:n], [src], [q])

        for (t0, n) in GROUPS5:
            nt = n // 128
            lat = t0 >= CTX
            l0 = t0 - CTX
            x = xg.next()
            P.dma(x[:, :, 0:n], xn_s[:, :, t0:t0 + n], reads=[xn_s], writes=[x])
            for (wt, nch, off, dstn, dim) in ((wcq, 3, 0, cqn, 384), (wckv, 2, 3, ckvn, 256)):
                for c in range(nch):
                    ps = pp.next()
                    for kc in range(8):
                        P.mm(ps[:, 0:n], wt[:, kc, c * 128:(c + 1) * 128], x[:, kc, 0:n], kc == 0, kc == 7, [wt, x], [ps])
                    P.act(raw[:, c, 0:n], ps[:, 0:n], AF.Copy, [ps], [raw])
                    P.act(sq3[:, c, 0:n], ps[:, 0:n], AF.Square, [ps], [sq3])
                pss, lnv, rs = pssR.next(), lnvR.next(), rsR.next()
                for c in range(nch):
                    P.mm(pss[:, 0:n], ones_f[:], sq3[:, c, 0:n], c == 0, c == nch - 1, [ones_f, sq3], [pss])
                P.act(lnv[:, 0:n], pss[:, 0:n], AF.Ln, [pss], [lnv], bias=EPS, scale=1.0 / dim)
                P.act(rs[:, 0:n], lnv[:, 0:n], AF.Exp, [lnv], [rs], scale=-0.5)
                for c in range(nch):
                    P.stt(dstn[:, c, 0:n], raw[:, c, 0:n], evc[:, off + c:off + c + 1], rs[:, 0:n], ALU.mult, ALU.mult,
                          [raw, evc, rs], [dstn])
            vt = vtm.next()
            for ti in range(nt):
                ps = pp.next()
                for c in range(2):
                    P.mm(ps[:], ckvn[:, c, ti * 128:(ti + 1) * 128], wkV[:, c].rearrange("p h d -> p (h d)"),
                         c == 0, c == 1, [ckvn, wkV], [ps])
                P.act(vt[:, ti, :], ps[:], AF.Copy, [ps], [vt])
            P.dma(mv_s[t0:t0 + n, :].rearrange("(a p) f -> p a f", p=128), vt[:, 0:nt, :], reads=[vt], writes=[mv_s])
            R = slice(64, 96)
            psr = pp.next()
            for kc in range(8):
                P.mm(psr[R, 0:n], wkr[:, kc, :], x[:, kc, 0:n], kc == 0, kc == 7, [wkr, x], [psr])
            sqr, krg = sqR.next(), qgR.next()
            P.act(sqr[R, 0:n], psr[R, 0:n], AF.Square, [psr], [sqr])
            for ti in range(nt):
                tt = t0 // 128 + ti
                P.mm(prkr[:, tt:tt + 1], sqr[R, ti * 128:(ti + 1) * 128], ones_f[R, 0:1], True, True, [sqr, ones_f], [prkr])
            P.act(krg[R, 0:n], psr[R, 0:n], AF.Identity, [psr], [krg], scale=evc[R, 6:7])
            krf = qo.next()
            if lat:
                prot, t1, t2 = protR.next(), t1R.next(), t2R.next()
                P.mm(prot[R, 0:n], cst[R, C_ROT96, 64:96], krg[R, 0:n], True, True, [cst, krg], [prot])
                P.tt(t1[R, 0:n], krg[R, 0:n], cs[R, 0, l0:l0 + n], ALU.mult, [krg, cs], [t1])
                P.tt(t2[R, 0:n], prot[R, 0:n], cs[R, 1, l0:l0 + n], ALU.mult, [prot, cs], [t2])
                P.tt(krf[R, 0:n], t1[R, 0:n], t2[R, 0:n], ALU.add, [t1, t2], [krf])
            else:
                P.copy(krf[R, 0:n], krg[R, 0:n], [krg], [krf])
            for h in range(8):
                P.dma(mkf_s[R, h, t0:t0 + n], krf[R, 0:n], reads=[krf], writes=[mkf_s])
            for h in range(8):
                ps = pp.next()
                for c in range(3):
                    P.mm(ps[0:96, 0:n], wqu[:, c, h * 96:(h + 1) * 96], cqn[:, c, 0:n], c == 0, c == 2, [wqu, cqn], [ps])
                sq, pss, lnv, rs, qg = sqR.next(), pssR.next(), lnvR.next(), rsR.next(), qgR.next()
                P.act(sq[0:96, 0:n], ps[0:96, 0:n], AF.Square, [ps], [sq])
                P.mm(pss[0:96, 0:n], ones_f[0:96, 0:96], sq[0:96, 0:n], True, True, [ones_f, sq], [pss])
                P.act(lnv[0:96, 0:n], pss[0:96, 0:n], AF.Ln, [pss], [lnv], bias=EPS, scale=1.0 / 96)
                P.act(rs[0:96, 0:n], lnv[0:96, 0:n], AF.Exp, [lnv], [rs], scale=-0.5)
                P.act(qg[0:96, 0:n], ps[0:96, 0:n], AF.Identity, [ps], [qg], scale=evc[0:96, 5:6])
                q = qo.next()
                rope_out(q, qg, rs, n, lat, l0, True)
                P.dma(mqf_s[:, h, t0:t0 + n], q[0:96, 0:n], reads=[q], writes=[mqf_s])
                ps = pp.next()
                for c in range(2):
                    P.mm(ps[0:64, 0:n], wkK[:, c, h, :], ckvn[:, c, 0:n], c == 0, c == 1, [wkK, ckvn], [ps])
                sq = sqR.next()
                P.act(sq[0:64, 0:n], ps[0:64, 0:n], AF.Square, [ps], [sq])
                for ti in range(nt):
                    tt = t0 // 128 + ti
                    P.mm(prk[:, tt, h:h + 1], sq[0:64, ti * 128:(ti + 1) * 128], ones_f[0:64, 0:1], True, True,
                         [sq, ones_f], [prk])
                q = qo.next()
                P.act(q[0:64, 0:n], ps[0:64, 0:n], AF.Identity, [ps], [q], scale=evc[0:64, 6:7])
                P.dma(mkf_s[0:64, h, t0:t0 + n], q[0:64, 0:n], reads=[q], writes=[mkf_s])
        rk3 = P.sbuf("rk3", [128, NT, 8], F32)
        rkr = P.sbuf("rkr", [128, NT], F32)
        P.act(rkr[:], prkr[:], AF.Copy, [prkr], [rkr])
        P.tt(rk3[:], prk[:], rkr[:].unsqueeze(2).broadcast_to([128, NT, 8]), ALU.add, [prk, rkr], [rk3])
        P.act(rsk[:], rk3[:].rearrange("p a b -> p (a b)"), AF.Ln, [rk3], [rsk], bias=EPS, scale=1.0 / 96)
        P.act(rsk[:], rsk[:], AF.Exp, [rsk], [rsk], scale=-0.5)
        P.ts(rsk[:], rsk[:], float(96 ** -0.5), ALU.mult, [rsk], [rsk])
        P.dma(mrk_s[:], rsk[:], reads=[rsk], writes=[mrk_s])
        P.emit()

    def mla_core():
        P = Prog(nc, sy, "mlc")
        rsk = P.sbuf("rsk", [128, NT, 8], F32)
        P.dma(rsk[:].rearrange("p a b -> p (a b)"), mrk_s[:], reads=[mrk_s], writes=[rsk])
        kf = Rot(lambda i: P.sbuf("kf%d" % i, [96, 2, T], BF16), 2)
        qf = Rot(lambda i: P.sbuf("qf%d" % i, [96, 2, T], BF16), 2)
        vv = Rot(lambda i: P.sbuf("vv%d" % i, [128, NT, 128], BF16), 2)
        pS = Rot(lambda i: P.psum("pS%d" % i, [128, 512]), 4)
        pO = Rot(lambda i: P.psum("pO%d" % i, [128, 512]), 2)
        pM = Rot(lambda i: P.psum("pM%d" % i, [128, 512]), 2)
        pT = Rot(lambda i: P.sbuf("pT%d" % i, [128, 512], BF16), 4)
        rden = P.sbuf("rden", [128, 512], F32)
        mo = Rot(lambda i: P.sbuf("mo%d" % i, [128, 512], BF16), 2)
        for j in range(4):
            k_, q_, v_ = kf.next(), qf.next(), vv.next()
            P.dma(k_[:], mkf_s[:, 2 * j:2 * j + 2, :], reads=[mkf_s], writes=[k_])
            P.dma(q_[:], mqf_s[:, 2 * j:2 * j + 2, :], reads=[mqf_s], writes=[q_])
            P.dma(v_[:], mv_s[:, j * 128:(j + 1) * 128].rearrange("(a p) f -> p a f", p=128), reads=[mv_s], writes=[v_])
            for (t0, n) in GROUPS5:
                kts = [0, 1] if t0 < CTX else list(range(NT))
                po, pm = pO.next(), pM.next()

                def scores(kt):
                    pts = []
                    for hh in range(2):
                        ps = pS.next()
                        P.mm(ps[:, 0:n], k_[:, hh, kt * 128:(kt + 1) * 128], q_[:, hh, t0:t0 + n], True, True, [k_, q_], [ps])
                        pt = pT.next()
                        P.act(pt[:, 0:n], ps[:, 0:n], AF.Exp, [ps, rsk], [pt], scale=rsk[:, kt, 2 * j + hh:2 * j + hh + 1])
                        pts.append(pt)
                    return pts

                nxt = scores(kts[0])
                for ki, kt in enumerate(kts):
                    first, last = ki == 0, ki == len(kts) - 1
                    pts = nxt
                    if not last:
                        nxt = scores(kts[ki + 1])
                    for hh in range(2):
                        sl = slice(hh * 64, (hh + 1) * 64)
                        P.mm(po[sl, 0:n], v_[:, kt, hh * 64:(hh + 1) * 64], pts[hh][:, 0:n], first, last, [v_, pts[hh]], [po])
                    for hh in range(2):
                        sl = slice(hh * 64, (hh + 1) * 64)
                        P.mm(pm[sl, 0:n], ones_b[:, 0:64], pts[hh][:, 0:n], first, last, [ones_b, pts[hh]], [pm])
                P.act(rden[:, 0:n], pm[:, 0:n], AF.Ln, [pm], [rden])
                P.act(rden[:, 0:n], rden[:, 0:n], AF.Exp, [rden], [rden], scale=-1.0)
                m = mo.next()
                P.tt(m[:, 0:n], po[:, 0:n], rden[:, 0:n], ALU.mult, [po, rden], [m])
                P.dma(mix_s[:, j, t0:t0 + n], m[:, 0:n], reads=[m], writes=[mix_s])
        P.emit()

    def gdn_prep():
        P = Prog(nc, sy, "gdp")
        wv_ = evw_d.rearrange("(kc p) n -> p kc n", p=128)
        xn = P.sbuf("xn", [128, 8, T], BF16)
        P.dma(xn[:], xn_s[:], reads=[xn_s], writes=[xn])
        cvw = P.sbuf("cvw", [128, 12, 5], F32)
        P.dma(cvw[:], gcv_d, writes=[cvw])
        grow = P.sbuf("grow", [128, 544], F32)
        P.dma(grow[:], grow_d, writes=[grow])
        st = P.sbuf("st", [128, 8, 512], F32)
        wch = Rot(lambda i: P.sbuf("wch%d" % i, [128, 8, 128], BF16), 2)
        wab = P.sbuf("wab", [128, 8, 32], BF16)
        wgt = P.sbuf("wgt", [128, 8, 512], BF16)
        load_w(P, wab, wab[:], wv_[:, :, 2208:2240], st, st[:, :, 0:32])
        load_w(P, wgt, wgt[:], wv_[:, :, 2240:2752], st, st[:])
        pp = Rot(lambda i: P.psum("pp%d" % i, [128, 512]), 3)
        pssR = Rot(lambda i: P.psum("pss%d" % i, [128, 512]), 2)
        ptr_ = Rot(lambda i: P.psum("ptr%d" % i, [128, 4, 128]), 2)
        zcR = Rot(lambda i: P.sbuf("zc%d" % i, [128, T], F32), 2)
        accR = Rot(lambda i: P.sbuf("acc%d" % i, [128, T], F32), 1)
        sqR = Rot(lambda i: P.sbuf("sq%d" % i, [128, 512], F32), 2)
        lnvR = Rot(lambda i: P.sbuf("lnv%d" % i, [128, 512], F32), 2)
        tmst = P.sbuf("tmst", [128, NT, 128], F32)
        gall = P.sbuf("gall", [128, NT, 16], F32)
        ball = P.sbuf("ball", [128, NT, 16], F32)
        negA = P.sbuf("negA", [128, 16], F32)
        P.act(negA[:], grow[:, 0:16], AF.Exp, [grow], [negA])
        P.ts(negA[:], negA[:], -1.0, ALU.mult, [negA], [negA])
        tg = P.sbuf("tg", [128, 16], F32)
        gt = Rot(lambda i: P.sbuf("gt%d" % i, [128, 512], BF16), 2)
        for tt in range(NT):
            tsl = slice(tt * 128, (tt + 1) * 128)
            ps = pp.next()
            for kc in range(8):
                P.mm(ps[:, 0:32], xn[:, kc, tsl], wab[:, kc, :], kc == 0, kc == 7, [xn, wab], [ps])
            P.tt(tg[:], ps[:, 0:16], grow[:, 16:32], ALU.add, [ps, grow], [tg])
            P.act(tg[:], tg[:], AF.Exp, [tg], [tg])
            P.act(tg[:], tg[:], AF.Ln, [tg], [tg], bias=1.0)
            P.tt(gall[:, tt, :], tg[:], negA[:], ALU.mult, [tg, negA], [gall])
            P.act(ball[:, tt, :], ps[:, 16:32], AF.Exp, [ps], [ball], scale=-1.0)
            P.ts(ball[:, tt, :], ball[:, tt, :], 1.0, ALU.add, [ball], [ball])
            P.op("vector", lambda e, tt=tt: e.reciprocal(out=ball[:, tt, :], in_=ball[:, tt, :]), [ball], [ball])
            ps = pp.next()
            for kc in range(8):
                P.mm(ps[:], xn[:, kc, tsl], wgt[:, kc, :], kc == 0, kc == 7, [xn, wgt], [ps])
            g_ = gt.next()
            P.act(g_[:], ps[:], AF.Silu, [ps], [g_])
            P.dma(ggate_s[tsl, :], g_[:], reads=[g_], writes=[ggate_s])
        P.dma(gg_s[:], gall[:].rearrange("p a b -> p (a b)"), reads=[gall], writes=[gg_s])
        P.dma(gb_s[:], ball[:].rearrange("p a b -> p (a b)"), reads=[ball], writes=[gb_s])
        ZW = T + 8

        def zoff(t):
            return t + 2 if t < CTX else t + 6

        zpR = Rot(lambda i: P.sbuf("zp%d" % i, [128, ZW], BF16), 2)
        for z_ in zpR.t:
            P.memset(z_[:], 0.0, [z_])
        dgR = Rot(lambda i: P.sbuf("dgw%d" % i, [128, 5, 128], BF16), 2)
        for cc in range(12):
            zc, acc = zcR.next(), accR.next()
            zp, dgw = zpR.next(), dgR.next()
            w = wch.next()
            load_w(P, w, w[:], wv_[:, :, 672 + cc * 128:672 + (cc + 1) * 128], st, st[:, :, 0:128])
            for j in range(5):
                P.ts(dgw[:, j, :], cst[:, C_ID, :], cvw[:, cc, j:j + 1], ALU.mult, [cst, cvw], [dgw], eng="gpsimd")
            for (t0, n) in GROUPS5:
                ps = pp.next()
                for kc in range(8):
                    P.mm(ps[:, 0:n], w[:, kc, :], xn[:, kc, t0:t0 + n], kc == 0, kc == 7, [w, xn], [ps])
                P.act(zp[:, zoff(t0):zoff(t0) + n], ps[:, 0:n], AF.Copy, [ps], [zp])
            for (t0, n) in GROUPS5:
                ps = pp.next()
                for j in range(5):
                    o_ = zoff(t0) + j - 2
                    P.mm(ps[:, 0:n], dgw[:, j, :], zp[:, o_:o_ + n], j == 0, j == 4, [dgw, zp], [ps])
                P.act(zc[:, t0:t0 + n], ps[:, 0:n], AF.Silu, [ps], [zc])
            if cc < 8:
                for (t0, n) in GROUPS5:
                    sq, lnv = sqR.next(), lnvR.next()
                    P.act(sq[:, 0:n], zc[:, t0:t0 + n], AF.Square, [zc], [sq])
                    pss = pssR.next()
                    P.mm(pss[:, 0:n], cst[:, C_BD64, :], sq[:, 0:n], True, True, [cst, sq], [pss])
                    P.act(lnv[:, 0:n], pss[:, 0:n], AF.Ln, [pss], [lnv], bias=EPS)
                    P.act(lnv[:, 0:n], lnv[:, 0:n], AF.Exp, [lnv], [lnv], scale=-0.5)
                    if cc < 4:
                        P.stt(acc[:, t0:t0 + n], zc[:, t0:t0 + n], 0.125, lnv[:, 0:n], ALU.mult, ALU.mult, [zc, lnv], [acc])
                    else:
                        P.tt(acc[:, t0:t0 + n], zc[:, t0:t0 + n], lnv[:, 0:n], ALU.mult, [zc, lnv], [acc])
                dst = gq_s if cc < 4 else gk_s
                r0 = (cc % 4) * 128
                P.dma(dst[r0:r0 + 128, :], acc[:], reads=[acc], writes=[dst])
                src = acc
            else:
                src = zc
            if cc >= 4:
                for t4 in range(0, NT, 4):
                    nn = min(4, NT - t4)
                    pt = ptr_.next()
                    for i in range(nn):
                        tt = t4 + i
                        P.tr(pt[:, i, :], src[:, tt * 128:(tt + 1) * 128], ident, [src, cst], [pt])
                    P.copy(tmst[:, t4:t4 + nn, :], pt[:, 0:nn, :], [pt], [tmst])
                dtm = gktm_s if cc < 8 else gvtm_s
                c0 = (cc % 4) * 128
                P.dma(dtm[:, c0:c0 + 128].rearrange("(a p) f -> p a f", p=128), tmst[:], reads=[tmst], writes=[dtm])
        P.emit()

    def gdn_core():
        P = Prog(nc, sy, "gdsp")
        P.dma(xt_s[:], XT[:, 4:8, :].rearrange("p c t -> p (c t)"), reads=[XT], writes=[xt_s])
        P.emit()
        P = Prog(nc, sy, "gdc")
        XTf = XT[:, 4:8, :].rearrange("p c t -> p (c t)")
        carve_off = [0]

        def carve(name, dtype):
            words = 512 if dtype == F32 else 256
            a = XTf[:, carve_off[0]:carve_off[0] + words]
            carve_off[0] += words
            if dtype != F32:
                a = a.bitcast(dtype)
            return Buf(a.rearrange("p (h i) -> p h i", h=4), 1, name)

        lvm = P.sbuf("lvm", [128, 7, 128], F32)
        nlv = P.sbuf("nlv", [128, 14, 128], BF16)
        for hf in range(2):
            P.dma(lvm[:], lvm_d[:, 7 * hf:7 * hf + 7, :], writes=[lvm])
            P.ts(nlv[:, 7 * hf:7 * hf + 7, :], lvm[:], -1.0, ALU.mult, [lvm], [nlv])
        idb = P.sbuf("idb", [128, 128], BF16)
        P.copy(idb[:], cst[:, C_ID, :], [cst], [idb])
        gall = P.sbuf("gall", [128, NT, 16], F32)
        ball = P.sbuf("ball", [128, NT, 16], F32)
        P.dma(gall[:].rearrange("p a b -> p (a b)"), gg_s[:], reads=[gg_s], writes=[gall])
        P.dma(ball[:].rearrange("p a b -> p (a b)"), gb_s[:], reads=[gb_s], writes=[ball])
        PS = Rot(lambda i: P.psum("ps%d" % i, [128, 512]), 6)
        PB = Rot(lambda i: P.psum("pb%d" % i, [128, 4, 128], BF16), 2)

        def b4(ap):
            return ap.unsqueeze(1).broadcast_to([128, 4, 128])

        def bl(ap, w):
            return ap.unsqueeze(2).broadcast_to([128, ap.shape[1], w])

        Ibc = b4(idb[:])
        D_ = []
        for d in range(2):
            n_ = lambda s, d=d: "%s_%d" % (s, d)
            o = K()
            o.S = P.sbuf(n_("S"), [128, 8, 64], F32)
            o.q = P.sbuf(n_("q"), [64, 8, 128], F32)
            o.k = P.sbuf(n_("k"), [64, 8, 128], F32)
            o.kt = P.sbuf(n_("kt"), [128, 8, 64], F32)
            o.vt = P.sbuf(n_("vt"), [128, 8, 64], F32)
            o.eg = P.sbuf(n_("eg"), [128, 24], F32)
            o.cfk = P.sbuf(n_("cfk"), [128, 8], F32)
            o.qkd = P.sbuf(n_("qkd"), [128, 8, 128], BF16)
            o.U = P.sbuf(n_("U"), [128, 8, 64], F32)
            o.wT = P.sbuf(n_("wT"), [64, 8, 128], BF16)
            o.dg = P.sbuf(n_("dg"), [128, 4, 128], F32)
            o.qdT = P.sbuf(n_("qdT"), [128, 8, 128], BF16)
            o.kdec = P.sbuf(n_("kdec"), [128, 8, 64], BF16)
            o.vnew = P.sbuf(n_("vnew"), [128, 8, 64], BF16)
            o.Sb = P.sbuf(n_("Sb"), [128, 8, 64], BF16)
            o.ost = P.sbuf(n_("ost"), [128, 512], F32)
            P.memset(o.qdT[:], 0.0, [o.qdT])
            P.memset(o.S[:], 0.0, [o.S])
            P.memset(o.Sb[:], 0.0, [o.Sb])
            o.order = list(range(NT)) if d == 0 else [1, 0] + list(range(NT - 1, 1, -1))
            o.Uinc = cst[:, C_UI, :] if d == 0 else cst[:, C_LI, :]
            o.Bst = cst[:, C_SL, :] if d == 0 else cst[:, C_SU, :]
            o.mX = (lambda lv: nlv[:, lv, :]) if d == 0 else (lambda lv: nlv[:, 7 + lv, :])
            o.mXT = (lambda lv: nlv[:, 7 + lv, :]) if d == 0 else (lambda lv: nlv[:, lv, :])
            o.d = d
            D_.append(o)
        CH = []
        for o in D_:
            for hg in range(2):
                c = K()
                c.o, c.hg, c.h0, c.hsl = o, hg, 4 * hg, slice(4 * hg, 4 * hg + 4)
                nm = lambda s, o=o, hg=hg: "%s_%d%d" % (s, o.d, hg)
                if hg == 0:
                    mk = lambda s, dt_: P.sbuf(nm(s), [128, 4, 128], dt_)
                else:
                    mk = lambda s, dt_: carve(nm(s), dt_)
                c.GU, c.eD, c.eDT = mk("GU", F32), mk("eD", F32), mk("eDT", F32)
                c.A, c.AT, c.Ym, c.Ymt = mk("A", BF16), mk("AT", BF16), mk("Ym", BF16), mk("Ymt", BF16)
                c.X = [mk("X%d" % i, BF16) for i in range(2)]
                c.XT = [mk("XT%d" % i, BF16) for i in range(2)]
                c.Rv = P.sbuf(nm("Rv"), [128, 4, 64], BF16)
                c.Rk = P.sbuf(nm("Rk"), [128, 4, 64], BF16)
                CH.append(c)

        def v4(ps):
            return ps[:].rearrange("p (h i) -> p h i", h=4)

        for s in range(NT):
            for o in D_:
                d = o.d
                tt = o.order[s]
                o.tt = tt
                tsl = slice(tt * 128, (tt + 1) * 128)
                P.dma(o.q[:], gq_s[:, tsl].rearrange("(h d) t -> d h t", d=64), reads=[gq_s], writes=[o.q])
                P.dma(o.k[:], gk_s[:, tsl].rearrange("(h d) t -> d h t", d=64), reads=[gk_s], writes=[o.k])
                P.dma(o.kt[:].rearrange("p h d -> p (h d)"), gktm_s[tsl, :], reads=[gktm_s], writes=[o.kt])
                P.dma(o.vt[:].rearrange("p h d -> p (h d)"), gvtm_s[tsl, :], reads=[gvtm_s], writes=[o.vt])
                gd = gall[:, tt, d * 8:(d + 1) * 8]
                pg = PS.next()
                P.mm(pg[:, 0:8], o.Uinc, gd, True, True, [cst, gall], [pg])
                P.mm(pg[:, 8:16], o.Bst, gd, True, True, [cst, gall], [pg])
                P.mm(pg[:, 16:24], ones_f[:], gd, True, True, [ones_f, gall], [pg])
                P.act(o.eg[:], pg[:, 0:24], AF.Exp, [pg], [o.eg])
                P.tt(o.cfk[:], ball[:, tt, d * 8:(d + 1) * 8], o.eg[:, 0:8], ALU.mult, [ball, o.eg], [o.cfk])
            for c in CH:
                o, h0 = c.o, c.h0
                d, tt = o.d, o.tt
                P.tt(c.GU[:], b4(o.Uinc), bl(gall[:, tt, d * 8 + h0:d * 8 + h0 + 4], 128), ALU.mult, [cst, gall], [c.GU],
                     eng="gpsimd")
            for c in CH:
                o = c.o
                c.pD, c.pDT = PS.next(), PS.next()
                for h in range(4):
                    P.mm(v4(c.pD)[:, h, :], c.GU[:, h, :], o.Bst, True, True, [c.GU, cst], [c.pD])
                P.mm(c.pDT[:], o.Bst, c.GU[:].rearrange("p h i -> p (h i)"), True, True, [c.GU, cst], [c.pDT])
                P.act(c.eD[:], v4(c.pD), AF.Exp, [c.pD], [c.eD])
                P.act(c.eDT[:], v4(c.pDT), AF.Exp, [c.pDT], [c.eDT])
            for c in CH:
                o, h0 = c.o, c.h0
                d, tt = o.d, o.tt
                pKK = PS.next()
                for h in range(4):
                    P.mm(v4(pKK)[:, h, :], o.k[:, h0 + h, :], o.k[:, h0 + h, :], True, True, [o.k], [pKK])
                P.tt(c.eD[:], v4(pKK), c.eD[:], ALU.mult, [pKK, c.eD], [c.eD])
                P.tt(c.eD[:], c.eD[:], b4(o.Bst), ALU.mult, [c.eD, cst], [c.eD], eng="gpsimd")
                P.tt(c.A[:], c.eD[:], bl(ball[:, tt, d * 8 + h0:d * 8 + h0 + 4], 128), ALU.mult, [c.eD, ball], [c.A],
                     eng="gpsimd")
            for c in CH:
                o, h0 = c.o, c.h0
                pQK = PS.next()
                for h in range(4):
                    P.mm(v4(pQK)[:, h, :], o.k[:, h0 + h, :], o.q[:, h0 + h, :], True, True, [o.k, o.q], [pQK])
                P.tt(c.eDT[:], v4(pQK), c.eDT[:], ALU.mult, [pQK, c.eDT], [c.eDT])
                P.tt(o.qkd[:, c.hsl, :], c.eDT[:], b4(o.Uinc), ALU.mult, [c.eDT, cst], [o.qkd], eng="gpsimd")
            for c in CH:
                pT_ = PB.next()
                for h in range(4):
                    P.tr(pT_[:, h, :], c.A[:, h, :], idb[:], [c.A, idb], [pT_])
                P.act(c.AT[:], pT_[:], AF.Copy, [pT_], [c.AT])
            for c in CH:
                o = c.o
                P.tt(c.Ym[:], c.A[:], b4(o.mX(0)), ALU.mult, [c.A, nlv], [c.Ym], eng="gpsimd")
                P.tt(c.X[0][:], c.Ym[:], Ibc, ALU.add, [c.Ym, idb], [c.X[0]], eng="gpsimd")
                P.tt(c.Ymt[:], c.AT[:], b4(o.mXT(0)), ALU.mult, [c.AT, nlv], [c.Ymt])
                P.tt(c.XT[0][:], c.Ymt[:], Ibc, ALU.add, [c.Ymt, idb], [c.XT[0]])
            cur = 0
            for lv in range(1, 7):
                last = lv == 6
                for c in CH:
                    o = c.o
                    if not last:
                        c.pY = PS.next()
                        for h in range(4):
                            P.mm(v4(c.pY)[:, h, :], c.AT[:, h, :], c.X[cur][:, h, :], True, True, [c.AT, c.X[cur]], [c.pY])
                        P.tt(c.Ym[:], v4(c.pY), b4(o.mX(lv)), ALU.mult, [c.pY, nlv], [c.Ym])
                    c.pY2 = PS.next()
                    for h in range(4):
                        P.mm(v4(c.pY2)[:, h, :], c.A[:, h, :], c.XT[cur][:, h, :], True, True, [c.A, c.XT[cur]], [c.pY2])
                    P.tt(c.Ymt[:], v4(c.pY2), b4(o.mXT(lv)), ALU.mult, [c.pY2, nlv], [c.Ymt])
                for c in CH:
                    if not last:
                        pZ = PS.next()
                        P.mm(pZ[:], idb[:], c.X[cur][:].rearrange("p h i -> p (h i)"), True, False, [idb, c.X[cur]], [pZ])
                        for h in range(4):
                            P.mm(v4(pZ)[:, h, :], c.XT[cur][:, h, :], c.Ym[:, h, :], False, True, [c.XT[cur], c.Ym], [pZ])
                        P.act(c.X[1 - cur][:], v4(pZ), AF.Copy, [pZ], [c.X[1 - cur]])
                    pZ = PS.next()
                    P.mm(pZ[:], idb[:], c.XT[cur][:].rearrange("p h i -> p (h i)"), True, False, [idb, c.XT[cur]], [pZ])
                    for h in range(4):
                        P.mm(v4(pZ)[:, h, :], c.X[cur][:, h, :], c.Ymt[:, h, :], False, True, [c.X[cur], c.Ymt], [pZ])
                    P.act(c.XT[1 - cur][:], v4(pZ), AF.Copy, [pZ], [c.XT[1 - cur]])
                cur = 1 - cur
            for c in CH:
                o, h0, hsl = c.o, c.h0, c.hsl
                d, tt = o.d, o.tt
                XTf_ = c.XT[cur]
                P.tt(c.Rv[:], o.vt[:, hsl, :], bl(ball[:, tt, d * 8 + h0:d * 8 + h0 + 4], 64), ALU.mult, [o.vt, ball], [c.Rv],
                     eng="gpsimd")
                P.tt(c.Rk[:], o.kt[:, hsl, :], bl(o.cfk[:, hsl], 64), ALU.mult, [o.kt, o.cfk], [c.Rk], eng="gpsimd")
                pU, pW = PS.next(), PS.next()
                pU3 = pU[:, 0:256].rearrange("p (h i) -> p h i", h=4)
                pW3 = pW[0:64, :].rearrange("p (h i) -> p h i", h=4)
                for h in range(4):
                    P.mm(pU3[:, h, :], XTf_[:, h, :], c.Rv[:, h, :], True, True, [XTf_, c.Rv], [pU])
                for h in range(4):
                    P.mm(pW3[:, h, :], c.Rk[:, h, :], XTf_[:, h, :], True, True, [XTf_, c.Rk], [pW])
                P.act(o.U[:, hsl, :], pU3, AF.Copy, [pU], [o.U])
                P.act(o.wT[:, hsl, :], pW3, AF.Copy, [pW], [o.wT])
            for o in D_:
                for half in range(2):
                    P.tt(o.dg[:], b4(cst[:, C_ID, :]), bl(o.eg[:, 4 * half:4 * half + 4], 128), ALU.mult, [cst, o.eg], [o.dg],
                         eng="gpsimd")
                    pE = PS.next()
                    P.mm(pE[0:64, :], ones_f[:, 0:64], o.dg[:].rearrange("p h i -> p (h i)"), True, True, [ones_f, o.dg], [pE])
                    P.tt(o.qdT[0:64, 4 * half:4 * half + 4, :], o.q[:, 4 * half:4 * half + 4, :],
                         pE[0:64, :].rearrange("p (h i) -> p h i", h=4), ALU.mult, [o.q, pE], [o.qdT])
                P.tt(o.kdec[:], o.kt[:], bl(o.eg[:, 8:16], 64), ALU.mult, [o.kt, o.eg], [o.kdec], eng="gpsimd")
            for o in D_:
                pWS = PS.next()
                pWS3 = pWS[:].rearrange("p (h e) -> p h e", h=8)
                for h in range(8):
                    P.mm(pWS3[:, h, :], o.wT[:, h, :], o.Sb[0:64, h, :], True, True, [o.wT, o.Sb], [pWS])
                P.tt(o.vnew[:], o.U[:], pWS3, ALU.subtract, [o.U, pWS], [o.vnew])
            for o in D_:
                tt = o.tt
                tsl = slice(tt * 128, (tt + 1) * 128)
                pOO = PS.next()
                pOO3 = pOO[:].rearrange("p (h e) -> p h e", h=8)
                for h in range(8):
                    P.mm(pOO3[:, h, :], o.qdT[:, h, :], o.Sb[:, h, :], True, False, [o.qdT, o.Sb], [pOO])
                    P.mm(pOO3[:, h, :], o.qkd[:, h, :], o.vnew[:, h, :], False, True, [o.qkd, o.vnew], [pOO])
                P.act(o.ost[:], pOO[:], AF.Copy, [pOO], [o.ost])
                P.dma(og_s[o.d][tsl, :], o.ost[:], reads=[o.ost], writes=[og_s[o.d]])
                pSn = PS.next()
                pSn3 = pSn[0:64, :].rearrange("p (h e) -> p h e", h=8)
                for h in range(8):
                    P.mm(pSn3[:, h, :], o.kdec[:, h, :], o.vnew[:, h, :], True, True, [o.kdec, o.vnew], [pSn])
                P.tt(o.S[0:64], o.S[0:64], o.eg[0:64, 16:24].unsqueeze(2).broadcast_to([64, 8, 64]), ALU.mult, [o.S, o.eg], [o.S])
                P.tt(o.S[0:64], o.S[0:64], pSn3, ALU.add, [o.S, pSn], [o.S])
                P.act(o.Sb[0:64], o.S[0:64], AF.Copy, [o.S], [o.Sb])
        P.emit()
        P = Prog(nc, sy, "gdf")
        grow = P.sbuf("grow", [128, 544], F32)
        P.dma(grow[:], grow_d, writes=[grow])
        sqoR = Rot(lambda i: P.sbuf("sqo%d" % i, [128, 8, 64], F32), 2)
        ssR = Rot(lambda i: P.sbuf("ss%d" % i, [128, 8], F32), 2)
        toR = Rot(lambda i: P.sbuf("to%d" % i, [128, 8, 64], F32), 2)
        tobR = Rot(lambda i: P.sbuf("tob%d" % i, [128, 512], BF16), 2)
        gtt = Rot(lambda i: P.sbuf("gtt%d" % i, [128, 512], BF16), 2)
        oa = Rot(lambda i: P.sbuf("oa%d" % i, [128, 512], F32), 2)
        ob_ = Rot(lambda i: P.sbuf("ob%d" % i, [128, 512], F32), 2)
        idb = P.sbuf("idb", [128, 128], BF16)
        P.copy(idb[:], cst[:, C_ID, :], [cst], [idb])
        mixo = Rot(lambda i: P.sbuf("mixo%d" % i, [128, 4, 128], BF16), 2)
        ptbR = Rot(lambda i: P.psum("ptb%d" % i, [128, 4, 128], BF16), 2)
        for tt in range(NT):
            sqo, ss, to, tob, ptb = sqoR.next(), ssR.next(), toR.next(), tobR.next(), ptbR.next()
            tsl = slice(tt * 128, (tt + 1) * 128)
            g_, a_, b_ = gtt.next(), oa.next(), ob_.next()
            P.dma(g_[:], ggate_s[tsl, :], reads=[ggate_s], writes=[g_])
            P.dma(a_[:], og_s[0][tsl, :], reads=[og_s[0]], writes=[a_])
            P.dma(b_[:], og_s[1][tsl, :], reads=[og_s[1]], writes=[b_])
            P.tt(a_[:], a_[:], b_[:], ALU.add, [a_, b_], [a_], eng="gpsimd")
            o3 = a_[:].rearrange("p (h e) -> p h e", h=8)
            P.act(sqo[:], o3, AF.Square, [a_], [sqo])
            P.op("vector", lambda e, ss=ss, sqo=sqo: e.tensor_reduce(out=ss[:], in_=sqo[:], axis=AX.X, op=ALU.add), [sqo], [ss])
            P.act(ss[:], ss[:], AF.Ln, [ss], [ss], bias=EPS, scale=1.0 / 64)
            P.act(ss[:], ss[:], AF.Exp, [ss], [ss], scale=-0.5)
            P.tt(to[:], o3, ss[:].unsqueeze(2).broadcast_to([128, 8, 64]), ALU.mult, [a_, ss], [to])
            P.tt(to[:].rearrange("p h e -> p (h e)"), to[:].rearrange("p h e -> p (h e)"), grow[:, 32:544], ALU.mult,
                 [to, grow], [to], eng="gpsimd")
            P.tt(tob[:], to[:].rearrange("p h e -> p (h e)"), g_[:], ALU.mult, [to, g_], [tob])
            for c in range(4):
                P.tr(ptb[:, c, :], tob[:, c * 128:(c + 1) * 128], idb[:], [tob, idb], [ptb])
            mo_ = mixo.next()
            P.act(mo_[:], ptb[:], AF.Copy, [ptb], [mo_])
            P.dma(mix_s[:, 4:8, tsl], mo_[:], reads=[mo_], writes=[mix_s])
        P.dma(XT[:, 4:8, :].rearrange("p c t -> p (c t)"), xt_s[:], reads=[xt_s], writes=[XT])
        P.emit()


    for l in layers:
        if stage >= 1 and not skip_ffn1:
            ffn(l, 0, FULL)
        dump_xt("xt_ffn1_%d" % l)
        if stage >= 2:
            make_xn(l)
            if l == 0:
                if parts is None or "mlp" in parts:
                    mla_prep()
                if parts is None or "mlc" in parts:
                    mla_core()
                if parts is None or "gdp" in parts:
                    gdn_prep()
                if parts is None or "gdc" in parts:
                    gdn_core()
                if parts is None or "op" in parts:
                    out_proj(l, evwo_d, GROUPS5)
            if l == 1:
                if parts is None or "swp" in parts:
                    swa_prep()
                if parts is None or "swc" in parts:
                    swa_core()
                if parts is None or "glp" in parts:
                    gla_prep()
                if parts is None or "glc" in parts:
                    gla_core()
                if parts is None or "op" in parts:
                    out_proj(l, odwo_d, LATG)
        if "mix" in dbg_d:
            P = Prog(nc, sy, "dbgmix%d" % l)
            mb_ = Rot(lambda i: P.sbuf("mb%d" % i, [128, 8, 128], BF16), 2)
            mf_ = Rot(lambda i: P.sbuf("mf%d" % i, [128, 8, 128], F32), 2)
            dv = dbg_d["mix"].rearrange("p (c t) -> p c t", c=8)
            for tt in range(NT):
                a_, f_ = mb_.next(), mf_.next()
                P.dma(a_[:], mix_s[:, :, tt * 128:(tt + 1) * 128], reads=[mix_s], writes=[a_])
                P.act(f_[:], a_[:], AF.Copy, [a_], [f_])
                P.dma(dv[:, :, tt * 128:(tt + 1) * 128], f_[:], reads=[f_])
            P.emit()
        dump_xt("xt_mix_%d" % l)
        if stage >= 3:
            ffn(l, 1, FULL if l == 0 else LAT)

    P = Prog(nc, sy, "pout")
    ost = [P.sbuf("ost%d" % i, [128, D], F32) for i in range(2)]
    pot = [P.psum("pot%d" % i, [128, 4, 128]) for i in range(2)]
    for tt in range(2, NT):
        o = ost[tt % 2]
        for half in range(2):
            ps = pot[half]
            for c4 in range(4):
                c = half * 4 + c4
                P.tr(ps[:, c4, :], XT[:, c, tt * 128:(tt + 1) * 128], ident, [(XT, tt), cst], [ps])
            if half == 0:
                P.copy(o[:, 0:512], ps[:].rearrange("p a b -> p (a b)"), [ps], [o])
            else:
                P.act(o[:, 512:1024], ps[:].rearrange("p a b -> p (a b)"), AF.Copy, [ps], [o])
        P.dma(out_d[(tt - 2) * 128:(tt - 1) * 128, :], o[:], reads=[o])
    P.emit()
    es.close()
    return nc


def rope_tables(rot_dim):
    f = np.float32
    t = np.arange(SEQ)
    row = (t // 64).astype(f)
    col = (t % 64).astype(f)
    nf = rot_dim // 4
    inv = (f(10000.0) ** (-np.arange(nf, dtype=f) / f(nf))).astype(f)
    ang = np.concatenate([row[:, None] * inv, col[:, None] * inv], axis=-1).astype(f)
    return np.cos(ang).astype(f), np.sin(ang).astype(f)


def host_consts():
    f = np.float32
    idx = np.arange(128)
    ident = np.eye(128)
    bd64 = (idx[:, None] // 64 == idx[None, :] // 64)
    Rm = np.zeros((64, 64))
    for i in range(32):
        Rm[i, i + 32] = -1.0
        Rm[i + 32, i] = 1.0
    rot = np.zeros((128, 128))
    rot[0:64, 0:64] = Rm.T
    rot[64:128, 64:128] = Rm.T
    UI = idx[:, None] <= idx[None, :]
    LI = idx[:, None] >= idx[None, :]
    SL = idx[:, None] > idx[None, :]
    SU = idx[:, None] < idx[None, :]
    rot96 = np.zeros((128, 128))
    for i in range(16):
        rot96[80 + i, 64 + i] = -1.0
        rot96[64 + i, 80 + i] = 1.0
    cst = np.stack([ident, bd64, rot, UI, LI, SL, SU, rot96], axis=1).astype(f)
    lv = []
    for k in range(7):
        bsz = 1 << k
        same = (idx[:, None] // (2 * bsz)) == (idx[None, :] // (2 * bsz))
        m = same & ((idx[:, None] % (2 * bsz)) >= bsz) & ((idx[None, :] % (2 * bsz)) < bsz)
        lv.append(m)
    lvm = np.stack(lv + [m.T for m in lv], axis=1).astype(f)
    c8, s8 = rope_tables(32)
    mcs = np.zeros((96, 2, SEQ), f)
    mcs[0:64, 0, :] = 1.0
    for dd in range(32):
        mcs[64 + dd, 0, :] = c8[:, dd % 16]
        mcs[64 + dd, 1, :] = s8[:, dd % 16]
    bdsel = np.stack([(idx // 64 == 0), (idx // 64 == 1)], axis=1).astype(f)
    c, s = rope_tables(64)
    d = idx % 64
    swcs = np.stack([c[:, d % 32].T, s[:, d % 32].T], axis=1).astype(f)
    return {"cst": np.ascontiguousarray(cst), "bdsel": np.ascontiguousarray(bdsel), "swa_cs": np.ascontiguousarray(swcs),
            "lvm": np.ascontiguousarray(lvm), "mla_cs": np.ascontiguousarray(mcs)}


_CONSTS = None


def host_inputs(inputs, b):
    global _CONSTS
    f = np.float32
    if _CONSTS is None:
        _CONSTS = host_consts()
    c = np.asarray(inputs["c"][b], f)
    cc = np.asarray(inputs["c_ctx"], f)
    ccol = np.concatenate([c.reshape(8, 128).T, cc.reshape(8, 128).T], axis=1)
    idx = np.arange(128)
    odc = np.zeros((128, 11), f)
    odc[:, 0] = np.asarray(inputs["od_swa_q_norm"], f)[0][idx % 64]
    odc[:, 1] = np.asarray(inputs["od_swa_k_norm"], f)[0][idx % 64]
    odc[:, 2] = np.asarray(inputs["od_gla_out_norm"], f)[0]
    odc[:, 3:11] = np.asarray(inputs["od_swa_sink"], f)[0][None, :]
    w2 = np.asarray(inputs["od_gla_gate_w2"], f)[0]
    gb = np.asarray(inputs["od_gla_gate_b"], f)[0]
    gw2b = np.concatenate([w2, gb[:, None, :]], axis=1).transpose(1, 0, 2)
    evc = np.zeros((128, 8), f)
    evc[:, 0:3] = np.asarray(inputs["ev_q_a_norm"], f)[0].reshape(3, 128).T
    evc[:, 3:5] = np.asarray(inputs["ev_kv_a_norm"], f)[0].reshape(2, 128).T
    evc[0:96, 5] = np.asarray(inputs["ev_mla_q_norm"], f)[0]
    evc[0:96, 6] = np.asarray(inputs["ev_mla_k_norm"], f)[0]
    gcv = np.asarray(inputs["ev_gdn_conv"], f)[0].reshape(5, 12, 128).transpose(2, 1, 0)
    grow = np.zeros((128, 544), f)
    grow[:, 0:16] = np.asarray(inputs["ev_gdn_a_log"], f)[0].reshape(16)[None, :]
    grow[:, 16:32] = np.asarray(inputs["ev_gdn_dt_bias"], f)[0].reshape(16)[None, :]
    grow[:, 32:544] = np.tile(np.asarray(inputs["ev_gdn_out_norm"], f)[0], 8)[None, :]
    m = {
        "ev_w_in": np.ascontiguousarray(np.asarray(inputs["ev_w_in"], f)[0]),
        "ev_w_out": np.ascontiguousarray(np.asarray(inputs["ev_w_out"], f)[0]),
        "ev_w_q_up": np.ascontiguousarray(np.asarray(inputs["ev_w_q_up"], f)[0]),
        "ev_w_kv_up": np.ascontiguousarray(np.asarray(inputs["ev_w_kv_up"], f)[0]),
        "ev_cols": evc,
        "gdn_conv_col": np.ascontiguousarray(gcv),
        "gdn_rows": grow,
        "x": np.ascontiguousarray(inputs["x"][b], dtype=f),
        "ctx": np.ascontiguousarray(inputs["ctx"][b], dtype=f),
        "ccol": np.ascontiguousarray(ccol),
        "ada_w": np.ascontiguousarray(inputs["ada_w"], dtype=f),
        "ada_b_col": np.ascontiguousarray(np.asarray(inputs["ada_b"], f).reshape(2, 72, 128).transpose(2, 0, 1)),
        "norm_g_col": np.ascontiguousarray(np.asarray(inputs["norm_g"], f).reshape(2, 3, 8, 128).transpose(3, 0, 1, 2)),
        "ffn_w_gate": np.ascontiguousarray(inputs["ffn_w_gate"], dtype=f),
        "ffn_w_up": np.ascontiguousarray(inputs["ffn_w_up"], dtype=f),
        "ffn_w_down": np.ascontiguousarray(inputs["ffn_w_down"], dtype=f),
        "od_w_in": np.ascontiguousarray(np.asarray(inputs["od_w_in"], f)[0]),
        "od_w_out": np.ascontiguousarray(np.asarray(inputs["od_w_out"], f)[0]),
        "od_cols": odc,
        "gla_w2b": np.ascontiguousarray(gw2b),
    }
    m.update(_CONSTS)
    return m


def kernel(**inputs):
    nc = build()
    in_maps = [host_inputs(inputs, b) for b in range(NCORES)]
    res = run_bass_kernel_spmd(nc, in_maps, core_ids=list(range(NCORES)))
    return np.stack([np.asarray(r["out"], np.float32) for r in res.results], axis=0)
```

```python
from contextlib import ExitStack
import numpy as np
import concourse.bass as bass
import concourse.mybir as mybir
from concourse.bass_utils import run_bass_kernel_spmd

F32 = mybir.dt.float32
BF16 = mybir.dt.bfloat16
AF = mybir.ActivationFunctionType
ALU = mybir.AluOpType
AX = mybir.AxisListType

D = 1024
DFF = 2816
NF = DFF // 128
SEQ = 2048
CTX = 256
T = SEQ + CTX
NT = T // 128
EPS = 1e-6
NCORES = 8

COMPUTE = ("tensor", "vector", "scalar", "gpsimd")
ENGINES = ("tensor", "vector", "scalar", "gpsimd", "sync")
N_DMA_SEMS = 24


class Buf:
    _n = 0

    def __init__(self, t, nslots=1, name=None):
        self.t = t
        self.nslots = nslots
        Buf._n += 1
        self.id = Buf._n
        self.name = name

    def __getitem__(self, idx):
        return self.t[idx]


class Sync:
    def __init__(self, nc, es):
        self.nc = nc
        self.sems = {e: es.enter_context(nc.semaphore("s_" + e)) for e in COMPUTE}
        self.dsems = [es.enter_context(nc.semaphore("d_%d" % i)) for i in range(N_DMA_SEMS)]
        self.pos = {e: 0 for e in COMPUTE}
        self.n_dma = 0
        self.dma_uses = [0] * N_DMA_SEMS


class Prog:
    def __init__(self, nc, sync, name="ph"):
        self.nc = nc
        self.sy = sync
        self.name = name
        self.es = ExitStack()
        self.ops = []
        self.state = {}
        self.dma_last = [None] * N_DMA_SEMS
        self.start_pos = dict(sync.pos)
        self.start_dma = [16 * u for u in sync.dma_uses]

    def sbuf(self, name, shape, dtype, nslots=1):
        t = self.es.enter_context(self.nc.sbuf_tensor(self.name + "_" + name, list(shape), dtype))
        return Buf(t, nslots, name)

    def psum(self, name, shape, dtype=F32, nslots=1):
        t = self.es.enter_context(self.nc.psum_tensor(self.name + "_" + name, list(shape), dtype))
        return Buf(t, nslots, name)

    def _expand(self, keys):
        out = []
        for k in keys:
            if isinstance(k, Buf):
                k = (k, None)
            b, s = k
            if s is None:
                out.extend((b.id, i) for i in range(b.nslots))
            elif isinstance(s, (list, tuple, range)):
                for i in s:
                    assert 0 <= i < b.nslots, (b.name, i, b.nslots)
                    out.append((b.id, i))
            else:
                assert 0 <= s < b.nslots, (b.name, s, b.nslots)
                out.append((b.id, s))
        return out

    def _deps(self, idx, reads, writes):
        deps = set()
        rk = self._expand(reads)
        wk = self._expand(writes)
        for k in rk:
            st = self.state.setdefault(k, [None, []])
            if st[0] is not None:
                deps.add(st[0])
        for k in wk:
            st = self.state.setdefault(k, [None, []])
            if st[0] is not None:
                deps.add(st[0])
            deps.update(st[1])
        for k in rk:
            self.state[k][1].append(idx)
        for k in wk:
            self.state[k] = [idx, []]
        deps.discard(idx)
        return deps

    def op(self, eng, fn, reads=(), writes=()):
        idx = len(self.ops)
        deps = self._deps(idx, reads, writes)
        self.ops.append(dict(eng=eng, fn=fn, deps=deps, dma=False, pos=self.sy.pos[eng]))
        self.sy.pos[eng] += 1
        return idx

    def dma(self, out, in_, reads=(), writes=(), q="sync"):
        idx = len(self.ops)
        deps = self._deps(idx, reads, writes)
        sy = self.sy
        s = sy.n_dma % N_DMA_SEMS
        sy.n_dma += 1
        prev = self.dma_last[s]
        sy.dma_uses[s] += 1
        self.dma_last[s] = idx
        self.ops.append(dict(eng=q, fn=lambda e: e.dma_start(out=out, in_=in_), deps=deps, dma=True, sem=s,
                             target=16 * sy.dma_uses[s], prev=prev))
        return idx

    def mm(self, out, lhsT, rhs, start, stop, r, w):
        return self.op("tensor", lambda e: e.matmul(out, lhsT=lhsT, rhs=rhs, start=start, stop=stop), r, w)

    def tr(self, out, in_, ident, r, w):
        return self.op("tensor", lambda e: e.transpose(out, in_, ident), r, w)

    def act(self, out, in_, func, r, w, bias=None, scale=None, accum_out=None):
        kw = {}
        if bias is not None:
            kw["bias"] = bias
        if scale is not None:
            kw["scale"] = scale
        if accum_out is not None:
            kw["accum_out"] = accum_out
        return self.op("scalar", lambda e: e.activation(out=out, in_=in_, func=func, **kw), r, w)

    def tt(self, out, in0, in1, op, r, w, eng="vector"):
        return self.op(eng, lambda e: e.tensor_tensor(out=out, in0=in0, in1=in1, op=op), r, w)

    def ts(self, out, in0, s1, op0, r, w, s2=None, op1=None, eng="vector"):
        if op1 is None:
            return self.op(eng, lambda e: e.tensor_scalar(out=out, in0=in0, scalar1=s1, scalar2=None, op0=op0), r, w)
        return self.op(eng, lambda e: e.tensor_scalar(out=out, in0=in0, scalar1=s1, scalar2=s2, op0=op0, op1=op1), r, w)

    def stt(self, out, in0, scalar, in1, op0, op1, r, w):
        return self.op("vector", lambda e: e.scalar_tensor_tensor(out=out, in0=in0, scalar=scalar, in1=in1,
                                                                  op0=op0, op1=op1), r, w)

    def copy(self, out, in_, r, w, eng="vector"):
        return self.op(eng, lambda e: e.tensor_copy(out=out, in_=in_), r, w)

    def memset(self, ap, val, w, eng="vector"):
        return self.op(eng, lambda e: e.memset(ap, val), (), w)

    def emit(self):
        nc = self.nc
        sy = self.sy
        ops = self.ops
        per_eng = {e: [] for e in ENGINES}
        for i, o in enumerate(ops):
            per_eng[o["eng"]].append(i)
        end_pos = dict(sy.pos)
        end_dma = [16 * u for u in sy.dma_uses]
        start_pos = self.start_pos
        start_dma = self.start_dma

        def replay(ename):
            def body(eng):
                waited = dict(start_pos)
                dwaited = list(start_dma)
                for i in per_eng[ename]:
                    o = ops[i]
                    need = {}
                    dneed = {}
                    deps = set(o["deps"])
                    if o["dma"] and o["prev"] is not None:
                        deps.add(o["prev"])
                    for j in deps:
                        d = ops[j]
                        if d["dma"]:
                            dneed[d["sem"]] = max(dneed.get(d["sem"], 0), d["target"])
                        else:
                            if d["eng"] == "tensor" and ename == "tensor" and not o["dma"]:
                                continue
                            need[d["eng"]] = max(need.get(d["eng"], 0), d["pos"] + 1)
                    for e, v in need.items():
                        if v > waited[e]:
                            eng.wait_ge(sy.sems[e], v)
                            waited[e] = v
                    for s, v in dneed.items():
                        if v > dwaited[s]:
                            eng.wait_ge(sy.dsems[s], v)
                            dwaited[s] = v
                    ins = o["fn"](eng)
                    if o["dma"]:
                        ins.then_inc(sy.dsems[o["sem"]], 16)
                    else:
                        ins.then_inc(sy.sems[ename], 1)
                if ename == "sync":
                    for s in range(N_DMA_SEMS):
                        if end_dma[s] > start_dma[s]:
                            eng.wait_ge(sy.dsems[s], end_dma[s])
                    for e in COMPUTE:
                        if end_pos[e] > start_pos[e]:
                            eng.wait_ge(sy.sems[e], end_pos[e])
            return body

        with nc.Block() as block:
            for e in ENGINES:
                if per_eng[e] or e == "sync":
                    getattr(block, e)(replay(e))
        self.es.close()


def segs(t0, n):
    out = []
    if t0 < CTX:
        m = min(n, CTX - t0)
        out.append((t0, m, 1))
        if n > m:
            out.append((CTX, n - m, 0))
    else:
        out.append((t0, n, 0))
    return out


def tiles_of(t0, n):
    return list(range(t0 // 128, (t0 + n + 127) // 128))


class K:
    pass


class Rot:
    def __init__(self, mk, n):
        self.t = [mk(i) for i in range(n)]
        self.i = 0

    def next(self):
        t = self.t[self.i % len(self.t)]
        self.i += 1
        return t


GROUPS5 = [(0, 256), (256, 512), (768, 512), (1280, 512), (1792, 512)]
LATG = GROUPS5[1:]
FULL = [[(0, 384), (384, 384), (768, 384)], [(1152, 384), (1536, 384), (1920, 384)]]
LAT = [[(256, 512), (768, 512)], [(1280, 512), (1792, 512)]]
C_ID, C_BD64, C_ROT64, C_UI, C_LI, C_SL, C_SU, C_ROT96 = range(8)
NCST = 8


def build(stage=99, dbg=(), layers=(0, 1), parts=None, skip_ffn1=False):
    nc = bass.Bass("TRN2", target_bir_lowering=False)
    dt = nc.dram_tensor

    def inp(name, shape, dtype=F32):
        return dt(name, list(shape), dtype, kind="ExternalInput").ap()

    def scratch(name, shape, dtype, nslots=1):
        return Buf(dt("scr_" + name, list(shape), dtype, kind="Internal").ap(), nslots, name)

    x_d = inp("x", [SEQ, D])
    ctx_d = inp("ctx", [CTX, D])
    ccol_d = inp("ccol", [128, 16])
    adaw_d = inp("ada_w", [2, D, 9 * D])
    adab_d = inp("ada_b_col", [128, 2, 72])
    ng_d = inp("norm_g_col", [128, 2, 3, 8])
    wg_d = inp("ffn_w_gate", [2, 2, D, DFF])
    wu_d = inp("ffn_w_up", [2, 2, D, DFF])
    wd_d = inp("ffn_w_down", [2, 2, DFF, D])
    cst_d = inp("cst", [128, NCST, 128])
    bdsel_d = inp("bdsel", [128, 2])
    odw_d = inp("od_w_in", [D, 2336])
    odwo_d = inp("od_w_out", [D, D])
    odcol_d = inp("od_cols", [128, 11])
    swcs_d = inp("swa_cs", [128, 2, SEQ])
    gw2b_d = inp("gla_w2b", [17, 2, 256])
    evw_d = inp("ev_w_in", [D, 2752])
    evwo_d = inp("ev_w_out", [D, D])
    evqu_d = inp("ev_w_q_up", [384, 768])
    evkv_d = inp("ev_w_kv_up", [256, 1024])
    evcol_d = inp("ev_cols", [128, 8])
    mlcs_d = inp("mla_cs", [96, 2, SEQ])
    lvm_d = inp("lvm", [128, 14, 128])
    gcv_d = inp("gdn_conv_col", [128, 12, 5])
    grow_d = inp("gdn_rows", [128, 544])
    out_d = dt("out", [SEQ, D], F32, kind="ExternalOutput").ap()
    dbg_d = {}
    for name, shape in dbg:
        dbg_d[name] = dt("dbg_" + name, list(shape), F32, kind="ExternalOutput").ap()

    xn_s = scratch("xn", [128, 8, T], BF16)
    mix_s = scratch("mix", [128, 8, T], BF16)
    swq_s = scratch("swq", [128, 4, SEQ], BF16)
    swk_s = scratch("swk", [128, T], BF16)
    swv_s = scratch("swv", [128, NT * 256], BF16)
    swr_s = scratch("swr", [128, NT * 2], F32)
    glq_s = [scratch("glq%d" % d, [128, 2, T], BF16) for d in range(2)]
    glk_s = [scratch("glk%d" % d, [128, 2, T], BF16) for d in range(2)]
    glkd_s = [scratch("glkd%d" % d, [T, 256], BF16) for d in range(2)]
    glet_s = [scratch("glet%d" % d, [128, 2, NT], F32) for d in range(2)]
    glv_s = scratch("glv", [T, 512], BF16)
    glr_s = scratch("glr", [128, 4, SEQ], BF16)

    mqf_s = scratch("mqf", [96, 8, T], BF16)
    mkf_s = scratch("mkf", [96, 8, T], BF16)
    mv_s = scratch("mv", [T, 512], BF16)
    mrk_s = scratch("mrk", [128, NT * 8], F32)
    gq_s = scratch("gq", [512, T], F32)
    gk_s = scratch("gk", [512, T], F32)
    gktm_s = scratch("gktm", [T, 512], F32)
    gvtm_s = scratch("gvtm", [T, 512], F32)
    gg_s = scratch("gg", [128, NT * 16], F32)
    gb_s = scratch("gb", [128, NT * 16], F32)
    ggate_s = scratch("ggate", [T, 512], BF16)
    og_s = [scratch("og%d" % d, [T, 512], F32) for d in range(2)]
    xt_s = scratch("xtpark", [128, 4 * T], F32)

    es = ExitStack()
    sy = Sync(nc, es)

    def persist(name, shape, dtype, nslots=1):
        return Buf(es.enter_context(nc.sbuf_tensor(name, list(shape), dtype)), nslots, name)

    XT = persist("XT", [128, 8, T], F32, NT)
    cst = persist("cst_s", [128, NCST, 128], F32)
    ident = cst[:, C_ID, :]
    ones_f = persist("ones_f", [128, 128], F32)
    ones_b = persist("ones_b", [128, 128], BF16)
    cstb = persist("cstb", [128, NCST, 128], BF16)
    modT = persist("modT", [128, 2, 72, 2], F32)
    AG = persist("AG", [128, 2, 3, 8, 2], F32)
    GH = persist("GH", [128, 2, 3, 8, 2], F32)
    ng = persist("ng", [128, 2, 3, 8], F32)

    def xk(t0, n):
        return [(XT, i) for i in tiles_of(t0, n)]

    P = Prog(nc, sy, "p0")
    P.dma(cst[:], cst_d, writes=[cst])
    P.dma(ng[:], ng_d, writes=[ng])
    P.memset(ones_f[:], 1.0, [ones_f])
    P.memset(ones_b[:], 1.0, [ones_b])
    P.copy(cstb[:], cst[:], [cst], [cstb], eng="gpsimd")
    xin = [P.sbuf("xin%d" % i, [128, D], F32) for i in range(2)]
    ptr = [P.psum("ptr%d" % i, [128, 4, 128]) for i in range(2)]
    for tt in range(NT):
        st = xin[tt % 2]
        src = ctx_d[tt * 128:(tt + 1) * 128, :] if tt < 2 else x_d[(tt - 2) * 128:(tt - 1) * 128, :]
        P.dma(st[:], src, writes=[st])
        for half in range(2):
            ps = ptr[half]
            for c4 in range(4):
                c = half * 4 + c4
                P.tr(ps[:, c4, :], st[:, c * 128:(c + 1) * 128], ident, [st, cst], [ps])
            if half == 0:
                P.copy(XT[:, 0:4, tt * 128:(tt + 1) * 128], ps[:], [ps], [(XT, tt)])
            else:
                P.act(XT[:, 4:8, tt * 128:(tt + 1) * 128], ps[:], AF.Copy, [ps], [(XT, tt)])

    ccol = P.sbuf("ccol", [128, 16], F32)
    sc3 = P.sbuf("sc3", [128, 8, 2], F32)
    adab = P.sbuf("adab", [128, 2, 72], F32)
    P.dma(ccol[:], ccol_d, writes=[ccol])
    P.dma(adab[:], adab_d, writes=[adab])
    P.act(sc3[:, :, 0], ccol[:, 0:8], AF.Silu, [ccol], [sc3])
    P.act(sc3[:, :, 1], ccol[:, 8:16], AF.Silu, [ccol], [sc3])
    ABLK = 256
    awst = [P.sbuf("awst%d" % i, [128, 8, ABLK], F32) for i in range(2)]
    pmod = P.psum("pmod", [128, 72, 2])
    nb = 0
    for l in range(2):
        aw = adaw_d[l].rearrange("(kc p) n -> p kc n", p=128)
        for blk in range(9 * D // ABLK):
            st = awst[nb % 2]
            nb += 1
            P.dma(st[:], aw[:, :, blk * ABLK:(blk + 1) * ABLK], writes=[st])
            for m in range(ABLK // 128):
                j = blk * (ABLK // 128) + m
                for kc in range(8):
                    P.mm(pmod[:, j, :], st[:, kc, m * 128:(m + 1) * 128], sc3[:, kc, :], kc == 0, kc == 7,
                         [st, sc3], [pmod])
        for s in range(2):
            P.tt(modT[:, l, :, s], pmod[:, :, s], adab[:, l, :], ALU.add, [pmod, adab], [modT])
    for l in range(2):
        for n in range(3):
            for s in range(2):
                P.stt(AG[:, l, n, :, s], modT[:, l, (3 * n + 1) * 8:(3 * n + 2) * 8, s], 1.0, ng[:, l, n, :],
                      ALU.add, ALU.mult, [modT, ng], [AG])
                P.ts(GH[:, l, n, :, s], modT[:, l, (3 * n + 2) * 8:(3 * n + 3) * 8, s], 0.5 if n != 1 else 1.0,
                     ALU.mult, [modT], [GH])
    if "modT" in dbg_d:
        P.dma(dbg_d["modT"], modT[:].rearrange("p l j s -> p (l j s)"), reads=[modT])
    P.emit()

    def norm_mod(P, W, l, n_idx, t0, n, dst_fn, dst_keys):
        sq, lnv, rs, pss, sqb = (w.next() for w in W)
        P.act(sqb[:, :, 0:n], XT[:, :, t0:t0 + n], AF.Square, xk(t0, n), [sqb])
        for c in range(8):
            P.mm(pss[:, 0:n], ones_b[:], sqb[:, c, 0:n], c == 0, c == 7, [sqb, ones_b], [pss])
        P.act(lnv[:, 0:n], pss[:, 0:n], AF.Ln, [pss], [lnv], bias=EPS, scale=1.0 / D)
        P.act(rs[:, 0:n], lnv[:, 0:n], AF.Exp, [lnv], [rs], scale=-0.5)
        P.tt(sq[:, :, 0:n], XT[:, :, t0:t0 + n], rs[:, 0:n].unsqueeze(1).broadcast_to([128, 8, n]), ALU.mult,
             xk(t0, n) + [rs, sq], [sq])
        for (s0, sn, s) in segs(t0, n):
            for c in range(8):
                P.act(dst_fn(c, s0, sn), sq[:, c, s0 - t0:s0 - t0 + sn], AF.Identity,
                      [sq], dst_keys, bias=modT[:, l, (3 * n_idx) * 8 + c, s:s + 1],
                      scale=AG[:, l, n_idx, c, s:s + 1])

    def norm_ws(P, nmax, nb=1):
        return (Rot(lambda i: P.sbuf("sq%d" % i, [128, 8, nmax], F32), nb),
                Rot(lambda i: P.sbuf("lnv%d" % i, [128, nmax], F32), nb),
                Rot(lambda i: P.sbuf("rs%d" % i, [128, nmax], F32), nb),
                Rot(lambda i: P.psum("pss%d" % i, [128, 512]), nb),
                Rot(lambda i: P.sbuf("sqb%d" % i, [128, 8, nmax], BF16), nb))

    def ffn(l, which, halves):
        n_idx = 0 if which == 0 else 2
        wg_v = wg_d[l, which].rearrange("(kc p) n -> p kc n", p=128)
        wu_v = wu_d[l, which].rearrange("(kc p) n -> p kc n", p=128)
        wd_v = wd_d[l, which].rearrange("(f p) n -> p f n", p=128)
        for hi, groups in enumerate(halves):
            h0 = groups[0][0]
            NH = sum(g[1] for g in groups)
            ng_ = len(groups)
            tag = "f%d%d%d" % (l, which, hi)
            esA = ExitStack()
            AT = Buf(esA.enter_context(nc.sbuf_tensor(tag + "AT", [128, NF, NH], BF16)), NF * ng_, "AT")
            P = Prog(nc, sy, tag + "a")
            hT = P.sbuf("hT", [128, 8, NH], BF16, ng_)
            nmax = max(g[1] for g in groups)
            W = norm_ws(P, nmax)
            for gi, (t0, n) in enumerate(groups):
                norm_mod(P, W, l, n_idx, t0, n, lambda c, s0, sn: hT[:, c, s0 - h0:s0 - h0 + sn], [(hT, gi)])
            stg = P.sbuf("stg", [128, 8, 256], F32)
            stu = P.sbuf("stu", [128, 8, 256], F32)
            wgb = [P.sbuf("wgb%d" % i, [128, 8, 256], BF16) for i in range(2)]
            wub = [P.sbuf("wub%d" % i, [128, 8, 256], BF16) for i in range(2)]
            psg = [P.psum("psg%d" % i, [128, 512]) for i in range(2)]
            psu = [P.psum("psu%d" % i, [128, 512]) for i in range(2)]
            sga = [P.sbuf("sga%d" % i, [128, nmax], F32) for i in range(2)]
            it = 0
            NB = NF // 2

            def fetch_a(fb):
                P.dma(stg[:], wg_v[:, :, fb * 256:(fb + 1) * 256], writes=[stg])
                P.dma(stu[:], wu_v[:, :, fb * 256:(fb + 1) * 256], writes=[stu])

            def cast_a(fb):
                P.act(wgb[fb % 2][:], stg[:], AF.Copy, [stg], [wgb[fb % 2]])
                P.copy(wub[fb % 2][:], stu[:], [stu], [wub[fb % 2]])

            fetch_a(0)
            cast_a(0)
            for fb in range(NB):
                pb = fb % 2
                if fb + 1 < NB:
                    fetch_a(fb + 1)
                cnt = 0
                for f2 in range(2):
                    f = fb * 2 + f2
                    for gi, (t0, n) in enumerate(groups):
                        o0 = t0 - h0
                        pg, pu, sg = psg[it % 2], psu[it % 2], sga[it % 2]
                        it += 1
                        for kc in range(8):
                            P.mm(pg[:, 0:n], wgb[pb][:, kc, f2 * 128:(f2 + 1) * 128], hT[:, kc, o0:o0 + n], kc == 0, kc == 7,
                                 [wgb[pb], (hT, gi)], [pg])
                        for kc in range(8):
                            P.mm(pu[:, 0:n], wub[pb][:, kc, f2 * 128:(f2 + 1) * 128], hT[:, kc, o0:o0 + n], kc == 0, kc == 7,
                                 [wub[pb], (hT, gi)], [pu])
                        P.act(sg[:, 0:n], pg[:, 0:n], AF.Silu, [pg], [sg])
                        P.tt(AT[:, f, o0:o0 + n], sg[:, 0:n], pu[:, 0:n], ALU.mult, [sg, pu], [(AT, f * ng_ + gi)])
                        cnt += 1
                        if cnt == ng_ and fb + 1 < NB:
                            cast_a(fb + 1)
            P.emit()
            P = Prog(nc, sy, tag + "b")
            std = P.sbuf("std", [128, NF, 256], F32)
            wdb = [P.sbuf("wdb%d" % i, [128, NF, 256], BF16) for i in range(2)]
            psy = [P.psum("psy%d" % i, [128, 512]) for i in range(2)]
            it = 0

            def fetch_b(db):
                P.dma(std[:], wd_v[:, :, db * 256:(db + 1) * 256], writes=[std])
                P.act(wdb[db % 2][:], std[:], AF.Copy, [std], [wdb[db % 2]])

            fetch_b(0)
            for db in range(4):
                pb = db % 2
                if db + 1 < 4:
                    fetch_b(db + 1)
                for d2 in range(2):
                    d = db * 2 + d2
                    for gi, (t0, n) in enumerate(groups):
                        o0 = t0 - h0
                        py = psy[it % 2]
                        it += 1
                        for f in range(NF):
                            P.mm(py[:, 0:n], wdb[pb][:, f, d2 * 128:(d2 + 1) * 128], AT[:, f, o0:o0 + n], f == 0, f == NF - 1,
                                 [wdb[pb], (AT, f * ng_ + gi)], [py])
                        for (s0, sn, s) in segs(t0, n):
                            P.stt(XT[:, d, s0:s0 + sn], py[:, s0 - t0:s0 - t0 + sn], GH[:, l, n_idx, d, s:s + 1],
                                  XT[:, d, s0:s0 + sn], ALU.mult, ALU.add, [py] + xk(s0, sn), xk(s0, sn))
            P.emit()
            esA.close()

    def dump_xt(name):
        if name in dbg_d:
            P = Prog(nc, sy, "dbg" + name)
            P.dma(dbg_d[name], XT[:].rearrange("p c t -> p (c t)"), reads=[XT])
            P.emit()

    def make_xn(l):
        P = Prog(nc, sy, "xn%d" % l)
        W = norm_ws(P, 384, 2)
        hb = Rot(lambda i: P.sbuf("hb%d" % i, [128, 8, 384], BF16), 2)
        for (t0, n) in FULL[0] + FULL[1]:
            h = hb.next()
            norm_mod(P, W, l, 1, t0, n, lambda c, s0, sn, h=h, t0=t0: h[:, c, s0 - t0:s0 - t0 + sn], [h])
            P.dma(xn_s[:, :, t0:t0 + n], h[:, :, 0:n], reads=[h], writes=[xn_s])
        P.emit()

    def load_w(P, dst, dst_ap, src_ap, stage, stage_ap):
        P.dma(stage_ap, src_ap, writes=[stage])
        P.copy(dst_ap, stage_ap, [stage], [dst], eng="gpsimd")

    def out_proj(l, wo_d, groups):
        P = Prog(nc, sy, "op%d" % l)
        wo = P.sbuf("wo", [128, 8, D], BF16)
        stwR = Rot(lambda i: P.sbuf("stw%d" % i, [128, 8, 256], F32), 2)
        wv = wo_d.rearrange("(kc p) n -> p kc n", p=128)
        for q in range(4):
            stw = stwR.next()
            P.dma(stw[:], wv[:, :, q * 256:(q + 1) * 256], writes=[stw])
            if q % 2 == 0:
                P.act(wo[:, :, q * 256:(q + 1) * 256], stw[:], AF.Copy, [stw], [wo])
            else:
                P.copy(wo[:, :, q * 256:(q + 1) * 256], stw[:], [stw], [wo])
        mb = Rot(lambda i: P.sbuf("mb%d" % i, [128, 8, 512], BF16), 2)
        pp = Rot(lambda i: P.psum("pp%d" % i, [128, 512]), 3)
        for (t0, n) in groups:
            m = mb.next()
            P.dma(m[:, :, 0:n], mix_s[:, :, t0:t0 + n], reads=[mix_s], writes=[m])
            for dc in range(8):
                ps = pp.next()
                for kc in range(8):
                    P.mm(ps[:, 0:n], wo[:, kc, dc * 128:(dc + 1) * 128], m[:, kc, 0:n], kc == 0, kc == 7, [wo, m], [ps])
                for (s0, sn, s) in segs(t0, n):
                    P.stt(XT[:, dc, s0:s0 + sn], ps[:, s0 - t0:s0 - t0 + sn], GH[:, l, 1, dc, s:s + 1],
                          XT[:, dc, s0:s0 + sn], ALU.mult, ALU.add, [ps] + xk(s0, sn), xk(s0, sn))
        P.emit()

    def swa_prep():
        P = Prog(nc, sy, "swp")
        wv_ = odw_d.rearrange("(kc p) n -> p kc n", p=128)
        odc = P.sbuf("odc", [128, 11], F32)
        P.dma(odc[:], odcol_d, writes=[odc])
        bdf = P.sbuf("bdf", [128, 2], F32)
        P.dma(bdf[:], bdsel_d, writes=[bdf])
        bds = P.sbuf("bds", [128, 2], BF16)
        P.copy(bds[:], bdf[:], [bdf], [bds])
        cs = P.sbuf("cs", [128, 2, SEQ], F32)
        P.dma(cs[:], swcs_d, writes=[cs])
        st = P.sbuf("st", [128, 8, 512], F32)
        wq = P.sbuf("wq", [128, 8, 4, 2, 64], BF16)
        wk = P.sbuf("wk", [128, 8, 128], BF16)
        wvv = P.sbuf("wvv", [128, 8, 128], BF16)
        P.dma(st[:], wv_[:, :, 1568:2080], writes=[st])
        st5 = st[:].rearrange("p k (g j d) -> p k g j d", g=2, j=4)
        for g in range(2):
            P.copy(wq[:, :, :, g, :], st5[:, :, g, :, :], [st], [wq], eng="gpsimd")
        load_w(P, wk, wk[:], wv_[:, :, 2080:2208], st, st[:, :, 0:128])
        load_w(P, wvv, wvv[:], wv_[:, :, 2208:2336], st, st[:, :, 0:128])
        xg = Rot(lambda i: P.sbuf("xg%d" % i, [128, 8, 512], BF16), 2)
        pp = Rot(lambda i: P.psum("pp%d" % i, [128, 512]), 3)
        pssR = Rot(lambda i: P.psum("pss%d" % i, [128, 512]), 2)
        prot = P.psum("prot", [128, 512])
        prk = P.psum("prk", [128, NT, 2])
        pv = P.psum("pv", [128, 128])
        sqR = Rot(lambda i: P.sbuf("sq%d" % i, [128, 512], BF16), 2)
        lnvR = Rot(lambda i: P.sbuf("lnv%d" % i, [128, 512], F32), 2)
        rsR = Rot(lambda i: P.sbuf("rs%d" % i, [128, 512], F32), 2)
        qgR = Rot(lambda i: P.sbuf("qg%d" % i, [128, 512], BF16), 2)
        t1R = Rot(lambda i: P.sbuf("t1%d" % i, [128, 512], F32), 2)
        t2R = Rot(lambda i: P.sbuf("t2%d" % i, [128, 512], F32), 2)
        qo = Rot(lambda i: P.sbuf("qo%d" % i, [128, 512], BF16), 2)
        vd = P.sbuf("vd", [128, NT, 2, 2, 64], BF16)
        rsk = P.sbuf("rsk", [128, NT * 2], F32)
        for (t0, n) in GROUPS5:
            x = xg.next()
            P.dma(x[:, :, 0:n], xn_s[:, :, t0:t0 + n], reads=[xn_s], writes=[x])
            lat = t0 >= CTX
            l0 = t0 - CTX
            if lat:
                for j in range(4):
                    ps = pp.next()
                    for kc in range(8):
                        P.mm(ps[:, 0:n], wq[:, kc, j].rearrange("p g d -> p (g d)"), x[:, kc, 0:n], kc == 0, kc == 7,
                             [wq, x], [ps])
                    sq, pss, lnv, rs, qg, t1, t2 = (sqR.next(), pssR.next(), lnvR.next(), rsR.next(), qgR.next(),
                                                    t1R.next(), t2R.next())
                    P.act(sq[:, 0:n], ps[:, 0:n], AF.Square, [ps], [sq])
                    P.mm(pss[:, 0:n], cstb[:, C_BD64, :], sq[:, 0:n], True, True, [cstb, sq], [pss])
                    P.act(lnv[:, 0:n], pss[:, 0:n], AF.Ln, [pss], [lnv], bias=EPS, scale=1.0 / 64)
                    P.act(rs[:, 0:n], lnv[:, 0:n], AF.Exp, [lnv], [rs], scale=-0.5)
                    P.act(qg[:, 0:n], ps[:, 0:n], AF.Identity, [ps], [qg], scale=odc[:, 0:1])
                    P.mm(prot[:, 0:n], cstb[:, C_ROT64, :], qg[:, 0:n], True, True, [cstb, qg], [prot])
                    P.tt(t1[:, 0:n], qg[:, 0:n], cs[:, 0, l0:l0 + n], ALU.mult, [qg, cs], [t1])
                    P.tt(t2[:, 0:n], prot[:, 0:n], cs[:, 1, l0:l0 + n], ALU.mult, [prot, cs], [t2])
                    P.tt(t1[:, 0:n], t1[:, 0:n], t2[:, 0:n], ALU.add, [t1, t2], [t1])
                    q = qo.next()
                    P.tt(q[:, 0:n], t1[:, 0:n], rs[:, 0:n], ALU.mult, [t1, rs], [q])
                    P.dma(swq_s[:, j, l0:l0 + n], q[:, 0:n], reads=[q], writes=[swq_s])
            ps = pp.next()
            for kc in range(8):
                P.mm(ps[:, 0:n], wk[:, kc, :], x[:, kc, 0:n], kc == 0, kc == 7, [wk, x], [ps])
            sq, qg, t1, t2 = sqR.next(), qgR.next(), t1R.next(), t2R.next()
            P.act(sq[:, 0:n], ps[:, 0:n], AF.Square, [ps], [sq])
            for ti in range(n // 128):
                tt = t0 // 128 + ti
                P.mm(prk[:, tt, :], sq[:, ti * 128:(ti + 1) * 128], bds[:], True, True, [sq, bds], [prk])
            q = qo.next()
            if lat:
                P.act(qg[:, 0:n], ps[:, 0:n], AF.Identity, [ps], [qg], scale=odc[:, 1:2])
                P.mm(prot[:, 0:n], cstb[:, C_ROT64, :], qg[:, 0:n], True, True, [cstb, qg], [prot])
                P.tt(t1[:, 0:n], qg[:, 0:n], cs[:, 0, l0:l0 + n], ALU.mult, [qg, cs], [t1])
                P.tt(t2[:, 0:n], prot[:, 0:n], cs[:, 1, l0:l0 + n], ALU.mult, [prot, cs], [t2])
                P.tt(q[:, 0:n], t1[:, 0:n], t2[:, 0:n], ALU.add, [t1, t2], [q])
            else:
                P.act(q[:, 0:n], ps[:, 0:n], AF.Identity, [ps], [q], scale=odc[:, 1:2])
            P.dma(swk_s[:, t0:t0 + n], q[:, 0:n], reads=[q], writes=[swk_s])
            for ti in range(n // 128):
                tt = t0 // 128 + ti
                for kc in range(8):
                    P.mm(pv[:], x[:, kc, ti * 128:(ti + 1) * 128], wvv[:, kc, :], kc == 0, kc == 7, [x, wvv], [pv])
                pv3 = pv[:].rearrange("p (g d) -> p g d", g=2)
                P.copy(vd[:, tt, :, 0, :], pv3, [pv], [vd])
                P.act(vd[:, tt, :, 1, :], pv3, AF.Copy, [pv], [vd])
        P.act(rsk[:], prk[:].rearrange("p a b -> p (a b)"), AF.Ln, [prk], [rsk], bias=EPS, scale=1.0 / 64)
        P.act(rsk[:], rsk[:], AF.Exp, [rsk], [rsk], scale=-0.5)
        P.ts(rsk[:], rsk[:], 0.125, ALU.mult, [rsk], [rsk])
        P.dma(swr_s[:], rsk[:], reads=[rsk], writes=[swr_s])
        P.dma(swv_s[:], vd[:].rearrange("p a b c d -> p (a b c d)"), reads=[vd], writes=[swv_s])
        P.emit()

    def swa_core():
        P = Prog(nc, sy, "swc")
        q = P.sbuf("q", [128, 4, SEQ], BF16)
        k = P.sbuf("k", [128, T], BF16)
        v = P.sbuf("v", [128, NT, 2, 128], BF16)
        rsk = P.sbuf("rsk", [128, NT, 2], F32)
        odc = P.sbuf("odc", [128, 11], F32)
        esk = P.sbuf("esk", [128, 8], F32)
        mlo = P.sbuf("mlo", [128, 128], BF16)
        mhi = P.sbuf("mhi", [128, 128], BF16)
        P.dma(q[:], swq_s[:], reads=[swq_s], writes=[q])
        P.dma(k[:], swk_s[:], reads=[swk_s], writes=[k])
        P.dma(v[:].rearrange("p a b c -> p (a b c)"), swv_s[:], reads=[swv_s], writes=[v])
        P.dma(rsk[:].rearrange("p a b -> p (a b)"), swr_s[:], reads=[swr_s], writes=[rsk])
        P.dma(odc[:], odcol_d, writes=[odc])
        P.act(esk[:], odc[:, 3:11], AF.Exp, [odc], [esk])
        P.copy(mlo[:], cst[:, C_LI, :], [cst], [mlo])
        P.copy(mhi[:], cst[:, C_UI, :], [cst], [mhi])
        msw = P.sbuf("msw", [128, 4, SEQ], BF16)
        pS = Rot(lambda i: P.psum("pS%d" % i, [128, 4, 128]), 4)
        pO = Rot(lambda i: P.psum("pO%d" % i, [128, 4, 128]), 2)
        pM = Rot(lambda i: P.psum("pM%d" % i, [128, 4, 128]), 2)
        pT = Rot(lambda i: P.sbuf("pT%d" % i, [128, 4, 128], BF16), 4)
        den = P.sbuf("den", [128, 4, 128], F32)
        rden = P.sbuf("rden", [128, 4, 128], F32)
        for n in range(16):
            for g in range(2):
                kts = [(0, None), (1, None)]
                if n - 1 >= 0:
                    kts.append((n + 1, mlo))
                kts.append((n + 2, None))
                if n + 1 <= 15:
                    kts.append((n + 3, mhi))
                po = pO.next()
                pm = pM.next()
                def scores(kt, mask):
                    ps = pS.next()
                    P.mm(ps[:], k[g * 64:(g + 1) * 64, kt * 128:(kt + 1) * 128],
                         q[g * 64:(g + 1) * 64, :, n * 128:(n + 1) * 128], True, True, [k, q], [ps])
                    pt = pT.next()
                    P.act(pt[:], ps[:], AF.Exp, [ps, rsk], [pt], scale=rsk[:, kt, g:g + 1])
                    if mask is not None:
                        P.tt(pt[:], pt[:], mask[:].unsqueeze(1).broadcast_to([128, 4, 128]), ALU.mult, [pt, mask], [pt])
                    return pt

                nxt = scores(*kts[0])
                for ki, (kt, mask) in enumerate(kts):
                    pt = nxt
                    if ki + 1 < len(kts):
                        nxt = scores(*kts[ki + 1])
                    P.mm(po[:], v[:, kt, g, :], pt[:], ki == 0, ki == len(kts) - 1, [v, pt], [po])
                    P.mm(pm[:], ones_b[:], pt[:], ki == 0, ki == len(kts) - 1, [ones_b, pt], [pm])
                P.tt(den[:], pm[:], esk[:, 4 * g:4 * g + 4].unsqueeze(2).broadcast_to([128, 4, 128]), ALU.add,
                     [pm, esk], [den])
                P.act(rden[:], den[:], AF.Ln, [den], [rden])
                P.act(rden[:], rden[:], AF.Exp, [rden], [rden], scale=-1.0)
                for par in range(2):
                    sl = slice(par * 64, (par + 1) * 64)
                    P.tt(msw[sl, 2 * g:2 * g + 2, n * 128:(n + 1) * 128], po[sl, par::2, :], rden[sl, par::2, :],
                         ALU.mult, [po, rden], [msw])
        P.dma(mix_s[:, 4:8, CTX:T], msw[:], reads=[msw], writes=[mix_s])
        P.emit()

    def gla_prep():
        P = Prog(nc, sy, "glp")
        wv_ = odw_d.rearrange("(kc p) n -> p kc n", p=128)
        stR = Rot(lambda i: P.sbuf("st%d" % i, [128, 8, 256], F32), 1)
        wqk = P.sbuf("wqk", [128, 8, 512], BF16)
        wvv = P.sbuf("wvv", [128, 8, 512], BF16)
        wgz = P.sbuf("wgz", [128, 8, 32], BF16)
        wr = P.sbuf("wr", [128, 8, 512], BF16)
        for (wt_, c0_) in ((wqk, 0), (wvv, 512), (wr, 1056)):
            for hf in range(2):
                st = stR.next()
                load_w(P, wt_, wt_[:, :, hf * 256:(hf + 1) * 256], wv_[:, :, c0_ + hf * 256:c0_ + (hf + 1) * 256], st, st[:])
        st = stR.next()
        load_w(P, wgz, wgz[:], wv_[:, :, 1024:1056], st, st[:, :, 0:32])
        g2f = P.sbuf("g2f", [17, 2, 256], F32)
        g2b = P.sbuf("g2b", [17, 2, 256], BF16)
        P.dma(g2f[:], gw2b_d, writes=[g2f])
        P.copy(g2b[:], g2f[:], [g2f], [g2b])
        msk = P.sbuf("msk", [128, 512], F32)
        P.memset(msk[:], 1.0, [msk])
        for i in range(4):
            P.memset(msk[:, i * 128:i * 128 + 1], 0.0, [msk])
        gz = [P.sbuf("gz%d" % d, [17, 512], BF16) for d in range(2)]
        for d in range(2):
            P.memset(gz[d][:], 1.0, [gz[d]])
        xg = Rot(lambda i: P.sbuf("xg%d" % i, [128, 8, 512], BF16), 2)
        pp = Rot(lambda i: P.psum("pp%d" % i, [128, 512]), 4)
        ptr_ = P.psum("ptr", [128, 2, 128], BF16)
        pgzD = [P.psum("pgz%d" % i, [16, 512]) for i in range(2)]
        qkraw = P.sbuf("qkraw", [128, 4, 512], F32)
        spD = [P.sbuf("sp%d" % i, [128, 2, 512], F32) for i in range(2)]
        preD = [P.sbuf("pre%d" % i, [128, 2, 512], F32) for i in range(2)]
        bcD = [P.sbuf("bc%d" % i, [128, 2, 512], F32) for i in range(2)]
        remD = [P.sbuf("rem%d" % i, [128, 2, 512], F32) for i in range(2)]
        eeR = Rot(lambda i: P.sbuf("ee%d" % i, [128, 2, 512], F32), 2)
        etD = [P.sbuf("et%d" % i, [128, 2, 4], F32) for i in range(2)]
        ob = Rot(lambda i: P.sbuf("ob%d" % i, [128, 2, 512], BF16), 3)
        kdt = Rot(lambda i: P.sbuf("kdt%d" % i, [128, 4, 256], BF16), 2)
        vtm = Rot(lambda i: P.sbuf("vtm%d" % i, [128, 4, 512], BF16), 2)
        rg = Rot(lambda i: P.sbuf("rg%d" % i, [128, 4, 512], BF16), 1)
        idb = P.sbuf("idb", [128, 128], BF16)
        P.copy(idb[:], cst[:, C_ID, :], [cst], [idb])
        for (t0, n) in GROUPS5:
            nt = n // 128
            x = xg.next()
            P.dma(x[:, :, 0:n], xn_s[:, :, t0:t0 + n], reads=[xn_s], writes=[x])
            for cq in range(4):
                ps = pp.next()
                for kc in range(8):
                    P.mm(ps[:, 0:n], wqk[:, kc, cq * 128:(cq + 1) * 128], x[:, kc, 0:n], kc == 0, kc == 7, [wqk, x], [ps])
                P.act(qkraw[:, cq, 0:n], ps[:, 0:n], AF.Copy, [ps], [qkraw])
            vt = vtm.next()
            for ti in range(nt):
                ps = pp.next()
                for kc in range(8):
                    P.mm(ps[:], x[:, kc, ti * 128:(ti + 1) * 128], wvv[:, kc, :], kc == 0, kc == 7, [x, wvv], [ps])
                P.act(vt[:, ti, :], ps[:], AF.Copy, [ps], [vt])
            P.dma(glv_s[t0:t0 + n, :].rearrange("(a p) f -> p a f", p=128), vt[:, 0:nt, :], reads=[vt], writes=[glv_s])
            if t0 >= CTX:
                r = rg.next()
                for h in range(4):
                    ps = pp.next()
                    for kc in range(8):
                        P.mm(ps[:, 0:n], wr[:, kc, h * 128:(h + 1) * 128], x[:, kc, 0:n], kc == 0, kc == 7, [wr, x], [ps])
                    P.act(r[:, h, 0:n], ps[:, 0:n], AF.Silu, [ps], [r])
                P.dma(glr_s[:, :, t0 - CTX:t0 - CTX + n], r[:, :, 0:n], reads=[r], writes=[glr_s])
            for d in range(2):
                for kc in range(8):
                    P.mm(pgzD[d][:, 0:n], wgz[:, kc, 16 * d:16 * d + 16], x[:, kc, 0:n], kc == 0, kc == 7, [wgz, x], [pgzD[d]])
                P.act(gz[d][0:16, 0:n], pgzD[d][:, 0:n], AF.Copy, [pgzD[d]], [gz[d]])
            for c in range(2):
                for d in range(2):
                    sp, pre = spD[d], preD[d]
                    ps = pp.next()
                    P.mm(ps[:, 0:n], g2b[:, d, c * 128:(c + 1) * 128], gz[d][:, 0:n], True, True, [g2b, gz[d]], [ps])
                    P.act(sp[:, c, 0:n], ps[:, 0:n], AF.Exp, [ps], [sp], scale=-1.0)
                    P.act(sp[:, c, 0:n], sp[:, c, 0:n], AF.Ln, [sp], [sp], bias=1.0)
                    P.op("vector", lambda e, c=c, n=n, pre=pre, sp=sp: e.tensor_tensor_scan(
                        out=pre[:, c, 0:n], data0=msk[:, 0:n], data1=sp[:, c, 0:n], initial=0.0,
                        op0=ALU.mult, op1=ALU.add), [msk, sp], [pre])
            for d in range(2):
                sp, pre, bc, rem, et = spD[d], preD[d], bcD[d], remD[d], etD[d]
                pre4 = pre[:, :, 0:n].rearrange("p c (a b) -> p c a b", b=128)
                totb = pre4[:, :, :, 127:128].broadcast_to([128, 2, nt, 128])
                bc4 = bc[:, :, 0:n].rearrange("p c (a b) -> p c a b", b=128)
                rem4 = rem[:, :, 0:n].rearrange("p c (a b) -> p c a b", b=128)
                if d == 0:
                    P.copy(bc[:, :, 0:n], pre[:, :, 0:n], [pre], [bc], eng="gpsimd")
                    P.tt(rem4, totb, pre4, ALU.subtract, [pre], [rem])
                else:
                    P.tt(rem[:, :, 0:n], pre[:, :, 0:n], sp[:, :, 0:n], ALU.subtract, [pre, sp], [rem])
                    P.tt(bc4, totb, rem4, ALU.subtract, [pre, rem], [bc])
                P.act(et[:, :, 0:nt], pre4[:, :, :, 127], AF.Exp, [pre], [et], scale=-1.0 / 16)
                P.dma(glet_s[d][:, :, t0 // 128:t0 // 128 + nt], et[:, :, 0:nt], reads=[et], writes=[glet_s[d]])
            outs = []
            for d in range(2):
                bc, rem = bcD[d], remD[d]
                ee = eeR.next()
                P.act(ee[:, :, 0:n], bc[:, :, 0:n], AF.Exp, [bc], [ee], scale=-1.0 / 16)
                o = ob.next()
                P.stt(o[:, :, 0:n], qkraw[:, 0:2, 0:n], 0.125, ee[:, :, 0:n], ALU.mult, ALU.mult, [qkraw, ee], [o])
                P.dma(glq_s[d][:, :, t0:t0 + n], o[:, :, 0:n], reads=[o], writes=[glq_s[d]])
                ee = eeR.next()
                P.act(ee[:, :, 0:n], bc[:, :, 0:n], AF.Exp, [bc], [ee], scale=1.0 / 16)
                o = ob.next()
                P.tt(o[:, :, 0:n], qkraw[:, 2:4, 0:n], ee[:, :, 0:n], ALU.mult, [qkraw, ee], [o])
                P.dma(glk_s[d][:, :, t0:t0 + n], o[:, :, 0:n], reads=[o], writes=[glk_s[d]])
                ee = eeR.next()
                P.act(ee[:, :, 0:n], rem[:, :, 0:n], AF.Exp, [rem], [ee], scale=-1.0 / 16)
                o = ob.next()
                P.tt(o[:, :, 0:n], qkraw[:, 2:4, 0:n], ee[:, :, 0:n], ALU.mult, [qkraw, ee], [o])
                kd = kdt.next()
                for ti in range(nt):
                    for c in range(2):
                        P.tr(ptr_[:, c, :], o[:, c, ti * 128:(ti + 1) * 128], idb[:], [o, idb], [ptr_])
                    P.copy(kd[:, ti, :], ptr_[:].rearrange("p a b -> p (a b)"), [ptr_], [kd])
                P.dma(glkd_s[d][t0:t0 + n, :].rearrange("(a p) f -> p a f", p=128), kd[:, 0:nt, :], reads=[kd],
                      writes=[glkd_s[d]])
        P.emit()

    def gla_core():
        P = Prog(nc, sy, "glc")
        odc = P.sbuf("odc", [128, 11], F32)
        P.dma(odc[:], odcol_d, writes=[odc])
        ogla = P.sbuf("ogla", [128, 4, SEQ], F32, 16)
        for i in range(4):
            P.memset(ogla[:, i, :], 0.0, [ogla], eng="gpsimd" if i % 2 else "vector")
        S = [[P.sbuf("S%d%d" % (d, c), [128, 128], F32) for c in range(2)] for d in range(2)]
        Sb = [[P.sbuf("Sb%d%d" % (d, h), [128, 128], BF16) for h in range(4)] for d in range(2)]
        qd = Rot(lambda i: P.sbuf("qd%d" % i, [128, 2, 128], BF16), 4)
        kp = Rot(lambda i: P.sbuf("kp%d" % i, [128, 2, 128], BF16), 4)
        kd = Rot(lambda i: P.sbuf("kd%d" % i, [128, 256], BF16), 4)
        vt = Rot(lambda i: P.sbuf("vt%d" % i, [128, 512], BF16), 4)
        et = [P.sbuf("et%d" % d, [128, 2, NT], F32) for d in range(2)]
        aT = Rot(lambda i: P.sbuf("aT%d" % i, [128, 4, 128], BF16), 2)
        pA0 = Rot(lambda i: P.psum("pA0%d" % i, [128, 2, 128]), 1)
        pA1 = Rot(lambda i: P.psum("pA1%d" % i, [128, 2, 128]), 1)
        pO = Rot(lambda i: P.psum("pO%d" % i, [128, 4, 128]), 2)
        pSt = Rot(lambda i: P.psum("pSt%d" % i, [128, 128]), 3)
        orders = [list(range(NT)), [1, 0] + list(range(NT - 1, 1, -1))]
        masks = [cst[:, C_UI, :], cst[:, C_LI, :]]
        for d in range(2):
            P.dma(et[d][:], glet_s[d][:], reads=[glet_s[d]], writes=[et[d]])
            for c in range(2):
                P.memset(S[d][c][:], 0.0, [S[d][c]])
            for h in range(4):
                P.memset(Sb[d][h][:], 0.0, [Sb[d][h]])
        for s_ in range(NT):
            for d in range(2):
                tt = orders[d][s_]
                mask = masks[d]
                q_, k_, kd_, v_ = qd.next(), kp.next(), kd.next(), vt.next()
                tsl = slice(tt * 128, (tt + 1) * 128)
                P.dma(q_[:], glq_s[d][:, :, tsl], reads=[glq_s[d]], writes=[q_])
                P.dma(k_[:], glk_s[d][:, :, tsl], reads=[glk_s[d]], writes=[k_])
                P.dma(kd_[:], glkd_s[d][tsl, :], reads=[glkd_s[d]], writes=[kd_])
                P.dma(v_[:], glv_s[tsl, :], reads=[glv_s], writes=[v_])
                if tt >= 2:
                    a_ = aT.next()
                    for par, pa_ in enumerate((pA0, pA1)):
                        pa = pa_.next()
                        hs = slice(par * 64, par * 64 + 64)
                        for c in range(2):
                            P.mm(pa[:, c, :], k_[hs, c, :], q_[hs, c, :], True, True, [k_, q_], [pa])
                        P.tt(a_[:, par::2, :], pa[:], mask.unsqueeze(1).broadcast_to([128, 2, 128]), ALU.mult,
                             [pa, cst], [a_])
                    po = pO.next()
                    for h in range(4):
                        P.mm(po[:, h, :], v_[:, h * 128:(h + 1) * 128], a_[:, h, :], True, False, [v_, a_], [po])
                        P.mm(po[:, h, :], Sb[d][h][:], q_[:, h // 2, :], False, True, [Sb[d][h], q_], [po])
                    lt = tt - 2
                    osl = ogla[:, :, lt * 128:(lt + 1) * 128]
                    P.tt(osl, osl, po[:], ALU.add, [po, (ogla, lt)], [(ogla, lt)])
                for c in range(2):
                    ps = pSt.next()
                    for hh in range(2):
                        h = 2 * c + hh
                        P.mm(ps[hh * 64:(hh + 1) * 64, :], kd_[:, h * 64:(h + 1) * 64], v_[:, h * 128:(h + 1) * 128],
                             True, True, [kd_, v_], [ps])
                    P.stt(S[d][c][:], S[d][c][:], et[d][:, c, tt:tt + 1], ps[:], ALU.mult, ALU.add,
                          [S[d][c], et[d], ps], [S[d][c]])
                    for hh in range(2):
                        hs = slice(hh * 64, hh * 64 + 64)
                        P.act(Sb[d][2 * c + hh][hs, :], S[d][c][hs, :], AF.Copy, [S[d][c]], [Sb[d][2 * c + hh]])
        sqR = Rot(lambda i: P.sbuf("sq%d" % i, [128, 512], BF16), 2)
        lnvR = Rot(lambda i: P.sbuf("lnv%d" % i, [128, 512], F32), 2)
        rsR = Rot(lambda i: P.sbuf("rs%d" % i, [128, 512], F32), 2)
        tmpR = Rot(lambda i: P.sbuf("tmp%d" % i, [128, 512], F32), 2)
        pss = P.psum("pss", [128, 512])
        rg = Rot(lambda i: P.sbuf("rg%d" % i, [128, 4, 512], BF16), 2)
        mo = Rot(lambda i: P.sbuf("mo%d" % i, [128, 4, 512], BF16), 2)
        for gi in range(4):
            r = rg.next()
            m = mo.next()
            gsl = slice(gi * 512, (gi + 1) * 512)
            keys = [(ogla, i) for i in range(gi * 4, gi * 4 + 4)]
            P.dma(r[:], glr_s[:, :, gsl], reads=[glr_s], writes=[r])
            for h in range(4):
                sq, lnv, rs, tmp = sqR.next(), lnvR.next(), rsR.next(), tmpR.next()
                P.act(sq[:], ogla[:, h, gsl], AF.Square, keys, [sq])
                P.mm(pss[:], ones_b[:], sq[:], True, True, [ones_b, sq], [pss])
                P.act(lnv[:], pss[:], AF.Ln, [pss], [lnv], bias=EPS, scale=1.0 / 128)
                P.act(rs[:], lnv[:], AF.Exp, [lnv], [rs], scale=-0.5)
                P.tt(tmp[:], ogla[:, h, gsl], rs[:], ALU.mult, keys + [rs], [tmp])
                P.stt(m[:, h, :], tmp[:], odc[:, 2:3], r[:, h, :], ALU.mult, ALU.mult, [tmp, odc, r], [m])
            P.dma(mix_s[:, 0:4, CTX + gi * 512:CTX + (gi + 1) * 512], m[:], reads=[m], writes=[mix_s])
        P.emit()

    def mla_prep():
        P = Prog(nc, sy, "mlp")
        wv_ = evw_d.rearrange("(kc p) n -> p kc n", p=128)
        evc = P.sbuf("evc", [128, 8], F32)
        P.dma(evc[:], evcol_d, writes=[evc])
        cs = P.sbuf("cs", [96, 2, SEQ], F32)
        P.dma(cs[:], mlcs_d, writes=[cs])
        st = P.sbuf("st", [128, 8, 384], F32)
        wcq = P.sbuf("wcq", [128, 8, 384], BF16)
        wckv = P.sbuf("wckv", [128, 8, 256], BF16)
        wkr = P.sbuf("wkr", [128, 8, 32], BF16)
        wqu = P.sbuf("wqu", [128, 3, 768], BF16)
        wkK = P.sbuf("wkK", [128, 2, 8, 64], BF16)
        wkV = P.sbuf("wkV", [128, 2, 8, 64], BF16)
        load_w(P, wcq, wcq[:], wv_[:, :, 0:384], st, st[:])
        load_w(P, wckv, wckv[:], wv_[:, :, 384:640], st, st[:, :, 0:256])
        load_w(P, wkr, wkr[:], wv_[:, :, 640:672], st, st[:, :, 0:32])
        stf = st[:].rearrange("p a b -> p (a b)")
        qu_v = evqu_d.rearrange("(kc p) n -> p kc n", p=128)
        st_qu = stf[:, 0:3 * 768].rearrange("p (a b) -> p a b", a=3)
        load_w(P, wqu, wqu[:], qu_v, st, st_qu)
        kv_v = evkv_d.rearrange("(kc p) n -> p kc n", p=128)
        st_kv = stf[:, 0:2048].rearrange("p (a b) -> p a b", a=2)
        P.dma(st_kv, kv_v, writes=[st])
        st_kv5 = stf[:, 0:2048].rearrange("p (a h t d) -> p a h t d", a=2, h=8, t=2)
        P.copy(wkK[:], st_kv5[:, :, :, 0, :], [st], [wkK], eng="gpsimd")
        P.copy(wkV[:], st_kv5[:, :, :, 1, :], [st], [wkV], eng="gpsimd")
        xg = Rot(lambda i: P.sbuf("xg%d" % i, [128, 8, 512], BF16), 2)
        pp = Rot(lambda i: P.psum("pp%d" % i, [128, 512]), 3)
        pssR = Rot(lambda i: P.psum("pss%d" % i, [128, 512]), 2)
        protR = Rot(lambda i: P.psum("prot%d" % i, [128, 512]), 1)
        prk = P.psum("prk", [128, NT, 8])
        prkr = P.psum("prkr", [128, NT])
        raw = P.sbuf("raw", [128, 3, 512], F32)
        sq3 = P.sbuf("sq3", [128, 3, 512], BF16)
        lnvR = Rot(lambda i: P.sbuf("lnv%d" % i, [128, 512], F32), 2)
        rsR = Rot(lambda i: P.sbuf("rs%d" % i, [128, 512], F32), 2)
        cqn = P.sbuf("cqn", [128, 3, 512], BF16)
        ckvn = P.sbuf("ckvn", [128, 2, 512], BF16)
        sqR = Rot(lambda i: P.sbuf("sq%d" % i, [128, 512], BF16), 2)
        qgR = Rot(lambda i: P.sbuf("qg%d" % i, [128, 512], BF16), 2)
        t1R = Rot(lambda i: P.sbuf("t1%d" % i, [128, 512], F32), 2)
        t2R = Rot(lambda i: P.sbuf("t2%d" % i, [128, 512], F32), 2)
        qo = Rot(lambda i: P.sbuf("qo%d" % i, [128, 512], BF16), 3)
        vtm = Rot(lambda i: P.sbuf("vtm%d" % i, [128, 4, 512], BF16), 2)
        rsk = P.sbuf("rsk", [128, NT * 8], F32)
        R96 = cstb[0:96, C_ROT96, 0:96]

        def rope_out(q, src, rs, n, lat, l0, with_rs):
            if lat:
                prot, t1, t2 = protR.next(), t1R.next(), t2R.next()
                P.mm(prot[0:96, 0:n], R96, src[0:96, 0:n], True, True, [cstb, src], [prot])
                P.tt(t1[0:96, 0:n], src[0:96, 0:n], cs[:, 0, l0:l0 + n], ALU.mult, [src, cs], [t1])
                P.tt(t2[0:96, 0:n], prot[0:96, 0:n], cs[:, 1, l0:l0 + n], ALU.mult, [prot, cs], [t2])
                if with_rs:
                    P.tt(t1[0:96, 0:n], t1[0:96, 0:n], t2[0:96, 0:n], ALU.add, [t1, t2], [t1])
                    P.tt(q[0:96, 0:n], t1[0:96, 0:n], rs[0:96, 0:n], ALU.mult, [t1, rs], [q])
                else:
                    P.tt(q[0:96, 0:n], t1[0:96, 0:n], t2[0:96, 0:n], ALU.add, [t1, t2], [q])
            else:
                if with_rs:
                    P.tt(q[0:96, 0:n], src[0:96, 0:n], rs[0:96, 0:n], ALU.mult, [src, rs], [q])
                else:
                    P.copy(q[0:96, 0:n], src[0:96, 0:n], [src], [q])

        for (t0, n) in GROUPS5:
            nt = n // 128
            lat = t0 >= CTX
            l0 = t0 - CTX
            x = xg.next()
            P.dma(x[:, :, 0:n], xn_s[:, :, t0:t0 + n], reads=[xn_s], writes=[x])
            for (wt, nch, off, dstn, dim) in ((wcq, 3, 0, cqn, 384), (wckv, 2, 3, ckvn, 256)):
                for c in range(nch):
                    ps = pp.next()
                    for kc in range(8):
                        P.mm(ps[:, 0:n], wt[:, kc, c * 128:(c + 1) * 128], x[:, kc, 0:n], kc == 0, kc == 7, [wt, x], [ps])
                    P.act(raw[:, c, 0:n], ps[:, 0:n], AF.Copy, [ps], [raw])
                    P.act(sq3[:, c, 0:n], ps[:, 0:n], AF.Square, [ps], [sq3])
                pss, lnv, rs = pssR.next(), lnvR.next(), rsR.next()
                for c in range(nch):
                    P.mm(pss[:, 0:n], ones_b[:], sq3[:, c, 0:n], c == 0, c == nch - 1, [ones_b, sq3], [pss])
                P.act(lnv[:, 0:n], pss[:, 0:n], AF.Ln, [pss], [lnv], bias=EPS, scale=1.0 / dim)
                P.act(rs[:, 0:n], lnv[:, 0:n], AF.Exp, [lnv], [rs], scale=-0.5)
                for c in range(nch):
                    P.stt(dstn[:, c, 0:n], raw[:, c, 0:n], evc[:, off + c:off + c + 1], rs[:, 0:n], ALU.mult, ALU.mult,
                          [raw, evc, rs], [dstn])
            vt = vtm.next()
            for ti in range(nt):
                ps = pp.next()
                for c in range(2):
                    P.mm(ps[:], ckvn[:, c, ti * 128:(ti + 1) * 128], wkV[:, c].rearrange("p h d -> p (h d)"),
                         c == 0, c == 1, [ckvn, wkV], [ps])
                P.act(vt[:, ti, :], ps[:], AF.Copy, [ps], [vt])
            P.dma(mv_s[t0:t0 + n, :].rearrange("(a p) f -> p a f", p=128), vt[:, 0:nt, :], reads=[vt], writes=[mv_s])
            R = slice(64, 96)
            psr = pp.next()
            for kc in range(8):
                P.mm(psr[R, 0:n], wkr[:, kc, :], x[:, kc, 0:n], kc == 0, kc == 7, [wkr, x], [psr])
            sqr, krg = sqR.next(), qgR.next()
            P.act(sqr[R, 0:n], psr[R, 0:n], AF.Square, [psr], [sqr])
            for ti in range(nt):
                tt = t0 // 128 + ti
                P.mm(prkr[:, tt:tt + 1], sqr[R, ti * 128:(ti + 1) * 128], ones_b[R, 0:1], True, True, [sqr, ones_b], [prkr])
            P.act(krg[R, 0:n], psr[R, 0:n], AF.Identity, [psr], [krg], scale=evc[R, 6:7])
            krf = qo.next()
            if lat:
                prot, t1, t2 = protR.next(), t1R.next(), t2R.next()
                P.mm(prot[R, 0:n], cstb[R, C_ROT96, 64:96], krg[R, 0:n], True, True, [cstb, krg], [prot])
                P.tt(t1[R, 0:n], krg[R, 0:n], cs[R, 0, l0:l0 + n], ALU.mult, [krg, cs], [t1])
                P.tt(t2[R, 0:n], prot[R, 0:n], cs[R, 1, l0:l0 + n], ALU.mult, [prot, cs], [t2])
                P.tt(krf[R, 0:n], t1[R, 0:n], t2[R, 0:n], ALU.add, [t1, t2], [krf])
            else:
                P.copy(krf[R, 0:n], krg[R, 0:n], [krg], [krf])
            for h in range(8):
                P.dma(mkf_s[R, h, t0:t0 + n], krf[R, 0:n], reads=[krf], writes=[mkf_s])
            for h in range(8):
                ps = pp.next()
                for c in range(3):
                    P.mm(ps[0:96, 0:n], wqu[:, c, h * 96:(h + 1) * 96], cqn[:, c, 0:n], c == 0, c == 2, [wqu, cqn], [ps])
                sq, pss, lnv, rs, qg = sqR.next(), pssR.next(), lnvR.next(), rsR.next(), qgR.next()
                P.act(sq[0:96, 0:n], ps[0:96, 0:n], AF.Square, [ps], [sq])
                P.mm(pss[0:96, 0:n], ones_b[0:96, 0:96], sq[0:96, 0:n], True, True, [ones_b, sq], [pss])
                P.act(lnv[0:96, 0:n], pss[0:96, 0:n], AF.Ln, [pss], [lnv], bias=EPS, scale=1.0 / 96)
                P.act(rs[0:96, 0:n], lnv[0:96, 0:n], AF.Exp, [lnv], [rs], scale=-0.5)
                P.act(qg[0:96, 0:n], ps[0:96, 0:n], AF.Identity, [ps], [qg], scale=evc[0:96, 5:6])
                q = qo.next()
                rope_out(q, qg, rs, n, lat, l0, True)
                P.dma(mqf_s[:, h, t0:t0 + n], q[0:96, 0:n], reads=[q], writes=[mqf_s])
                ps = pp.next()
                for c in range(2):
                    P.mm(ps[0:64, 0:n], wkK[:, c, h, :], ckvn[:, c, 0:n], c == 0, c == 1, [wkK, ckvn], [ps])
                sq = sqR.next()
                P.act(sq[0:64, 0:n], ps[0:64, 0:n], AF.Square, [ps], [sq])
                for ti in range(nt):
                    tt = t0 // 128 + ti
                    P.mm(prk[:, tt, h:h + 1], sq[0:64, ti * 128:(ti + 1) * 128], ones_b[0:64, 0:1], True, True,
                         [sq, ones_b], [prk])
                q = qo.next()
                P.act(q[0:64, 0:n], ps[0:64, 0:n], AF.Identity, [ps], [q], scale=evc[0:64, 6:7])
                P.dma(mkf_s[0:64, h, t0:t0 + n], q[0:64, 0:n], reads=[q], writes=[mkf_s])
        rk3 = P.sbuf("rk3", [128, NT, 8], F32)
        rkr = P.sbuf("rkr", [128, NT], F32)
        P.act(rkr[:], prkr[:], AF.Copy, [prkr], [rkr])
        P.tt(rk3[:], prk[:], rkr[:].unsqueeze(2).broadcast_to([128, NT, 8]), ALU.add, [prk, rkr], [rk3])
        P.act(rsk[:], rk3[:].rearrange("p a b -> p (a b)"), AF.Ln, [rk3], [rsk], bias=EPS, scale=1.0 / 96)
        P.act(rsk[:], rsk[:], AF.Exp, [rsk], [rsk], scale=-0.5)
        P.ts(rsk[:], rsk[:], float(96 ** -0.5), ALU.mult, [rsk], [rsk])
        P.dma(mrk_s[:], rsk[:], reads=[rsk], writes=[mrk_s])
        P.emit()

    def mla_core():
        P = Prog(nc, sy, "mlc")
        rsk = P.sbuf("rsk", [128, NT, 8], F32)
        P.dma(rsk[:].rearrange("p a b -> p (a b)"), mrk_s[:], reads=[mrk_s], writes=[rsk])
        kf = Rot(lambda i: P.sbuf("kf%d" % i, [96, 2, T], BF16), 2)
        qf = Rot(lambda i: P.sbuf("qf%d" % i, [96, 2, T], BF16), 2)
        vv = Rot(lambda i: P.sbuf("vv%d" % i, [128, NT, 128], BF16), 2)
        pS = Rot(lambda i: P.psum("pS%d" % i, [128, 512]), 4)
        pO = Rot(lambda i: P.psum("pO%d" % i, [128, 512]), 2)
        pM = Rot(lambda i: P.psum("pM%d" % i, [128, 512]), 2)
        pT = Rot(lambda i: P.sbuf("pT%d" % i, [128, 512], BF16), 4)
        rden = P.sbuf("rden", [128, 512], F32)
        mo = Rot(lambda i: P.sbuf("mo%d" % i, [128, 512], BF16), 2)
        for j in range(4):
            k_, q_, v_ = kf.next(), qf.next(), vv.next()
            P.dma(k_[:], mkf_s[:, 2 * j:2 * j + 2, :], reads=[mkf_s], writes=[k_])
            P.dma(q_[:], mqf_s[:, 2 * j:2 * j + 2, :], reads=[mqf_s], writes=[q_])
            P.dma(v_[:], mv_s[:, j * 128:(j + 1) * 128].rearrange("(a p) f -> p a f", p=128), reads=[mv_s], writes=[v_])
            for (t0, n) in GROUPS5:
                kts = [0, 1] if t0 < CTX else list(range(NT))
                po, pm = pO.next(), pM.next()

                def scores(kt):
                    pts = []
                    for hh in range(2):
                        ps = pS.next()
                        P.mm(ps[:, 0:n], k_[:, hh, kt * 128:(kt + 1) * 128], q_[:, hh, t0:t0 + n], True, True, [k_, q_], [ps])
                        pt = pT.next()
                        P.act(pt[:, 0:n], ps[:, 0:n], AF.Exp, [ps, rsk], [pt], scale=rsk[:, kt, 2 * j + hh:2 * j + hh + 1])
                        pts.append(pt)
                    return pts

                nxt = scores(kts[0])
                for ki, kt in enumerate(kts):
                    first, last = ki == 0, ki == len(kts) - 1
                    pts = nxt
                    if not last:
                        nxt = scores(kts[ki + 1])
                    for hh in range(2):
                        sl = slice(hh * 64, (hh + 1) * 64)
                        P.mm(po[sl, 0:n], v_[:, kt, hh * 64:(hh + 1) * 64], pts[hh][:, 0:n], first, last, [v_, pts[hh]], [po])
                    for hh in range(2):
                        sl = slice(hh * 64, (hh + 1) * 64)
                        P.mm(pm[sl, 0:n], ones_b[:, 0:64], pts[hh][:, 0:n], first, last, [ones_b, pts[hh]], [pm])
                P.act(rden[:, 0:n], pm[:, 0:n], AF.Ln, [pm], [rden])
                P.act(rden[:, 0:n], rden[:, 0:n], AF.Exp, [rden], [rden], scale=-1.0)
                m = mo.next()
                P.tt(m[:, 0:n], po[:, 0:n], rden[:, 0:n], ALU.mult, [po, rden], [m])
                P.dma(mix_s[:, j, t0:t0 + n], m[:, 0:n], reads=[m], writes=[mix_s])
        P.emit()

    def gdn_prep():
        P = Prog(nc, sy, "gdp")
        wv_ = evw_d.rearrange("(kc p) n -> p kc n", p=128)
        xn = P.sbuf("xn", [128, 8, T], BF16)
        P.dma(xn[:], xn_s[:], reads=[xn_s], writes=[xn])
        cvw = P.sbuf("cvw", [128, 12, 5], F32)
        P.dma(cvw[:], gcv_d, writes=[cvw])
        grow = P.sbuf("grow", [128, 544], F32)
        P.dma(grow[:], grow_d, writes=[grow])
        st = P.sbuf("st", [128, 8, 512], F32)
        wch = Rot(lambda i: P.sbuf("wch%d" % i, [128, 8, 128], BF16), 2)
        wab = P.sbuf("wab", [128, 8, 32], BF16)
        wgt = P.sbuf("wgt", [128, 8, 512], BF16)
        load_w(P, wab, wab[:], wv_[:, :, 2208:2240], st, st[:, :, 0:32])
        load_w(P, wgt, wgt[:], wv_[:, :, 2240:2752], st, st[:])
        pp = Rot(lambda i: P.psum("pp%d" % i, [128, 512]), 3)
        pssR = Rot(lambda i: P.psum("pss%d" % i, [128, 512]), 2)
        ptr_ = Rot(lambda i: P.psum("ptr%d" % i, [128, 4, 128]), 2)
        zcR = Rot(lambda i: P.sbuf("zc%d" % i, [128, T], F32), 2)
        accR = Rot(lambda i: P.sbuf("acc%d" % i, [128, T], F32), 1)
        sqR = Rot(lambda i: P.sbuf("sq%d" % i, [128, 512], BF16), 2)
        lnvR = Rot(lambda i: P.sbuf("lnv%d" % i, [128, 512], F32), 2)
        tmst = P.sbuf("tmst", [128, NT, 128], F32)
        gall = P.sbuf("gall", [128, NT, 16], F32)
        ball = P.sbuf("ball", [128, NT, 16], F32)
        negA = P.sbuf("negA", [128, 16], F32)
        P.act(negA[:], grow[:, 0:16], AF.Exp, [grow], [negA])
        P.ts(negA[:], negA[:], -1.0, ALU.mult, [negA], [negA])
        tg = P.sbuf("tg", [128, 16], F32)
        gt = Rot(lambda i: P.sbuf("gt%d" % i, [128, 512], BF16), 2)
        for tt in range(NT):
            tsl = slice(tt * 128, (tt + 1) * 128)
            ps = pp.next()
            for kc in range(8):
                P.mm(ps[:, 0:32], xn[:, kc, tsl], wab[:, kc, :], kc == 0, kc == 7, [xn, wab], [ps])
            P.tt(tg[:], ps[:, 0:16], grow[:, 16:32], ALU.add, [ps, grow], [tg])
            P.act(tg[:], tg[:], AF.Exp, [tg], [tg])
            P.act(tg[:], tg[:], AF.Ln, [tg], [tg], bias=1.0)
            P.tt(gall[:, tt, :], tg[:], negA[:], ALU.mult, [tg, negA], [gall])
            P.act(ball[:, tt, :], ps[:, 16:32], AF.Exp, [ps], [ball], scale=-1.0)
            P.ts(ball[:, tt, :], ball[:, tt, :], 1.0, ALU.add, [ball], [ball])
            P.op("vector", lambda e, tt=tt: e.reciprocal(out=ball[:, tt, :], in_=ball[:, tt, :]), [ball], [ball])
            ps = pp.next()
            for kc in range(8):
                P.mm(ps[:], xn[:, kc, tsl], wgt[:, kc, :], kc == 0, kc == 7, [xn, wgt], [ps])
            g_ = gt.next()
            P.act(g_[:], ps[:], AF.Silu, [ps], [g_])
            P.dma(ggate_s[tsl, :], g_[:], reads=[g_], writes=[ggate_s])
        P.dma(gg_s[:], gall[:].rearrange("p a b -> p (a b)"), reads=[gall], writes=[gg_s])
        P.dma(gb_s[:], ball[:].rearrange("p a b -> p (a b)"), reads=[ball], writes=[gb_s])
        ZW = T + 8

        def zoff(t):
            return t + 2 if t < CTX else t + 6

        zpR = Rot(lambda i: P.sbuf("zp%d" % i, [128, ZW], BF16), 2)
        for z_ in zpR.t:
            P.memset(z_[:], 0.0, [z_])
        dgR = Rot(lambda i: P.sbuf("dgw%d" % i, [128, 5, 128], BF16), 2)
        for cc in range(12):
            zc, acc = zcR.next(), accR.next()
            zp, dgw = zpR.next(), dgR.next()
            w = wch.next()
            load_w(P, w, w[:], wv_[:, :, 672 + cc * 128:672 + (cc + 1) * 128], st, st[:, :, 0:128])
            for j in range(5):
                P.ts(dgw[:, j, :], cst[:, C_ID, :], cvw[:, cc, j:j + 1], ALU.mult, [cst, cvw], [dgw], eng="gpsimd")
            for (t0, n) in GROUPS5:
                ps = pp.next()
                for kc in range(8):
                    P.mm(ps[:, 0:n], w[:, kc, :], xn[:, kc, t0:t0 + n], kc == 0, kc == 7, [w, xn], [ps])
                P.act(zp[:, zoff(t0):zoff(t0) + n], ps[:, 0:n], AF.Copy, [ps], [zp])
            for (t0, n) in GROUPS5:
                ps = pp.next()
                for j in range(5):
                    o_ = zoff(t0) + j - 2
                    P.mm(ps[:, 0:n], dgw[:, j, :], zp[:, o_:o_ + n], j == 0, j == 4, [dgw, zp], [ps])
                P.act(zc[:, t0:t0 + n], ps[:, 0:n], AF.Silu, [ps], [zc])
            if cc < 8:
                for (t0, n) in GROUPS5:
                    sq, lnv = sqR.next(), lnvR.next()
                    P.act(sq[:, 0:n], zc[:, t0:t0 + n], AF.Square, [zc], [sq])
                    pss = pssR.next()
                    P.mm(pss[:, 0:n], cstb[:, C_BD64, :], sq[:, 0:n], True, True, [cstb, sq], [pss])
                    P.act(lnv[:, 0:n], pss[:, 0:n], AF.Ln, [pss], [lnv], bias=EPS)
                    P.act(lnv[:, 0:n], lnv[:, 0:n], AF.Exp, [lnv], [lnv], scale=-0.5)
                    if cc < 4:
                        P.stt(acc[:, t0:t0 + n], zc[:, t0:t0 + n], 0.125, lnv[:, 0:n], ALU.mult, ALU.mult, [zc, lnv], [acc])
                    else:
                        P.tt(acc[:, t0:t0 + n], zc[:, t0:t0 + n], lnv[:, 0:n], ALU.mult, [zc, lnv], [acc])
                dst = gq_s if cc < 4 else gk_s
                r0 = (cc % 4) * 128
                P.dma(dst[r0:r0 + 128, :], acc[:], reads=[acc], writes=[dst])
                src = acc
            else:
                src = zc
            if cc >= 4:
                for t4 in range(0, NT, 4):
                    nn = min(4, NT - t4)
                    pt = ptr_.next()
                    for i in range(nn):
                        tt = t4 + i
                        P.tr(pt[:, i, :], src[:, tt * 128:(tt + 1) * 128], ident, [src, cst], [pt])
                    P.copy(tmst[:, t4:t4 + nn, :], pt[:, 0:nn, :], [pt], [tmst])
                dtm = gktm_s if cc < 8 else gvtm_s
                c0 = (cc % 4) * 128
                P.dma(dtm[:, c0:c0 + 128].rearrange("(a p) f -> p a f", p=128), tmst[:], reads=[tmst], writes=[dtm])
        P.emit()

    def gdn_core():
        P = Prog(nc, sy, "gdsp")
        P.dma(xt_s[:], XT[:, 4:8, :].rearrange("p c t -> p (c t)"), reads=[XT], writes=[xt_s])
        P.emit()
        P = Prog(nc, sy, "gdc")
        XTf = XT[:, 4:8, :].rearrange("p c t -> p (c t)")
        carve_off = [0]

        def carve(name, dtype):
            words = 512 if dtype == F32 else 256
            a = XTf[:, carve_off[0]:carve_off[0] + words]
            carve_off[0] += words
            if dtype != F32:
                a = a.bitcast(dtype)
            return Buf(a.rearrange("p (h i) -> p h i", h=4), 1, name)

        lvm = P.sbuf("lvm", [128, 7, 128], F32)
        nlv = P.sbuf("nlv", [128, 14, 128], BF16)
        for hf in range(2):
            P.dma(lvm[:], lvm_d[:, 7 * hf:7 * hf + 7, :], writes=[lvm])
            P.ts(nlv[:, 7 * hf:7 * hf + 7, :], lvm[:], -1.0, ALU.mult, [lvm], [nlv])
        idb = P.sbuf("idb", [128, 128], BF16)
        P.copy(idb[:], cst[:, C_ID, :], [cst], [idb])
        gall = P.sbuf("gall", [128, NT, 16], F32)
        ball = P.sbuf("ball", [128, NT, 16], F32)
        P.dma(gall[:].rearrange("p a b -> p (a b)"), gg_s[:], reads=[gg_s], writes=[gall])
        P.dma(ball[:].rearrange("p a b -> p (a b)"), gb_s[:], reads=[gb_s], writes=[ball])
        PS = Rot(lambda i: P.psum("ps%d" % i, [128, 512]), 6)
        PB = Rot(lambda i: P.psum("pb%d" % i, [128, 4, 128], BF16), 2)

        def b4(ap):
            return ap.unsqueeze(1).broadcast_to([128, 4, 128])

        def bl(ap, w):
            return ap.unsqueeze(2).broadcast_to([128, ap.shape[1], w])

        Ibc = b4(idb[:])
        D_ = []
        for d in range(2):
            n_ = lambda s, d=d: "%s_%d" % (s, d)
            o = K()
            o.S = P.sbuf(n_("S"), [128, 8, 64], F32)
            o.q = P.sbuf(n_("q"), [64, 8, 128], F32)
            o.k = P.sbuf(n_("k"), [64, 8, 128], F32)
            o.kt = P.sbuf(n_("kt"), [128, 8, 64], F32)
            o.vt = P.sbuf(n_("vt"), [128, 8, 64], F32)
            o.eg = P.sbuf(n_("eg"), [128, 24], F32)
            o.cfk = P.sbuf(n_("cfk"), [128, 8], F32)
            o.qkd = P.sbuf(n_("qkd"), [128, 8, 128], BF16)
            o.U = P.sbuf(n_("U"), [128, 8, 64], F32)
            o.wT = P.sbuf(n_("wT"), [64, 8, 128], BF16)
            o.dg = P.sbuf(n_("dg"), [128, 4, 128], F32)
            o.qdT = P.sbuf(n_("qdT"), [128, 8, 128], BF16)
            o.kdec = P.sbuf(n_("kdec"), [128, 8, 64], BF16)
            o.vnew = P.sbuf(n_("vnew"), [128, 8, 64], BF16)
            o.Sb = P.sbuf(n_("Sb"), [128, 8, 64], BF16)
            o.ost = P.sbuf(n_("ost"), [128, 512], F32)
            P.memset(o.qdT[:], 0.0, [o.qdT])
            P.memset(o.S[:], 0.0, [o.S])
            P.memset(o.Sb[:], 0.0, [o.Sb])
            o.order = list(range(NT)) if d == 0 else [1, 0] + list(range(NT - 1, 1, -1))
            o.Uinc = cst[:, C_UI, :] if d == 0 else cst[:, C_LI, :]
            o.Bst = cst[:, C_SL, :] if d == 0 else cst[:, C_SU, :]
            o.mX = (lambda lv: nlv[:, lv, :]) if d == 0 else (lambda lv: nlv[:, 7 + lv, :])
            o.mXT = (lambda lv: nlv[:, 7 + lv, :]) if d == 0 else (lambda lv: nlv[:, lv, :])
            o.d = d
            D_.append(o)
        CH = []
        for o in D_:
            for hg in range(2):
                c = K()
                c.o, c.hg, c.h0, c.hsl = o, hg, 4 * hg, slice(4 * hg, 4 * hg + 4)
                nm = lambda s, o=o, hg=hg: "%s_%d%d" % (s, o.d, hg)
                if hg == 0:
                    mk = lambda s, dt_: P.sbuf(nm(s), [128, 4, 128], dt_)
                else:
                    mk = lambda s, dt_: carve(nm(s), dt_)
                c.GU, c.eD, c.eDT = mk("GU", F32), mk("eD", F32), mk("eDT", F32)
                c.A, c.AT, c.Ym, c.Ymt = mk("A", BF16), mk("AT", BF16), mk("Ym", BF16), mk("Ymt", BF16)
                c.X = [mk("X%d" % i, BF16) for i in range(2)]
                c.XT = [mk("XT%d" % i, BF16) for i in range(2)]
                c.Rv = P.sbuf(nm("Rv"), [128, 4, 64], BF16)
                c.Rk = P.sbuf(nm("Rk"), [128, 4, 64], BF16)
                CH.append(c)

        def v4(ps):
            return ps[:].rearrange("p (h i) -> p h i", h=4)

        for s in range(NT):
            for o in D_:
                d = o.d
                tt = o.order[s]
                o.tt = tt
                tsl = slice(tt * 128, (tt + 1) * 128)
                P.dma(o.q[:], gq_s[:, tsl].rearrange("(h d) t -> d h t", d=64), reads=[gq_s], writes=[o.q])
                P.dma(o.k[:], gk_s[:, tsl].rearrange("(h d) t -> d h t", d=64), reads=[gk_s], writes=[o.k])
                P.dma(o.kt[:].rearrange("p h d -> p (h d)"), gktm_s[tsl, :], reads=[gktm_s], writes=[o.kt])
                P.dma(o.vt[:].rearrange("p h d -> p (h d)"), gvtm_s[tsl, :], reads=[gvtm_s], writes=[o.vt])
                gd = gall[:, tt, d * 8:(d + 1) * 8]
                pg = PS.next()
                P.mm(pg[:, 0:8], o.Uinc, gd, True, True, [cst, gall], [pg])
                P.mm(pg[:, 8:16], o.Bst, gd, True, True, [cst, gall], [pg])
                P.mm(pg[:, 16:24], ones_f[:], gd, True, True, [ones_f, gall], [pg])
                P.act(o.eg[:], pg[:, 0:24], AF.Exp, [pg], [o.eg])
                P.tt(o.cfk[:], ball[:, tt, d * 8:(d + 1) * 8], o.eg[:, 0:8], ALU.mult, [ball, o.eg], [o.cfk])
            for c in CH:
                o, h0 = c.o, c.h0
                d, tt = o.d, o.tt
                P.tt(c.GU[:], b4(o.Uinc), bl(gall[:, tt, d * 8 + h0:d * 8 + h0 + 4], 128), ALU.mult, [cst, gall], [c.GU],
                     eng="gpsimd")
            for c in CH:
                o = c.o
                c.pD, c.pDT = PS.next(), PS.next()
                for h in range(4):
                    P.mm(v4(c.pD)[:, h, :], c.GU[:, h, :], o.Bst, True, True, [c.GU, cst], [c.pD])
                P.mm(c.pDT[:], o.Bst, c.GU[:].rearrange("p h i -> p (h i)"), True, True, [c.GU, cst], [c.pDT])
                P.act(c.eD[:], v4(c.pD), AF.Exp, [c.pD], [c.eD])
                P.act(c.eDT[:], v4(c.pDT), AF.Exp, [c.pDT], [c.eDT])
            for c in CH:
                o, h0 = c.o, c.h0
                d, tt = o.d, o.tt
                pKK = PS.next()
                for h in range(4):
                    P.mm(v4(pKK)[:, h, :], o.k[:, h0 + h, :], o.k[:, h0 + h, :], True, True, [o.k], [pKK])
                P.tt(c.eD[:], v4(pKK), c.eD[:], ALU.mult, [pKK, c.eD], [c.eD])
                P.tt(c.eD[:], c.eD[:], b4(o.Bst), ALU.mult, [c.eD, cst], [c.eD], eng="gpsimd")
                P.tt(c.A[:], c.eD[:], bl(ball[:, tt, d * 8 + h0:d * 8 + h0 + 4], 128), ALU.mult, [c.eD, ball], [c.A],
                     eng="gpsimd")
            for c in CH:
                o, h0 = c.o, c.h0
                pQK = PS.next()
                for h in range(4):
                    P.mm(v4(pQK)[:, h, :], o.k[:, h0 + h, :], o.q[:, h0 + h, :], True, True, [o.k, o.q], [pQK])
                P.tt(c.eDT[:], v4(pQK), c.eDT[:], ALU.mult, [pQK, c.eDT], [c.eDT])
                P.tt(o.qkd[:, c.hsl, :], c.eDT[:], b4(o.Uinc), ALU.mult, [c.eDT, cst], [o.qkd], eng="gpsimd")
            for c in CH:
                pT_ = PB.next()
                for h in range(4):
                    P.tr(pT_[:, h, :], c.A[:, h, :], idb[:], [c.A, idb], [pT_])
                P.act(c.AT[:], pT_[:], AF.Copy, [pT_], [c.AT])
            for c in CH:
                o = c.o
                P.tt(c.Ym[:], c.A[:], b4(o.mX(0)), ALU.mult, [c.A, nlv], [c.Ym], eng="gpsimd")
                P.tt(c.X[0][:], c.Ym[:], Ibc, ALU.add, [c.Ym, idb], [c.X[0]], eng="gpsimd")
                P.tt(c.Ymt[:], c.AT[:], b4(o.mXT(0)), ALU.mult, [c.AT, nlv], [c.Ymt])
                P.tt(c.XT[0][:], c.Ymt[:], Ibc, ALU.add, [c.Ymt, idb], [c.XT[0]])
            cur = 0
            for lv in range(1, 7):
                last = lv == 6
                for c in CH:
                    o = c.o
                    if not last:
                        c.pY = PS.next()
                        for h in range(4):
                            P.mm(v4(c.pY)[:, h, :], c.AT[:, h, :], c.X[cur][:, h, :], True, True, [c.AT, c.X[cur]], [c.pY])
                        P.tt(c.Ym[:], v4(c.pY), b4(o.mX(lv)), ALU.mult, [c.pY, nlv], [c.Ym])
                    c.pY2 = PS.next()
                    for h in range(4):
                        P.mm(v4(c.pY2)[:, h, :], c.A[:, h, :], c.XT[cur][:, h, :], True, True, [c.A, c.XT[cur]], [c.pY2])
                    P.tt(c.Ymt[:], v4(c.pY2), b4(o.mXT(lv)), ALU.mult, [c.pY2, nlv], [c.Ymt])
                for c in CH:
                    if not last:
                        pZ = PS.next()
                        P.mm(pZ[:], idb[:], c.X[cur][:].rearrange("p h i -> p (h i)"), True, False, [idb, c.X[cur]], [pZ])
                        for h in range(4):
                            P.mm(v4(pZ)[:, h, :], c.XT[cur][:, h, :], c.Ym[:, h, :], False, True, [c.XT[cur], c.Ym], [pZ])
                        P.act(c.X[1 - cur][:], v4(pZ), AF.Copy, [pZ], [c.X[1 - cur]])
                    pZ = PS.next()
                    P.mm(pZ[:], idb[:], c.XT[cur][:].rearrange("p h i -> p (h i)"), True, False, [idb, c.XT[cur]], [pZ])
                    for h in range(4):
                        P.mm(v4(pZ)[:, h, :], c.X[cur][:, h, :], c.Ymt[:, h, :], False, True, [c.X[cur], c.Ymt], [pZ])
                    P.act(c.XT[1 - cur][:], v4(pZ), AF.Copy, [pZ], [c.XT[1 - cur]])
                cur = 1 - cur
            for c in CH:
                o, h0, hsl = c.o, c.h0, c.hsl
                d, tt = o.d, o.tt
                XTf_ = c.XT[cur]
                P.tt(c.Rv[:], o.vt[:, hsl, :], bl(ball[:, tt, d * 8 + h0:d * 8 + h0 + 4], 64), ALU.mult, [o.vt, ball], [c.Rv],
                     eng="gpsimd")
                P.tt(c.Rk[:], o.kt[:, hsl, :], bl(o.cfk[:, hsl], 64), ALU.mult, [o.kt, o.cfk], [c.Rk], eng="gpsimd")
                pU, pW = PS.next(), PS.next()
                pU3 = pU[:, 0:256].rearrange("p (h i) -> p h i", h=4)
                pW3 = pW[0:64, :].rearrange("p (h i) -> p h i", h=4)
                for h in range(4):
                    P.mm(pU3[:, h, :], XTf_[:, h, :], c.Rv[:, h, :], True, True, [XTf_, c.Rv], [pU])
                for h in range(4):
                    P.mm(pW3[:, h, :], c.Rk[:, h, :], XTf_[:, h, :], True, True, [XTf_, c.Rk], [pW])
                P.act(o.U[:, hsl, :], pU3, AF.Copy, [pU], [o.U])
                P.act(o.wT[:, hsl, :], pW3, AF.Copy, [pW], [o.wT])
            for o in D_:
                for half in range(2):
                    P.tt(o.dg[:], b4(cst[:, C_ID, :]), bl(o.eg[:, 4 * half:4 * half + 4], 128), ALU.mult, [cst, o.eg], [o.dg],
                         eng="gpsimd")
                    pE = PS.next()
                    P.mm(pE[0:64, :], ones_f[:, 0:64], o.dg[:].rearrange("p h i -> p (h i)"), True, True, [ones_f, o.dg], [pE])
                    P.tt(o.qdT[0:64, 4 * half:4 * half + 4, :], o.q[:, 4 * half:4 * half + 4, :],
                         pE[0:64, :].rearrange("p (h i) -> p h i", h=4), ALU.mult, [o.q, pE], [o.qdT])
                P.tt(o.kdec[:], o.kt[:], bl(o.eg[:, 8:16], 64), ALU.mult, [o.kt, o.eg], [o.kdec], eng="gpsimd")
            for o in D_:
                pWS = PS.next()
                pWS3 = pWS[:].rearrange("p (h e) -> p h e", h=8)
                for h in range(8):
                    P.mm(pWS3[:, h, :], o.wT[:, h, :], o.Sb[0:64, h, :], True, True, [o.wT, o.Sb], [pWS])
                P.tt(o.vnew[:], o.U[:], pWS3, ALU.subtract, [o.U, pWS], [o.vnew])
            for o in D_:
                tt = o.tt
                tsl = slice(tt * 128, (tt + 1) * 128)
                pOO = PS.next()
                pOO3 = pOO[:].rearrange("p (h e) -> p h e", h=8)
                for h in range(8):
                    P.mm(pOO3[:, h, :], o.qdT[:, h, :], o.Sb[:, h, :], True, False, [o.qdT, o.Sb], [pOO])
                    P.mm(pOO3[:, h, :], o.qkd[:, h, :], o.vnew[:, h, :], False, True, [o.qkd, o.vnew], [pOO])
                P.act(o.ost[:], pOO[:], AF.Copy, [pOO], [o.ost])
                P.dma(og_s[o.d][tsl, :], o.ost[:], reads=[o.ost], writes=[og_s[o.d]])
                pSn = PS.next()
                pSn3 = pSn[0:64, :].rearrange("p (h e) -> p h e", h=8)
                for h in range(8):
                    P.mm(pSn3[:, h, :], o.kdec[:, h, :], o.vnew[:, h, :], True, True, [o.kdec, o.vnew], [pSn])
                P.tt(o.S[0:64], o.S[0:64], o.eg[0:64, 16:24].unsqueeze(2).broadcast_to([64, 8, 64]), ALU.mult, [o.S, o.eg], [o.S])
                P.tt(o.S[0:64], o.S[0:64], pSn3, ALU.add, [o.S, pSn], [o.S])
                P.act(o.Sb[0:64], o.S[0:64], AF.Copy, [o.S], [o.Sb])
        P.emit()
        P = Prog(nc, sy, "gdf")
        grow = P.sbuf("grow", [128, 544], F32)
        P.dma(grow[:], grow_d, writes=[grow])
        sqoR = Rot(lambda i: P.sbuf("sqo%d" % i, [128, 8, 64], F32), 4)
        ssR = Rot(lambda i: P.sbuf("ss%d" % i, [128, 8], F32), 4)
        toR = Rot(lambda i: P.sbuf("to%d" % i, [128, 8, 64], F32), 4)
        tobR = Rot(lambda i: P.sbuf("tob%d" % i, [128, 512], BF16), 4)
        gtt = Rot(lambda i: P.sbuf("gtt%d" % i, [128, 512], BF16), 4)
        oa = Rot(lambda i: P.sbuf("oa%d" % i, [128, 512], F32), 4)
        ob_ = Rot(lambda i: P.sbuf("ob%d" % i, [128, 512], F32), 4)
        idb = P.sbuf("idb", [128, 128], BF16)
        P.copy(idb[:], cst[:, C_ID, :], [cst], [idb])
        mixo = Rot(lambda i: P.sbuf("mixo%d" % i, [128, 4, 128], BF16), 4)
        ptbR = Rot(lambda i: P.psum("ptb%d" % i, [128, 4, 128], BF16), 4)
        for tt in range(NT):
            sqo, ss, to, tob, ptb = sqoR.next(), ssR.next(), toR.next(), tobR.next(), ptbR.next()
            tsl = slice(tt * 128, (tt + 1) * 128)
            g_, a_, b_ = gtt.next(), oa.next(), ob_.next()
            P.dma(g_[:], ggate_s[tsl, :], reads=[ggate_s], writes=[g_])
            P.dma(a_[:], og_s[0][tsl, :], reads=[og_s[0]], writes=[a_])
            P.dma(b_[:], og_s[1][tsl, :], reads=[og_s[1]], writes=[b_])
            P.tt(a_[:], a_[:], b_[:], ALU.add, [a_, b_], [a_], eng="gpsimd")
            o3 = a_[:].rearrange("p (h e) -> p h e", h=8)
            P.act(sqo[:], o3, AF.Square, [a_], [sqo])
            P.op("vector", lambda e, ss=ss, sqo=sqo: e.tensor_reduce(out=ss[:], in_=sqo[:], axis=AX.X, op=ALU.add), [sqo], [ss])
            P.act(ss[:], ss[:], AF.Ln, [ss], [ss], bias=EPS, scale=1.0 / 64)
            P.act(ss[:], ss[:], AF.Exp, [ss], [ss], scale=-0.5)
            P.tt(to[:], o3, ss[:].unsqueeze(2).broadcast_to([128, 8, 64]), ALU.mult, [a_, ss], [to])
            P.tt(to[:].rearrange("p h e -> p (h e)"), to[:].rearrange("p h e -> p (h e)"), grow[:, 32:544], ALU.mult,
                 [to, grow], [to], eng="gpsimd")
            P.tt(tob[:], to[:].rearrange("p h e -> p (h e)"), g_[:], ALU.mult, [to, g_], [tob])
            for c in range(4):
                P.tr(ptb[:, c, :], tob[:, c * 128:(c + 1) * 128], idb[:], [tob, idb], [ptb])
            mo_ = mixo.next()
            P.act(mo_[:], ptb[:], AF.Copy, [ptb], [mo_])
            P.dma(mix_s[:, 4:8, tsl], mo_[:], reads=[mo_], writes=[mix_s])
        P.dma(XT[:, 4:8, :].rearrange("p c t -> p (c t)"), xt_s[:], reads=[xt_s], writes=[XT])
        P.emit()


    for l in layers:
        if stage >= 1 and not skip_ffn1:
            ffn(l, 0, FULL)
        dump_xt("xt_ffn1_%d" % l)
        if stage >= 2:
            make_xn(l)
            if l == 0:
                if parts is None or "mlp" in parts:
                    mla_prep()
                if parts is None or "mlc" in parts:
                    mla_core()
                if parts is None or "gdp" in parts:
                    gdn_prep()
                if parts is None or "gdc" in parts:
                    gdn_core()
                if parts is None or "op" in parts:
                    out_proj(l, evwo_d, GROUPS5)
            if l == 1:
                if parts is None or "swp" in parts:
                    swa_prep()
                if parts is None or "swc" in parts:
                    swa_core()
                if parts is None or "glp" in parts:
                    gla_prep()
                if parts is None or "glc" in parts:
                    gla_core()
                if parts is None or "op" in parts:
                    out_proj(l, odwo_d, LATG)
        if "mix" in dbg_d:
            P = Prog(nc, sy, "dbgmix%d" % l)
            mb_ = Rot(lambda i: P.sbuf("mb%d" % i, [128, 8, 128], BF16), 2)
            mf_ = Rot(lambda i: P.sbuf("mf%d" % i, [128, 8, 128], F32), 2)
            dv = dbg_d["mix"].rearrange("p (c t) -> p c t", c=8)
            for tt in range(NT):
                a_, f_ = mb_.next(), mf_.next()
                P.dma(a_[:], mix_s[:, :, tt * 128:(tt + 1) * 128], reads=[mix_s], writes=[a_])
                P.act(f_[:], a_[:], AF.Copy, [a_], [f_])
                P.dma(dv[:, :, tt * 128:(tt + 1) * 128], f_[:], reads=[f_])
            P.emit()
        dump_xt("xt_mix_%d" % l)
        if stage >= 3:
            ffn(l, 1, FULL if l == 0 else LAT)

    P = Prog(nc, sy, "pout")
    ost = [P.sbuf("ost%d" % i, [128, D], F32) for i in range(2)]
    pot = [P.psum("pot%d" % i, [128, 4, 128]) for i in range(2)]
    for tt in range(2, NT):
        o = ost[tt % 2]
        for half in range(2):
            ps = pot[half]
            for c4 in range(4):
                c = half * 4 + c4
                P.tr(ps[:, c4, :], XT[:, c, tt * 128:(tt + 1) * 128], ident, [(XT, tt), cst], [ps])
            if half == 0:
                P.copy(o[:, 0:512], ps[:].rearrange("p a b -> p (a b)"), [ps], [o])
            else:
                P.act(o[:, 512:1024], ps[:].rearrange("p a b -> p (a b)"), AF.Copy, [ps], [o])
        P.dma(out_d[(tt - 2) * 128:(tt - 1) * 128, :], o[:], reads=[o])
    P.emit()
    es.close()
    return nc


def rope_tables(rot_dim):
    f = np.float32
    t = np.arange(SEQ)
    row = (t // 64).astype(f)
    col = (t % 64).astype(f)
    nf = rot_dim // 4
    inv = (f(10000.0) ** (-np.arange(nf, dtype=f) / f(nf))).astype(f)
    ang = np.concatenate([row[:, None] * inv, col[:, None] * inv], axis=-1).astype(f)
    return np.cos(ang).astype(f), np.sin(ang).astype(f)


def host_consts():
    f = np.float32
    idx = np.arange(128)
    ident = np.eye(128)
    bd64 = (idx[:, None] // 64 == idx[None, :] // 64)
    Rm = np.zeros((64, 64))
    for i in range(32):
        Rm[i, i + 32] = -1.0
        Rm[i + 32, i] = 1.0
    rot = np.zeros((128, 128))
    rot[0:64, 0:64] = Rm.T
    rot[64:128, 64:128] = Rm.T
    UI = idx[:, None] <= idx[None, :]
    LI = idx[:, None] >= idx[None, :]
    SL = idx[:, None] > idx[None, :]
    SU = idx[:, None] < idx[None, :]
    rot96 = np.zeros((128, 128))
    for i in range(16):
        rot96[80 + i, 64 + i] = -1.0
        rot96[64 + i, 80 + i] = 1.0
    cst = np.stack([ident, bd64, rot, UI, LI, SL, SU, rot96], axis=1).astype(f)
    lv = []
    for k in range(7):
        bsz = 1 << k
        same = (idx[:, None] // (2 * bsz)) == (idx[None, :] // (2 * bsz))
        m = same & ((idx[:, None] % (2 * bsz)) >= bsz) & ((idx[None, :] % (2 * bsz)) < bsz)
        lv.append(m)
    lvm = np.stack(lv + [m.T for m in lv], axis=1).astype(f)
    c8, s8 = rope_tables(32)
    mcs = np.zeros((96, 2, SEQ), f)
    mcs[0:64, 0, :] = 1.0
    for dd in range(32):
        mcs[64 + dd, 0, :] = c8[:, dd % 16]
        mcs[64 + dd, 1, :] = s8[:, dd % 16]
    bdsel = np.stack([(idx // 64 == 0), (idx // 64 == 1)], axis=1).astype(f)
    c, s = rope_tables(64)
    d = idx % 64
    swcs = np.stack([c[:, d % 32].T, s[:, d % 32].T], axis=1).astype(f)
    return {"cst": np.ascontiguousarray(cst), "bdsel": np.ascontiguousarray(bdsel), "swa_cs": np.ascontiguousarray(swcs),
            "lvm": np.ascontiguousarray(lvm), "mla_cs": np.ascontiguousarray(mcs)}


_CONSTS = None


def host_inputs(inputs, b):
    global _CONSTS
    f = np.float32
    if _CONSTS is None:
        _CONSTS = host_consts()
    c = np.asarray(inputs["c"][b], f)
    cc = np.asarray(inputs["c_ctx"], f)
    ccol = np.concatenate([c.reshape(8, 128).T, cc.reshape(8, 128).T], axis=1)
    idx = np.arange(128)
    odc = np.zeros((128, 11), f)
    odc[:, 0] = np.asarray(inputs["od_swa_q_norm"], f)[0][idx % 64]
    odc[:, 1] = np.asarray(inputs["od_swa_k_norm"], f)[0][idx % 64]
    odc[:, 2] = np.asarray(inputs["od_gla_out_norm"], f)[0]
    odc[:, 3:11] = np.asarray(inputs["od_swa_sink"], f)[0][None, :]
    w2 = np.asarray(inputs["od_gla_gate_w2"], f)[0]
    gb = np.asarray(inputs["od_gla_gate_b"], f)[0]
    gw2b = np.concatenate([w2, gb[:, None, :]], axis=1).transpose(1, 0, 2)
    evc = np.zeros((128, 8), f)
    evc[:, 0:3] = np.asarray(inputs["ev_q_a_norm"], f)[0].reshape(3, 128).T
    evc[:, 3:5] = np.asarray(inputs["ev_kv_a_norm"], f)[0].reshape(2, 128).T
    evc[0:96, 5] = np.asarray(inputs["ev_mla_q_norm"], f)[0]
    evc[0:96, 6] = np.asarray(inputs["ev_mla_k_norm"], f)[0]
    gcv = np.asarray(inputs["ev_gdn_conv"], f)[0].reshape(5, 12, 128).transpose(2, 1, 0)
    grow = np.zeros((128, 544), f)
    grow[:, 0:16] = np.asarray(inputs["ev_gdn_a_log"], f)[0].reshape(16)[None, :]
    grow[:, 16:32] = np.asarray(inputs["ev_gdn_dt_bias"], f)[0].reshape(16)[None, :]
    grow[:, 32:544] = np.tile(np.asarray(inputs["ev_gdn_out_norm"], f)[0], 8)[None, :]
    m = {
        "ev_w_in": np.ascontiguousarray(np.asarray(inputs["ev_w_in"], f)[0]),
        "ev_w_out": np.ascontiguousarray(np.asarray(inputs["ev_w_out"], f)[0]),
        "ev_w_q_up": np.ascontiguousarray(np.asarray(inputs["ev_w_q_up"], f)[0]),
        "ev_w_kv_up": np.ascontiguousarray(np.asarray(inputs["ev_w_kv_up"], f)[0]),
        "ev_cols": evc,
        "gdn_conv_col": np.ascontiguousarray(gcv),
        "gdn_rows": grow,
        "x": np.ascontiguousarray(inputs["x"][b], dtype=f),
        "ctx": np.ascontiguousarray(inputs["ctx"][b], dtype=f),
        "ccol": np.ascontiguousarray(ccol),
        "ada_w": np.ascontiguousarray(inputs["ada_w"], dtype=f),
        "ada_b_col": np.ascontiguousarray(np.asarray(inputs["ada_b"], f).reshape(2, 72, 128).transpose(2, 0, 1)),
        "norm_g_col": np.ascontiguousarray(np.asarray(inputs["norm_g"], f).reshape(2, 3, 8, 128).transpose(3, 0, 1, 2)),
        "ffn_w_gate": np.ascontiguousarray(inputs["ffn_w_gate"], dtype=f),
        "ffn_w_up": np.ascontiguousarray(inputs["ffn_w_up"], dtype=f),
        "ffn_w_down": np.ascontiguousarray(inputs["ffn_w_down"], dtype=f),
        "od_w_in": np.ascontiguousarray(np.asarray(inputs["od_w_in"], f)[0]),
        "od_w_out": np.ascontiguousarray(np.asarray(inputs["od_w_out"], f)[0]),
        "od_cols": odc,
        "gla_w2b": np.ascontiguousarray(gw2b),
    }
    m.update(_CONSTS)
    return m


def kernel(**inputs):
    nc = build()
    in_maps = [host_inputs(inputs, b) for b in range(NCORES)]
    res = run_bass_kernel_spmd(nc, in_maps, core_ids=list(range(NCORES)))
    return np.stack([np.asarray(r["out"], np.float32) for r in res.results], axis=0)
```

```python
from contextlib import ExitStack
import numpy as np
import concourse.bass as bass
import concourse.mybir as mybir
from concourse.bass_utils import run_bass_kernel_spmd

F32 = mybir.dt.float32
BF16 = mybir.dt.bfloat16
AF = mybir.ActivationFunctionType
ALU = mybir.AluOpType
AX = mybir.AxisListType

D = 1024
DFF = 2816
NF = DFF // 128
SEQ = 2048
CTX = 256
T = SEQ + CTX
NT = T // 128
EPS = 1e-6
NCORES = 8

COMPUTE = ("tensor", "vector", "scalar", "gpsimd")
ENGINES = ("tensor", "vector", "scalar", "gpsimd", "sync")
N_DMA_SEMS = 24


class Buf:
    _n = 0

    def __init__(self, t, nslots=1, name=None):
        self.t = t
        self.nslots = nslots
        Buf._n += 1
        self.id = Buf._n
        self.name = name

    def __getitem__(self, idx):
        return self.t[idx]


class Sync:
    def __init__(self, nc, es):
        self.nc = nc
        self.sems = {e: es.enter_context(nc.semaphore("s_" + e)) for e in COMPUTE}
        self.dsems = [es.enter_context(nc.semaphore("d_%d" % i)) for i in range(N_DMA_SEMS)]
        self.pos = {e: 0 for e in COMPUTE}
        self.n_dma = 0
        self.dma_uses = [0] * N_DMA_SEMS


class Prog:
    def __init__(self, nc, sync, name="ph"):
        self.nc = nc
        self.sy = sync
        self.name = name
        self.es = ExitStack()
        self.ops = []
        self.state = {}
        self.dma_last = [None] * N_DMA_SEMS
        self.start_pos = dict(sync.pos)
        self.start_dma = [16 * u for u in sync.dma_uses]

    def sbuf(self, name, shape, dtype, nslots=1):
        t = self.es.enter_context(self.nc.sbuf_tensor(self.name + "_" + name, list(shape), dtype))
        return Buf(t, nslots, name)

    def psum(self, name, shape, dtype=F32, nslots=1):
        t = self.es.enter_context(self.nc.psum_tensor(self.name + "_" + name, list(shape), dtype))
        return Buf(t, nslots, name)

    def _expand(self, keys):
        out = []
        for k in keys:
            if isinstance(k, Buf):
                k = (k, None)
            b, s = k
            if s is None:
                out.extend((b.id, i) for i in range(b.nslots))
            elif isinstance(s, (list, tuple, range)):
                for i in s:
                    assert 0 <= i < b.nslots, (b.name, i, b.nslots)
                    out.append((b.id, i))
            else:
                assert 0 <= s < b.nslots, (b.name, s, b.nslots)
                out.append((b.id, s))
        return out

    def _deps(self, idx, reads, writes):
        deps = set()
        rk = self._expand(reads)
        wk = self._expand(writes)
        for k in rk:
            st = self.state.setdefault(k, [None, []])
            if st[0] is not None:
                deps.add(st[0])
        for k in wk:
            st = self.state.setdefault(k, [None, []])
            if st[0] is not None:
                deps.add(st[0])
            deps.update(st[1])
        for k in rk:
            self.state[k][1].append(idx)
        for k in wk:
            self.state[k] = [idx, []]
        deps.discard(idx)
        return deps

    def op(self, eng, fn, reads=(), writes=()):
        idx = len(self.ops)
        deps = self._deps(idx, reads, writes)
        self.ops.append(dict(eng=eng, fn=fn, deps=deps, dma=False, pos=self.sy.pos[eng]))
        self.sy.pos[eng] += 1
        return idx

    def dma(self, out, in_, reads=(), writes=(), q="sync"):
        idx = len(self.ops)
        deps = self._deps(idx, reads, writes)
        sy = self.sy
        s = sy.n_dma % N_DMA_SEMS
        sy.n_dma += 1
        prev = self.dma_last[s]
        sy.dma_uses[s] += 1
        self.dma_last[s] = idx
        self.ops.append(dict(eng=q, fn=lambda e: e.dma_start(out=out, in_=in_), deps=deps, dma=True, sem=s,
                             target=16 * sy.dma_uses[s], prev=prev))
        return idx

    def mm(self, out, lhsT, rhs, start, stop, r, w):
        return self.op("tensor", lambda e: e.matmul(out, lhsT=lhsT, rhs=rhs, start=start, stop=stop), r, w)

    def tr(self, out, in_, ident, r, w):
        return self.op("tensor", lambda e: e.transpose(out, in_, ident), r, w)

    def act(self, out, in_, func, r, w, bias=None, scale=None, accum_out=None):
        kw = {}
        if bias is not None:
            kw["bias"] = bias
        if scale is not None:
            kw["scale"] = scale
        if accum_out is not None:
            kw["accum_out"] = accum_out
        return self.op("scalar", lambda e: e.activation(out=out, in_=in_, func=func, **kw), r, w)

    def tt(self, out, in0, in1, op, r, w, eng="vector"):
        return self.op(eng, lambda e: e.tensor_tensor(out=out, in0=in0, in1=in1, op=op), r, w)

    def ts(self, out, in0, s1, op0, r, w, s2=None, op1=None, eng="vector"):
        if op1 is None:
            return self.op(eng, lambda e: e.tensor_scalar(out=out, in0=in0, scalar1=s1, scalar2=None, op0=op0), r, w)
        return self.op(eng, lambda e: e.tensor_scalar(out=out, in0=in0, scalar1=s1, scalar2=s2, op0=op0, op1=op1), r, w)

    def stt(self, out, in0, scalar, in1, op0, op1, r, w):
        return self.op("vector", lambda e: e.scalar_tensor_tensor(out=out, in0=in0, scalar=scalar, in1=in1,
                                                                  op0=op0, op1=op1), r, w)

    def copy(self, out, in_, r, w, eng="vector"):
        return self.op(eng, lambda e: e.tensor_copy(out=out, in_=in_), r, w)

    def memset(self, ap, val, w, eng="vector"):
        return self.op(eng, lambda e: e.memset(ap, val), (), w)

    def emit(self):
        nc = self.nc
        sy = self.sy
        ops = self.ops
        per_eng = {e: [] for e in ENGINES}
        for i, o in enumerate(ops):
            per_eng[o["eng"]].append(i)
        end_pos = dict(sy.pos)
        end_dma = [16 * u for u in sy.dma_uses]
        start_pos = self.start_pos
        start_dma = self.start_dma

        def replay(ename):
            def body(eng):
                waited = dict(start_pos)
                dwaited = list(start_dma)
                for i in per_eng[ename]:
                    o = ops[i]
                    need = {}
                    dneed = {}
                    deps = set(o["deps"])
                    if o["dma"] and o["prev"] is not None:
                        deps.add(o["prev"])
                    for j in deps:
                        d = ops[j]
                        if d["dma"]:
                            dneed[d["sem"]] = max(dneed.get(d["sem"], 0), d["target"])
                        else:
                            if d["eng"] == "tensor" and ename == "tensor" and not o["dma"]:
                                continue
                            need[d["eng"]] = max(need.get(d["eng"], 0), d["pos"] + 1)
                    for e, v in need.items():
                        if v > waited[e]:
                            eng.wait_ge(sy.sems[e], v)
                            waited[e] = v
                    for s, v in dneed.items():
                        if v > dwaited[s]:
                            eng.wait_ge(sy.dsems[s], v)
                            dwaited[s] = v
                    ins = o["fn"](eng)
                    if o["dma"]:
                        ins.then_inc(sy.dsems[o["sem"]], 16)
                    else:
                        ins.then_inc(sy.sems[ename], 1)
                if ename == "sync":
                    for s in range(N_DMA_SEMS):
                        if end_dma[s] > start_dma[s]:
                            eng.wait_ge(sy.dsems[s], end_dma[s])
                    for e in COMPUTE:
                        if end_pos[e] > start_pos[e]:
                            eng.wait_ge(sy.sems[e], end_pos[e])
            return body

        with nc.Block() as block:
            for e in ENGINES:
                if per_eng[e] or e == "sync":
                    getattr(block, e)(replay(e))
        self.es.close()


def segs(t0, n):
    out = []
    if t0 < CTX:
        m = min(n, CTX - t0)
        out.append((t0, m, 1))
        if n > m:
            out.append((CTX, n - m, 0))
    else:
        out.append((t0, n, 0))
    return out


def tiles_of(t0, n):
    return list(range(t0 // 128, (t0 + n + 127) // 128))


class K:
    pass


class Rot:
    def __init__(self, mk, n):
        self.t = [mk(i) for i in range(n)]
        self.i = 0

    def next(self):
        t = self.t[self.i % len(self.t)]
        self.i += 1
        return t


GROUPS5 = [(0, 256), (256, 512), (768, 512), (1280, 512), (1792, 512)]
LATG = GROUPS5[1:]
FULL = [[(0, 384), (384, 384), (768, 384)], [(1152, 384), (1536, 384), (1920, 384)]]
LAT = [[(256, 512), (768, 512)], [(1280, 512), (1792, 512)]]
C_ID, C_BD64, C_ROT64, C_UI, C_LI, C_SL, C_SU, C_ROT96 = range(8)
NCST = 8


def build(stage=99, dbg=(), layers=(0, 1), parts=None, skip_ffn1=False):
    nc = bass.Bass("TRN2", target_bir_lowering=False)
    dt = nc.dram_tensor

    def inp(name, shape, dtype=F32):
        return dt(name, list(shape), dtype, kind="ExternalInput").ap()

    def scratch(name, shape, dtype, nslots=1):
        return Buf(dt("scr_" + name, list(shape), dtype, kind="Internal").ap(), nslots, name)

    x_d = inp("x", [SEQ, D])
    ctx_d = inp("ctx", [CTX, D])
    ccol_d = inp("ccol", [128, 16])
    adaw_d = inp("ada_w", [2, D, 9 * D])
    adab_d = inp("ada_b_col", [128, 2, 72])
    ng_d = inp("norm_g_col", [128, 2, 3, 8])
    wg_d = inp("ffn_w_gate", [2, 2, D, DFF])
    wu_d = inp("ffn_w_up", [2, 2, D, DFF])
    wd_d = inp("ffn_w_down", [2, 2, DFF, D])
    cst_d = inp("cst", [128, NCST, 128])
    bdsel_d = inp("bdsel", [128, 2])
    odw_d = inp("od_w_in", [D, 2336])
    odwo_d = inp("od_w_out", [D, D])
    odcol_d = inp("od_cols", [128, 11])
    swcs_d = inp("swa_cs", [128, 2, SEQ])
    gw2b_d = inp("gla_w2b", [17, 2, 256])
    evw_d = inp("ev_w_in", [D, 2752])
    evwo_d = inp("ev_w_out", [D, D])
    evqu_d = inp("ev_w_q_up", [384, 768])
    evkv_d = inp("ev_w_kv_up", [256, 1024])
    evcol_d = inp("ev_cols", [128, 8])
    mlcs_d = inp("mla_cs", [96, 2, SEQ])
    lvm_d = inp("lvm", [128, 14, 128])
    gcv_d = inp("gdn_conv_col", [128, 12, 5])
    grow_d = inp("gdn_rows", [128, 544])
    out_d = dt("out", [SEQ, D], F32, kind="ExternalOutput").ap()
    dbg_d = {}
    for name, shape in dbg:
        dbg_d[name] = dt("dbg_" + name, list(shape), F32, kind="ExternalOutput").ap()

    xn_s = scratch("xn", [128, 8, T], BF16)
    mix_s = scratch("mix", [128, 8, T], BF16)
    swq_s = scratch("swq", [128, 4, SEQ], BF16)
    swk_s = scratch("swk", [128, T], BF16)
    swv_s = scratch("swv", [128, NT * 256], BF16)
    swr_s = scratch("swr", [128, NT * 2], F32)
    glq_s = [scratch("glq%d" % d, [128, 2, T], BF16) for d in range(2)]
    glk_s = [scratch("glk%d" % d, [128, 2, T], BF16) for d in range(2)]
    glkd_s = [scratch("glkd%d" % d, [T, 256], BF16) for d in range(2)]
    glet_s = [scratch("glet%d" % d, [128, 2, NT], F32) for d in range(2)]
    glv_s = scratch("glv", [T, 512], BF16)
    glr_s = scratch("glr", [128, 4, SEQ], BF16)

    mqf_s = scratch("mqf", [96, 8, T], BF16)
    mkf_s = scratch("mkf", [96, 8, T], BF16)
    mv_s = scratch("mv", [T, 512], BF16)
    mrk_s = scratch("mrk", [128, NT * 8], F32)
    gq_s = scratch("gq", [512, T], F32)
    gk_s = scratch("gk", [512, T], F32)
    gktm_s = scratch("gktm", [T, 512], F32)
    gvtm_s = scratch("gvtm", [T, 512], F32)
    gg_s = scratch("gg", [128, NT * 16], F32)
    gb_s = scratch("gb", [128, NT * 16], F32)
    ggate_s = scratch("ggate", [T, 512], BF16)
    og_s = [scratch("og%d" % d, [T, 512], F32) for d in range(2)]
    xt_s = scratch("xtpark", [128, 4 * T], F32)

    es = ExitStack()
    sy = Sync(nc, es)

    def persist(name, shape, dtype, nslots=1):
        return Buf(es.enter_context(nc.sbuf_tensor(name, list(shape), dtype)), nslots, name)

    XT = persist("XT", [128, 8, T], F32, NT)
    cst = persist("cst_s", [128, NCST, 128], F32)
    ident = cst[:, C_ID, :]
    ones_f = persist("ones_f", [128, 128], F32)
    ones_b = persist("ones_b", [128, 128], BF16)
    cstb = persist("cstb", [128, NCST, 128], BF16)
    modT = persist("modT", [128, 2, 72, 2], F32)
    AG = persist("AG", [128, 2, 3, 8, 2], F32)
    GH = persist("GH", [128, 2, 3, 8, 2], F32)
    ng = persist("ng", [128, 2, 3, 8], F32)

    def xk(t0, n):
        return [(XT, i) for i in tiles_of(t0, n)]

    P = Prog(nc, sy, "p0")
    P.dma(cst[:], cst_d, writes=[cst])
    P.dma(ng[:], ng_d, writes=[ng])
    P.memset(ones_f[:], 1.0, [ones_f])
    P.memset(ones_b[:], 1.0, [ones_b])
    P.copy(cstb[:], cst[:], [cst], [cstb], eng="gpsimd")
    xin = [P.sbuf("xin%d" % i, [128, D], F32) for i in range(2)]
    ptr = [P.psum("ptr%d" % i, [128, 4, 128]) for i in range(2)]
    for tt in range(NT):
        st = xin[tt % 2]
        src = ctx_d[tt * 128:(tt + 1) * 128, :] if tt < 2 else x_d[(tt - 2) * 128:(tt - 1) * 128, :]
        P.dma(st[:], src, writes=[st])
        for half in range(2):
            ps = ptr[half]
            for c4 in range(4):
                c = half * 4 + c4
                P.tr(ps[:, c4, :], st[:, c * 128:(c + 1) * 128], ident, [st, cst], [ps])
            if half == 0:
                P.copy(XT[:, 0:4, tt * 128:(tt + 1) * 128], ps[:], [ps], [(XT, tt)])
            else:
                P.act(XT[:, 4:8, tt * 128:(tt + 1) * 128], ps[:], AF.Copy, [ps], [(XT, tt)])

    ccol = P.sbuf("ccol", [128, 16], F32)
    sc3 = P.sbuf("sc3", [128, 8, 2], F32)
    adab = P.sbuf("adab", [128, 2, 72], F32)
    P.dma(ccol[:], ccol_d, writes=[ccol])
    P.dma(adab[:], adab_d, writes=[adab])
    P.act(sc3[:, :, 0], ccol[:, 0:8], AF.Silu, [ccol], [sc3])
    P.act(sc3[:, :, 1], ccol[:, 8:16], AF.Silu, [ccol], [sc3])
    ABLK = 512
    awst = [P.sbuf("awst%d" % i, [128, 8, ABLK], F32) for i in range(2)]
    prow = [P.psum("prow%d" % i, [2, ABLK]) for i in range(2)]
    mrow = P.sbuf("mrow", [2, 9 * D], F32)
    pmod = P.psum("pmod", [128, 72, 2])
    nb = 0
    for l in range(2):
        aw = adaw_d[l].rearrange("(kc p) n -> p kc n", p=128)
        for blk in range(9 * D // ABLK):
            st = awst[nb % 2]
            pr = prow[nb % 2]
            nb += 1
            P.dma(st[:], aw[:, :, blk * ABLK:(blk + 1) * ABLK], writes=[st])
            for kc in range(8):
                P.mm(pr[:], sc3[:, kc, :], st[:, kc, :], kc == 0, kc == 7, [st, sc3], [pr])
            P.act(mrow[:, blk * ABLK:(blk + 1) * ABLK], pr[:], AF.Copy, [pr], [mrow])
        for j in range(72):
            P.tr(pmod[:, j, :], mrow[:, j * 128:(j + 1) * 128], cst[0:2, C_ID, 0:2], [mrow, cst], [pmod])
        for s in range(2):
            P.tt(modT[:, l, :, s], pmod[:, :, s], adab[:, l, :], ALU.add, [pmod, adab], [modT])
    for l in range(2):
        for n in range(3):
            for s in range(2):
                P.stt(AG[:, l, n, :, s], modT[:, l, (3 * n + 1) * 8:(3 * n + 2) * 8, s], 1.0, ng[:, l, n, :],
                      ALU.add, ALU.mult, [modT, ng], [AG])
                P.ts(GH[:, l, n, :, s], modT[:, l, (3 * n + 2) * 8:(3 * n + 3) * 8, s], 0.5 if n != 1 else 1.0,
                     ALU.mult, [modT], [GH])
    if "modT" in dbg_d:
        P.dma(dbg_d["modT"], modT[:].rearrange("p l j s -> p (l j s)"), reads=[modT])
    P.emit()

    def norm_mod(P, W, l, n_idx, t0, n, dst_fn, dst_keys):
        sq, lnv, rs, pss, sqb = (w.next() for w in W)
        P.act(sqb[:, :, 0:n], XT[:, :, t0:t0 + n], AF.Square, xk(t0, n), [sqb])
        for c in range(8):
            P.mm(pss[:, 0:n], ones_b[:], sqb[:, c, 0:n], c == 0, c == 7, [sqb, ones_b], [pss])
        P.act(lnv[:, 0:n], pss[:, 0:n], AF.Ln, [pss], [lnv], bias=EPS, scale=1.0 / D)
        P.act(rs[:, 0:n], lnv[:, 0:n], AF.Exp, [lnv], [rs], scale=-0.5)
        P.tt(sq[:, :, 0:n], XT[:, :, t0:t0 + n], rs[:, 0:n].unsqueeze(1).broadcast_to([128, 8, n]), ALU.mult,
             xk(t0, n) + [rs, sq], [sq])
        for (s0, sn, s) in segs(t0, n):
            for c in range(8):
                P.act(dst_fn(c, s0, sn), sq[:, c, s0 - t0:s0 - t0 + sn], AF.Identity,
                      [sq], dst_keys, bias=modT[:, l, (3 * n_idx) * 8 + c, s:s + 1],
                      scale=AG[:, l, n_idx, c, s:s + 1])

    def norm_ws(P, nmax, nb=1):
        return (Rot(lambda i: P.sbuf("sq%d" % i, [128, 8, nmax], F32), nb),
                Rot(lambda i: P.sbuf("lnv%d" % i, [128, nmax], F32), nb),
                Rot(lambda i: P.sbuf("rs%d" % i, [128, nmax], F32), nb),
                Rot(lambda i: P.psum("pss%d" % i, [128, 512]), nb),
                Rot(lambda i: P.sbuf("sqb%d" % i, [128, 8, nmax], BF16), nb))

    def ffn(l, which, halves):
        n_idx = 0 if which == 0 else 2
        wg_v = wg_d[l, which].rearrange("(kc p) n -> p kc n", p=128)
        wu_v = wu_d[l, which].rearrange("(kc p) n -> p kc n", p=128)
        wd_v = wd_d[l, which].rearrange("(f p) n -> p f n", p=128)
        for hi, groups in enumerate(halves):
            h0 = groups[0][0]
            NH = sum(g[1] for g in groups)
            ng_ = len(groups)
            tag = "f%d%d%d" % (l, which, hi)
            esA = ExitStack()
            AT = Buf(esA.enter_context(nc.sbuf_tensor(tag + "AT", [128, NF, NH], BF16)), NF * ng_, "AT")
            P = Prog(nc, sy, tag + "a")
            hT = P.sbuf("hT", [128, 8, NH], BF16, ng_)
            nmax = max(g[1] for g in groups)
            W = norm_ws(P, nmax)
            for gi, (t0, n) in enumerate(groups):
                norm_mod(P, W, l, n_idx, t0, n, lambda c, s0, sn: hT[:, c, s0 - h0:s0 - h0 + sn], [(hT, gi)])
            stg = P.sbuf("stg", [128, 8, 256], F32)
            stu = P.sbuf("stu", [128, 8, 256], F32)
            wgb = [P.sbuf("wgb%d" % i, [128, 8, 256], BF16) for i in range(2)]
            wub = [P.sbuf("wub%d" % i, [128, 8, 256], BF16) for i in range(2)]
            psg = [P.psum("psg%d" % i, [128, 512]) for i in range(2)]
            psu = [P.psum("psu%d" % i, [128, 512]) for i in range(2)]
            sga = [P.sbuf("sga%d" % i, [128, nmax], F32) for i in range(2)]
            it = 0
            NB = NF // 2

            def fetch_a(fb):
                P.dma(stg[:], wg_v[:, :, fb * 256:(fb + 1) * 256], writes=[stg])
                P.dma(stu[:], wu_v[:, :, fb * 256:(fb + 1) * 256], writes=[stu])

            def cast_a(fb):
                P.act(wgb[fb % 2][:], stg[:], AF.Copy, [stg], [wgb[fb % 2]])
                P.copy(wub[fb % 2][:], stu[:], [stu], [wub[fb % 2]])

            fetch_a(0)
            cast_a(0)
            for fb in range(NB):
                pb = fb % 2
                if fb + 1 < NB:
                    fetch_a(fb + 1)
                cnt = 0
                for f2 in range(2):
                    f = fb * 2 + f2
                    for gi, (t0, n) in enumerate(groups):
                        o0 = t0 - h0
                        pg, pu, sg = psg[it % 2], psu[it % 2], sga[it % 2]
                        it += 1
                        for kc in range(8):
                            P.mm(pg[:, 0:n], wgb[pb][:, kc, f2 * 128:(f2 + 1) * 128], hT[:, kc, o0:o0 + n], kc == 0, kc == 7,
                                 [wgb[pb], (hT, gi)], [pg])
                        for kc in range(8):
                            P.mm(pu[:, 0:n], wub[pb][:, kc, f2 * 128:(f2 + 1) * 128], hT[:, kc, o0:o0 + n], kc == 0, kc == 7,
                                 [wub[pb], (hT, gi)], [pu])
                        P.act(sg[:, 0:n], pg[:, 0:n], AF.Silu, [pg], [sg])
                        P.tt(AT[:, f, o0:o0 + n], sg[:, 0:n], pu[:, 0:n], ALU.mult, [sg, pu], [(AT, f * ng_ + gi)])
                        cnt += 1
                        if cnt == ng_ and fb + 1 < NB:
                            cast_a(fb + 1)
            P.emit()
            P = Prog(nc, sy, tag + "b")
            std = P.sbuf("std", [128, NF, 256], F32)
            wdb = [P.sbuf("wdb%d" % i, [128, NF, 256], BF16) for i in range(2)]
            psy = [P.psum("psy%d" % i, [128, 512]) for i in range(2)]
            it = 0

            def fetch_b(db):
                P.dma(std[:], wd_v[:, :, db * 256:(db + 1) * 256], writes=[std])
                P.act(wdb[db % 2][:], std[:], AF.Copy, [std], [wdb[db % 2]])

            fetch_b(0)
            for db in range(4):
                pb = db % 2
                if db + 1 < 4:
                    fetch_b(db + 1)
                for d2 in range(2):
                    d = db * 2 + d2
                    for gi, (t0, n) in enumerate(groups):
                        o0 = t0 - h0
                        py = psy[it % 2]
                        it += 1
                        for f in range(NF):
                            P.mm(py[:, 0:n], wdb[pb][:, f, d2 * 128:(d2 + 1) * 128], AT[:, f, o0:o0 + n], f == 0, f == NF - 1,
                                 [wdb[pb], (AT, f * ng_ + gi)], [py])
                        for (s0, sn, s) in segs(t0, n):
                            P.stt(XT[:, d, s0:s0 + sn], py[:, s0 - t0:s0 - t0 + sn], GH[:, l, n_idx, d, s:s + 1],
                                  XT[:, d, s0:s0 + sn], ALU.mult, ALU.add, [py] + xk(s0, sn), xk(s0, sn))
            P.emit()
            esA.close()

    def dump_xt(name):
        if name in dbg_d:
            P = Prog(nc, sy, "dbg" + name)
            P.dma(dbg_d[name], XT[:].rearrange("p c t -> p (c t)"), reads=[XT])
            P.emit()

    def make_xn(l):
        P = Prog(nc, sy, "xn%d" % l)
        W = norm_ws(P, 384, 2)
        hb = Rot(lambda i: P.sbuf("hb%d" % i, [128, 8, 384], BF16), 2)
        for (t0, n) in FULL[0] + FULL[1]:
            h = hb.next()
            norm_mod(P, W, l, 1, t0, n, lambda c, s0, sn, h=h, t0=t0: h[:, c, s0 - t0:s0 - t0 + sn], [h])
            P.dma(xn_s[:, :, t0:t0 + n], h[:, :, 0:n], reads=[h], writes=[xn_s])
        P.emit()

    def load_w(P, dst, dst_ap, src_ap, stage, stage_ap):
        P.dma(stage_ap, src_ap, writes=[stage])
        P.copy(dst_ap, stage_ap, [stage], [dst], eng="gpsimd")

    def out_proj(l, wo_d, groups):
        P = Prog(nc, sy, "op%d" % l)
        wo = P.sbuf("wo", [128, 8, D], BF16)
        stwR = Rot(lambda i: P.sbuf("stw%d" % i, [128, 8, 256], F32), 2)
        wv = wo_d.rearrange("(kc p) n -> p kc n", p=128)
        for q in range(4):
            stw = stwR.next()
            P.dma(stw[:], wv[:, :, q * 256:(q + 1) * 256], writes=[stw])
            if q % 2 == 0:
                P.act(wo[:, :, q * 256:(q + 1) * 256], stw[:], AF.Copy, [stw], [wo])
            else:
                P.copy(wo[:, :, q * 256:(q + 1) * 256], stw[:], [stw], [wo])
        mb = Rot(lambda i: P.sbuf("mb%d" % i, [128, 8, 512], BF16), 2)
        pp = Rot(lambda i: P.psum("pp%d" % i, [128, 512]), 3)
        for (t0, n) in groups:
            m = mb.next()
            P.dma(m[:, :, 0:n], mix_s[:, :, t0:t0 + n], reads=[mix_s], writes=[m])
            for dc in range(8):
                ps = pp.next()
                for kc in range(8):
                    P.mm(ps[:, 0:n], wo[:, kc, dc * 128:(dc + 1) * 128], m[:, kc, 0:n], kc == 0, kc == 7, [wo, m], [ps])
                for (s0, sn, s) in segs(t0, n):
                    P.stt(XT[:, dc, s0:s0 + sn], ps[:, s0 - t0:s0 - t0 + sn], GH[:, l, 1, dc, s:s + 1],
                          XT[:, dc, s0:s0 + sn], ALU.mult, ALU.add, [ps] + xk(s0, sn), xk(s0, sn))
        P.emit()

    def swa_prep():
        P = Prog(nc, sy, "swp")
        wv_ = odw_d.rearrange("(kc p) n -> p kc n", p=128)
        odc = P.sbuf("odc", [128, 11], F32)
        P.dma(odc[:], odcol_d, writes=[odc])
        bdf = P.sbuf("bdf", [128, 2], F32)
        P.dma(bdf[:], bdsel_d, writes=[bdf])
        bds = P.sbuf("bds", [128, 2], BF16)
        P.copy(bds[:], bdf[:], [bdf], [bds])
        cs = P.sbuf("cs", [128, 2, SEQ], F32)
        P.dma(cs[:], swcs_d, writes=[cs])
        st = P.sbuf("st", [128, 8, 512], F32)
        wq = P.sbuf("wq", [128, 8, 4, 2, 64], BF16)
        wk = P.sbuf("wk", [128, 8, 128], BF16)
        wvv = P.sbuf("wvv", [128, 8, 128], BF16)
        P.dma(st[:], wv_[:, :, 1568:2080], writes=[st])
        st5 = st[:].rearrange("p k (g j d) -> p k g j d", g=2, j=4)
        for g in range(2):
            P.copy(wq[:, :, :, g, :], st5[:, :, g, :, :], [st], [wq], eng="gpsimd")
        load_w(P, wk, wk[:], wv_[:, :, 2080:2208], st, st[:, :, 0:128])
        load_w(P, wvv, wvv[:], wv_[:, :, 2208:2336], st, st[:, :, 0:128])
        xg = Rot(lambda i: P.sbuf("xg%d" % i, [128, 8, 512], BF16), 2)
        pp = Rot(lambda i: P.psum("pp%d" % i, [128, 512]), 3)
        pssR = Rot(lambda i: P.psum("pss%d" % i, [128, 512]), 2)
        prot = P.psum("prot", [128, 512])
        prk = P.psum("prk", [128, NT, 2])
        pv = P.psum("pv", [128, 128])
        sqR = Rot(lambda i: P.sbuf("sq%d" % i, [128, 512], BF16), 2)
        lnvR = Rot(lambda i: P.sbuf("lnv%d" % i, [128, 512], F32), 2)
        rsR = Rot(lambda i: P.sbuf("rs%d" % i, [128, 512], F32), 2)
        qgR = Rot(lambda i: P.sbuf("qg%d" % i, [128, 512], BF16), 2)
        t1R = Rot(lambda i: P.sbuf("t1%d" % i, [128, 512], F32), 2)
        t2R = Rot(lambda i: P.sbuf("t2%d" % i, [128, 512], F32), 2)
        qo = Rot(lambda i: P.sbuf("qo%d" % i, [128, 512], BF16), 2)
        vd = P.sbuf("vd", [128, NT, 2, 2, 64], BF16)
        rsk = P.sbuf("rsk", [128, NT * 2], F32)
        for (t0, n) in GROUPS5:
            x = xg.next()
            P.dma(x[:, :, 0:n], xn_s[:, :, t0:t0 + n], reads=[xn_s], writes=[x])
            lat = t0 >= CTX
            l0 = t0 - CTX
            if lat:
                for j in range(4):
                    ps = pp.next()
                    for kc in range(8):
                        P.mm(ps[:, 0:n], wq[:, kc, j].rearrange("p g d -> p (g d)"), x[:, kc, 0:n], kc == 0, kc == 7,
                             [wq, x], [ps])
                    sq, pss, lnv, rs, qg, t1, t2 = (sqR.next(), pssR.next(), lnvR.next(), rsR.next(), qgR.next(),
                                                    t1R.next(), t2R.next())
                    P.act(sq[:, 0:n], ps[:, 0:n], AF.Square, [ps], [sq])
                    P.mm(pss[:, 0:n], cstb[:, C_BD64, :], sq[:, 0:n], True, True, [cstb, sq], [pss])
                    P.act(lnv[:, 0:n], pss[:, 0:n], AF.Ln, [pss], [lnv], bias=EPS, scale=1.0 / 64)
                    P.act(rs[:, 0:n], lnv[:, 0:n], AF.Exp, [lnv], [rs], scale=-0.5)
                    P.act(qg[:, 0:n], ps[:, 0:n], AF.Identity, [ps], [qg], scale=odc[:, 0:1])
                    P.mm(prot[:, 0:n], cstb[:, C_ROT64, :], qg[:, 0:n], True, True, [cstb, qg], [prot])
                    P.tt(t1[:, 0:n], qg[:, 0:n], cs[:, 0, l0:l0 + n], ALU.mult, [qg, cs], [t1])
                    P.tt(t2[:, 0:n], prot[:, 0:n], cs[:, 1, l0:l0 + n], ALU.mult, [prot, cs], [t2])
                    P.tt(t1[:, 0:n], t1[:, 0:n], t2[:, 0:n], ALU.add, [t1, t2], [t1])
                    q = qo.next()
                    P.tt(q[:, 0:n], t1[:, 0:n], rs[:, 0:n], ALU.mult, [t1, rs], [q])
                    P.dma(swq_s[:, j, l0:l0 + n], q[:, 0:n], reads=[q], writes=[swq_s])
            ps = pp.next()
            for kc in range(8):
                P.mm(ps[:, 0:n], wk[:, kc, :], x[:, kc, 0:n], kc == 0, kc == 7, [wk, x], [ps])
            sq, qg, t1, t2 = sqR.next(), qgR.next(), t1R.next(), t2R.next()
            P.act(sq[:, 0:n], ps[:, 0:n], AF.Square, [ps], [sq])
            for ti in range(n // 128):
                tt = t0 // 128 + ti
                P.mm(prk[:, tt, :], sq[:, ti * 128:(ti + 1) * 128], bds[:], True, True, [sq, bds], [prk])
            q = qo.next()
            if lat:
                P.act(qg[:, 0:n], ps[:, 0:n], AF.Identity, [ps], [qg], scale=odc[:, 1:2])
                P.mm(prot[:, 0:n], cstb[:, C_ROT64, :], qg[:, 0:n], True, True, [cstb, qg], [prot])
                P.tt(t1[:, 0:n], qg[:, 0:n], cs[:, 0, l0:l0 + n], ALU.mult, [qg, cs], [t1])
                P.tt(t2[:, 0:n], prot[:, 0:n], cs[:, 1, l0:l0 + n], ALU.mult, [prot, cs], [t2])
                P.tt(q[:, 0:n], t1[:, 0:n], t2[:, 0:n], ALU.add, [t1, t2], [q])
            else:
                P.act(q[:, 0:n], ps[:, 0:n], AF.Identity, [ps], [q], scale=odc[:, 1:2])
            P.dma(swk_s[:, t0:t0 + n], q[:, 0:n], reads=[q], writes=[swk_s])
            for ti in range(n // 128):
                tt = t0 // 128 + ti
                for kc in range(8):
                    P.mm(pv[:], x[:, kc, ti * 128:(ti + 1) * 128], wvv[:, kc, :], kc == 0, kc == 7, [x, wvv], [pv])
                pv3 = pv[:].rearrange("p (g d) -> p g d", g=2)
                P.copy(vd[:, tt, :, 0, :], pv3, [pv], [vd])
                P.act(vd[:, tt, :, 1, :], pv3, AF.Copy, [pv], [vd])
        P.act(rsk[:], prk[:].rearrange("p a b -> p (a b)"), AF.Ln, [prk], [rsk], bias=EPS, scale=1.0 / 64)
        P.act(rsk[:], rsk[:], AF.Exp, [rsk], [rsk], scale=-0.5)
        P.ts(rsk[:], rsk[:], 0.125, ALU.mult, [rsk], [rsk])
        P.dma(swr_s[:], rsk[:], reads=[rsk], writes=[swr_s])
        P.dma(swv_s[:], vd[:].rearrange("p a b c d -> p (a b c d)"), reads=[vd], writes=[swv_s])
        P.emit()

    def swa_core():
        P = Prog(nc, sy, "swc")
        q = P.sbuf("q", [128, 4, SEQ], BF16)
        k = P.sbuf("k", [128, T], BF16)
        v = P.sbuf("v", [128, NT, 2, 128], BF16)
        rsk = P.sbuf("rsk", [128, NT, 2], F32)
        odc = P.sbuf("odc", [128, 11], F32)
        esk = P.sbuf("esk", [128, 8], F32)
        mlo = P.sbuf("mlo", [128, 128], BF16)
        mhi = P.sbuf("mhi", [128, 128], BF16)
        P.dma(q[:], swq_s[:], reads=[swq_s], writes=[q])
        P.dma(k[:], swk_s[:], reads=[swk_s], writes=[k])
        P.dma(v[:].rearrange("p a b c -> p (a b c)"), swv_s[:], reads=[swv_s], writes=[v])
        P.dma(rsk[:].rearrange("p a b -> p (a b)"), swr_s[:], reads=[swr_s], writes=[rsk])
        P.dma(odc[:], odcol_d, writes=[odc])
        P.act(esk[:], odc[:, 3:11], AF.Exp, [odc], [esk])
        P.copy(mlo[:], cst[:, C_LI, :], [cst], [mlo])
        P.copy(mhi[:], cst[:, C_UI, :], [cst], [mhi])
        msw = P.sbuf("msw", [128, 4, SEQ], BF16)
        pS = Rot(lambda i: P.psum("pS%d" % i, [128, 4, 128]), 4)
        pO = Rot(lambda i: P.psum("pO%d" % i, [128, 4, 128]), 2)
        pM = Rot(lambda i: P.psum("pM%d" % i, [128, 4, 128]), 2)
        pT = Rot(lambda i: P.sbuf("pT%d" % i, [128, 4, 128], BF16), 4)
        den = P.sbuf("den", [128, 4, 128], F32)
        rden = P.sbuf("rden", [128, 4, 128], F32)
        for n in range(16):
            for g in range(2):
                kts = [(0, None), (1, None)]
                if n - 1 >= 0:
                    kts.append((n + 1, mlo))
                kts.append((n + 2, None))
                if n + 1 <= 15:
                    kts.append((n + 3, mhi))
                po = pO.next()
                pm = pM.next()
                def scores(kt, mask):
                    ps = pS.next()
                    P.mm(ps[:], k[g * 64:(g + 1) * 64, kt * 128:(kt + 1) * 128],
                         q[g * 64:(g + 1) * 64, :, n * 128:(n + 1) * 128], True, True, [k, q], [ps])
                    pt = pT.next()
                    P.act(pt[:], ps[:], AF.Exp, [ps, rsk], [pt], scale=rsk[:, kt, g:g + 1])
                    if mask is not None:
                        P.tt(pt[:], pt[:], mask[:].unsqueeze(1).broadcast_to([128, 4, 128]), ALU.mult, [pt, mask], [pt])
                    return pt

                nxt = scores(*kts[0])
                for ki, (kt, mask) in enumerate(kts):
                    pt = nxt
                    if ki + 1 < len(kts):
                        nxt = scores(*kts[ki + 1])
                    P.mm(po[:], v[:, kt, g, :], pt[:], ki == 0, ki == len(kts) - 1, [v, pt], [po])
                    P.mm(pm[:], ones_b[:], pt[:], ki == 0, ki == len(kts) - 1, [ones_b, pt], [pm])
                P.tt(den[:], pm[:], esk[:, 4 * g:4 * g + 4].unsqueeze(2).broadcast_to([128, 4, 128]), ALU.add,
                     [pm, esk], [den])
                P.act(rden[:], den[:], AF.Ln, [den], [rden])
                P.act(rden[:], rden[:], AF.Exp, [rden], [rden], scale=-1.0)
                for par in range(2):
                    sl = slice(par * 64, (par + 1) * 64)
                    P.tt(msw[sl, 2 * g:2 * g + 2, n * 128:(n + 1) * 128], po[sl, par::2, :], rden[sl, par::2, :],
                         ALU.mult, [po, rden], [msw])
        P.dma(mix_s[:, 4:8, CTX:T], msw[:], reads=[msw], writes=[mix_s])
        P.emit()

    def gla_prep():
        P = Prog(nc, sy, "glp")
        wv_ = odw_d.rearrange("(kc p) n -> p kc n", p=128)
        stR = Rot(lambda i: P.sbuf("st%d" % i, [128, 8, 256], F32), 1)
        wqk = P.sbuf("wqk", [128, 8, 512], BF16)
        wvv = P.sbuf("wvv", [128, 8, 512], BF16)
        wgz = P.sbuf("wgz", [128, 8, 32], BF16)
        wr = P.sbuf("wr", [128, 8, 512], BF16)
        for (wt_, c0_) in ((wqk, 0), (wvv, 512), (wr, 1056)):
            for hf in range(2):
                st = stR.next()
                load_w(P, wt_, wt_[:, :, hf * 256:(hf + 1) * 256], wv_[:, :, c0_ + hf * 256:c0_ + (hf + 1) * 256], st, st[:])
        st = stR.next()
        load_w(P, wgz, wgz[:], wv_[:, :, 1024:1056], st, st[:, :, 0:32])
        g2f = P.sbuf("g2f", [17, 2, 256], F32)
        g2b = P.sbuf("g2b", [17, 2, 256], BF16)
        P.dma(g2f[:], gw2b_d, writes=[g2f])
        P.copy(g2b[:], g2f[:], [g2f], [g2b])
        msk = P.sbuf("msk", [128, 512], F32)
        P.memset(msk[:], 1.0, [msk])
        for i in range(4):
            P.memset(msk[:, i * 128:i * 128 + 1], 0.0, [msk])
        gz = [P.sbuf("gz%d" % d, [17, 512], BF16) for d in range(2)]
        for d in range(2):
            P.memset(gz[d][:], 1.0, [gz[d]])
        xg = Rot(lambda i: P.sbuf("xg%d" % i, [128, 8, 512], BF16), 2)
        pp = Rot(lambda i: P.psum("pp%d" % i, [128, 512]), 4)
        ptr_ = P.psum("ptr", [128, 2, 128], BF16)
        pgzD = [P.psum("pgz%d" % i, [16, 512]) for i in range(2)]
        qkraw = P.sbuf("qkraw", [128, 4, 512], F32)
        spD = [P.sbuf("sp%d" % i, [128, 2, 512], F32) for i in range(2)]
        preD = [P.sbuf("pre%d" % i, [128, 2, 512], F32) for i in range(2)]
        bcD = [P.sbuf("bc%d" % i, [128, 2, 512], F32) for i in range(2)]
        remD = [P.sbuf("rem%d" % i, [128, 2, 512], F32) for i in range(2)]
        eeR = Rot(lambda i: P.sbuf("ee%d" % i, [128, 2, 512], F32), 2)
        etD = [P.sbuf("et%d" % i, [128, 2, 4], F32) for i in range(2)]
        ob = Rot(lambda i: P.sbuf("ob%d" % i, [128, 2, 512], BF16), 3)
        kdt = Rot(lambda i: P.sbuf("kdt%d" % i, [128, 4, 256], BF16), 2)
        vtm = Rot(lambda i: P.sbuf("vtm%d" % i, [128, 4, 512], BF16), 2)
        rg = Rot(lambda i: P.sbuf("rg%d" % i, [128, 4, 512], BF16), 1)
        idb = P.sbuf("idb", [128, 128], BF16)
        P.copy(idb[:], cst[:, C_ID, :], [cst], [idb])
        for (t0, n) in GROUPS5:
            nt = n // 128
            x = xg.next()
            P.dma(x[:, :, 0:n], xn_s[:, :, t0:t0 + n], reads=[xn_s], writes=[x])
            for cq in range(4):
                ps = pp.next()
                for kc in range(8):
                    P.mm(ps[:, 0:n], wqk[:, kc, cq * 128:(cq + 1) * 128], x[:, kc, 0:n], kc == 0, kc == 7, [wqk, x], [ps])
                P.act(qkraw[:, cq, 0:n], ps[:, 0:n], AF.Copy, [ps], [qkraw])
            vt = vtm.next()
            for ti in range(nt):
                ps = pp.next()
                for kc in range(8):
                    P.mm(ps[:], x[:, kc, ti * 128:(ti + 1) * 128], wvv[:, kc, :], kc == 0, kc == 7, [x, wvv], [ps])
                P.act(vt[:, ti, :], ps[:], AF.Copy, [ps], [vt])
            P.dma(glv_s[t0:t0 + n, :].rearrange("(a p) f -> p a f", p=128), vt[:, 0:nt, :], reads=[vt], writes=[glv_s])
            if t0 >= CTX:
                r = rg.next()
                for h in range(4):
                    ps = pp.next()
                    for kc in range(8):
                        P.mm(ps[:, 0:n], wr[:, kc, h * 128:(h + 1) * 128], x[:, kc, 0:n], kc == 0, kc == 7, [wr, x], [ps])
                    P.act(r[:, h, 0:n], ps[:, 0:n], AF.Silu, [ps], [r])
                P.dma(glr_s[:, :, t0 - CTX:t0 - CTX + n], r[:, :, 0:n], reads=[r], writes=[glr_s])
            for d in range(2):
                for kc in range(8):
                    P.mm(pgzD[d][:, 0:n], wgz[:, kc, 16 * d:16 * d + 16], x[:, kc, 0:n], kc == 0, kc == 7, [wgz, x], [pgzD[d]])
                P.act(gz[d][0:16, 0:n], pgzD[d][:, 0:n], AF.Copy, [pgzD[d]], [gz[d]])
            for c in range(2):
                for d in range(2):
                    sp, pre = spD[d], preD[d]
                    ps = pp.next()
                    P.mm(ps[:, 0:n], g2b[:, d, c * 128:(c + 1) * 128], gz[d][:, 0:n], True, True, [g2b, gz[d]], [ps])
                    P.act(sp[:, c, 0:n], ps[:, 0:n], AF.Exp, [ps], [sp], scale=-1.0)
                    P.act(sp[:, c, 0:n], sp[:, c, 0:n], AF.Ln, [sp], [sp], bias=1.0)
                    P.op("vector", lambda e, c=c, n=n, pre=pre, sp=sp: e.tensor_tensor_scan(
                        out=pre[:, c, 0:n], data0=msk[:, 0:n], data1=sp[:, c, 0:n], initial=0.0,
                        op0=ALU.mult, op1=ALU.add), [msk, sp], [pre])
            for d in range(2):
                sp, pre, bc, rem, et = spD[d], preD[d], bcD[d], remD[d], etD[d]
                pre4 = pre[:, :, 0:n].rearrange("p c (a b) -> p c a b", b=128)
                totb = pre4[:, :, :, 127:128].broadcast_to([128, 2, nt, 128])
                bc4 = bc[:, :, 0:n].rearrange("p c (a b) -> p c a b", b=128)
                rem4 = rem[:, :, 0:n].rearrange("p c (a b) -> p c a b", b=128)
                if d == 0:
                    P.copy(bc[:, :, 0:n], pre[:, :, 0:n], [pre], [bc], eng="gpsimd")
                    P.tt(rem4, totb, pre4, ALU.subtract, [pre], [rem])
                else:
                    P.tt(rem[:, :, 0:n], pre[:, :, 0:n], sp[:, :, 0:n], ALU.subtract, [pre, sp], [rem])
                    P.tt(bc4, totb, rem4, ALU.subtract, [pre, rem], [bc])
                P.act(et[:, :, 0:nt], pre4[:, :, :, 127], AF.Exp, [pre], [et], scale=-1.0 / 16)
                P.dma(glet_s[d][:, :, t0 // 128:t0 // 128 + nt], et[:, :, 0:nt], reads=[et], writes=[glet_s[d]])
            outs = []
            for d in range(2):
                bc, rem = bcD[d], remD[d]
                ee = eeR.next()
                P.act(ee[:, :, 0:n], bc[:, :, 0:n], AF.Exp, [bc], [ee], scale=-1.0 / 16)
                o = ob.next()
                P.stt(o[:, :, 0:n], qkraw[:, 0:2, 0:n], 0.125, ee[:, :, 0:n], ALU.mult, ALU.mult, [qkraw, ee], [o])
                P.dma(glq_s[d][:, :, t0:t0 + n], o[:, :, 0:n], reads=[o], writes=[glq_s[d]])
                ee = eeR.next()
                P.act(ee[:, :, 0:n], bc[:, :, 0:n], AF.Exp, [bc], [ee], scale=1.0 / 16)
                o = ob.next()
                P.tt(o[:, :, 0:n], qkraw[:, 2:4, 0:n], ee[:, :, 0:n], ALU.mult, [qkraw, ee], [o])
                P.dma(glk_s[d][:, :, t0:t0 + n], o[:, :, 0:n], reads=[o], writes=[glk_s[d]])
                ee = eeR.next()
                P.act(ee[:, :, 0:n], rem[:, :, 0:n], AF.Exp, [rem], [ee], scale=-1.0 / 16)
                o = ob.next()
                P.tt(o[:, :, 0:n], qkraw[:, 2:4, 0:n], ee[:, :, 0:n], ALU.mult, [qkraw, ee], [o])
                kd = kdt.next()
                for ti in range(nt):
                    for c in range(2):
                        P.tr(ptr_[:, c, :], o[:, c, ti * 128:(ti + 1) * 128], idb[:], [o, idb], [ptr_])
                    P.copy(kd[:, ti, :], ptr_[:].rearrange("p a b -> p (a b)"), [ptr_], [kd])
                P.dma(glkd_s[d][t0:t0 + n, :].rearrange("(a p) f -> p a f", p=128), kd[:, 0:nt, :], reads=[kd],
                      writes=[glkd_s[d]])
        P.emit()

    def gla_core():
        P = Prog(nc, sy, "glc")
        odc = P.sbuf("odc", [128, 11], F32)
        P.dma(odc[:], odcol_d, writes=[odc])
        ogla = P.sbuf("ogla", [128, 4, SEQ], F32, 16)
        for i in range(4):
            P.memset(ogla[:, i, :], 0.0, [ogla], eng="gpsimd" if i % 2 else "vector")
        S = [[P.sbuf("S%d%d" % (d, c), [128, 128], F32) for c in range(2)] for d in range(2)]
        Sb = [[P.sbuf("Sb%d%d" % (d, h), [128, 128], BF16) for h in range(4)] for d in range(2)]
        qd = Rot(lambda i: P.sbuf("qd%d" % i, [128, 2, 128], BF16), 4)
        kp = Rot(lambda i: P.sbuf("kp%d" % i, [128, 2, 128], BF16), 4)
        kd = Rot(lambda i: P.sbuf("kd%d" % i, [128, 256], BF16), 4)
        vt = Rot(lambda i: P.sbuf("vt%d" % i, [128, 512], BF16), 4)
        et = [P.sbuf("et%d" % d, [128, 2, NT], F32) for d in range(2)]
        aT = Rot(lambda i: P.sbuf("aT%d" % i, [128, 4, 128], BF16), 2)
        pA0 = Rot(lambda i: P.psum("pA0%d" % i, [128, 2, 128]), 1)
        pA1 = Rot(lambda i: P.psum("pA1%d" % i, [128, 2, 128]), 1)
        pO = Rot(lambda i: P.psum("pO%d" % i, [128, 4, 128]), 2)
        pSt = Rot(lambda i: P.psum("pSt%d" % i, [128, 128]), 3)
        orders = [list(range(NT)), [1, 0] + list(range(NT - 1, 1, -1))]
        masks = [cst[:, C_UI, :], cst[:, C_LI, :]]
        for d in range(2):
            P.dma(et[d][:], glet_s[d][:], reads=[glet_s[d]], writes=[et[d]])
            for c in range(2):
                P.memset(S[d][c][:], 0.0, [S[d][c]])
            for h in range(4):
                P.memset(Sb[d][h][:], 0.0, [Sb[d][h]])
        for s_ in range(NT):
            for d in range(2):
                tt = orders[d][s_]
                mask = masks[d]
                q_, k_, kd_, v_ = qd.next(), kp.next(), kd.next(), vt.next()
                tsl = slice(tt * 128, (tt + 1) * 128)
                P.dma(q_[:], glq_s[d][:, :, tsl], reads=[glq_s[d]], writes=[q_])
                P.dma(k_[:], glk_s[d][:, :, tsl], reads=[glk_s[d]], writes=[k_])
                P.dma(kd_[:], glkd_s[d][tsl, :], reads=[glkd_s[d]], writes=[kd_])
                P.dma(v_[:], glv_s[tsl, :], reads=[glv_s], writes=[v_])
                if tt >= 2:
                    a_ = aT.next()
                    for par, pa_ in enumerate((pA0, pA1)):
                        pa = pa_.next()
                        hs = slice(par * 64, par * 64 + 64)
                        for c in range(2):
                            P.mm(pa[:, c, :], k_[hs, c, :], q_[hs, c, :], True, True, [k_, q_], [pa])
                        P.tt(a_[:, par::2, :], pa[:], mask.unsqueeze(1).broadcast_to([128, 2, 128]), ALU.mult,
                             [pa, cst], [a_])
                    po = pO.next()
                    for h in range(4):
                        P.mm(po[:, h, :], v_[:, h * 128:(h + 1) * 128], a_[:, h, :], True, False, [v_, a_], [po])
                        P.mm(po[:, h, :], Sb[d][h][:], q_[:, h // 2, :], False, True, [Sb[d][h], q_], [po])
                    lt = tt - 2
                    osl = ogla[:, :, lt * 128:(lt + 1) * 128]
                    P.tt(osl, osl, po[:], ALU.add, [po, (ogla, lt)], [(ogla, lt)])
                for c in range(2):
                    ps = pSt.next()
                    for hh in range(2):
                        h = 2 * c + hh
                        P.mm(ps[hh * 64:(hh + 1) * 64, :], kd_[:, h * 64:(h + 1) * 64], v_[:, h * 128:(h + 1) * 128],
                             True, True, [kd_, v_], [ps])
                    P.stt(S[d][c][:], S[d][c][:], et[d][:, c, tt:tt + 1], ps[:], ALU.mult, ALU.add,
                          [S[d][c], et[d], ps], [S[d][c]])
                    for hh in range(2):
                        hs = slice(hh * 64, hh * 64 + 64)
                        P.act(Sb[d][2 * c + hh][hs, :], S[d][c][hs, :], AF.Copy, [S[d][c]], [Sb[d][2 * c + hh]])
        sqR = Rot(lambda i: P.sbuf("sq%d" % i, [128, 512], BF16), 2)
        lnvR = Rot(lambda i: P.sbuf("lnv%d" % i, [128, 512], F32), 2)
        rsR = Rot(lambda i: P.sbuf("rs%d" % i, [128, 512], F32), 2)
        tmpR = Rot(lambda i: P.sbuf("tmp%d" % i, [128, 512], F32), 2)
        pss = P.psum("pss", [128, 512])
        rg = Rot(lambda i: P.sbuf("rg%d" % i, [128, 4, 512], BF16), 2)
        mo = Rot(lambda i: P.sbuf("mo%d" % i, [128, 4, 512], BF16), 2)
        for gi in range(4):
            r = rg.next()
            m = mo.next()
            gsl = slice(gi * 512, (gi + 1) * 512)
            keys = [(ogla, i) for i in range(gi * 4, gi * 4 + 4)]
            P.dma(r[:], glr_s[:, :, gsl], reads=[glr_s], writes=[r])
            for h in range(4):
                sq, lnv, rs, tmp = sqR.next(), lnvR.next(), rsR.next(), tmpR.next()
                P.act(sq[:], ogla[:, h, gsl], AF.Square, keys, [sq])
                P.mm(pss[:], ones_b[:], sq[:], True, True, [ones_b, sq], [pss])
                P.act(lnv[:], pss[:], AF.Ln, [pss], [lnv], bias=EPS, scale=1.0 / 128)
                P.act(rs[:], lnv[:], AF.Exp, [lnv], [rs], scale=-0.5)
                P.tt(tmp[:], ogla[:, h, gsl], rs[:], ALU.mult, keys + [rs], [tmp])
                P.stt(m[:, h, :], tmp[:], odc[:, 2:3], r[:, h, :], ALU.mult, ALU.mult, [tmp, odc, r], [m])
            P.dma(mix_s[:, 0:4, CTX + gi * 512:CTX + (gi + 1) * 512], m[:], reads=[m], writes=[mix_s])
        P.emit()

    def mla_prep():
        P = Prog(nc, sy, "mlp")
        wv_ = evw_d.rearrange("(kc p) n -> p kc n", p=128)
        evc = P.sbuf("evc", [128, 8], F32)
        P.dma(evc[:], evcol_d, writes=[evc])
        cs = P.sbuf("cs", [96, 2, SEQ], F32)
        P.dma(cs[:], mlcs_d, writes=[cs])
        st = P.sbuf("st", [128, 8, 384], F32)
        wcq = P.sbuf("wcq", [128, 8, 384], BF16)
        wckv = P.sbuf("wckv", [128, 8, 256], BF16)
        wkr = P.sbuf("wkr", [128, 8, 32], BF16)
        wqu = P.sbuf("wqu", [128, 3, 768], BF16)
        wkK = P.sbuf("wkK", [128, 2, 8, 64], BF16)
        wkV = P.sbuf("wkV", [128, 2, 8, 64], BF16)
        load_w(P, wcq, wcq[:], wv_[:, :, 0:384], st, st[:])
        load_w(P, wckv, wckv[:], wv_[:, :, 384:640], st, st[:, :, 0:256])
        load_w(P, wkr, wkr[:], wv_[:, :, 640:672], st, st[:, :, 0:32])
        stf = st[:].rearrange("p a b -> p (a b)")
        qu_v = evqu_d.rearrange("(kc p) n -> p kc n", p=128)
        st_qu = stf[:, 0:3 * 768].rearrange("p (a b) -> p a b", a=3)
        load_w(P, wqu, wqu[:], qu_v, st, st_qu)
        kv_v = evkv_d.rearrange("(kc p) n -> p kc n", p=128)
        st_kv = stf[:, 0:2048].rearrange("p (a b) -> p a b", a=2)
        P.dma(st_kv, kv_v, writes=[st])
        st_kv5 = stf[:, 0:2048].rearrange("p (a h t d) -> p a h t d", a=2, h=8, t=2)
        P.copy(wkK[:], st_kv5[:, :, :, 0, :], [st], [wkK], eng="gpsimd")
        P.copy(wkV[:], st_kv5[:, :, :, 1, :], [st], [wkV], eng="gpsimd")
        xg = Rot(lambda i: P.sbuf("xg%d" % i, [128, 8, 512], BF16), 2)
        pp = Rot(lambda i: P.psum("pp%d" % i, [128, 512]), 3)
        pssR = Rot(lambda i: P.psum("pss%d" % i, [128, 512]), 2)
        protR = Rot(lambda i: P.psum("prot%d" % i, [128, 512]), 1)
        prk = P.psum("prk", [128, NT, 8])
        prkr = P.psum("prkr", [128, NT])
        raw = P.sbuf("raw", [128, 3, 512], F32)
        sq3 = P.sbuf("sq3", [128, 3, 512], BF16)
        lnvR = Rot(lambda i: P.sbuf("lnv%d" % i, [128, 512], F32), 2)
        rsR = Rot(lambda i: P.sbuf("rs%d" % i, [128, 512], F32), 2)
        cqn = P.sbuf("cqn", [128, 3, 512], BF16)
        ckvn = P.sbuf("ckvn", [128, 2, 512], BF16)
        sqR = Rot(lambda i: P.sbuf("sq%d" % i, [128, 512], BF16), 2)
        qgR = Rot(lambda i: P.sbuf("qg%d" % i, [128, 512], BF16), 2)
        t1R = Rot(lambda i: P.sbuf("t1%d" % i, [128, 512], F32), 2)
        t2R = Rot(lambda i: P.sbuf("t2%d" % i, [128, 512], F32), 2)
        qo = Rot(lambda i: P.sbuf("qo%d" % i, [128, 512], BF16), 3)
        vtm = Rot(lambda i: P.sbuf("vtm%d" % i, [128, 4, 512], BF16), 2)
        rsk = P.sbuf("rsk", [128, NT * 8], F32)
        R96 = cstb[0:96, C_ROT96, 0:96]

        def rope_out(q, src, rs, n, lat, l0, with_rs):
            if lat:
                prot, t1, t2 = protR.next(), t1R.next(), t2R.next()
                P.mm(prot[0:96, 0:n], R96, src[0:96, 0:n], True, True, [cstb, src], [prot])
                P.tt(t1[0:96, 0:n], src[0:96, 0:n], cs[:, 0, l0:l0 + n], ALU.mult, [src, cs], [t1])
                P.tt(t2[0:96, 0:n], prot[0:96, 0:n], cs[:, 1, l0:l0 + n], ALU.mult, [prot, cs], [t2])
                if with_rs:
                    P.tt(t1[0:96, 0:n], t1[0:96, 0:n], t2[0:96, 0:n], ALU.add, [t1, t2], [t1])
                    P.tt(q[0:96, 0:n], t1[0:96, 0:n], rs[0:96, 0:n], ALU.mult, [t1, rs], [q])
                else:
                    P.tt(q[0:96, 0:n], t1[0:96, 0:n], t2[0:96, 0:n], ALU.add, [t1, t2], [q])
            else:
                if with_rs:
                    P.tt(q[0:96, 0:n], src[0:96, 0:n], rs[0:96, 0:n], ALU.mult, [src, rs], [q])
                else:
                    P.copy(q[0:96, 0:n], src[0:96, 0:n], [src], [q])

        for (t0, n) in GROUPS5:
            nt = n // 128
            lat = t0 >= CTX
            l0 = t0 - CTX
            x = xg.next()
            P.dma(x[:, :, 0:n], xn_s[:, :, t0:t0 + n], reads=[xn_s], writes=[x])
            for (wt, nch, off, dstn, dim) in ((wcq, 3, 0, cqn, 384), (wckv, 2, 3, ckvn, 256)):
                for c in range(nch):
                    ps = pp.next()
                    for kc in range(8):
                        P.mm(ps[:, 0:n], wt[:, kc, c * 128:(c + 1) * 128], x[:, kc, 0:n], kc == 0, kc == 7, [wt, x], [ps])
                    P.act(raw[:, c, 0:n], ps[:, 0:n], AF.Copy, [ps], [raw])
                    P.act(sq3[:, c, 0:n], ps[:, 0:n], AF.Square, [ps], [sq3])
                pss, lnv, rs = pssR.next(), lnvR.next(), rsR.next()
                for c in range(nch):
                    P.mm(pss[:, 0:n], ones_b[:], sq3[:, c, 0:n], c == 0, c == nch - 1, [ones_b, sq3], [pss])
                P.act(lnv[:, 0:n], pss[:, 0:n], AF.Ln, [pss], [lnv], bias=EPS, scale=1.0 / dim)
                P.act(rs[:, 0:n], lnv[:, 0:n], AF.Exp, [lnv], [rs], scale=-0.5)
                for c in range(nch):
                    P.stt(dstn[:, c, 0:n], raw[:, c, 0:n], evc[:, off + c:off + c + 1], rs[:, 0:n], ALU.mult, ALU.mult,
                          [raw, evc, rs], [dstn])
            vt = vtm.next()
            for ti in range(nt):
                ps = pp.next()
                for c in range(2):
                    P.mm(ps[:], ckvn[:, c, ti * 128:(ti + 1) * 128], wkV[:, c].rearrange("p h d -> p (h d)"),
                         c == 0, c == 1, [ckvn, wkV], [ps])
                P.act(vt[:, ti, :], ps[:], AF.Copy, [ps], [vt])
            P.dma(mv_s[t0:t0 + n, :].rearrange("(a p) f -> p a f", p=128), vt[:, 0:nt, :], reads=[vt], writes=[mv_s])
            R = slice(64, 96)
            psr = pp.next()
            for kc in range(8):
                P.mm(psr[R, 0:n], wkr[:, kc, :], x[:, kc, 0:n], kc == 0, kc == 7, [wkr, x], [psr])
            sqr, krg = sqR.next(), qgR.next()
            P.act(sqr[R, 0:n], psr[R, 0:n], AF.Square, [psr], [sqr])
            for ti in range(nt):
                tt = t0 // 128 + ti
                P.mm(prkr[:, tt:tt + 1], sqr[R, ti * 128:(ti + 1) * 128], ones_b[R, 0:1], True, True, [sqr, ones_b], [prkr])
            P.act(krg[R, 0:n], psr[R, 0:n], AF.Identity, [psr], [krg], scale=evc[R, 6:7])
            krf = qo.next()
            if lat:
                prot, t1, t2 = protR.next(), t1R.next(), t2R.next()
                P.mm(prot[R, 0:n], cstb[R, C_ROT96, 64:96], krg[R, 0:n], True, True, [cstb, krg], [prot])
                P.tt(t1[R, 0:n], krg[R, 0:n], cs[R, 0, l0:l0 + n], ALU.mult, [krg, cs], [t1])
                P.tt(t2[R, 0:n], prot[R, 0:n], cs[R, 1, l0:l0 + n], ALU.mult, [prot, cs], [t2])
                P.tt(krf[R, 0:n], t1[R, 0:n], t2[R, 0:n], ALU.add, [t1, t2], [krf])
            else:
                P.copy(krf[R, 0:n], krg[R, 0:n], [krg], [krf])
            for h in range(8):
                P.dma(mkf_s[R, h, t0:t0 + n], krf[R, 0:n], reads=[krf], writes=[mkf_s])
            for h in range(8):
                ps = pp.next()
                for c in range(3):
                    P.mm(ps[0:96, 0:n], wqu[:, c, h * 96:(h + 1) * 96], cqn[:, c, 0:n], c == 0, c == 2, [wqu, cqn], [ps])
                sq, pss, lnv, rs, qg = sqR.next(), pssR.next(), lnvR.next(), rsR.next(), qgR.next()
                P.act(sq[0:96, 0:n], ps[0:96, 0:n], AF.Square, [ps], [sq])
                P.mm(pss[0:96, 0:n], ones_b[0:96, 0:96], sq[0:96, 0:n], True, True, [ones_b, sq], [pss])
                P.act(lnv[0:96, 0:n], pss[0:96, 0:n], AF.Ln, [pss], [lnv], bias=EPS, scale=1.0 / 96)
                P.act(rs[0:96, 0:n], lnv[0:96, 0:n], AF.Exp, [lnv], [rs], scale=-0.5)
                P.act(qg[0:96, 0:n], ps[0:96, 0:n], AF.Identity, [ps], [qg], scale=evc[0:96, 5:6])
                q = qo.next()
                rope_out(q, qg, rs, n, lat, l0, True)
                P.dma(mqf_s[:, h, t0:t0 + n], q[0:96, 0:n], reads=[q], writes=[mqf_s])
                ps = pp.next()
                for c in range(2):
                    P.mm(ps[0:64, 0:n], wkK[:, c, h, :], ckvn[:, c, 0:n], c == 0, c == 1, [wkK, ckvn], [ps])
                sq = sqR.next()
                P.act(sq[0:64, 0:n], ps[0:64, 0:n], AF.Square, [ps], [sq])
                for ti in range(nt):
                    tt = t0 // 128 + ti
                    P.mm(prk[:, tt, h:h + 1], sq[0:64, ti * 128:(ti + 1) * 128], ones_b[0:64, 0:1], True, True,
                         [sq, ones_b], [prk])
                q = qo.next()
                P.act(q[0:64, 0:n], ps[0:64, 0:n], AF.Identity, [ps], [q], scale=evc[0:64, 6:7])
                P.dma(mkf_s[0:64, h, t0:t0 + n], q[0:64, 0:n], reads=[q], writes=[mkf_s])
        rk3 = P.sbuf("rk3", [128, NT, 8], F32)
        rkr = P.sbuf("rkr", [128, NT], F32)
        P.act(rkr[:], prkr[:], AF.Copy, [prkr], [rkr])
        P.tt(rk3[:], prk[:], rkr[:].unsqueeze(2).broadcast_to([128, NT, 8]), ALU.add, [prk, rkr], [rk3])
        P.act(rsk[:], rk3[:].rearrange("p a b -> p (a b)"), AF.Ln, [rk3], [rsk], bias=EPS, scale=1.0 / 96)
        P.act(rsk[:], rsk[:], AF.Exp, [rsk], [rsk], scale=-0.5)
        P.ts(rsk[:], rsk[:], float(96 ** -0.5), ALU.mult, [rsk], [rsk])
        P.dma(mrk_s[:], rsk[:], reads=[rsk], writes=[mrk_s])
        P.emit()

    def mla_core():
        P = Prog(nc, sy, "mlc")
        rsk = P.sbuf("rsk", [128, NT, 8], F32)
        P.dma(rsk[:].rearrange("p a b -> p (a b)"), mrk_s[:], reads=[mrk_s], writes=[rsk])
        kf = Rot(lambda i: P.sbuf("kf%d" % i, [96, 2, T], BF16), 2)
        qf = Rot(lambda i: P.sbuf("qf%d" % i, [96, 2, T], BF16), 2)
        vv = Rot(lambda i: P.sbuf("vv%d" % i, [128, NT, 128], BF16), 2)
        pS = Rot(lambda i: P.psum("pS%d" % i, [128, 512]), 4)
        pO = Rot(lambda i: P.psum("pO%d" % i, [128, 512]), 2)
        pM = Rot(lambda i: P.psum("pM%d" % i, [128, 512]), 2)
        pT = Rot(lambda i: P.sbuf("pT%d" % i, [128, 512], BF16), 4)
        rden = P.sbuf("rden", [128, 512], F32)
        mo = Rot(lambda i: P.sbuf("mo%d" % i, [128, 512], BF16), 2)
        for j in range(4):
            k_, q_, v_ = kf.next(), qf.next(), vv.next()
            P.dma(k_[:], mkf_s[:, 2 * j:2 * j + 2, :], reads=[mkf_s], writes=[k_])
            P.dma(q_[:], mqf_s[:, 2 * j:2 * j + 2, :], reads=[mqf_s], writes=[q_])
            P.dma(v_[:], mv_s[:, j * 128:(j + 1) * 128].rearrange("(a p) f -> p a f", p=128), reads=[mv_s], writes=[v_])
            for (t0, n) in GROUPS5:
                kts = [0, 1] if t0 < CTX else list(range(NT))
                po, pm = pO.next(), pM.next()

                def scores(kt):
                    pts = []
                    for hh in range(2):
                        ps = pS.next()
                        P.mm(ps[:, 0:n], k_[:, hh, kt * 128:(kt + 1) * 128], q_[:, hh, t0:t0 + n], True, True, [k_, q_], [ps])
                        pt = pT.next()
                        P.act(pt[:, 0:n], ps[:, 0:n], AF.Exp, [ps, rsk], [pt], scale=rsk[:, kt, 2 * j + hh:2 * j + hh + 1])
                        pts.append(pt)
                    return pts

                nxt = scores(kts[0])
                for ki, kt in enumerate(kts):
                    first, last = ki == 0, ki == len(kts) - 1
                    pts = nxt
                    if not last:
                        nxt = scores(kts[ki + 1])
                    for hh in range(2):
                        sl = slice(hh * 64, (hh + 1) * 64)
                        P.mm(po[sl, 0:n], v_[:, kt, hh * 64:(hh + 1) * 64], pts[hh][:, 0:n], first, last, [v_, pts[hh]], [po])
                    for hh in range(2):
                        sl = slice(hh * 64, (hh + 1) * 64)
                        P.mm(pm[sl, 0:n], ones_b[:, 0:64], pts[hh][:, 0:n], first, last, [ones_b, pts[hh]], [pm])
                P.act(rden[:, 0:n], pm[:, 0:n], AF.Ln, [pm], [rden])
                P.act(rden[:, 0:n], rden[:, 0:n], AF.Exp, [rden], [rden], scale=-1.0)
                m = mo.next()
                P.tt(m[:, 0:n], po[:, 0:n], rden[:, 0:n], ALU.mult, [po, rden], [m])
                P.dma(mix_s[:, j, t0:t0 + n], m[:, 0:n], reads=[m], writes=[mix_s])
        P.emit()

    def gdn_prep():
        P = Prog(nc, sy, "gdp")
        wv_ = evw_d.rearrange("(kc p) n -> p kc n", p=128)
        xn = P.sbuf("xn", [128, 8, T], BF16)
        P.dma(xn[:], xn_s[:], reads=[xn_s], writes=[xn])
        cvw = P.sbuf("cvw", [128, 12, 5], F32)
        P.dma(cvw[:], gcv_d, writes=[cvw])
        grow = P.sbuf("grow", [128, 544], F32)
        P.dma(grow[:], grow_d, writes=[grow])
        st = P.sbuf("st", [128, 8, 512], F32)
        wch = Rot(lambda i: P.sbuf("wch%d" % i, [128, 8, 128], BF16), 2)
        wab = P.sbuf("wab", [128, 8, 32], BF16)
        wgt = P.sbuf("wgt", [128, 8, 512], BF16)
        load_w(P, wab, wab[:], wv_[:, :, 2208:2240], st, st[:, :, 0:32])
        load_w(P, wgt, wgt[:], wv_[:, :, 2240:2752], st, st[:])
        pp = Rot(lambda i: P.psum("pp%d" % i, [128, 512]), 3)
        pssR = Rot(lambda i: P.psum("pss%d" % i, [128, 512]), 2)
        ptr_ = Rot(lambda i: P.psum("ptr%d" % i, [128, 4, 128]), 2)
        zcR = Rot(lambda i: P.sbuf("zc%d" % i, [128, T], F32), 2)
        accR = Rot(lambda i: P.sbuf("acc%d" % i, [128, T], F32), 1)
        sqR = Rot(lambda i: P.sbuf("sq%d" % i, [128, 512], BF16), 2)
        lnvR = Rot(lambda i: P.sbuf("lnv%d" % i, [128, 512], F32), 2)
        tmst = P.sbuf("tmst", [128, NT, 128], F32)
        gall = P.sbuf("gall", [128, NT, 16], F32)
        ball = P.sbuf("ball", [128, NT, 16], F32)
        negA = P.sbuf("negA", [128, 16], F32)
        P.act(negA[:], grow[:, 0:16], AF.Exp, [grow], [negA])
        P.ts(negA[:], negA[:], -1.0, ALU.mult, [negA], [negA])
        tg = P.sbuf("tg", [128, 16], F32)
        gt = Rot(lambda i: P.sbuf("gt%d" % i, [128, 512], BF16), 2)
        for tt in range(NT):
            tsl = slice(tt * 128, (tt + 1) * 128)
            ps = pp.next()
            for kc in range(8):
                P.mm(ps[:, 0:32], xn[:, kc, tsl], wab[:, kc, :], kc == 0, kc == 7, [xn, wab], [ps])
            P.tt(tg[:], ps[:, 0:16], grow[:, 16:32], ALU.add, [ps, grow], [tg])
            P.act(tg[:], tg[:], AF.Exp, [tg], [tg])
            P.act(tg[:], tg[:], AF.Ln, [tg], [tg], bias=1.0)
            P.tt(gall[:, tt, :], tg[:], negA[:], ALU.mult, [tg, negA], [gall])
            P.act(ball[:, tt, :], ps[:, 16:32], AF.Exp, [ps], [ball], scale=-1.0)
            P.ts(ball[:, tt, :], ball[:, tt, :], 1.0, ALU.add, [ball], [ball])
            P.op("vector", lambda e, tt=tt: e.reciprocal(out=ball[:, tt, :], in_=ball[:, tt, :]), [ball], [ball])
            ps = pp.next()
            for kc in range(8):
                P.mm(ps[:], xn[:, kc, tsl], wgt[:, kc, :], kc == 0, kc == 7, [xn, wgt], [ps])
            g_ = gt.next()
            P.act(g_[:], ps[:], AF.Silu, [ps], [g_])
            P.dma(ggate_s[tsl, :], g_[:], reads=[g_], writes=[ggate_s])
        P.dma(gg_s[:], gall[:].rearrange("p a b -> p (a b)"), reads=[gall], writes=[gg_s])
        P.dma(gb_s[:], ball[:].rearrange("p a b -> p (a b)"), reads=[ball], writes=[gb_s])
        ZW = T + 8

        def zoff(t):
            return t + 2 if t < CTX else t + 6

        zpR = Rot(lambda i: P.sbuf("zp%d" % i, [128, ZW], BF16), 2)
        for z_ in zpR.t:
            P.memset(z_[:], 0.0, [z_])
        dgR = Rot(lambda i: P.sbuf("dgw%d" % i, [128, 5, 128], BF16), 2)
        for cc in range(12):
            zc, acc = zcR.next(), accR.next()
            zp, dgw = zpR.next(), dgR.next()
            w = wch.next()
            load_w(P, w, w[:], wv_[:, :, 672 + cc * 128:672 + (cc + 1) * 128], st, st[:, :, 0:128])
            for j in range(5):
                P.ts(dgw[:, j, :], cst[:, C_ID, :], cvw[:, cc, j:j + 1], ALU.mult, [cst, cvw], [dgw], eng="gpsimd")
            for (t0, n) in GROUPS5:
                ps = pp.next()
                for kc in range(8):
                    P.mm(ps[:, 0:n], w[:, kc, :], xn[:, kc, t0:t0 + n], kc == 0, kc == 7, [w, xn], [ps])
                P.act(zp[:, zoff(t0):zoff(t0) + n], ps[:, 0:n], AF.Copy, [ps], [zp])
            for (t0, n) in GROUPS5:
                ps = pp.next()
                for j in range(5):
                    o_ = zoff(t0) + j - 2
                    P.mm(ps[:, 0:n], dgw[:, j, :], zp[:, o_:o_ + n], j == 0, j == 4, [dgw, zp], [ps])
                P.act(zc[:, t0:t0 + n], ps[:, 0:n], AF.Silu, [ps], [zc])
            if cc < 8:
                for (t0, n) in GROUPS5:
                    sq, lnv = sqR.next(), lnvR.next()
                    P.act(sq[:, 0:n], zc[:, t0:t0 + n], AF.Square, [zc], [sq])
                    pss = pssR.next()
                    P.mm(pss[:, 0:n], cstb[:, C_BD64, :], sq[:, 0:n], True, True, [cstb, sq], [pss])
                    P.act(lnv[:, 0:n], pss[:, 0:n], AF.Ln, [pss], [lnv], bias=EPS)
                    P.act(lnv[:, 0:n], lnv[:, 0:n], AF.Exp, [lnv], [lnv], scale=-0.5)
                    if cc < 4:
                        P.stt(acc[:, t0:t0 + n], zc[:, t0:t0 + n], 0.125, lnv[:, 0:n], ALU.mult, ALU.mult, [zc, lnv], [acc])
                    else:
                        P.tt(acc[:, t0:t0 + n], zc[:, t0:t0 + n], lnv[:, 0:n], ALU.mult, [zc, lnv], [acc])
                dst = gq_s if cc < 4 else gk_s
                r0 = (cc % 4) * 128
                P.dma(dst[r0:r0 + 128, :], acc[:], reads=[acc], writes=[dst])
                src = acc
            else:
                src = zc
            if cc >= 4:
                for t4 in range(0, NT, 4):
                    nn = min(4, NT - t4)
                    pt = ptr_.next()
                    for i in range(nn):
                        tt = t4 + i
                        P.tr(pt[:, i, :], src[:, tt * 128:(tt + 1) * 128], ident, [src, cst], [pt])
                    P.copy(tmst[:, t4:t4 + nn, :], pt[:, 0:nn, :], [pt], [tmst])
                dtm = gktm_s if cc < 8 else gvtm_s
                c0 = (cc % 4) * 128
                P.dma(dtm[:, c0:c0 + 128].rearrange("(a p) f -> p a f", p=128), tmst[:], reads=[tmst], writes=[dtm])
        P.emit()

    def gdn_core():
        P = Prog(nc, sy, "gdsp")
        P.dma(xt_s[:], XT[:, 4:8, :].rearrange("p c t -> p (c t)"), reads=[XT], writes=[xt_s])
        P.emit()
        P = Prog(nc, sy, "gdc")
        XTf = XT[:, 4:8, :].rearrange("p c t -> p (c t)")
        carve_off = [0]

        def carve(name, dtype):
            words = 512 if dtype == F32 else 256
            a = XTf[:, carve_off[0]:carve_off[0] + words]
            carve_off[0] += words
            if dtype != F32:
                a = a.bitcast(dtype)
            return Buf(a.rearrange("p (h i) -> p h i", h=4), 1, name)

        lvm = P.sbuf("lvm", [128, 7, 128], F32)
        nlv = P.sbuf("nlv", [128, 14, 128], BF16)
        for hf in range(2):
            P.dma(lvm[:], lvm_d[:, 7 * hf:7 * hf + 7, :], writes=[lvm])
            P.ts(nlv[:, 7 * hf:7 * hf + 7, :], lvm[:], -1.0, ALU.mult, [lvm], [nlv])
        idb = P.sbuf("idb", [128, 128], BF16)
        P.copy(idb[:], cst[:, C_ID, :], [cst], [idb])
        gall = P.sbuf("gall", [128, NT, 16], F32)
        ball = P.sbuf("ball", [128, NT, 16], F32)
        P.dma(gall[:].rearrange("p a b -> p (a b)"), gg_s[:], reads=[gg_s], writes=[gall])
        P.dma(ball[:].rearrange("p a b -> p (a b)"), gb_s[:], reads=[gb_s], writes=[ball])
        PS = Rot(lambda i: P.psum("ps%d" % i, [128, 512]), 6)
        PB = Rot(lambda i: P.psum("pb%d" % i, [128, 4, 128], BF16), 2)

        def b4(ap):
            return ap.unsqueeze(1).broadcast_to([128, 4, 128])

        def bl(ap, w):
            return ap.unsqueeze(2).broadcast_to([128, ap.shape[1], w])

        Ibc = b4(idb[:])
        D_ = []
        for d in range(2):
            n_ = lambda s, d=d: "%s_%d" % (s, d)
            o = K()
            o.S = P.sbuf(n_("S"), [128, 8, 64], F32)
            o.q = P.sbuf(n_("q"), [64, 8, 128], F32)
            o.k = P.sbuf(n_("k"), [64, 8, 128], F32)
            o.kt = P.sbuf(n_("kt"), [128, 8, 64], F32)
            o.vt = P.sbuf(n_("vt"), [128, 8, 64], F32)
            o.eg = P.sbuf(n_("eg"), [128, 24], F32)
            o.cfk = P.sbuf(n_("cfk"), [128, 8], F32)
            o.qkd = P.sbuf(n_("qkd"), [128, 8, 128], BF16)
            o.U = P.sbuf(n_("U"), [128, 8, 64], F32)
            o.wT = P.sbuf(n_("wT"), [64, 8, 128], BF16)
            o.dg = P.sbuf(n_("dg"), [128, 4, 128], F32)
            o.qdT = P.sbuf(n_("qdT"), [128, 8, 128], BF16)
            o.kdec = P.sbuf(n_("kdec"), [128, 8, 64], BF16)
            o.vnew = P.sbuf(n_("vnew"), [128, 8, 64], BF16)
            o.Sb = P.sbuf(n_("Sb"), [128, 8, 64], BF16)
            o.ost = P.sbuf(n_("ost"), [128, 512], F32)
            P.memset(o.qdT[:], 0.0, [o.qdT])
            P.memset(o.S[:], 0.0, [o.S])
            P.memset(o.Sb[:], 0.0, [o.Sb])
            o.order = list(range(NT)) if d == 0 else [1, 0] + list(range(NT - 1, 1, -1))
            o.Uinc = cst[:, C_UI, :] if d == 0 else cst[:, C_LI, :]
            o.Bst = cst[:, C_SL, :] if d == 0 else cst[:, C_SU, :]
            o.mX = (lambda lv: nlv[:, lv, :]) if d == 0 else (lambda lv: nlv[:, 7 + lv, :])
            o.mXT = (lambda lv: nlv[:, 7 + lv, :]) if d == 0 else (lambda lv: nlv[:, lv, :])
            o.d = d
            D_.append(o)
        CH = []
        for o in D_:
            for hg in range(2):
                c = K()
                c.o, c.hg, c.h0, c.hsl = o, hg, 4 * hg, slice(4 * hg, 4 * hg + 4)
                nm = lambda s, o=o, hg=hg: "%s_%d%d" % (s, o.d, hg)
                if hg == 0:
                    mk = lambda s, dt_: P.sbuf(nm(s), [128, 4, 128], dt_)
                else:
                    mk = lambda s, dt_: carve(nm(s), dt_)
                c.GU, c.eD, c.eDT = mk("GU", F32), mk("eD", F32), mk("eDT", F32)
                c.A, c.AT, c.Ym, c.Ymt = mk("A", BF16), mk("AT", BF16), mk("Ym", BF16), mk("Ymt", BF16)
                c.X = [mk("X%d" % i, BF16) for i in range(2)]
                c.XT = [mk("XT%d" % i, BF16) for i in range(2)]
                c.Rv = P.sbuf(nm("Rv"), [128, 4, 64], BF16)
                c.Rk = P.sbuf(nm("Rk"), [128, 4, 64], BF16)
                CH.append(c)

        def v4(ps):
            return ps[:].rearrange("p (h i) -> p h i", h=4)

        for s in range(NT):
            for o in D_:
                d = o.d
                tt = o.order[s]
                o.tt = tt
                tsl = slice(tt * 128, (tt + 1) * 128)
                P.dma(o.q[:], gq_s[:, tsl].rearrange("(h d) t -> d h t", d=64), reads=[gq_s], writes=[o.q])
                P.dma(o.k[:], gk_s[:, tsl].rearrange("(h d) t -> d h t", d=64), reads=[gk_s], writes=[o.k])
                P.dma(o.kt[:].rearrange("p h d -> p (h d)"), gktm_s[tsl, :], reads=[gktm_s], writes=[o.kt])
                P.dma(o.vt[:].rearrange("p h d -> p (h d)"), gvtm_s[tsl, :], reads=[gvtm_s], writes=[o.vt])
                gd = gall[:, tt, d * 8:(d + 1) * 8]
                pg = PS.next()
                P.mm(pg[:, 0:8], o.Uinc, gd, True, True, [cst, gall], [pg])
                P.mm(pg[:, 8:16], o.Bst, gd, True, True, [cst, gall], [pg])
                P.mm(pg[:, 16:24], ones_f[:], gd, True, True, [ones_f, gall], [pg])
                P.act(o.eg[:], pg[:, 0:24], AF.Exp, [pg], [o.eg])
                P.tt(o.cfk[:], ball[:, tt, d * 8:(d + 1) * 8], o.eg[:, 0:8], ALU.mult, [ball, o.eg], [o.cfk])
            for c in CH:
                o, h0 = c.o, c.h0
                d, tt = o.d, o.tt
                P.tt(c.GU[:], b4(o.Uinc), bl(gall[:, tt, d * 8 + h0:d * 8 + h0 + 4], 128), ALU.mult, [cst, gall], [c.GU],
                     eng="gpsimd")
            for c in CH:
                o = c.o
                c.pD, c.pDT = PS.next(), PS.next()
                for h in range(4):
                    P.mm(v4(c.pD)[:, h, :], c.GU[:, h, :], o.Bst, True, True, [c.GU, cst], [c.pD])
                P.mm(c.pDT[:], o.Bst, c.GU[:].rearrange("p h i -> p (h i)"), True, True, [c.GU, cst], [c.pDT])
                P.act(c.eD[:], v4(c.pD), AF.Exp, [c.pD], [c.eD])
                P.act(c.eDT[:], v4(c.pDT), AF.Exp, [c.pDT], [c.eDT])
            for c in CH:
                o, h0 = c.o, c.h0
                d, tt = o.d, o.tt
                pKK = PS.next()
                for h in range(4):
                    P.mm(v4(pKK)[:, h, :], o.k[:, h0 + h, :], o.k[:, h0 + h, :], True, True, [o.k], [pKK])
                P.tt(c.eD[:], v4(pKK), c.eD[:], ALU.mult, [pKK, c.eD], [c.eD])
                P.tt(c.eD[:], c.eD[:], b4(o.Bst), ALU.mult, [c.eD, cst], [c.eD], eng="gpsimd")
                P.tt(c.A[:], c.eD[:], bl(ball[:, tt, d * 8 + h0:d * 8 + h0 + 4], 128), ALU.mult, [c.eD, ball], [c.A],
                     eng="gpsimd")
            for c in CH:
                o, h0 = c.o, c.h0
                pQK = PS.next()
                for h in range(4):
                    P.mm(v4(pQK)[:, h, :], o.k[:, h0 + h, :], o.q[:, h0 + h, :], True, True, [o.k, o.q], [pQK])
                P.tt(c.eDT[:], v4(pQK), c.eDT[:], ALU.mult, [pQK, c.eDT], [c.eDT])
                P.tt(o.qkd[:, c.hsl, :], c.eDT[:], b4(o.Uinc), ALU.mult, [c.eDT, cst], [o.qkd], eng="gpsimd")
            for c in CH:
                pT_ = PB.next()
                for h in range(4):
                    P.tr(pT_[:, h, :], c.A[:, h, :], idb[:], [c.A, idb], [pT_])
                P.act(c.AT[:], pT_[:], AF.Copy, [pT_], [c.AT])
            for c in CH:
                o = c.o
                P.tt(c.Ym[:], c.A[:], b4(o.mX(0)), ALU.mult, [c.A, nlv], [c.Ym], eng="gpsimd")
                P.tt(c.X[0][:], c.Ym[:], Ibc, ALU.add, [c.Ym, idb], [c.X[0]], eng="gpsimd")
                P.tt(c.Ymt[:], c.AT[:], b4(o.mXT(0)), ALU.mult, [c.AT, nlv], [c.Ymt])
                P.tt(c.XT[0][:], c.Ymt[:], Ibc, ALU.add, [c.Ymt, idb], [c.XT[0]])
            cur = 0
            for lv in range(1, 7):
                last = lv == 6
                for c in CH:
                    o = c.o
                    if not last:
                        c.pY = PS.next()
                        for h in range(4):
                            P.mm(v4(c.pY)[:, h, :], c.AT[:, h, :], c.X[cur][:, h, :], True, True, [c.AT, c.X[cur]], [c.pY])
                        P.tt(c.Ym[:], v4(c.pY), b4(o.mX(lv)), ALU.mult, [c.pY, nlv], [c.Ym])
                    c.pY2 = PS.next()
                    for h in range(4):
                        P.mm(v4(c.pY2)[:, h, :], c.A[:, h, :], c.XT[cur][:, h, :], True, True, [c.A, c.XT[cur]], [c.pY2])
                    P.tt(c.Ymt[:], v4(c.pY2), b4(o.mXT(lv)), ALU.mult, [c.pY2, nlv], [c.Ymt])
                for c in CH:
                    if not last:
                        pZ = PS.next()
                        P.mm(pZ[:], idb[:], c.X[cur][:].rearrange("p h i -> p (h i)"), True, False, [idb, c.X[cur]], [pZ])
                        for h in range(4):
                            P.mm(v4(pZ)[:, h, :], c.XT[cur][:, h, :], c.Ym[:, h, :], False, True, [c.XT[cur], c.Ym], [pZ])
                        P.act(c.X[1 - cur][:], v4(pZ), AF.Copy, [pZ], [c.X[1 - cur]])
                    pZ = PS.next()
                    P.mm(pZ[:], idb[:], c.XT[cur][:].rearrange("p h i -> p (h i)"), True, False, [idb, c.XT[cur]], [pZ])
                    for h in range(4):
                        P.mm(v4(pZ)[:, h, :], c.X[cur][:, h, :], c.Ymt[:, h, :], False, True, [c.X[cur], c.Ymt], [pZ])
                    P.act(c.XT[1 - cur][:], v4(pZ), AF.Copy, [pZ], [c.XT[1 - cur]])
                cur = 1 - cur
            for c in CH:
                o, h0, hsl = c.o, c.h0, c.hsl
                d, tt = o.d, o.tt
                XTf_ = c.XT[cur]
                P.tt(c.Rv[:], o.vt[:, hsl, :], bl(ball[:, tt, d * 8 + h0:d * 8 + h0 + 4], 64), ALU.mult, [o.vt, ball], [c.Rv],
                     eng="gpsimd")
                P.tt(c.Rk[:], o.kt[:, hsl, :], bl(o.cfk[:, hsl], 64), ALU.mult, [o.kt, o.cfk], [c.Rk], eng="gpsimd")
                pU, pW = PS.next(), PS.next()
                pU3 = pU[:, 0:256].rearrange("p (h i) -> p h i", h=4)
                pW3 = pW[0:64, :].rearrange("p (h i) -> p h i", h=4)
                for h in range(4):
                    P.mm(pU3[:, h, :], XTf_[:, h, :], c.Rv[:, h, :], True, True, [XTf_, c.Rv], [pU])
                for h in range(4):
                    P.mm(pW3[:, h, :], c.Rk[:, h, :], XTf_[:, h, :], True, True, [XTf_, c.Rk], [pW])
                P.act(o.U[:, hsl, :], pU3, AF.Copy, [pU], [o.U])
                P.act(o.wT[:, hsl, :], pW3, AF.Copy, [pW], [o.wT])
            for o in D_:
                for half in range(2):
                    P.tt(o.dg[:], b4(cst[:, C_ID, :]), bl(o.eg[:, 4 * half:4 * half + 4], 128), ALU.mult, [cst, o.eg], [o.dg],
                         eng="gpsimd")
                    pE = PS.next()
                    P.mm(pE[0:64, :], ones_f[:, 0:64], o.dg[:].rearrange("p h i -> p (h i)"), True, True, [ones_f, o.dg], [pE])
                    P.tt(o.qdT[0:64, 4 * half:4 * half + 4, :], o.q[:, 4 * half:4 * half + 4, :],
                         pE[0:64, :].rearrange("p (h i) -> p h i", h=4), ALU.mult, [o.q, pE], [o.qdT])
                P.tt(o.kdec[:], o.kt[:], bl(o.eg[:, 8:16], 64), ALU.mult, [o.kt, o.eg], [o.kdec], eng="gpsimd")
            for o in D_:
                pWS = PS.next()
                pWS3 = pWS[:].rearrange("p (h e) -> p h e", h=8)
                for h in range(8):
                    P.mm(pWS3[:, h, :], o.wT[:, h, :], o.Sb[0:64, h, :], True, True, [o.wT, o.Sb], [pWS])
                P.tt(o.vnew[:], o.U[:], pWS3, ALU.subtract, [o.U, pWS], [o.vnew])
            for o in D_:
                tt = o.tt
                tsl = slice(tt * 128, (tt + 1) * 128)
                pOO = PS.next()
                pOO3 = pOO[:].rearrange("p (h e) -> p h e", h=8)
                for h in range(8):
                    P.mm(pOO3[:, h, :], o.qdT[:, h, :], o.Sb[:, h, :], True, False, [o.qdT, o.Sb], [pOO])
                    P.mm(pOO3[:, h, :], o.qkd[:, h, :], o.vnew[:, h, :], False, True, [o.qkd, o.vnew], [pOO])
                P.act(o.ost[:], pOO[:], AF.Copy, [pOO], [o.ost])
                P.dma(og_s[o.d][tsl, :], o.ost[:], reads=[o.ost], writes=[og_s[o.d]])
                pSn = PS.next()
                pSn3 = pSn[0:64, :].rearrange("p (h e) -> p h e", h=8)
                for h in range(8):
                    P.mm(pSn3[:, h, :], o.kdec[:, h, :], o.vnew[:, h, :], True, True, [o.kdec, o.vnew], [pSn])
                P.tt(o.S[0:64], o.S[0:64], o.eg[0:64, 16:24].unsqueeze(2).broadcast_to([64, 8, 64]), ALU.mult, [o.S, o.eg], [o.S])
                P.tt(o.S[0:64], o.S[0:64], pSn3, ALU.add, [o.S, pSn], [o.S])
                P.act(o.Sb[0:64], o.S[0:64], AF.Copy, [o.S], [o.Sb])
        P.emit()
        P = Prog(nc, sy, "gdf")
        grow = P.sbuf("grow", [128, 544], F32)
        P.dma(grow[:], grow_d, writes=[grow])
        sqoR = Rot(lambda i: P.sbuf("sqo%d" % i, [128, 8, 64], F32), 4)
        ssR = Rot(lambda i: P.sbuf("ss%d" % i, [128, 8], F32), 4)
        toR = Rot(lambda i: P.sbuf("to%d" % i, [128, 8, 64], F32), 4)
        tobR = Rot(lambda i: P.sbuf("tob%d" % i, [128, 512], BF16), 4)
        gtt = Rot(lambda i: P.sbuf("gtt%d" % i, [128, 512], BF16), 4)
        oa = Rot(lambda i: P.sbuf("oa%d" % i, [128, 512], F32), 4)
        ob_ = Rot(lambda i: P.sbuf("ob%d" % i, [128, 512], F32), 4)
        idb = P.sbuf("idb", [128, 128], BF16)
        P.copy(idb[:], cst[:, C_ID, :], [cst], [idb])
        mixo = Rot(lambda i: P.sbuf("mixo%d" % i, [128, 4, 128], BF16), 4)
        ptbR = Rot(lambda i: P.psum("ptb%d" % i, [128, 4, 128], BF16), 4)
        for tt in range(NT):
            sqo, ss, to, tob, ptb = sqoR.next(), ssR.next(), toR.next(), tobR.next(), ptbR.next()
            tsl = slice(tt * 128, (tt + 1) * 128)
            g_, a_, b_ = gtt.next(), oa.next(), ob_.next()
            P.dma(g_[:], ggate_s[tsl, :], reads=[ggate_s], writes=[g_])
            P.dma(a_[:], og_s[0][tsl, :], reads=[og_s[0]], writes=[a_])
            P.dma(b_[:], og_s[1][tsl, :], reads=[og_s[1]], writes=[b_])
            P.tt(a_[:], a_[:], b_[:], ALU.add, [a_, b_], [a_], eng="gpsimd")
            o3 = a_[:].rearrange("p (h e) -> p h e", h=8)
            P.act(sqo[:], o3, AF.Square, [a_], [sqo])
            P.op("vector", lambda e, ss=ss, sqo=sqo: e.tensor_reduce(out=ss[:], in_=sqo[:], axis=AX.X, op=ALU.add), [sqo], [ss])
            P.act(ss[:], ss[:], AF.Ln, [ss], [ss], bias=EPS, scale=1.0 / 64)
            P.act(ss[:], ss[:], AF.Exp, [ss], [ss], scale=-0.5)
            P.tt(to[:], o3, ss[:].unsqueeze(2).broadcast_to([128, 8, 64]), ALU.mult, [a_, ss], [to])
            P.tt(to[:].rearrange("p h e -> p (h e)"), to[:].rearrange("p h e -> p (h e)"), grow[:, 32:544], ALU.mult,
                 [to, grow], [to], eng="gpsimd")
            P.tt(tob[:], to[:].rearrange("p h e -> p (h e)"), g_[:], ALU.mult, [to, g_], [tob])
            for c in range(4):
                P.tr(ptb[:, c, :], tob[:, c * 128:(c + 1) * 128], idb[:], [tob, idb], [ptb])
            mo_ = mixo.next()
            P.act(mo_[:], ptb[:], AF.Copy, [ptb], [mo_])
            P.dma(mix_s[:, 4:8, tsl], mo_[:], reads=[mo_], writes=[mix_s])
        P.dma(XT[:, 4:8, :].rearrange("p c t -> p (c t)"), xt_s[:], reads=[xt_s], writes=[XT])
        P.emit()


    for l in layers:
        if stage >= 1 and not skip_ffn1:
            ffn(l, 0, FULL)
        dump_xt("xt_ffn1_%d" % l)
        if stage >= 2:
            make_xn(l)
            if l == 0:
                if parts is None or "mlp" in parts:
                    mla_prep()
                if parts is None or "mlc" in parts:
                    mla_core()
                if parts is None or "gdp" in parts:
                    gdn_prep()
                if parts is None or "gdc" in parts:
                    gdn_core()
                if parts is None or "op" in parts:
                    out_proj(l, evwo_d, GROUPS5)
            if l == 1:
                if parts is None or "swp" in parts:
                    swa_prep()
                if parts is None or "swc" in parts:
                    swa_core()
                if parts is None or "glp" in parts:
                    gla_prep()
                if parts is None or "glc" in parts:
                    gla_core()
                if parts is None or "op" in parts:
                    out_proj(l, odwo_d, LATG)
        if "mix" in dbg_d:
            P = Prog(nc, sy, "dbgmix%d" % l)
            mb_ = Rot(lambda i: P.sbuf("mb%d" % i, [128, 8, 128], BF16), 2)
            mf_ = Rot(lambda i: P.sbuf("mf%d" % i, [128, 8, 128], F32), 2)
            dv = dbg_d["mix"].rearrange("p (c t) -> p c t", c=8)
            for tt in range(NT):
                a_, f_ = mb_.next(), mf_.next()
                P.dma(a_[:], mix_s[:, :, tt * 128:(tt + 1) * 128], reads=[mix_s], writes=[a_])
                P.act(f_[:], a_[:], AF.Copy, [a_], [f_])
                P.dma(dv[:, :, tt * 128:(tt + 1) * 128], f_[:], reads=[f_])
            P.emit()
        dump_xt("xt_mix_%d" % l)
        if stage >= 3:
            ffn(l, 1, FULL if l == 0 else LAT)

    P = Prog(nc, sy, "pout")
    ost = [P.sbuf("ost%d" % i, [128, D], F32) for i in range(2)]
    pot = [P.psum("pot%d" % i, [128, 4, 128]) for i in range(2)]
    for tt in range(2, NT):
        o = ost[tt % 2]
        for half in range(2):
            ps = pot[half]
            for c4 in range(4):
                c = half * 4 + c4
                P.tr(ps[:, c4, :], XT[:, c, tt * 128:(tt + 1) * 128], ident, [(XT, tt), cst], [ps])
            if half == 0:
                P.copy(o[:, 0:512], ps[:].rearrange("p a b -> p (a b)"), [ps], [o])
            else:
                P.act(o[:, 512:1024], ps[:].rearrange("p a b -> p (a b)"), AF.Copy, [ps], [o])
        P.dma(out_d[(tt - 2) * 128:(tt - 1) * 128, :], o[:], reads=[o])
    P.emit()
    es.close()
    return nc


def rope_tables(rot_dim):
    f = np.float32
    t = np.arange(SEQ)
    row = (t // 64).astype(f)
    col = (t % 64).astype(f)
    nf = rot_dim // 4
    inv = (f(10000.0) ** (-np.arange(nf, dtype=f) / f(nf))).astype(f)
    ang = np.concatenate([row[:, None] * inv, col[:, None] * inv], axis=-1).astype(f)
    return np.cos(ang).astype(f), np.sin(ang).astype(f)


def host_consts():
    f = np.float32
    idx = np.arange(128)
    ident = np.eye(128)
    bd64 = (idx[:, None] // 64 == idx[None, :] // 64)
    Rm = np.zeros((64, 64))
    for i in range(32):
        Rm[i, i + 32] = -1.0
        Rm[i + 32, i] = 1.0
    rot = np.zeros((128, 128))
    rot[0:64, 0:64] = Rm.T
    rot[64:128, 64:128] = Rm.T
    UI = idx[:, None] <= idx[None, :]
    LI = idx[:, None] >= idx[None, :]
    SL = idx[:, None] > idx[None, :]
    SU = idx[:, None] < idx[None, :]
    rot96 = np.zeros((128, 128))
    for i in range(16):
        rot96[80 + i, 64 + i] = -1.0
        rot96[64 + i, 80 + i] = 1.0
    cst = np.stack([ident, bd64, rot, UI, LI, SL, SU, rot96], axis=1).astype(f)
    lv = []
    for k in range(7):
        bsz = 1 << k
        same = (idx[:, None] // (2 * bsz)) == (idx[None, :] // (2 * bsz))
        m = same & ((idx[:, None] % (2 * bsz)) >= bsz) & ((idx[None, :] % (2 * bsz)) < bsz)
        lv.append(m)
    lvm = np.stack(lv + [m.T for m in lv], axis=1).astype(f)
    c8, s8 = rope_tables(32)
    mcs = np.zeros((96, 2, SEQ), f)
    mcs[0:64, 0, :] = 1.0
    for dd in range(32):
        mcs[64 + dd, 0, :] = c8[:, dd % 16]
        mcs[64 + dd, 1, :] = s8[:, dd % 16]
    bdsel = np.stack([(idx // 64 == 0), (idx // 64 == 1)], axis=1).astype(f)
    c, s = rope_tables(64)
    d = idx % 64
    swcs = np.stack([c[:, d % 32].T, s[:, d % 32].T], axis=1).astype(f)
    return {"cst": np.ascontiguousarray(cst), "bdsel": np.ascontiguousarray(bdsel), "swa_cs": np.ascontiguousarray(swcs),
            "lvm": np.ascontiguousarray(lvm), "mla_cs": np.ascontiguousarray(mcs)}


_CONSTS = None


def host_inputs(inputs, b):
    global _CONSTS
    f = np.float32
    if _CONSTS is None:
        _CONSTS = host_consts()
    c = np.asarray(inputs["c"][b], f)
    cc = np.asarray(inputs["c_ctx"], f)
    ccol = np.concatenate([c.reshape(8, 128).T, cc.reshape(8, 128).T], axis=1)
    idx = np.arange(128)
    odc = np.zeros((128, 11), f)
    odc[:, 0] = np.asarray(inputs["od_swa_q_norm"], f)[0][idx % 64]
    odc[:, 1] = np.asarray(inputs["od_swa_k_norm"], f)[0][idx % 64]
    odc[:, 2] = np.asarray(inputs["od_gla_out_norm"], f)[0]
    odc[:, 3:11] = np.asarray(inputs["od_swa_sink"], f)[0][None, :]
    w2 = np.asarray(inputs["od_gla_gate_w2"], f)[0]
    gb = np.asarray(inputs["od_gla_gate_b"], f)[0]
    gw2b = np.concatenate([w2, gb[:, None, :]], axis=1).transpose(1, 0, 2)
    evc = np.zeros((128, 8), f)
    evc[:, 0:3] = np.asarray(inputs["ev_q_a_norm"], f)[0].reshape(3, 128).T
    evc[:, 3:5] = np.asarray(inputs["ev_kv_a_norm"], f)[0].reshape(2, 128).T
    evc[0:96, 5] = np.asarray(inputs["ev_mla_q_norm"], f)[0]
    evc[0:96, 6] = np.asarray(inputs["ev_mla_k_norm"], f)[0]
    gcv = np.asarray(inputs["ev_gdn_conv"], f)[0].reshape(5, 12, 128).transpose(2, 1, 0)
    grow = np.zeros((128, 544), f)
    grow[:, 0:16] = np.asarray(inputs["ev_gdn_a_log"], f)[0].reshape(16)[None, :]
    grow[:, 16:32] = np.asarray(inputs["ev_gdn_dt_bias"], f)[0].reshape(16)[None, :]
    grow[:, 32:544] = np.tile(np.asarray(inputs["ev_gdn_out_norm"], f)[0], 8)[None, :]
    m = {
        "ev_w_in": np.ascontiguousarray(np.asarray(inputs["ev_w_in"], f)[0]),
        "ev_w_out": np.ascontiguousarray(np.asarray(inputs["ev_w_out"], f)[0]),
        "ev_w_q_up": np.ascontiguousarray(np.asarray(inputs["ev_w_q_up"], f)[0]),
        "ev_w_kv_up": np.ascontiguousarray(np.asarray(inputs["ev_w_kv_up"], f)[0]),
        "ev_cols": evc,
        "gdn_conv_col": np.ascontiguousarray(gcv),
        "gdn_rows": grow,
        "x": np.ascontiguousarray(inputs["x"][b], dtype=f),
        "ctx": np.ascontiguousarray(inputs["ctx"][b], dtype=f),
        "ccol": np.ascontiguousarray(ccol),
        "ada_w": np.ascontiguousarray(inputs["ada_w"], dtype=f),
        "ada_b_col": np.ascontiguousarray(np.asarray(inputs["ada_b"], f).reshape(2, 72, 128).transpose(2, 0, 1)),
        "norm_g_col": np.ascontiguousarray(np.asarray(inputs["norm_g"], f).reshape(2, 3, 8, 128).transpose(3, 0, 1, 2)),
        "ffn_w_gate": np.ascontiguousarray(inputs["ffn_w_gate"], dtype=f),
        "ffn_w_up": np.ascontiguousarray(inputs["ffn_w_up"], dtype=f),
        "ffn_w_down": np.ascontiguousarray(inputs["ffn_w_down"], dtype=f),
        "od_w_in": np.ascontiguousarray(np.asarray(inputs["od_w_in"], f)[0]),
        "od_w_out": np.ascontiguousarray(np.asarray(inputs["od_w_out"], f)[0]),
        "od_cols": odc,
        "gla_w2b": np.ascontiguousarray(gw2b),
    }
    m.update(_CONSTS)
    return m


def kernel(**inputs):
    nc = build()
    in_maps = [host_inputs(inputs, b) for b in range(NCORES)]
    res = run_bass_kernel_spmd(nc, in_maps, core_ids=list(range(NCORES)))
    return np.stack([np.asarray(r["out"], np.float32) for r in res.results], axis=0)
```

```python
from contextlib import ExitStack
import numpy as np
import concourse.bass as bass
import concourse.mybir as mybir
from concourse.bass_utils import run_bass_kernel_spmd

F32 = mybir.dt.float32
BF16 = mybir.dt.bfloat16
AF = mybir.ActivationFunctionType
ALU = mybir.AluOpType
AX = mybir.AxisListType

D = 1024
DFF = 2816
NF = DFF // 128
SEQ = 2048
CTX = 256
T = SEQ + CTX
NT = T // 128
EPS = 1e-6
NCORES = 8

COMPUTE = ("tensor", "vector", "scalar", "gpsimd")
ENGINES = ("tensor", "vector", "scalar", "gpsimd", "sync")
N_DMA_SEMS = 24


class Buf:
    _n = 0

    def __init__(self, t, nslots=1, name=None):
        self.t = t
        self.nslots = nslots
        Buf._n += 1
        self.id = Buf._n
        self.name = name

    def __getitem__(self, idx):
        return self.t[idx]


class Sync:
    def __init__(self, nc, es):
        self.nc = nc
        self.sems = {e: es.enter_context(nc.semaphore("s_" + e)) for e in COMPUTE}
        self.dsems = [es.enter_context(nc.semaphore("d_%d" % i)) for i in range(N_DMA_SEMS)]
        self.pos = {e: 0 for e in COMPUTE}
        self.n_dma = 0
        self.dma_uses = [0] * N_DMA_SEMS


class Prog:
    def __init__(self, nc, sync, name="ph"):
        self.nc = nc
        self.sy = sync
        self.name = name
        self.es = ExitStack()
        self.ops = []
        self.state = {}
        self.dma_last = [None] * N_DMA_SEMS
        self.start_pos = dict(sync.pos)
        self.start_dma = [16 * u for u in sync.dma_uses]

    def sbuf(self, name, shape, dtype, nslots=1):
        t = self.es.enter_context(self.nc.sbuf_tensor(self.name + "_" + name, list(shape), dtype))
        return Buf(t, nslots, name)

    def psum(self, name, shape, dtype=F32, nslots=1):
        t = self.es.enter_context(self.nc.psum_tensor(self.name + "_" + name, list(shape), dtype))
        return Buf(t, nslots, name)

    def _expand(self, keys):
        out = []
        for k in keys:
            if isinstance(k, Buf):
                k = (k, None)
            b, s = k
            if s is None:
                out.extend((b.id, i) for i in range(b.nslots))
            elif isinstance(s, (list, tuple, range)):
                for i in s:
                    assert 0 <= i < b.nslots, (b.name, i, b.nslots)
                    out.append((b.id, i))
            else:
                assert 0 <= s < b.nslots, (b.name, s, b.nslots)
                out.append((b.id, s))
        return out

    def _deps(self, idx, reads, writes):
        deps = set()
        rk = self._expand(reads)
        wk = self._expand(writes)
        for k in rk:
            st = self.state.setdefault(k, [None, []])
            if st[0] is not None:
                deps.add(st[0])
        for k in wk:
            st = self.state.setdefault(k, [None, []])
            if st[0] is not None:
                deps.add(st[0])
            deps.update(st[1])
        for k in rk:
            self.state[k][1].append(idx)
        for k in wk:
            self.state[k] = [idx, []]
        deps.discard(idx)
        return deps

    def op(self, eng, fn, reads=(), writes=()):
        idx = len(self.ops)
        deps = self._deps(idx, reads, writes)
        self.ops.append(dict(eng=eng, fn=fn, deps=deps, dma=False, pos=self.sy.pos[eng]))
        self.sy.pos[eng] += 1
        return idx

    def dma(self, out, in_, reads=(), writes=(), q="sync"):
        idx = len(self.ops)
        deps = self._deps(idx, reads, writes)
        sy = self.sy
        s = sy.n_dma % N_DMA_SEMS
        sy.n_dma += 1
        prev = self.dma_last[s]
        sy.dma_uses[s] += 1
        self.dma_last[s] = idx
        self.ops.append(dict(eng=q, fn=lambda e: e.dma_start(out=out, in_=in_), deps=deps, dma=True, sem=s,
                             target=16 * sy.dma_uses[s], prev=prev))
        return idx

    def mm(self, out, lhsT, rhs, start, stop, r, w):
        return self.op("tensor", lambda e: e.matmul(out, lhsT=lhsT, rhs=rhs, start=start, stop=stop), r, w)

    def tr(self, out, in_, ident, r, w):
        return self.op("tensor", lambda e: e.transpose(out, in_, ident), r, w)

    def act(self, out, in_, func, r, w, bias=None, scale=None, accum_out=None):
        kw = {}
        if bias is not None:
            kw["bias"] = bias
        if scale is not None:
            kw["scale"] = scale
        if accum_out is not None:
            kw["accum_out"] = accum_out
        return self.op("scalar", lambda e: e.activation(out=out, in_=in_, func=func, **kw), r, w)

    def tt(self, out, in0, in1, op, r, w, eng="vector"):
        return self.op(eng, lambda e: e.tensor_tensor(out=out, in0=in0, in1=in1, op=op), r, w)

    def ts(self, out, in0, s1, op0, r, w, s2=None, op1=None, eng="vector"):
        if op1 is None:
            return self.op(eng, lambda e: e.tensor_scalar(out=out, in0=in0, scalar1=s1, scalar2=None, op0=op0), r, w)
        return self.op(eng, lambda e: e.tensor_scalar(out=out, in0=in0, scalar1=s1, scalar2=s2, op0=op0, op1=op1), r, w)

    def stt(self, out, in0, scalar, in1, op0, op1, r, w):
        return self.op("vector", lambda e: e.scalar_tensor_tensor(out=out, in0=in0, scalar=scalar, in1=in1,
                                                                  op0=op0, op1=op1), r, w)

    def copy(self, out, in_, r, w, eng="vector"):
        return self.op(eng, lambda e: e.tensor_copy(out=out, in_=in_), r, w)

    def memset(self, ap, val, w, eng="vector"):
        return self.op(eng, lambda e: e.memset(ap, val), (), w)

    def emit(self):
        nc = self.nc
        sy = self.sy
        ops = self.ops
        per_eng = {e: [] for e in ENGINES}
        for i, o in enumerate(ops):
            per_eng[o["eng"]].append(i)
        end_pos = dict(sy.pos)
        end_dma = [16 * u for u in sy.dma_uses]
        start_pos = self.start_pos
        start_dma = self.start_dma

        def replay(ename):
            def body(eng):
                waited = dict(start_pos)
                dwaited = list(start_dma)
                for i in per_eng[ename]:
                    o = ops[i]
                    need = {}
                    dneed = {}
                    deps = set(o["deps"])
                    if o["dma"] and o["prev"] is not None:
                        deps.add(o["prev"])
                    for j in deps:
                        d = ops[j]
                        if d["dma"]:
                            dneed[d["sem"]] = max(dneed.get(d["sem"], 0), d["target"])
                        else:
                            if d["eng"] == "tensor" and ename == "tensor" and not o["dma"]:
                                continue
                            need[d["eng"]] = max(need.get(d["eng"], 0), d["pos"] + 1)
                    for e, v in need.items():
                        if v > waited[e]:
                            eng.wait_ge(sy.sems[e], v)
                            waited[e] = v
                    for s, v in dneed.items():
                        if v > dwaited[s]:
                            eng.wait_ge(sy.dsems[s], v)
                            dwaited[s] = v
                    ins = o["fn"](eng)
                    if o["dma"]:
                        ins.then_inc(sy.dsems[o["sem"]], 16)
                    else:
                        ins.then_inc(sy.sems[ename], 1)
                if ename == "sync":
                    for s in range(N_DMA_SEMS):
                        if end_dma[s] > start_dma[s]:
                            eng.wait_ge(sy.dsems[s], end_dma[s])
                    for e in COMPUTE:
                        if end_pos[e] > start_pos[e]:
                            eng.wait_ge(sy.sems[e], end_pos[e])
            return body

        with nc.Block() as block:
            for e in ENGINES:
                if per_eng[e] or e == "sync":
                    getattr(block, e)(replay(e))
        self.es.close()


def segs(t0, n):
    out = []
    if t0 < CTX:
        m = min(n, CTX - t0)
        out.append((t0, m, 1))
        if n > m:
            out.append((CTX, n - m, 0))
    else:
        out.append((t0, n, 0))
    return out


def tiles_of(t0, n):
    return list(range(t0 // 128, (t0 + n + 127) // 128))


class K:
    pass


class Rot:
    def __init__(self, mk, n):
        self.t = [mk(i) for i in range(n)]
        self.i = 0

    def next(self):
        t = self.t[self.i % len(self.t)]
        self.i += 1
        return t


GROUPS5 = [(0, 256), (256, 512), (768, 512), (1280, 512), (1792, 512)]
LATG = GROUPS5[1:]
FULL = [[(0, 384), (384, 384), (768, 384)], [(1152, 384), (1536, 384), (1920, 384)]]
LAT = [[(256, 512), (768, 512)], [(1280, 512), (1792, 512)]]
C_ID, C_BD64, C_ROT64, C_UI, C_LI, C_SL, C_SU, C_ROT96 = range(8)
NCST = 8


def build(stage=99, dbg=(), layers=(0, 1), parts=None, skip_ffn1=False):
    nc = bass.Bass("TRN2", target_bir_lowering=False)
    dt = nc.dram_tensor

    def inp(name, shape, dtype=F32):
        return dt(name, list(shape), dtype, kind="ExternalInput").ap()

    def scratch(name, shape, dtype, nslots=1):
        return Buf(dt("scr_" + name, list(shape), dtype, kind="Internal").ap(), nslots, name)

    x_d = inp("x", [SEQ, D])
    ctx_d = inp("ctx", [CTX, D])
    ccol_d = inp("ccol", [128, 16])
    adaw_d = inp("ada_w", [2, D, 9 * D])
    adab_d = inp("ada_b_col", [128, 2, 72])
    ng_d = inp("norm_g_col", [128, 2, 3, 8])
    wg_d = inp("ffn_w_gate", [2, 2, D, DFF])
    wu_d = inp("ffn_w_up", [2, 2, D, DFF])
    wd_d = inp("ffn_w_down", [2, 2, DFF, D])
    cst_d = inp("cst", [128, NCST, 128])
    bdsel_d = inp("bdsel", [128, 2])
    odw_d = inp("od_w_in", [D, 2336])
    odwo_d = inp("od_w_out", [D, D])
    odcol_d = inp("od_cols", [128, 11])
    swcs_d = inp("swa_cs", [128, 2, SEQ])
    gw2b_d = inp("gla_w2b", [17, 2, 256])
    evw_d = inp("ev_w_in", [D, 2752])
    evwo_d = inp("ev_w_out", [D, D])
    evqu_d = inp("ev_w_q_up", [384, 768])
    evkv_d = inp("ev_w_kv_up", [256, 1024])
    evcol_d = inp("ev_cols", [128, 8])
    mlcs_d = inp("mla_cs", [96, 2, SEQ])
    lvm_d = inp("lvm", [128, 14, 128])
    gcv_d = inp("gdn_conv_col", [128, 12, 5])
    grow_d = inp("gdn_rows", [128, 544])
    out_d = dt("out", [SEQ, D], F32, kind="ExternalOutput").ap()
    dbg_d = {}
    for name, shape in dbg:
        dbg_d[name] = dt("dbg_" + name, list(shape), F32, kind="ExternalOutput").ap()

    xn_s = scratch("xn", [128, 8, T], BF16)
    mix_s = scratch("mix", [128, 8, T], BF16)
    swq_s = scratch("swq", [128, 4, SEQ], BF16)
    swk_s = scratch("swk", [128, T], BF16)
    swv_s = scratch("swv", [128, NT * 256], BF16)
    swr_s = scratch("swr", [128, NT * 2], F32)
    glq_s = [scratch("glq%d" % d, [128, 2, T], BF16) for d in range(2)]
    glk_s = [scratch("glk%d" % d, [128, 2, T], BF16) for d in range(2)]
    glkd_s = [scratch("glkd%d" % d, [T, 256], BF16) for d in range(2)]
    glet_s = [scratch("glet%d" % d, [128, 2, NT], F32) for d in range(2)]
    glv_s = scratch("glv", [T, 512], BF16)
    glr_s = scratch("glr", [128, 4, SEQ], BF16)

    mqf_s = scratch("mqf", [96, 8, T], BF16)
    mkf_s = scratch("mkf", [96, 8, T], BF16)
    mv_s = scratch("mv", [T, 512], BF16)
    mrk_s = scratch("mrk", [128, NT * 8], F32)
    gq_s = scratch("gq", [512, T], F32)
    gk_s = scratch("gk", [512, T], F32)
    gktm_s = scratch("gktm", [T, 512], F32)
    gvtm_s = scratch("gvtm", [T, 512], F32)
    gg_s = scratch("gg", [128, NT * 16], F32)
    gb_s = scratch("gb", [128, NT * 16], F32)
    ggate_s = scratch("ggate", [T, 512], BF16)
    og_s = [scratch("og%d" % d, [T, 512], F32) for d in range(2)]
    xt_s = scratch("xtpark", [128, 4 * T], F32)

    es = ExitStack()
    sy = Sync(nc, es)

    def persist(name, shape, dtype, nslots=1):
        return Buf(es.enter_context(nc.sbuf_tensor(name, list(shape), dtype)), nslots, name)

    XT = persist("XT", [128, 8, T], F32, NT)
    cst = persist("cst_s", [128, NCST, 128], F32)
    ident = cst[:, C_ID, :]
    ones_f = persist("ones_f", [128, 128], F32)
    ones_b = persist("ones_b", [128, 128], BF16)
    cstb = persist("cstb", [128, NCST, 128], BF16)
    modT = persist("modT", [128, 2, 72, 2], F32)
    AG = persist("AG", [128, 2, 3, 8, 2], F32)
    GH = persist("GH", [128, 2, 3, 8, 2], F32)
    ng = persist("ng", [128, 2, 3, 8], F32)

    def xk(t0, n):
        return [(XT, i) for i in tiles_of(t0, n)]

    P = Prog(nc, sy, "p0")
    P.dma(cst[:], cst_d, writes=[cst])
    P.dma(ng[:], ng_d, writes=[ng])
    P.memset(ones_f[:], 1.0, [ones_f])
    P.memset(ones_b[:], 1.0, [ones_b])
    P.copy(cstb[:], cst[:], [cst], [cstb], eng="gpsimd")
    xin = [P.sbuf("xin%d" % i, [128, D], F32) for i in range(2)]
    ptr = [P.psum("ptr%d" % i, [128, 4, 128]) for i in range(2)]
    for tt in range(NT):
        st = xin[tt % 2]
        src = ctx_d[tt * 128:(tt + 1) * 128, :] if tt < 2 else x_d[(tt - 2) * 128:(tt - 1) * 128, :]
        P.dma(st[:], src, writes=[st])
        for half in range(2):
            ps = ptr[half]
            for c4 in range(4):
                c = half * 4 + c4
                P.tr(ps[:, c4, :], st[:, c * 128:(c + 1) * 128], ident, [st, cst], [ps])
            if half == 0:
                P.copy(XT[:, 0:4, tt * 128:(tt + 1) * 128], ps[:], [ps], [(XT, tt)])
            else:
                P.act(XT[:, 4:8, tt * 128:(tt + 1) * 128], ps[:], AF.Copy, [ps], [(XT, tt)])

    ccol = P.sbuf("ccol", [128, 16], F32)
    sc3 = P.sbuf("sc3", [128, 8, 2], F32)
    adab = P.sbuf("adab", [128, 2, 72], F32)
    P.dma(ccol[:], ccol_d, writes=[ccol])
    P.dma(adab[:], adab_d, writes=[adab])
    P.act(sc3[:, :, 0], ccol[:, 0:8], AF.Silu, [ccol], [sc3])
    P.act(sc3[:, :, 1], ccol[:, 8:16], AF.Silu, [ccol], [sc3])
    ABLK = 512
    awst = [P.sbuf("awst%d" % i, [128, 8, ABLK], F32) for i in range(2)]
    prow = [P.psum("prow%d" % i, [2, ABLK]) for i in range(2)]
    mrow = P.sbuf("mrow", [2, 9 * D], F32)
    pmod = P.psum("pmod", [128, 72, 2])
    nb = 0
    for l in range(2):
        aw = adaw_d[l].rearrange("(kc p) n -> p kc n", p=128)
        for blk in range(9 * D // ABLK):
            st = awst[nb % 2]
            pr = prow[nb % 2]
            nb += 1
            P.dma(st[:], aw[:, :, blk * ABLK:(blk + 1) * ABLK], writes=[st])
            for kc in range(8):
                P.mm(pr[:], sc3[:, kc, :], st[:, kc, :], kc == 0, kc == 7, [st, sc3], [pr])
            P.act(mrow[:, blk * ABLK:(blk + 1) * ABLK], pr[:], AF.Copy, [pr], [mrow])
        for j in range(72):
            P.tr(pmod[:, j, :], mrow[:, j * 128:(j + 1) * 128], cst[0:2, C_ID, 0:2], [mrow, cst], [pmod])
        for s in range(2):
            P.tt(modT[:, l, :, s], pmod[:, :, s], adab[:, l, :], ALU.add, [pmod, adab], [modT])
    for l in range(2):
        for n in range(3):
            for s in range(2):
                P.stt(AG[:, l, n, :, s], modT[:, l, (3 * n + 1) * 8:(3 * n + 2) * 8, s], 1.0, ng[:, l, n, :],
                      ALU.add, ALU.mult, [modT, ng], [AG])
                P.ts(GH[:, l, n, :, s], modT[:, l, (3 * n + 2) * 8:(3 * n + 3) * 8, s], 0.5 if n != 1 else 1.0,
                     ALU.mult, [modT], [GH])
    if "modT" in dbg_d:
        P.dma(dbg_d["modT"], modT[:].rearrange("p l j s -> p (l j s)"), reads=[modT])
    P.emit()

    def norm_mod(P, W, l, n_idx, t0, n, dst_fn, dst_keys):
        sq, lnv, rs, pss, sqb = (w.next() for w in W)
        P.act(sqb[:, :, 0:n], XT[:, :, t0:t0 + n], AF.Square, xk(t0, n), [sqb])
        for c in range(8):
            P.mm(pss[:, 0:n], ones_b[:], sqb[:, c, 0:n], c == 0, c == 7, [sqb, ones_b], [pss])
        P.act(lnv[:, 0:n], pss[:, 0:n], AF.Ln, [pss], [lnv], bias=EPS, scale=1.0 / D)
        P.act(rs[:, 0:n], lnv[:, 0:n], AF.Exp, [lnv], [rs], scale=-0.5)
        P.tt(sq[:, :, 0:n], XT[:, :, t0:t0 + n], rs[:, 0:n].unsqueeze(1).broadcast_to([128, 8, n]), ALU.mult,
             xk(t0, n) + [rs, sq], [sq])
        for (s0, sn, s) in segs(t0, n):
            for c in range(8):
                P.act(dst_fn(c, s0, sn), sq[:, c, s0 - t0:s0 - t0 + sn], AF.Identity,
                      [sq], dst_keys, bias=modT[:, l, (3 * n_idx) * 8 + c, s:s + 1],
                      scale=AG[:, l, n_idx, c, s:s + 1])

    def norm_ws(P, nmax, nb=1):
        return (Rot(lambda i: P.sbuf("sq%d" % i, [128, 8, nmax], F32), nb),
                Rot(lambda i: P.sbuf("lnv%d" % i, [128, nmax], F32), nb),
                Rot(lambda i: P.sbuf("rs%d" % i, [128, nmax], F32), nb),
                Rot(lambda i: P.psum("pss%d" % i, [128, 512]), nb),
                Rot(lambda i: P.sbuf("sqb%d" % i, [128, 8, nmax], BF16), nb))

    def ffn(l, which, halves):
        n_idx = 0 if which == 0 else 2
        wg_v = wg_d[l, which].rearrange("(kc p) n -> p kc n", p=128)
        wu_v = wu_d[l, which].rearrange("(kc p) n -> p kc n", p=128)
        wd_v = wd_d[l, which].rearrange("(f p) n -> p f n", p=128)
        for hi, groups in enumerate(halves):
            h0 = groups[0][0]
            NH = sum(g[1] for g in groups)
            ng_ = len(groups)
            tag = "f%d%d%d" % (l, which, hi)
            esA = ExitStack()
            AT = Buf(esA.enter_context(nc.sbuf_tensor(tag + "AT", [128, NF, NH], BF16)), NF * ng_, "AT")
            P = Prog(nc, sy, tag + "a")
            hT = P.sbuf("hT", [128, 8, NH], BF16, ng_)
            nmax = max(g[1] for g in groups)
            W = norm_ws(P, nmax)
            for gi, (t0, n) in enumerate(groups):
                norm_mod(P, W, l, n_idx, t0, n, lambda c, s0, sn: hT[:, c, s0 - h0:s0 - h0 + sn], [(hT, gi)])
            stg = P.sbuf("stg", [128, 8, 256], F32)
            stu = P.sbuf("stu", [128, 8, 256], F32)
            wgb = [P.sbuf("wgb%d" % i, [128, 8, 256], BF16) for i in range(2)]
            wub = [P.sbuf("wub%d" % i, [128, 8, 256], BF16) for i in range(2)]
            psg = [P.psum("psg%d" % i, [128, 512]) for i in range(2)]
            psu = [P.psum("psu%d" % i, [128, 512]) for i in range(2)]
            sga = [P.sbuf("sga%d" % i, [128, nmax], F32) for i in range(2)]
            it = 0
            NB = NF // 2

            def fetch_a(fb):
                P.dma(stg[:], wg_v[:, :, fb * 256:(fb + 1) * 256], writes=[stg])
                P.dma(stu[:], wu_v[:, :, fb * 256:(fb + 1) * 256], writes=[stu])

            def cast_a(fb):
                P.act(wgb[fb % 2][:], stg[:], AF.Copy, [stg], [wgb[fb % 2]])
                P.copy(wub[fb % 2][:], stu[:], [stu], [wub[fb % 2]])

            fetch_a(0)
            cast_a(0)
            for fb in range(NB):
                pb = fb % 2
                if fb + 1 < NB:
                    fetch_a(fb + 1)
                cnt = 0
                for f2 in range(2):
                    f = fb * 2 + f2
                    for gi, (t0, n) in enumerate(groups):
                        o0 = t0 - h0
                        pg, pu, sg = psg[it % 2], psu[it % 2], sga[it % 2]
                        it += 1
                        for kc in range(8):
                            P.mm(pg[:, 0:n], wgb[pb][:, kc, f2 * 128:(f2 + 1) * 128], hT[:, kc, o0:o0 + n], kc == 0, kc == 7,
                                 [wgb[pb], (hT, gi)], [pg])
                        for kc in range(8):
                            P.mm(pu[:, 0:n], wub[pb][:, kc, f2 * 128:(f2 + 1) * 128], hT[:, kc, o0:o0 + n], kc == 0, kc == 7,
                                 [wub[pb], (hT, gi)], [pu])
                        P.act(sg[:, 0:n], pg[:, 0:n], AF.Silu, [pg], [sg])
                        P.tt(AT[:, f, o0:o0 + n], sg[:, 0:n], pu[:, 0:n], ALU.mult, [sg, pu], [(AT, f * ng_ + gi)])
                        cnt += 1
                        if cnt == ng_ and fb + 1 < NB:
                            cast_a(fb + 1)
            P.emit()
            P = Prog(nc, sy, tag + "b")
            std = P.sbuf("std", [128, NF, 256], F32)
            wdb = [P.sbuf("wdb%d" % i, [128, NF, 256], BF16) for i in range(2)]
            psy = [P.psum("psy%d" % i, [128, 512]) for i in range(2)]
            it = 0

            def fetch_b(db):
                P.dma(std[:], wd_v[:, :, db * 256:(db + 1) * 256], writes=[std])
                P.act(wdb[db % 2][:], std[:], AF.Copy, [std], [wdb[db % 2]])

            fetch_b(0)
            for db in range(4):
                pb = db % 2
                if db + 1 < 4:
                    fetch_b(db + 1)
                for d2 in range(2):
                    d = db * 2 + d2
                    for gi, (t0, n) in enumerate(groups):
                        o0 = t0 - h0
                        py = psy[it % 2]
                        it += 1
                        for f in range(NF):
                            P.mm(py[:, 0:n], wdb[pb][:, f, d2 * 128:(d2 + 1) * 128], AT[:, f, o0:o0 + n], f == 0, f == NF - 1,
                                 [wdb[pb], (AT, f * ng_ + gi)], [py])
                        for (s0, sn, s) in segs(t0, n):
                            P.stt(XT[:, d, s0:s0 + sn], py[:, s0 - t0:s0 - t0 + sn], GH[:, l, n_idx, d, s:s + 1],
                                  XT[:, d, s0:s0 + sn], ALU.mult, ALU.add, [py] + xk(s0, sn), xk(s0, sn))
            P.emit()
            esA.close()

    def dump_xt(name):
        if name in dbg_d:
            P = Prog(nc, sy, "dbg" + name)
            P.dma(dbg_d[name], XT[:].rearrange("p c t -> p (c t)"), reads=[XT])
            P.emit()

    def make_xn(l):
        P = Prog(nc, sy, "xn%d" % l)
        W = norm_ws(P, 384, 2)
        hb = Rot(lambda i: P.sbuf("hb%d" % i, [128, 8, 384], BF16), 2)
        for (t0, n) in FULL[0] + FULL[1]:
            h = hb.next()
            norm_mod(P, W, l, 1, t0, n, lambda c, s0, sn, h=h, t0=t0: h[:, c, s0 - t0:s0 - t0 + sn], [h])
            P.dma(xn_s[:, :, t0:t0 + n], h[:, :, 0:n], reads=[h], writes=[xn_s])
        P.emit()

    def load_w(P, dst, dst_ap, src_ap, stage, stage_ap):
        P.dma(stage_ap, src_ap, writes=[stage])
        P.copy(dst_ap, stage_ap, [stage], [dst], eng="gpsimd")

    def out_proj(l, wo_d, groups):
        P = Prog(nc, sy, "op%d" % l)
        wo = P.sbuf("wo", [128, 8, D], BF16)
        stwR = Rot(lambda i: P.sbuf("stw%d" % i, [128, 8, 256], F32), 2)
        wv = wo_d.rearrange("(kc p) n -> p kc n", p=128)
        for q in range(4):
            stw = stwR.next()
            P.dma(stw[:], wv[:, :, q * 256:(q + 1) * 256], writes=[stw])
            if q % 2 == 0:
                P.act(wo[:, :, q * 256:(q + 1) * 256], stw[:], AF.Copy, [stw], [wo])
            else:
                P.copy(wo[:, :, q * 256:(q + 1) * 256], stw[:], [stw], [wo])
        mb = Rot(lambda i: P.sbuf("mb%d" % i, [128, 8, 512], BF16), 2)
        pp = Rot(lambda i: P.psum("pp%d" % i, [128, 512]), 3)
        for (t0, n) in groups:
            m = mb.next()
            P.dma(m[:, :, 0:n], mix_s[:, :, t0:t0 + n], reads=[mix_s], writes=[m])
            for dc in range(8):
                ps = pp.next()
                for kc in range(8):
                    P.mm(ps[:, 0:n], wo[:, kc, dc * 128:(dc + 1) * 128], m[:, kc, 0:n], kc == 0, kc == 7, [wo, m], [ps])
                for (s0, sn, s) in segs(t0, n):
                    P.stt(XT[:, dc, s0:s0 + sn], ps[:, s0 - t0:s0 - t0 + sn], GH[:, l, 1, dc, s:s + 1],
                          XT[:, dc, s0:s0 + sn], ALU.mult, ALU.add, [ps] + xk(s0, sn), xk(s0, sn))
        P.emit()

    def swa_prep():
        P = Prog(nc, sy, "swp")
        wv_ = odw_d.rearrange("(kc p) n -> p kc n", p=128)
        odc = P.sbuf("odc", [128, 11], F32)
        P.dma(odc[:], odcol_d, writes=[odc])
        bdf = P.sbuf("bdf", [128, 2], F32)
        P.dma(bdf[:], bdsel_d, writes=[bdf])
        bds = P.sbuf("bds", [128, 2], BF16)
        P.copy(bds[:], bdf[:], [bdf], [bds])
        cs = P.sbuf("cs", [128, 2, SEQ], F32)
        P.dma(cs[:], swcs_d, writes=[cs])
        st = P.sbuf("st", [128, 8, 512], F32)
        wq = P.sbuf("wq", [128, 8, 4, 2, 64], BF16)
        wk = P.sbuf("wk", [128, 8, 128], BF16)
        wvv = P.sbuf("wvv", [128, 8, 128], BF16)
        P.dma(st[:], wv_[:, :, 1568:2080], writes=[st])
        st5 = st[:].rearrange("p k (g j d) -> p k g j d", g=2, j=4)
        for g in range(2):
            P.copy(wq[:, :, :, g, :], st5[:, :, g, :, :], [st], [wq], eng="gpsimd")
        load_w(P, wk, wk[:], wv_[:, :, 2080:2208], st, st[:, :, 0:128])
        load_w(P, wvv, wvv[:], wv_[:, :, 2208:2336], st, st[:, :, 0:128])
        xg = Rot(lambda i: P.sbuf("xg%d" % i, [128, 8, 512], BF16), 2)
        pp = Rot(lambda i: P.psum("pp%d" % i, [128, 512]), 3)
        pssR = Rot(lambda i: P.psum("pss%d" % i, [128, 512]), 2)
        prot = P.psum("prot", [128, 512])
        prk = P.psum("prk", [128, NT, 2])
        pv = P.psum("pv", [128, 128])
        sqR = Rot(lambda i: P.sbuf("sq%d" % i, [128, 512], BF16), 2)
        lnvR = Rot(lambda i: P.sbuf("lnv%d" % i, [128, 512], F32), 2)
        rsR = Rot(lambda i: P.sbuf("rs%d" % i, [128, 512], F32), 2)
        qgR = Rot(lambda i: P.sbuf("qg%d" % i, [128, 512], BF16), 2)
        t1R = Rot(lambda i: P.sbuf("t1%d" % i, [128, 512], F32), 2)
        t2R = Rot(lambda i: P.sbuf("t2%d" % i, [128, 512], F32), 2)
        qo = Rot(lambda i: P.sbuf("qo%d" % i, [128, 512], BF16), 2)
        vd = P.sbuf("vd", [128, NT, 2, 2, 64], BF16)
        rsk = P.sbuf("rsk", [128, NT * 2], F32)
        for (t0, n) in GROUPS5:
            x = xg.next()
            P.dma(x[:, :, 0:n], xn_s[:, :, t0:t0 + n], reads=[xn_s], writes=[x])
            lat = t0 >= CTX
            l0 = t0 - CTX
            if lat:
                for j in range(4):
                    ps = pp.next()
                    for kc in range(8):
                        P.mm(ps[:, 0:n], wq[:, kc, j].rearrange("p g d -> p (g d)"), x[:, kc, 0:n], kc == 0, kc == 7,
                             [wq, x], [ps])
                    sq, pss, lnv, rs, qg, t1, t2 = (sqR.next(), pssR.next(), lnvR.next(), rsR.next(), qgR.next(),
                                                    t1R.next(), t2R.next())
                    P.act(sq[:, 0:n], ps[:, 0:n], AF.Square, [ps], [sq])
                    P.mm(pss[:, 0:n], cstb[:, C_BD64, :], sq[:, 0:n], True, True, [cstb, sq], [pss])
                    P.act(lnv[:, 0:n], pss[:, 0:n], AF.Ln, [pss], [lnv], bias=EPS, scale=1.0 / 64)
                    P.act(rs[:, 0:n], lnv[:, 0:n], AF.Exp, [lnv], [rs], scale=-0.5)
                    P.act(qg[:, 0:n], ps[:, 0:n], AF.Identity, [ps], [qg], scale=odc[:, 0:1])
                    P.mm(prot[:, 0:n], cstb[:, C_ROT64, :], qg[:, 0:n], True, True, [cstb, qg], [prot])
                    P.tt(t1[:, 0:n], qg[:, 0:n], cs[:, 0, l0:l0 + n], ALU.mult, [qg, cs], [t1])
                    P.tt(t2[:, 0:n], prot[:, 0:n], cs[:, 1, l0:l0 + n], ALU.mult, [prot, cs], [t2])
                    P.tt(t1[:, 0:n], t1[:, 0:n], t2[:, 0:n], ALU.add, [t1, t2], [t1])
                    q = qo.next()
                    P.tt(q[:, 0:n], t1[:, 0:n], rs[:, 0:n], ALU.mult, [t1, rs], [q])
                    P.dma(swq_s[:, j, l0:l0 + n], q[:, 0:n], reads=[q], writes=[swq_s])
            ps = pp.next()
            for kc in range(8):
                P.mm(ps[:, 0:n], wk[:, kc, :], x[:, kc, 0:n], kc == 0, kc == 7, [wk, x], [ps])
            sq, qg, t1, t2 = sqR.next(), qgR.next(), t1R.next(), t2R.next()
            P.act(sq[:, 0:n], ps[:, 0:n], AF.Square, [ps], [sq])
            for ti in range(n // 128):
                tt = t0 // 128 + ti
                P.mm(prk[:, tt, :], sq[:, ti * 128:(ti + 1) * 128], bds[:], True, True, [sq, bds], [prk])
            q = qo.next()
            if lat:
                P.act(qg[:, 0:n], ps[:, 0:n], AF.Identity, [ps], [qg], scale=odc[:, 1:2])
                P.mm(prot[:, 0:n], cstb[:, C_ROT64, :], qg[:, 0:n], True, True, [cstb, qg], [prot])
                P.tt(t1[:, 0:n], qg[:, 0:n], cs[:, 0, l0:l0 + n], ALU.mult, [qg, cs], [t1])
                P.tt(t2[:, 0:n], prot[:, 0:n], cs[:, 1, l0:l0 + n], ALU.mult, [prot, cs], [t2])
                P.tt(q[:, 0:n], t1[:, 0:n], t2[:, 0:n], ALU.add, [t1, t2], [q])
            else:
                P.act(q[:, 0:n], ps[:, 0:n], AF.Identity, [ps], [q], scale=odc[:, 1:2])
            P.dma(swk_s[:, t0:t0 + n], q[:, 0:n], reads=[q], writes=[swk_s])
            for ti in range(n // 128):
                tt = t0 // 128 + ti
                for kc in range(8):
                    P.mm(pv[:], x[:, kc, ti * 128:(ti + 1) * 128], wvv[:, kc, :], kc == 0, kc == 7, [x, wvv], [pv])
                pv3 = pv[:].rearrange("p (g d) -> p g d", g=2)
                P.copy(vd[:, tt, :, 0, :], pv3, [pv], [vd])
                P.act(vd[:, tt, :, 1, :], pv3, AF.Copy, [pv], [vd])
        P.act(rsk[:], prk[:].rearrange("p a b -> p (a b)"), AF.Ln, [prk], [rsk], bias=EPS, scale=1.0 / 64)
        P.act(rsk[:], rsk[:], AF.Exp, [rsk], [rsk], scale=-0.5)
        P.ts(rsk[:], rsk[:], 0.125, ALU.mult, [rsk], [rsk])
        P.dma(swr_s[:], rsk[:], reads=[rsk], writes=[swr_s])
        P.dma(swv_s[:], vd[:].rearrange("p a b c d -> p (a b c d)"), reads=[vd], writes=[swv_s])
        P.emit()

    def swa_core():
        P = Prog(nc, sy, "swc")
        q = P.sbuf("q", [128, 4, SEQ], BF16)
        k = P.sbuf("k", [128, T], BF16)
        v = P.sbuf("v", [128, NT, 2, 128], BF16)
        rsk = P.sbuf("rsk", [128, NT, 2], F32)
        odc = P.sbuf("odc", [128, 11], F32)
        esk = P.sbuf("esk", [128, 8], F32)
        mlo = P.sbuf("mlo", [128, 128], BF16)
        mhi = P.sbuf("mhi", [128, 128], BF16)
        P.dma(q[:], swq_s[:], reads=[swq_s], writes=[q])
        P.dma(k[:], swk_s[:], reads=[swk_s], writes=[k])
        P.dma(v[:].rearrange("p a b c -> p (a b c)"), swv_s[:], reads=[swv_s], writes=[v])
        P.dma(rsk[:].rearrange("p a b -> p (a b)"), swr_s[:], reads=[swr_s], writes=[rsk])
        P.dma(odc[:], odcol_d, writes=[odc])
        P.act(esk[:], odc[:, 3:11], AF.Exp, [odc], [esk])
        P.copy(mlo[:], cst[:, C_LI, :], [cst], [mlo])
        P.copy(mhi[:], cst[:, C_UI, :], [cst], [mhi])
        msw = P.sbuf("msw", [128, 4, SEQ], BF16)
        pS = Rot(lambda i: P.psum("pS%d" % i, [128, 4, 128]), 4)
        pO = Rot(lambda i: P.psum("pO%d" % i, [128, 4, 128]), 2)
        pM = Rot(lambda i: P.psum("pM%d" % i, [128, 4, 128]), 2)
        pT = Rot(lambda i: P.sbuf("pT%d" % i, [128, 4, 128], BF16), 4)
        den = P.sbuf("den", [128, 4, 128], F32)
        rden = P.sbuf("rden", [128, 4, 128], F32)
        for n in range(16):
            for g in range(2):
                kts = [(0, None), (1, None)]
                if n - 1 >= 0:
                    kts.append((n + 1, mlo))
                kts.append((n + 2, None))
                if n + 1 <= 15:
                    kts.append((n + 3, mhi))
                po = pO.next()
                pm = pM.next()
                def scores(kt, mask):
                    ps = pS.next()
                    P.mm(ps[:], k[g * 64:(g + 1) * 64, kt * 128:(kt + 1) * 128],
                         q[g * 64:(g + 1) * 64, :, n * 128:(n + 1) * 128], True, True, [k, q], [ps])
                    pt = pT.next()
                    P.act(pt[:], ps[:], AF.Exp, [ps, rsk], [pt], scale=rsk[:, kt, g:g + 1])
                    if mask is not None:
                        P.tt(pt[:], pt[:], mask[:].unsqueeze(1).broadcast_to([128, 4, 128]), ALU.mult, [pt, mask], [pt])
                    return pt

                nxt = scores(*kts[0])
                for ki, (kt, mask) in enumerate(kts):
                    pt = nxt
                    if ki + 1 < len(kts):
                        nxt = scores(*kts[ki + 1])
                    P.mm(po[:], v[:, kt, g, :], pt[:], ki == 0, ki == len(kts) - 1, [v, pt], [po])
                    P.mm(pm[:], ones_b[:], pt[:], ki == 0, ki == len(kts) - 1, [ones_b, pt], [pm])
                P.tt(den[:], pm[:], esk[:, 4 * g:4 * g + 4].unsqueeze(2).broadcast_to([128, 4, 128]), ALU.add,
                     [pm, esk], [den])
                P.act(rden[:], den[:], AF.Ln, [den], [rden])
                P.act(rden[:], rden[:], AF.Exp, [rden], [rden], scale=-1.0)
                for par in range(2):
                    sl = slice(par * 64, (par + 1) * 64)
                    P.tt(msw[sl, 2 * g:2 * g + 2, n * 128:(n + 1) * 128], po[sl, par::2, :], rden[sl, par::2, :],
                         ALU.mult, [po, rden], [msw])
        P.dma(mix_s[:, 4:8, CTX:T], msw[:], reads=[msw], writes=[mix_s])
        P.emit()

    def gla_prep():
        P = Prog(nc, sy, "glp")
        wv_ = odw_d.rearrange("(kc p) n -> p kc n", p=128)
        stR = Rot(lambda i: P.sbuf("st%d" % i, [128, 8, 256], F32), 1)
        wqk = P.sbuf("wqk", [128, 8, 512], BF16)
        wvv = P.sbuf("wvv", [128, 8, 512], BF16)
        wgz = P.sbuf("wgz", [128, 8, 32], BF16)
        wr = P.sbuf("wr", [128, 8, 512], BF16)
        for (wt_, c0_) in ((wqk, 0), (wvv, 512), (wr, 1056)):
            for hf in range(2):
                st = stR.next()
                load_w(P, wt_, wt_[:, :, hf * 256:(hf + 1) * 256], wv_[:, :, c0_ + hf * 256:c0_ + (hf + 1) * 256], st, st[:])
        st = stR.next()
        load_w(P, wgz, wgz[:], wv_[:, :, 1024:1056], st, st[:, :, 0:32])
        g2f = P.sbuf("g2f", [17, 2, 256], F32)
        g2b = P.sbuf("g2b", [17, 2, 256], BF16)
        P.dma(g2f[:], gw2b_d, writes=[g2f])
        P.copy(g2b[:], g2f[:], [g2f], [g2b])
        msk = P.sbuf("msk", [128, 512], F32)
        P.memset(msk[:], 1.0, [msk])
        for i in range(4):
            P.memset(msk[:, i * 128:i * 128 + 1], 0.0, [msk])
        gz = [P.sbuf("gz%d" % d, [17, 512], BF16) for d in range(2)]
        for d in range(2):
            P.memset(gz[d][:], 1.0, [gz[d]])
        xg = Rot(lambda i: P.sbuf("xg%d" % i, [128, 8, 512], BF16), 2)
        pp = Rot(lambda i: P.psum("pp%d" % i, [128, 512]), 4)
        ptr_ = P.psum("ptr", [128, 2, 128], BF16)
        pgzD = [P.psum("pgz%d" % i, [16, 512]) for i in range(2)]
        qkraw = P.sbuf("qkraw", [128, 4, 512], F32)
        spD = [P.sbuf("sp%d" % i, [128, 2, 512], F32) for i in range(2)]
        preD = [P.sbuf("pre%d" % i, [128, 2, 512], F32) for i in range(2)]
        bcD = [P.sbuf("bc%d" % i, [128, 2, 512], F32) for i in range(2)]
        remD = [P.sbuf("rem%d" % i, [128, 2, 512], F32) for i in range(2)]
        eeR = Rot(lambda i: P.sbuf("ee%d" % i, [128, 2, 512], F32), 2)
        etD = [P.sbuf("et%d" % i, [128, 2, 4], F32) for i in range(2)]
        ob = Rot(lambda i: P.sbuf("ob%d" % i, [128, 2, 512], BF16), 3)
        kdt = Rot(lambda i: P.sbuf("kdt%d" % i, [128, 4, 256], BF16), 2)
        vtm = Rot(lambda i: P.sbuf("vtm%d" % i, [128, 4, 512], BF16), 2)
        rg = Rot(lambda i: P.sbuf("rg%d" % i, [128, 4, 512], BF16), 1)
        idb = P.sbuf("idb", [128, 128], BF16)
        P.copy(idb[:], cst[:, C_ID, :], [cst], [idb])
        for (t0, n) in GROUPS5:
            nt = n // 128
            x = xg.next()
            P.dma(x[:, :, 0:n], xn_s[:, :, t0:t0 + n], reads=[xn_s], writes=[x])
            for cq in range(4):
                ps = pp.next()
                for kc in range(8):
                    P.mm(ps[:, 0:n], wqk[:, kc, cq * 128:(cq + 1) * 128], x[:, kc, 0:n], kc == 0, kc == 7, [wqk, x], [ps])
                P.act(qkraw[:, cq, 0:n], ps[:, 0:n], AF.Copy, [ps], [qkraw])
            vt = vtm.next()
            for ti in range(nt):
                ps = pp.next()
                for kc in range(8):
                    P.mm(ps[:], x[:, kc, ti * 128:(ti + 1) * 128], wvv[:, kc, :], kc == 0, kc == 7, [x, wvv], [ps])
                P.act(vt[:, ti, :], ps[:], AF.Copy, [ps], [vt])
            P.dma(glv_s[t0:t0 + n, :].rearrange("(a p) f -> p a f", p=128), vt[:, 0:nt, :], reads=[vt], writes=[glv_s])
            if t0 >= CTX:
                r = rg.next()
                for h in range(4):
                    ps = pp.next()
                    for kc in range(8):
                        P.mm(ps[:, 0:n], wr[:, kc, h * 128:(h + 1) * 128], x[:, kc, 0:n], kc == 0, kc == 7, [wr, x], [ps])
                    P.act(r[:, h, 0:n], ps[:, 0:n], AF.Silu, [ps], [r])
                P.dma(glr_s[:, :, t0 - CTX:t0 - CTX + n], r[:, :, 0:n], reads=[r], writes=[glr_s])
            for d in range(2):
                for kc in range(8):
                    P.mm(pgzD[d][:, 0:n], wgz[:, kc, 16 * d:16 * d + 16], x[:, kc, 0:n], kc == 0, kc == 7, [wgz, x], [pgzD[d]])
                P.act(gz[d][0:16, 0:n], pgzD[d][:, 0:n], AF.Copy, [pgzD[d]], [gz[d]])
            for c in range(2):
                for d in range(2):
                    sp, pre = spD[d], preD[d]
                    ps = pp.next()
                    P.mm(ps[:, 0:n], g2b[:, d, c * 128:(c + 1) * 128], gz[d][:, 0:n], True, True, [g2b, gz[d]], [ps])
                    P.act(sp[:, c, 0:n], ps[:, 0:n], AF.Exp, [ps], [sp], scale=-1.0)
                    P.act(sp[:, c, 0:n], sp[:, c, 0:n], AF.Ln, [sp], [sp], bias=1.0)
                    P.op("vector", lambda e, c=c, n=n, pre=pre, sp=sp: e.tensor_tensor_scan(
                        out=pre[:, c, 0:n], data0=msk[:, 0:n], data1=sp[:, c, 0:n], initial=0.0,
                        op0=ALU.mult, op1=ALU.add), [msk, sp], [pre])
            for d in range(2):
                sp, pre, bc, rem, et = spD[d], preD[d], bcD[d], remD[d], etD[d]
                pre4 = pre[:, :, 0:n].rearrange("p c (a b) -> p c a b", b=128)
                totb = pre4[:, :, :, 127:128].broadcast_to([128, 2, nt, 128])
                bc4 = bc[:, :, 0:n].rearrange("p c (a b) -> p c a b", b=128)
                rem4 = rem[:, :, 0:n].rearrange("p c (a b) -> p c a b", b=128)
                if d == 0:
                    P.copy(bc[:, :, 0:n], pre[:, :, 0:n], [pre], [bc], eng="gpsimd")
                    P.tt(rem4, totb, pre4, ALU.subtract, [pre], [rem])
                else:
                    P.tt(rem[:, :, 0:n], pre[:, :, 0:n], sp[:, :, 0:n], ALU.subtract, [pre, sp], [rem])
                    P.tt(bc4, totb, rem4, ALU.subtract, [pre, rem], [bc])
                P.act(et[:, :, 0:nt], pre4[:, :, :, 127], AF.Exp, [pre], [et], scale=-1.0 / 16)
                P.dma(glet_s[d][:, :, t0 // 128:t0 // 128 + nt], et[:, :, 0:nt], reads=[et], writes=[glet_s[d]])
            outs = []
            for d in range(2):
                bc, rem = bcD[d], remD[d]
                ee = eeR.next()
                P.act(ee[:, :, 0:n], bc[:, :, 0:n], AF.Exp, [bc], [ee], scale=-1.0 / 16)
                o = ob.next()
                P.stt(o[:, :, 0:n], qkraw[:, 0:2, 0:n], 0.125, ee[:, :, 0:n], ALU.mult, ALU.mult, [qkraw, ee], [o])
                P.dma(glq_s[d][:, :, t0:t0 + n], o[:, :, 0:n], reads=[o], writes=[glq_s[d]])
                ee = eeR.next()
                P.act(ee[:, :, 0:n], bc[:, :, 0:n], AF.Exp, [bc], [ee], scale=1.0 / 16)
                o = ob.next()
                P.tt(o[:, :, 0:n], qkraw[:, 2:4, 0:n], ee[:, :, 0:n], ALU.mult, [qkraw, ee], [o])
                P.dma(glk_s[d][:, :, t0:t0 + n], o[:, :, 0:n], reads=[o], writes=[glk_s[d]])
                ee = eeR.next()
                P.act(ee[:, :, 0:n], rem[:, :, 0:n], AF.Exp, [rem], [ee], scale=-1.0 / 16)
                o = ob.next()
                P.tt(o[:, :, 0:n], qkraw[:, 2:4, 0:n], ee[:, :, 0:n], ALU.mult, [qkraw, ee], [o])
                kd = kdt.next()
                for ti in range(nt):
                    for c in range(2):
                        P.tr(ptr_[:, c, :], o[:, c, ti * 128:(ti + 1) * 128], idb[:], [o, idb], [ptr_])
                    P.copy(kd[:, ti, :], ptr_[:].rearrange("p a b -> p (a b)"), [ptr_], [kd])
                P.dma(glkd_s[d][t0:t0 + n, :].rearrange("(a p) f -> p a f", p=128), kd[:, 0:nt, :], reads=[kd],
                      writes=[glkd_s[d]])
        P.emit()

    def gla_core():
        P = Prog(nc, sy, "glc")
        odc = P.sbuf("odc", [128, 11], F32)
        P.dma(odc[:], odcol_d, writes=[odc])
        ogla = P.sbuf("ogla", [128, 4, SEQ], F32, 16)
        for i in range(4):
            P.memset(ogla[:, i, :], 0.0, [ogla], eng="gpsimd" if i % 2 else "vector")
        S = [[P.sbuf("S%d%d" % (d, c), [128, 128], F32) for c in range(2)] for d in range(2)]
        Sb = [[P.sbuf("Sb%d%d" % (d, h), [128, 128], BF16) for h in range(4)] for d in range(2)]
        qd = Rot(lambda i: P.sbuf("qd%d" % i, [128, 2, 128], BF16), 4)
        kp = Rot(lambda i: P.sbuf("kp%d" % i, [128, 2, 128], BF16), 4)
        kd = Rot(lambda i: P.sbuf("kd%d" % i, [128, 256], BF16), 4)
        vt = Rot(lambda i: P.sbuf("vt%d" % i, [128, 512], BF16), 4)
        et = [P.sbuf("et%d" % d, [128, 2, NT], F32) for d in range(2)]
        aT = Rot(lambda i: P.sbuf("aT%d" % i, [128, 4, 128], BF16), 2)
        pA0 = Rot(lambda i: P.psum("pA0%d" % i, [128, 2, 128]), 1)
        pA1 = Rot(lambda i: P.psum("pA1%d" % i, [128, 2, 128]), 1)
        pO = Rot(lambda i: P.psum("pO%d" % i, [128, 4, 128]), 2)
        pSt = Rot(lambda i: P.psum("pSt%d" % i, [128, 128]), 3)
        orders = [list(range(NT)), [1, 0] + list(range(NT - 1, 1, -1))]
        masks = [cst[:, C_UI, :], cst[:, C_LI, :]]
        for d in range(2):
            P.dma(et[d][:], glet_s[d][:], reads=[glet_s[d]], writes=[et[d]])
            for c in range(2):
                P.memset(S[d][c][:], 0.0, [S[d][c]])
            for h in range(4):
                P.memset(Sb[d][h][:], 0.0, [Sb[d][h]])
        for s_ in range(NT):
            for d in range(2):
                tt = orders[d][s_]
                mask = masks[d]
                q_, k_, kd_, v_ = qd.next(), kp.next(), kd.next(), vt.next()
                tsl = slice(tt * 128, (tt + 1) * 128)
                P.dma(q_[:], glq_s[d][:, :, tsl], reads=[glq_s[d]], writes=[q_])
                P.dma(k_[:], glk_s[d][:, :, tsl], reads=[glk_s[d]], writes=[k_])
                P.dma(kd_[:], glkd_s[d][tsl, :], reads=[glkd_s[d]], writes=[kd_])
                P.dma(v_[:], glv_s[tsl, :], reads=[glv_s], writes=[v_])
                if tt >= 2:
                    a_ = aT.next()
                    for par, pa_ in enumerate((pA0, pA1)):
                        pa = pa_.next()
                        hs = slice(par * 64, par * 64 + 64)
                        for c in range(2):
                            P.mm(pa[:, c, :], k_[hs, c, :], q_[hs, c, :], True, True, [k_, q_], [pa])
                        P.tt(a_[:, par::2, :], pa[:], mask.unsqueeze(1).broadcast_to([128, 2, 128]), ALU.mult,
                             [pa, cst], [a_])
                    po = pO.next()
                    for h in range(4):
                        P.mm(po[:, h, :], v_[:, h * 128:(h + 1) * 128], a_[:, h, :], True, False, [v_, a_], [po])
                        P.mm(po[:, h, :], Sb[d][h][:], q_[:, h // 2, :], False, True, [Sb[d][h], q_], [po])
                    lt = tt - 2
                    osl = ogla[:, :, lt * 128:(lt + 1) * 128]
                    P.tt(osl, osl, po[:], ALU.add, [po, (ogla, lt)], [(ogla, lt)])
                for c in range(2):
                    ps = pSt.next()
                    for hh in range(2):
                        h = 2 * c + hh
                        P.mm(ps[hh * 64:(hh + 1) * 64, :], kd_[:, h * 64:(h + 1) * 64], v_[:, h * 128:(h + 1) * 128],
                             True, True, [kd_, v_], [ps])
                    P.stt(S[d][c][:], S[d][c][:], et[d][:, c, tt:tt + 1], ps[:], ALU.mult, ALU.add,
                          [S[d][c], et[d], ps], [S[d][c]])
                    for hh in range(2):
                        hs = slice(hh * 64, hh * 64 + 64)
                        P.act(Sb[d][2 * c + hh][hs, :], S[d][c][hs, :], AF.Copy, [S[d][c]], [Sb[d][2 * c + hh]])
        sqR = Rot(lambda i: P.sbuf("sq%d" % i, [128, 512], BF16), 2)
        lnvR = Rot(lambda i: P.sbuf("lnv%d" % i, [128, 512], F32), 2)
        rsR = Rot(lambda i: P.sbuf("rs%d" % i, [128, 512], F32), 2)
        tmpR = Rot(lambda i: P.sbuf("tmp%d" % i, [128, 512], F32), 2)
        pss = P.psum("pss", [128, 512])
        rg = Rot(lambda i: P.sbuf("rg%d" % i, [128, 4, 512], BF16), 2)
        mo = Rot(lambda i: P.sbuf("mo%d" % i, [128, 4, 512], BF16), 2)
        for gi in range(4):
            r = rg.next()
            m = mo.next()
            gsl = slice(gi * 512, (gi + 1) * 512)
            keys = [(ogla, i) for i in range(gi * 4, gi * 4 + 4)]
            P.dma(r[:], glr_s[:, :, gsl], reads=[glr_s], writes=[r])
            for h in range(4):
                sq, lnv, rs, tmp = sqR.next(), lnvR.next(), rsR.next(), tmpR.next()
                P.act(sq[:], ogla[:, h, gsl], AF.Square, keys, [sq])
                P.mm(pss[:], ones_b[:], sq[:], True, True, [ones_b, sq], [pss])
                P.act(lnv[:], pss[:], AF.Ln, [pss], [lnv], bias=EPS, scale=1.0 / 128)
                P.act(rs[:], lnv[:], AF.Exp, [lnv], [rs], scale=-0.5)
                P.tt(tmp[:], ogla[:, h, gsl], rs[:], ALU.mult, keys + [rs], [tmp])
                P.stt(m[:, h, :], tmp[:], odc[:, 2:3], r[:, h, :], ALU.mult, ALU.mult, [tmp, odc, r], [m])
            P.dma(mix_s[:, 0:4, CTX + gi * 512:CTX + (gi + 1) * 512], m[:], reads=[m], writes=[mix_s])
        P.emit()

    def mla_prep():
        P = Prog(nc, sy, "mlp")
        wv_ = evw_d.rearrange("(kc p) n -> p kc n", p=128)
        evc = P.sbuf("evc", [128, 8], F32)
        P.dma(evc[:], evcol_d, writes=[evc])
        cs = P.sbuf("cs", [96, 2, SEQ], F32)
        P.dma(cs[:], mlcs_d, writes=[cs])
        st = P.sbuf("st", [128, 8, 384], F32)
        wcq = P.sbuf("wcq", [128, 8, 384], BF16)
        wckv = P.sbuf("wckv", [128, 8, 256], BF16)
        wkr = P.sbuf("wkr", [128, 8, 32], BF16)
        wqu = P.sbuf("wqu", [128, 3, 768], BF16)
        wkK = P.sbuf("wkK", [128, 2, 8, 64], BF16)
        wkV = P.sbuf("wkV", [128, 2, 8, 64], BF16)
        load_w(P, wcq, wcq[:], wv_[:, :, 0:384], st, st[:])
        load_w(P, wckv, wckv[:], wv_[:, :, 384:640], st, st[:, :, 0:256])
        load_w(P, wkr, wkr[:], wv_[:, :, 640:672], st, st[:, :, 0:32])
        stf = st[:].rearrange("p a b -> p (a b)")
        qu_v = evqu_d.rearrange("(kc p) n -> p kc n", p=128)
        st_qu = stf[:, 0:3 * 768].rearrange("p (a b) -> p a b", a=3)
        load_w(P, wqu, wqu[:], qu_v, st, st_qu)
        kv_v = evkv_d.rearrange("(kc p) n -> p kc n", p=128)
        st_kv = stf[:, 0:2048].rearrange("p (a b) -> p a b", a=2)
        P.dma(st_kv, kv_v, writes=[st])
        st_kv5 = stf[:, 0:2048].rearrange("p (a h t d) -> p a h t d", a=2, h=8, t=2)
        P.copy(wkK[:], st_kv5[:, :, :, 0, :], [st], [wkK], eng="gpsimd")
        P.copy(wkV[:], st_kv5[:, :, :, 1, :], [st], [wkV], eng="gpsimd")
        xg = Rot(lambda i: P.sbuf("xg%d" % i, [128, 8, 512], BF16), 2)
        pp = Rot(lambda i: P.psum("pp%d" % i, [128, 512]), 3)
        pssR = Rot(lambda i: P.psum("pss%d" % i, [128, 512]), 2)
        protR = Rot(lambda i: P.psum("prot%d" % i, [128, 512]), 1)
        prk = P.psum("prk", [128, NT, 8])
        prkr = P.psum("prkr", [128, NT])
        raw = P.sbuf("raw", [128, 3, 512], F32)
        sq3 = P.sbuf("sq3", [128, 3, 512], BF16)
        lnvR = Rot(lambda i: P.sbuf("lnv%d" % i, [128, 512], F32), 2)
        rsR = Rot(lambda i: P.sbuf("rs%d" % i, [128, 512], F32), 2)
        cqn = P.sbuf("cqn", [128, 3, 512], BF16)
        ckvn = P.sbuf("ckvn", [128, 2, 512], BF16)
        sqR = Rot(lambda i: P.sbuf("sq%d" % i, [128, 512], BF16), 2)
        qgR = Rot(lambda i: P.sbuf("qg%d" % i, [128, 512], BF16), 2)
        t1R = Rot(lambda i: P.sbuf("t1%d" % i, [128, 512], F32), 2)
        t2R = Rot(lambda i: P.sbuf("t2%d" % i, [128, 512], F32), 2)
        qo = Rot(lambda i: P.sbuf("qo%d" % i, [128, 512], BF16), 3)
        vtm = Rot(lambda i: P.sbuf("vtm%d" % i, [128, 4, 512], BF16), 2)
        rsk = P.sbuf("rsk", [128, NT * 8], F32)
        R96 = cstb[0:96, C_ROT96, 0:96]

        def rope_out(q, src, rs, n, lat, l0, with_rs):
            if lat:
                prot, t1, t2 = protR.next(), t1R.next(), t2R.next()
                P.mm(prot[0:96, 0:n], R96, src[0:96, 0:n], True, True, [cstb, src], [prot])
                P.tt(t1[0:96, 0:n], src[0:96, 0:n], cs[:, 0, l0:l0 + n], ALU.mult, [src, cs], [t1])
                P.tt(t2[0:96, 0:n], prot[0:96, 0:n], cs[:, 1, l0:l0 + n], ALU.mult, [prot, cs], [t2])
                if with_rs:
                    P.tt(t1[0:96, 0:n], t1[0:96, 0:n], t2[0:96, 0:n], ALU.add, [t1, t2], [t1])
                    P.tt(q[0:96, 0:n], t1[0:96, 0:n], rs[0:96, 0:n], ALU.mult, [t1, rs], [q])
                else:
                    P.tt(q[0:96, 0:n], t1[0:96, 0:n], t2[0:96, 0:n], ALU.add, [t1, t2], [q])
            else:
                if with_rs:
                    P.tt(q[0:96, 0:n], src[0:96, 0:n], rs[0:96, 0:n], ALU.mult, [src, rs], [q])
                else:
                    P.copy(q[0:96, 0:n], src[0:96, 0:n], [src], [q])

        for (t0, n) in GROUPS5:
            nt = n // 128
            lat = t0 >= CTX
            l0 = t0 - CTX
            x = xg.next()
            P.dma(x[:, :, 0:n], xn_s[:, :, t0:t0 + n], reads=[xn_s], writes=[x])
            for (wt, nch, off, dstn, dim) in ((wcq, 3, 0, cqn, 384), (wckv, 2, 3, ckvn, 256)):
                for c in range(nch):
                    ps = pp.next()
                    for kc in range(8):
                        P.mm(ps[:, 0:n], wt[:, kc, c * 128:(c + 1) * 128], x[:, kc, 0:n], kc == 0, kc == 7, [wt, x], [ps])
                    P.act(raw[:, c, 0:n], ps[:, 0:n], AF.Copy, [ps], [raw])
                    P.act(sq3[:, c, 0:n], ps[:, 0:n], AF.Square, [ps], [sq3])
                pss, lnv, rs = pssR.next(), lnvR.next(), rsR.next()
                for c in range(nch):
                    P.mm(pss[:, 0:n], ones_b[:], sq3[:, c, 0:n], c == 0, c == nch - 1, [ones_b, sq3], [pss])
                P.act(lnv[:, 0:n], pss[:, 0:n], AF.Ln, [pss], [lnv], bias=EPS, scale=1.0 / dim)
                P.act(rs[:, 0:n], lnv[:, 0:n], AF.Exp, [lnv], [rs], scale=-0.5)
                for c in range(nch):
                    P.stt(dstn[:, c, 0:n], raw[:, c, 0:n], evc[:, off + c:off + c + 1], rs[:, 0:n], ALU.mult, ALU.mult,
                          [raw, evc, rs], [dstn])
            vt = vtm.next()
            for ti in range(nt):
                ps = pp.next()
                for c in range(2):
                    P.mm(ps[:], ckvn[:, c, ti * 128:(ti + 1) * 128], wkV[:, c].rearrange("p h d -> p (h d)"),
                         c == 0, c == 1, [ckvn, wkV], [ps])
                P.act(vt[:, ti, :], ps[:], AF.Copy, [ps], [vt])
            P.dma(mv_s[t0:t0 + n, :].rearrange("(a p) f -> p a f", p=128), vt[:, 0:nt, :], reads=[vt], writes=[mv_s])
            R = slice(64, 96)
            psr = pp.next()
            for kc in range(8):
                P.mm(psr[R, 0:n], wkr[:, kc, :], x[:, kc, 0:n], kc == 0, kc == 7, [wkr, x], [psr])
            sqr, krg = sqR.next(), qgR.next()
            P.act(sqr[R, 0:n], psr[R, 0:n], AF.Square, [psr], [sqr])
            for ti in range(nt):
                tt = t0 // 128 + ti
                P.mm(prkr[:, tt:tt + 1], sqr[R, ti * 128:(ti + 1) * 128], ones_b[R, 0:1], True, True, [sqr, ones_b], [prkr])
            P.act(krg[R, 0:n], psr[R, 0:n], AF.Identity, [psr], [krg], scale=evc[R, 6:7])
            krf = qo.next()
            if lat:
                prot, t1, t2 = protR.next(), t1R.next(), t2R.next()
                P.mm(prot[R, 0:n], cstb[R, C_ROT96, 64:96], krg[R, 0:n], True, True, [cstb, krg], [prot])
                P.tt(t1[R, 0:n], krg[R, 0:n], cs[R, 0, l0:l0 + n], ALU.mult, [krg, cs], [t1])
                P.tt(t2[R, 0:n], prot[R, 0:n], cs[R, 1, l0:l0 + n], ALU.mult, [prot, cs], [t2])
                P.tt(krf[R, 0:n], t1[R, 0:n], t2[R, 0:n], ALU.add, [t1, t2], [krf])
            else:
                P.copy(krf[R, 0:n], krg[R, 0:n], [krg], [krf])
            for h in range(8):
                P.dma(mkf_s[R, h, t0:t0 + n], krf[R, 0:n], reads=[krf], writes=[mkf_s])
            for h in range(8):
                ps = pp.next()
                for c in range(3):
                    P.mm(ps[0:96, 0:n], wqu[:, c, h * 96:(h + 1) * 96], cqn[:, c, 0:n], c == 0, c == 2, [wqu, cqn], [ps])
                sq, pss, lnv, rs, qg = sqR.next(), pssR.next(), lnvR.next(), rsR.next(), qgR.next()
                P.act(sq[0:96, 0:n], ps[0:96, 0:n], AF.Square, [ps], [sq])
                P.mm(pss[0:96, 0:n], ones_b[0:96, 0:96], sq[0:96, 0:n], True, True, [ones_b, sq], [pss])
                P.act(lnv[0:96, 0:n], pss[0:96, 0:n], AF.Ln, [pss], [lnv], bias=EPS, scale=1.0 / 96)
                P.act(rs[0:96, 0:n], lnv[0:96, 0:n], AF.Exp, [lnv], [rs], scale=-0.5)
                P.act(qg[0:96, 0:n], ps[0:96, 0:n], AF.Identity, [ps], [qg], scale=evc[0:96, 5:6])
                q = qo.next()
                rope_out(q, qg, rs, n, lat, l0, True)
                P.dma(mqf_s[:, h, t0:t0 + n], q[0:96, 0:n], reads=[q], writes=[mqf_s])
                ps = pp.next()
                for c in range(2):
                    P.mm(ps[0:64, 0:n], wkK[:, c, h, :], ckvn[:, c, 0:n], c == 0, c == 1, [wkK, ckvn], [ps])
                sq = sqR.next()
                P.act(sq[0:64, 0:n], ps[0:64, 0:n], AF.Square, [ps], [sq])
                for ti in range(nt):
                    tt = t0 // 128 + ti
                    P.mm(prk[:, tt, h:h + 1], sq[0:64, ti * 128:(ti + 1) * 128], ones_b[0:64, 0:1], True, True,
                         [sq, ones_b], [prk])
                q = qo.next()
                P.act(q[0:64, 0:n], ps[0:64, 0:n], AF.Identity, [ps], [q], scale=evc[0:64, 6:7])
                P.dma(mkf_s[0:64, h, t0:t0 + n], q[0:64, 0:n], reads=[q], writes=[mkf_s])
        rk3 = P.sbuf("rk3", [128, NT, 8], F32)
        rkr = P.sbuf("rkr", [128, NT], F32)
        P.act(rkr[:], prkr[:], AF.Copy, [prkr], [rkr])
        P.tt(rk3[:], prk[:], rkr[:].unsqueeze(2).broadcast_to([128, NT, 8]), ALU.add, [prk, rkr], [rk3])
        P.act(rsk[:], rk3[:].rearrange("p a b -> p (a b)"), AF.Ln, [rk3], [rsk], bias=EPS, scale=1.0 / 96)
        P.act(rsk[:], rsk[:], AF.Exp, [rsk], [rsk], scale=-0.5)
        P.ts(rsk[:], rsk[:], float(96 ** -0.5), ALU.mult, [rsk], [rsk])
        P.dma(mrk_s[:], rsk[:], reads=[rsk], writes=[mrk_s])
        P.emit()

    def mla_core():
        P = Prog(nc, sy, "mlc")
        rsk = P.sbuf("rsk", [128, NT, 8], F32)
        P.dma(rsk[:].rearrange("p a b -> p (a b)"), mrk_s[:], reads=[mrk_s], writes=[rsk])
        kf = Rot(lambda i: P.sbuf("kf%d" % i, [96, 2, T], BF16), 2)
        qf = Rot(lambda i: P.sbuf("qf%d" % i, [96, 2, T], BF16), 2)
        vv = Rot(lambda i: P.sbuf("vv%d" % i, [128, NT, 128], BF16), 2)
        pS = Rot(lambda i: P.psum("pS%d" % i, [128, 512]), 4)
        pO = Rot(lambda i: P.psum("pO%d" % i, [128, 512]), 2)
        pM = Rot(lambda i: P.psum("pM%d" % i, [128, 512]), 2)
        pT = Rot(lambda i: P.sbuf("pT%d" % i, [128, 512], BF16), 4)
        rden = P.sbuf("rden", [128, 512], F32)
        mo = Rot(lambda i: P.sbuf("mo%d" % i, [128, 512], BF16), 2)
        for j in range(4):
            k_, q_, v_ = kf.next(), qf.next(), vv.next()
            P.dma(k_[:], mkf_s[:, 2 * j:2 * j + 2, :], reads=[mkf_s], writes=[k_])
            P.dma(q_[:], mqf_s[:, 2 * j:2 * j + 2, :], reads=[mqf_s], writes=[q_])
            P.dma(v_[:], mv_s[:, j * 128:(j + 1) * 128].rearrange("(a p) f -> p a f", p=128), reads=[mv_s], writes=[v_])
            for (t0, n) in GROUPS5:
                kts = [0, 1] if t0 < CTX else list(range(NT))
                po, pm = pO.next(), pM.next()

                def scores(kt):
                    pts = []
                    for hh in range(2):
                        ps = pS.next()
                        P.mm(ps[:, 0:n], k_[:, hh, kt * 128:(kt + 1) * 128], q_[:, hh, t0:t0 + n], True, True, [k_, q_], [ps])
                        pt = pT.next()
                        P.act(pt[:, 0:n], ps[:, 0:n], AF.Exp, [ps, rsk], [pt], scale=rsk[:, kt, 2 * j + hh:2 * j + hh + 1])
                        pts.append(pt)
                    return pts

                nxt = scores(kts[0])
                for ki, kt in enumerate(kts):
                    first, last = ki == 0, ki == len(kts) - 1
                    pts = nxt
                    if not last:
                        nxt = scores(kts[ki + 1])
                    for hh in range(2):
                        sl = slice(hh * 64, (hh + 1) * 64)
                        P.mm(po[sl, 0:n], v_[:, kt, hh * 64:(hh + 1) * 64], pts[hh][:, 0:n], first, last, [v_, pts[hh]], [po])
                    for hh in range(2):
                        sl = slice(hh * 64, (hh + 1) * 64)
                        P.mm(pm[sl, 0:n], ones_b[:, 0:64], pts[hh][:, 0:n], first, last, [ones_b, pts[hh]], [pm])
                P.act(rden[:, 0:n], pm[:, 0:n], AF.Ln, [pm], [rden])
                P.act(rden[:, 0:n], rden[:, 0:n], AF.Exp, [rden], [rden], scale=-1.0)
                m = mo.next()
                P.tt(m[:, 0:n], po[:, 0:n], rden[:, 0:n], ALU.mult, [po, rden], [m])
                P.dma(mix_s[:, j, t0:t0 + n], m[:, 0:n], reads=[m], writes=[mix_s])
        P.emit()

    def gdn_prep():
        P = Prog(nc, sy, "gdp")
        wv_ = evw_d.rearrange("(kc p) n -> p kc n", p=128)
        xn = P.sbuf("xn", [128, 8, T], BF16)
        P.dma(xn[:], xn_s[:], reads=[xn_s], writes=[xn])
        cvw = P.sbuf("cvw", [128, 12, 5], F32)
        P.dma(cvw[:], gcv_d, writes=[cvw])
        grow = P.sbuf("grow", [128, 544], F32)
        P.dma(grow[:], grow_d, writes=[grow])
        st = P.sbuf("st", [128, 8, 512], F32)
        wch = Rot(lambda i: P.sbuf("wch%d" % i, [128, 8, 128], BF16), 2)
        wab = P.sbuf("wab", [128, 8, 32], BF16)
        wgt = P.sbuf("wgt", [128, 8, 512], BF16)
        load_w(P, wab, wab[:], wv_[:, :, 2208:2240], st, st[:, :, 0:32])
        load_w(P, wgt, wgt[:], wv_[:, :, 2240:2752], st, st[:])
        pp = Rot(lambda i: P.psum("pp%d" % i, [128, 512]), 3)
        pssR = Rot(lambda i: P.psum("pss%d" % i, [128, 512]), 2)
        ptr_ = Rot(lambda i: P.psum("ptr%d" % i, [128, 4, 128]), 2)
        zcR = Rot(lambda i: P.sbuf("zc%d" % i, [128, T], F32), 2)
        accR = Rot(lambda i: P.sbuf("acc%d" % i, [128, T], F32), 1)
        sqR = Rot(lambda i: P.sbuf("sq%d" % i, [128, 512], BF16), 2)
        lnvR = Rot(lambda i: P.sbuf("lnv%d" % i, [128, 512], F32), 2)
        tmst = P.sbuf("tmst", [128, NT, 128], F32)
        gall = P.sbuf("gall", [128, NT, 16], F32)
        ball = P.sbuf("ball", [128, NT, 16], F32)
        negA = P.sbuf("negA", [128, 16], F32)
        P.act(negA[:], grow[:, 0:16], AF.Exp, [grow], [negA])
        P.ts(negA[:], negA[:], -1.0, ALU.mult, [negA], [negA])
        tg = P.sbuf("tg", [128, 16], F32)
        gt = Rot(lambda i: P.sbuf("gt%d" % i, [128, 512], BF16), 2)
        for tt in range(NT):
            tsl = slice(tt * 128, (tt + 1) * 128)
            ps = pp.next()
            for kc in range(8):
                P.mm(ps[:, 0:32], xn[:, kc, tsl], wab[:, kc, :], kc == 0, kc == 7, [xn, wab], [ps])
            P.tt(tg[:], ps[:, 0:16], grow[:, 16:32], ALU.add, [ps, grow], [tg])
            P.act(tg[:], tg[:], AF.Exp, [tg], [tg])
            P.act(tg[:], tg[:], AF.Ln, [tg], [tg], bias=1.0)
            P.tt(gall[:, tt, :], tg[:], negA[:], ALU.mult, [tg, negA], [gall])
            P.act(ball[:, tt, :], ps[:, 16:32], AF.Exp, [ps], [ball], scale=-1.0)
            P.ts(ball[:, tt, :], ball[:, tt, :], 1.0, ALU.add, [ball], [ball])
            P.op("vector", lambda e, tt=tt: e.reciprocal(out=ball[:, tt, :], in_=ball[:, tt, :]), [ball], [ball])
            ps = pp.next()
            for kc in range(8):
                P.mm(ps[:], xn[:, kc, tsl], wgt[:, kc, :], kc == 0, kc == 7, [xn, wgt], [ps])
            g_ = gt.next()
            P.act(g_[:], ps[:], AF.Silu, [ps], [g_])
            P.dma(ggate_s[tsl, :], g_[:], reads=[g_], writes=[ggate_s])
        P.dma(gg_s[:], gall[:].rearrange("p a b -> p (a b)"), reads=[gall], writes=[gg_s])
        P.dma(gb_s[:], ball[:].rearrange("p a b -> p (a b)"), reads=[ball], writes=[gb_s])
        ZW = T + 8

        def zoff(t):
            return t + 2 if t < CTX else t + 6

        zpR = Rot(lambda i: P.sbuf("zp%d" % i, [128, ZW], BF16), 2)
        for z_ in zpR.t:
            P.memset(z_[:], 0.0, [z_])
        dgR = Rot(lambda i: P.sbuf("dgw%d" % i, [128, 5, 128], BF16), 2)
        for cc in range(12):
            zc, acc = zcR.next(), accR.next()
            zp, dgw = zpR.next(), dgR.next()
            w = wch.next()
            load_w(P, w, w[:], wv_[:, :, 672 + cc * 128:672 + (cc + 1) * 128], st, st[:, :, 0:128])
            for j in range(5):
                P.ts(dgw[:, j, :], cst[:, C_ID, :], cvw[:, cc, j:j + 1], ALU.mult, [cst, cvw], [dgw], eng="gpsimd")
            for (t0, n) in GROUPS5:
                ps = pp.next()
                for kc in range(8):
                    P.mm(ps[:, 0:n], w[:, kc, :], xn[:, kc, t0:t0 + n], kc == 0, kc == 7, [w, xn], [ps])
                P.act(zp[:, zoff(t0):zoff(t0) + n], ps[:, 0:n], AF.Copy, [ps], [zp])
            for (t0, n) in GROUPS5:
                ps = pp.next()
                for j in range(5):
                    o_ = zoff(t0) + j - 2
                    P.mm(ps[:, 0:n], dgw[:, j, :], zp[:, o_:o_ + n], j == 0, j == 4, [dgw, zp], [ps])
                P.act(zc[:, t0:t0 + n], ps[:, 0:n], AF.Silu, [ps], [zc])
            if cc < 8:
                for (t0, n) in GROUPS5:
                    sq, lnv = sqR.next(), lnvR.next()
                    P.act(sq[:, 0:n], zc[:, t0:t0 + n], AF.Square, [zc], [sq])
                    pss = pssR.next()
                    P.mm(pss[:, 0:n], cstb[:, C_BD64, :], sq[:, 0:n], True, True, [cstb, sq], [pss])
                    P.act(lnv[:, 0:n], pss[:, 0:n], AF.Ln, [pss], [lnv], bias=EPS)
                    P.act(lnv[:, 0:n], lnv[:, 0:n], AF.Exp, [lnv], [lnv], scale=-0.5)
                    if cc < 4:
                        P.stt(acc[:, t0:t0 + n], zc[:, t0:t0 + n], 0.125, lnv[:, 0:n], ALU.mult, ALU.mult, [zc, lnv], [acc])
                    else:
                        P.tt(acc[:, t0:t0 + n], zc[:, t0:t0 + n], lnv[:, 0:n], ALU.mult, [zc, lnv], [acc])
                dst = gq_s if cc < 4 else gk_s
                r0 = (cc % 4) * 128
                P.dma(dst[r0:r0 + 128, :], acc[:], reads=[acc], writes=[dst])
                src = acc
            else:
                src = zc
            if cc >= 4:
                for t4 in range(0, NT, 4):
                    nn = min(4, NT - t4)
                    pt = ptr_.next()
                    for i in range(nn):
                        tt = t4 + i
                        P.tr(pt[:, i, :], src[:, tt * 128:(tt + 1) * 128], ident, [src, cst], [pt])
                    P.copy(tmst[:, t4:t4 + nn, :], pt[:, 0:nn, :], [pt], [tmst])
                dtm = gktm_s if cc < 8 else gvtm_s
                c0 = (cc % 4) * 128
                P.dma(dtm[:, c0:c0 + 128].rearrange("(a p) f -> p a f", p=128), tmst[:], reads=[tmst], writes=[dtm])
        P.emit()

    def gdn_core():
        P = Prog(nc, sy, "gdsp")
        P.dma(xt_s[:], XT[:, 4:8, :].rearrange("p c t -> p (c t)"), reads=[XT], writes=[xt_s])
        P.emit()
        P = Prog(nc, sy, "gdc")
        XTf = XT[:, 4:8, :].rearrange("p c t -> p (c t)")
        carve_off = [0]

        def carve(name, dtype):
            words = 512 if dtype == F32 else 256
            a = XTf[:, carve_off[0]:carve_off[0] + words]
            carve_off[0] += words
            if dtype != F32:
                a = a.bitcast(dtype)
            return Buf(a.rearrange("p (h i) -> p h i", h=4), 1, name)

        lvm = P.sbuf("lvm", [128, 7, 128], F32)
        nlv = P.sbuf("nlv", [128, 14, 128], BF16)
        for hf in range(2):
            P.dma(lvm[:], lvm_d[:, 7 * hf:7 * hf + 7, :], writes=[lvm])
            P.ts(nlv[:, 7 * hf:7 * hf + 7, :], lvm[:], -1.0, ALU.mult, [lvm], [nlv])
        idb = P.sbuf("idb", [128, 128], BF16)
        P.copy(idb[:], cst[:, C_ID, :], [cst], [idb])
        gall = P.sbuf("gall", [128, NT, 16], F32)
        ball = P.sbuf("ball", [128, NT, 16], F32)
        P.dma(gall[:].rearrange("p a b -> p (a b)"), gg_s[:], reads=[gg_s], writes=[gall])
        P.dma(ball[:].rearrange("p a b -> p (a b)"), gb_s[:], reads=[gb_s], writes=[ball])
        PS = Rot(lambda i: P.psum("ps%d" % i, [128, 512]), 6)
        PB = Rot(lambda i: P.psum("pb%d" % i, [128, 4, 128], BF16), 2)

        def b4(ap):
            return ap.unsqueeze(1).broadcast_to([128, 4, 128])

        def bl(ap, w):
            return ap.unsqueeze(2).broadcast_to([128, ap.shape[1], w])

        Ibc = b4(idb[:])
        D_ = []
        for d in range(2):
            n_ = lambda s, d=d: "%s_%d" % (s, d)
            o = K()
            o.S = P.sbuf(n_("S"), [128, 8, 64], F32)
            o.q = P.sbuf(n_("q"), [64, 8, 128], F32)
            o.k = P.sbuf(n_("k"), [64, 8, 128], F32)
            o.kt = P.sbuf(n_("kt"), [128, 8, 64], F32)
            o.vt = P.sbuf(n_("vt"), [128, 8, 64], F32)
            o.eg = P.sbuf(n_("eg"), [128, 24], F32)
            o.cfk = P.sbuf(n_("cfk"), [128, 8], F32)
            o.qkd = P.sbuf(n_("qkd"), [128, 8, 128], BF16)
            o.U = P.sbuf(n_("U"), [128, 8, 64], F32)
            o.wT = P.sbuf(n_("wT"), [64, 8, 128], BF16)
            o.dg = P.sbuf(n_("dg"), [128, 4, 128], F32)
            o.qdT = P.sbuf(n_("qdT"), [128, 8, 128], BF16)
            o.kdec = P.sbuf(n_("kdec"), [128, 8, 64], BF16)
            o.vnew = P.sbuf(n_("vnew"), [128, 8, 64], BF16)
            o.Sb = P.sbuf(n_("Sb"), [128, 8, 64], BF16)
            o.ost = P.sbuf(n_("ost"), [128, 512], F32)
            P.memset(o.qdT[:], 0.0, [o.qdT])
            P.memset(o.S[:], 0.0, [o.S])
            P.memset(o.Sb[:], 0.0, [o.Sb])
            o.order = list(range(NT)) if d == 0 else [1, 0] + list(range(NT - 1, 1, -1))
            o.Uinc = cst[:, C_UI, :] if d == 0 else cst[:, C_LI, :]
            o.Bst = cst[:, C_SL, :] if d == 0 else cst[:, C_SU, :]
            o.mX = (lambda lv: nlv[:, lv, :]) if d == 0 else (lambda lv: nlv[:, 7 + lv, :])
            o.mXT = (lambda lv: nlv[:, 7 + lv, :]) if d == 0 else (lambda lv: nlv[:, lv, :])
            o.d = d
            D_.append(o)
        CH = []
        for o in D_:
            for hg in range(2):
                c = K()
                c.o, c.hg, c.h0, c.hsl = o, hg, 4 * hg, slice(4 * hg, 4 * hg + 4)
                nm = lambda s, o=o, hg=hg: "%s_%d%d" % (s, o.d, hg)
                if hg == 0:
                    mk = lambda s, dt_: P.sbuf(nm(s), [128, 4, 128], dt_)
                else:
                    mk = lambda s, dt_: carve(nm(s), dt_)
                c.GU, c.eD, c.eDT = mk("GU", F32), mk("eD", F32), mk("eDT", F32)
                c.A, c.AT, c.Ym, c.Ymt = mk("A", BF16), mk("AT", BF16), mk("Ym", BF16), mk("Ymt", BF16)
                c.X = [mk("X%d" % i, BF16) for i in range(2)]
                c.XT = [mk("XT%d" % i, BF16) for i in range(2)]
                c.Rv = P.sbuf(nm("Rv"), [128, 4, 64], BF16)
                c.Rk = P.sbuf(nm("Rk"), [128, 4, 64], BF16)
                CH.append(c)

        def v4(ps):
            return ps[:].rearrange("p (h i) -> p h i", h=4)

        for s in range(NT):
            for o in D_:
                d = o.d
                tt = o.order[s]
                o.tt = tt
                tsl = slice(tt * 128, (tt + 1) * 128)
                P.dma(o.q[:], gq_s[:, tsl].rearrange("(h d) t -> d h t", d=64), reads=[gq_s], writes=[o.q])
                P.dma(o.k[:], gk_s[:, tsl].rearrange("(h d) t -> d h t", d=64), reads=[gk_s], writes=[o.k])
                P.dma(o.kt[:].rearrange("p h d -> p (h d)"), gktm_s[tsl, :], reads=[gktm_s], writes=[o.kt])
                P.dma(o.vt[:].rearrange("p h d -> p (h d)"), gvtm_s[tsl, :], reads=[gvtm_s], writes=[o.vt])
                gd = gall[:, tt, d * 8:(d + 1) * 8]
                pg = PS.next()
                P.mm(pg[:, 0:8], o.Uinc, gd, True, True, [cst, gall], [pg])
                P.mm(pg[:, 8:16], o.Bst, gd, True, True, [cst, gall], [pg])
                P.mm(pg[:, 16:24], ones_f[:], gd, True, True, [ones_f, gall], [pg])
                P.act(o.eg[:], pg[:, 0:24], AF.Exp, [pg], [o.eg])
                P.tt(o.cfk[:], ball[:, tt, d * 8:(d + 1) * 8], o.eg[:, 0:8], ALU.mult, [ball, o.eg], [o.cfk])
            for c in CH:
                o, h0 = c.o, c.h0
                d, tt = o.d, o.tt
                P.tt(c.GU[:], b4(o.Uinc), bl(gall[:, tt, d * 8 + h0:d * 8 + h0 + 4], 128), ALU.mult, [cst, gall], [c.GU],
                     eng="gpsimd")
            for c in CH:
                o = c.o
                c.pD, c.pDT = PS.next(), PS.next()
                for h in range(4):
                    P.mm(v4(c.pD)[:, h, :], c.GU[:, h, :], o.Bst, True, True, [c.GU, cst], [c.pD])
                P.mm(c.pDT[:], o.Bst, c.GU[:].rearrange("p h i -> p (h i)"), True, True, [c.GU, cst], [c.pDT])
                P.act(c.eD[:], v4(c.pD), AF.Exp, [c.pD], [c.eD])
                P.act(c.eDT[:], v4(c.pDT), AF.Exp, [c.pDT], [c.eDT])
            for c in CH:
                o, h0 = c.o, c.h0
                d, tt = o.d, o.tt
                pKK = PS.next()
                for h in range(4):
                    P.mm(v4(pKK)[:, h, :], o.k[:, h0 + h, :], o.k[:, h0 + h, :], True, True, [o.k], [pKK])
                P.tt(c.eD[:], v4(pKK), c.eD[:], ALU.mult, [pKK, c.eD], [c.eD])
                P.tt(c.eD[:], c.eD[:], b4(o.Bst), ALU.mult, [c.eD, cst], [c.eD], eng="gpsimd")
                P.tt(c.A[:], c.eD[:], bl(ball[:, tt, d * 8 + h0:d * 8 + h0 + 4], 128), ALU.mult, [c.eD, ball], [c.A],
                     eng="gpsimd")
            for c in CH:
                o, h0 = c.o, c.h0
                pQK = PS.next()
                for h in range(4):
                    P.mm(v4(pQK)[:, h, :], o.k[:, h0 + h, :], o.q[:, h0 + h, :], True, True, [o.k, o.q], [pQK])
                P.tt(c.eDT[:], v4(pQK), c.eDT[:], ALU.mult, [pQK, c.eDT], [c.eDT])
                P.tt(o.qkd[:, c.hsl, :], c.eDT[:], b4(o.Uinc), ALU.mult, [c.eDT, cst], [o.qkd], eng="gpsimd")
            for c in CH:
                pT_ = PB.next()
                for h in range(4):
                    P.tr(pT_[:, h, :], c.A[:, h, :], idb[:], [c.A, idb], [pT_])
                P.act(c.AT[:], pT_[:], AF.Copy, [pT_], [c.AT])
            for c in CH:
                o = c.o
                P.tt(c.Ym[:], c.A[:], b4(o.mX(0)), ALU.mult, [c.A, nlv], [c.Ym], eng="gpsimd")
                P.tt(c.X[0][:], c.Ym[:], Ibc, ALU.add, [c.Ym, idb], [c.X[0]], eng="gpsimd")
                P.tt(c.Ymt[:], c.AT[:], b4(o.mXT(0)), ALU.mult, [c.AT, nlv], [c.Ymt])
                P.tt(c.XT[0][:], c.Ymt[:], Ibc, ALU.add, [c.Ymt, idb], [c.XT[0]])
            cur = 0
            for lv in range(1, 7):
                last = lv == 6
                for c in CH:
                    o = c.o
                    if not last:
                        c.pY = PS.next()
                        for h in range(4):
                            P.mm(v4(c.pY)[:, h, :], c.AT[:, h, :], c.X[cur][:, h, :], True, True, [c.AT, c.X[cur]], [c.pY])
                        P.tt(c.Ym[:], v4(c.pY), b4(o.mX(lv)), ALU.mult, [c.pY, nlv], [c.Ym])
                    c.pY2 = PS.next()
                    for h in range(4):
                        P.mm(v4(c.pY2)[:, h, :], c.A[:, h, :], c.XT[cur][:, h, :], True, True, [c.A, c.XT[cur]], [c.pY2])
                    P.tt(c.Ymt[:], v4(c.pY2), b4(o.mXT(lv)), ALU.mult, [c.pY2, nlv], [c.Ymt])
                for c in CH:
                    if not last:
                        pZ = PS.next()
                        P.mm(pZ[:], idb[:], c.X[cur][:].rearrange("p h i -> p (h i)"), True, False, [idb, c.X[cur]], [pZ])
                        for h in range(4):
                            P.mm(v4(pZ)[:, h, :], c.XT[cur][:, h, :], c.Ym[:, h, :], False, h == 3, [c.XT[cur], c.Ym], [pZ])
                        P.act(c.X[1 - cur][:], v4(pZ), AF.Copy, [pZ], [c.X[1 - cur]])
                    pZ = PS.next()
                    P.mm(pZ[:], idb[:], c.XT[cur][:].rearrange("p h i -> p (h i)"), True, False, [idb, c.XT[cur]], [pZ])
                    for h in range(4):
                        P.mm(v4(pZ)[:, h, :], c.X[cur][:, h, :], c.Ymt[:, h, :], False, h == 3, [c.X[cur], c.Ymt], [pZ])
                    P.act(c.XT[1 - cur][:], v4(pZ), AF.Copy, [pZ], [c.XT[1 - cur]])
                cur = 1 - cur
            for c in CH:
                o, h0, hsl = c.o, c.h0, c.hsl
                d, tt = o.d, o.tt
                XTf_ = c.XT[cur]
                P.tt(c.Rv[:], o.vt[:, hsl, :], bl(ball[:, tt, d * 8 + h0:d * 8 + h0 + 4], 64), ALU.mult, [o.vt, ball], [c.Rv],
                     eng="gpsimd")
                P.tt(c.Rk[:], o.kt[:, hsl, :], bl(o.cfk[:, hsl], 64), ALU.mult, [o.kt, o.cfk], [c.Rk], eng="gpsimd")
                pU, pW = PS.next(), PS.next()
                pU3 = pU[:, 0:256].rearrange("p (h i) -> p h i", h=4)
                pW3 = pW[0:64, :].rearrange("p (h i) -> p h i", h=4)
                for h in range(4):
                    P.mm(pU3[:, h, :], XTf_[:, h, :], c.Rv[:, h, :], True, True, [XTf_, c.Rv], [pU])
                for h in range(4):
                    P.mm(pW3[:, h, :], c.Rk[:, h, :], XTf_[:, h, :], True, True, [XTf_, c.Rk], [pW])
                P.act(o.U[:, hsl, :], pU3, AF.Copy, [pU], [o.U])
                P.act(o.wT[:, hsl, :], pW3, AF.Copy, [pW], [o.wT])
            for o in D_:
                for half in range(2):
                    P.tt(o.dg[:], b4(cst[:, C_ID, :]), bl(o.eg[:, 4 * half:4 * half + 4], 128), ALU.mult, [cst, o.eg], [o.dg],
                         eng="gpsimd")
                    pE = PS.next()
                    P.mm(pE[0:64, :], ones_f[:, 0:64], o.dg[:].rearrange("p h i -> p (h i)"), True, True, [ones_f, o.dg], [pE])
                    P.tt(o.qdT[0:64, 4 * half:4 * half + 4, :], o.q[:, 4 * half:4 * half + 4, :],
                         pE[0:64, :].rearrange("p (h i) -> p h i", h=4), ALU.mult, [o.q, pE], [o.qdT])
                P.tt(o.kdec[:], o.kt[:], bl(o.eg[:, 8:16], 64), ALU.mult, [o.kt, o.eg], [o.kdec], eng="gpsimd")
            for o in D_:
                pWS = PS.next()
                pWS3 = pWS[:].rearrange("p (h e) -> p h e", h=8)
                for h in range(8):
                    P.mm(pWS3[:, h, :], o.wT[:, h, :], o.Sb[0:64, h, :], True, True, [o.wT, o.Sb], [pWS])
                P.tt(o.vnew[:], o.U[:], pWS3, ALU.subtract, [o.U, pWS], [o.vnew])
            for o in D_:
                tt = o.tt
                tsl = slice(tt * 128, (tt + 1) * 128)
                pOO = PS.next()
                pOO3 = pOO[:].rearrange("p (h e) -> p h e", h=8)
                for h in range(8):
                    P.mm(pOO3[:, h, :], o.qdT[:, h, :], o.Sb[:, h, :], True, False, [o.qdT, o.Sb], [pOO])
                    P.mm(pOO3[:, h, :], o.qkd[:, h, :], o.vnew[:, h, :], False, True, [o.qkd, o.vnew], [pOO])
                P.act(o.ost[:], pOO[:], AF.Copy, [pOO], [o.ost])
                P.dma(og_s[o.d][tsl, :], o.ost[:], reads=[o.ost], writes=[og_s[o.d]])
                pSn = PS.next()
                pSn3 = pSn[0:64, :].rearrange("p (h e) -> p h e", h=8)
                for h in range(8):
                    P.mm(pSn3[:, h, :], o.kdec[:, h, :], o.vnew[:, h, :], True, True, [o.kdec, o.vnew], [pSn])
                P.tt(o.S[0:64], o.S[0:64], o.eg[0:64, 16:24].unsqueeze(2).broadcast_to([64, 8, 64]), ALU.mult, [o.S, o.eg], [o.S])
                P.tt(o.S[0:64], o.S[0:64], pSn3, ALU.add, [o.S, pSn], [o.S])
                P.act(o.Sb[0:64], o.S[0:64], AF.Copy, [o.S], [o.Sb])
        P.emit()
        P = Prog(nc, sy, "gdf")
        grow = P.sbuf("grow", [128, 544], F32)
        P.dma(grow[:], grow_d, writes=[grow])
        sqoR = Rot(lambda i: P.sbuf("sqo%d" % i, [128, 8, 64], F32), 4)
        ssR = Rot(lambda i: P.sbuf("ss%d" % i, [128, 8], F32), 4)
        toR = Rot(lambda i: P.sbuf("to%d" % i, [128, 8, 64], F32), 4)
        tobR = Rot(lambda i: P.sbuf("tob%d" % i, [128, 512], BF16), 4)
        gtt = Rot(lambda i: P.sbuf("gtt%d" % i, [128, 512], BF16), 4)
        oa = Rot(lambda i: P.sbuf("oa%d" % i, [128, 512], F32), 4)
        ob_ = Rot(lambda i: P.sbuf("ob%d" % i, [128, 512], F32), 4)
        idb = P.sbuf("idb", [128, 128], BF16)
        P.copy(idb[:], cst[:, C_ID, :], [cst], [idb])
        mixo = Rot(lambda i: P.sbuf("mixo%d" % i, [128, 4, 128], BF16), 4)
        ptbR = Rot(lambda i: P.psum("ptb%d" % i, [128, 4, 128], BF16), 4)
        for tt in range(NT):
            sqo, ss, to, tob, ptb = sqoR.next(), ssR.next(), toR.next(), tobR.next(), ptbR.next()
            tsl = slice(tt * 128, (tt + 1) * 128)
            g_, a_, b_ = gtt.next(), oa.next(), ob_.next()
            P.dma(g_[:], ggate_s[tsl, :], reads=[ggate_s], writes=[g_])
            P.dma(a_[:], og_s[0][tsl, :], reads=[og_s[0]], writes=[a_])
            P.dma(b_[:], og_s[1][tsl, :], reads=[og_s[1]], writes=[b_])
            P.tt(a_[:], a_[:], b_[:], ALU.add, [a_, b_], [a_], eng="gpsimd")
            o3 = a_[:].rearrange("p (h e) -> p h e", h=8)
            P.act(sqo[:], o3, AF.Square, [a_], [sqo])
            P.op("vector", lambda e, ss=ss, sqo=sqo: e.tensor_reduce(out=ss[:], in_=sqo[:], axis=AX.X, op=ALU.add), [sqo], [ss])
            P.act(ss[:], ss[:], AF.Ln, [ss], [ss], bias=EPS, scale=1.0 / 64)
            P.act(ss[:], ss[:], AF.Exp, [ss], [ss], scale=-0.5)
            P.tt(to[:], o3, ss[:].unsqueeze(2).broadcast_to([128, 8, 64]), ALU.mult, [a_, ss], [to])
            P.tt(to[:].rearrange("p h e -> p (h e)"), to[:].rearrange("p h e -> p (h e)"), grow[:, 32:544], ALU.mult,
                 [to, grow], [to], eng="gpsimd")
            P.tt(tob[:], to[:].rearrange("p h e -> p (h e)"), g_[:], ALU.mult, [to, g_], [tob])
            for c in range(4):
                P.tr(ptb[:, c, :], tob[:, c * 128:(c + 1) * 128], idb[:], [tob, idb], [ptb])
            mo_ = mixo.next()
            P.act(mo_[:], ptb[:], AF.Copy, [ptb], [mo_])
            P.dma(mix_s[:, 4:8, tsl], mo_[:], reads=[mo_], writes=[mix_s])
        P.dma(XT[:, 4:8, :].rearrange("p c t -> p (c t)"), xt_s[:], reads=[xt_s], writes=[XT])
        P.emit()


    for l in layers:
        if stage >= 1 and not skip_ffn1:
            ffn(l, 0, FULL)
        dump_xt("xt_ffn1_%d" % l)
        if stage >= 2:
            make_xn(l)
            if l == 0:
                if parts is None or "mlp" in parts:
                    mla_prep()
                if parts is None or "mlc" in parts:
                    mla_core()
                if parts is None or "gdp" in parts:
                    gdn_prep()
                if parts is None or "gdc" in parts:
                    gdn_core()
                if parts is None or "op" in parts:
                    out_proj(l, evwo_d, GROUPS5)
            if l == 1:
                if parts is None or "swp" in parts:
                    swa_prep()
                if parts is None or "swc" in parts:
                    swa_core()
                if parts is None or "glp" in parts:
                    gla_prep()
                if parts is None or "glc" in parts:
                    gla_core()
                if parts is None or "op" in parts:
                    out_proj(l, odwo_d, LATG)
        if "mix" in dbg_d:
            P = Prog(nc, sy, "dbgmix%d" % l)
            mb_ = Rot(lambda i: P.sbuf("mb%d" % i, [128, 8, 128], BF16), 2)
            mf_ = Rot(lambda i: P.sbuf("mf%d" % i, [128, 8, 128], F32), 2)
            dv = dbg_d["mix"].rearrange("p (c t) -> p c t", c=8)
            for tt in range(NT):
                a_, f_ = mb_.next(), mf_.next()
                P.dma(a_[:], mix_s[:, :, tt * 128:(tt + 1) * 128], reads=[mix_s], writes=[a_])
                P.act(f_[:], a_[:], AF.Copy, [a_], [f_])
                P.dma(dv[:, :, tt * 128:(tt + 1) * 128], f_[:], reads=[f_])
            P.emit()
        dump_xt("xt_mix_%d" % l)
        if stage >= 3:
            ffn(l, 1, FULL if l == 0 else LAT)

    P = Prog(nc, sy, "pout")
    ost = [P.sbuf("ost%d" % i, [128, D], F32) for i in range(2)]
    pot = [P.psum("pot%d" % i, [128, 4, 128]) for i in range(2)]
    for tt in range(2, NT):
        o = ost[tt % 2]
        for half in range(2):
            ps = pot[half]
            for c4 in range(4):
                c = half * 4 + c4
                P.tr(ps[:, c4, :], XT[:, c, tt * 128:(tt + 1) * 128], ident, [(XT, tt), cst], [ps])
            if half == 0:
                P.copy(o[:, 0:512], ps[:].rearrange("p a b -> p (a b)"), [ps], [o])
            else:
                P.act(o[:, 512:1024], ps[:].rearrange("p a b -> p (a b)"), AF.Copy, [ps], [o])
        P.dma(out_d[(tt - 2) * 128:(tt - 1) * 128, :], o[:], reads=[o])
    P.emit()
    es.close()
    return nc


def rope_tables(rot_dim):
    f = np.float32
    t = np.arange(SEQ)
    row = (t // 64).astype(f)
    col = (t % 64).astype(f)
    nf = rot_dim // 4
    inv = (f(10000.0) ** (-np.arange(nf, dtype=f) / f(nf))).astype(f)
    ang = np.concatenate([row[:, None] * inv, col[:, None] * inv], axis=-1).astype(f)
    return np.cos(ang).astype(f), np.sin(ang).astype(f)


def host_consts():
    f = np.float32
    idx = np.arange(128)
    ident = np.eye(128)
    bd64 = (idx[:, None] // 64 == idx[None, :] // 64)
    Rm = np.zeros((64, 64))
    for i in range(32):
        Rm[i, i + 32] = -1.0
        Rm[i + 32, i] = 1.0
    rot = np.zeros((128, 128))
    rot[0:64, 0:64] = Rm.T
    rot[64:128, 64:128] = Rm.T
    UI = idx[:, None] <= idx[None, :]
    LI = idx[:, None] >= idx[None, :]
    SL = idx[:, None] > idx[None, :]
    SU = idx[:, None] < idx[None, :]
    rot96 = np.zeros((128, 128))
    for i in range(16):
        rot96[80 + i, 64 + i] = -1.0
        rot96[64 + i, 80 + i] = 1.0
    cst = np.stack([ident, bd64, rot, UI, LI, SL, SU, rot96], axis=1).astype(f)
    lv = []
    for k in range(7):
        bsz = 1 << k
        same = (idx[:, None] // (2 * bsz)) == (idx[None, :] // (2 * bsz))
        m = same & ((idx[:, None] % (2 * bsz)) >= bsz) & ((idx[None, :] % (2 * bsz)) < bsz)
        lv.append(m)
    lvm = np.stack(lv + [m.T for m in lv], axis=1).astype(f)
    c8, s8 = rope_tables(32)
    mcs = np.zeros((96, 2, SEQ), f)
    mcs[0:64, 0, :] = 1.0
    for dd in range(32):
        mcs[64 + dd, 0, :] = c8[:, dd % 16]
        mcs[64 + dd, 1, :] = s8[:, dd % 16]
    bdsel = np.stack([(idx // 64 == 0), (idx // 64 == 1)], axis=1).astype(f)
    c, s = rope_tables(64)
    d = idx % 64
    swcs = np.stack([c[:, d % 32].T, s[:, d % 32].T], axis=1).astype(f)
    return {"cst": np.ascontiguousarray(cst), "bdsel": np.ascontiguousarray(bdsel), "swa_cs": np.ascontiguousarray(swcs),
            "lvm": np.ascontiguousarray(lvm), "mla_cs": np.ascontiguousarray(mcs)}


_CONSTS = None


def host_inputs(inputs, b):
    global _CONSTS
    f = np.float32
    if _CONSTS is None:
        _CONSTS = host_consts()
    c = np.asarray(inputs["c"][b], f)
    cc = np.asarray(inputs["c_ctx"], f)
    ccol = np.concatenate([c.reshape(8, 128).T, cc.reshape(8, 128).T], axis=1)
    idx = np.arange(128)
    odc = np.zeros((128, 11), f)
    odc[:, 0] = np.asarray(inputs["od_swa_q_norm"], f)[0][idx % 64]
    odc[:, 1] = np.asarray(inputs["od_swa_k_norm"], f)[0][idx % 64]
    odc[:, 2] = np.asarray(inputs["od_gla_out_norm"], f)[0]
    odc[:, 3:11] = np.asarray(inputs["od_swa_sink"], f)[0][None, :]
    w2 = np.asarray(inputs["od_gla_gate_w2"], f)[0]
    gb = np.asarray(inputs["od_gla_gate_b"], f)[0]
    gw2b = np.concatenate([w2, gb[:, None, :]], axis=1).transpose(1, 0, 2)
    evc = np.zeros((128, 8), f)
    evc[:, 0:3] = np.asarray(inputs["ev_q_a_norm"], f)[0].reshape(3, 128).T
    evc[:, 3:5] = np.asarray(inputs["ev_kv_a_norm"], f)[0].reshape(2, 128).T
    evc[0:96, 5] = np.asarray(inputs["ev_mla_q_norm"], f)[0]
    evc[0:96, 6] = np.asarray(inputs["ev_mla_k_norm"], f)[0]
    gcv = np.asarray(inputs["ev_gdn_conv"], f)[0].reshape(5, 12, 128).transpose(2, 1, 0)
    grow = np.zeros((128, 544), f)
    grow[:, 0:16] = np.asarray(inputs["ev_gdn_a_log"], f)[0].reshape(16)[None, :]
    grow[:, 16:32] = np.asarray(inputs["ev_gdn_dt_bias"], f)[0].reshape(16)[None, :]
    grow[:, 32:544] = np.tile(np.asarray(inputs["ev_gdn_out_norm"], f)[0], 8)[None, :]
    m = {
        "ev_w_in": np.ascontiguousarray(np.asarray(inputs["ev_w_in"], f)[0]),
        "ev_w_out": np.ascontiguousarray(np.asarray(inputs["ev_w_out"], f)[0]),
        "ev_w_q_up": np.ascontiguousarray(np.asarray(inputs["ev_w_q_up"], f)[0]),
        "ev_w_kv_up": np.ascontiguousarray(np.asarray(inputs["ev_w_kv_up"], f)[0]),
        "ev_cols": evc,
        "gdn_conv_col": np.ascontiguousarray(gcv),
        "gdn_rows": grow,
        "x": np.ascontiguousarray(inputs["x"][b], dtype=f),
        "ctx": np.ascontiguousarray(inputs["ctx"][b], dtype=f),
        "ccol": np.ascontiguousarray(ccol),
        "ada_w": np.ascontiguousarray(inputs["ada_w"], dtype=f),
        "ada_b_col": np.ascontiguousarray(np.asarray(inputs["ada_b"], f).reshape(2, 72, 128).transpose(2, 0, 1)),
        "norm_g_col": np.ascontiguousarray(np.asarray(inputs["norm_g"], f).reshape(2, 3, 8, 128).transpose(3, 0, 1, 2)),
        "ffn_w_gate": np.ascontiguousarray(inputs["ffn_w_gate"], dtype=f),
        "ffn_w_up": np.ascontiguousarray(inputs["ffn_w_up"], dtype=f),
        "ffn_w_down": np.ascontiguousarray(inputs["ffn_w_down"], dtype=f),
        "od_w_in": np.ascontiguousarray(np.asarray(inputs["od_w_in"], f)[0]),
        "od_w_out": np.ascontiguousarray(np.asarray(inputs["od_w_out"], f)[0]),
        "od_cols": odc,
        "gla_w2b": np.ascontiguousarray(gw2b),
    }
    m.update(_CONSTS)
    return m


def kernel(**inputs):
    nc = build()
    in_maps = [host_inputs(inputs, b) for b in range(NCORES)]
    res = run_bass_kernel_spmd(nc, in_maps, core_ids=list(range(NCORES)))
    return np.stack([np.asarray(r["out"], np.float32) for r in res.results], axis=0)
```

```python
from contextlib import ExitStack
import numpy as np
import concourse.bass as bass
import concourse.mybir as mybir
from concourse.bass_utils import run_bass_kernel_spmd

F32 = mybir.dt.float32
BF16 = mybir.dt.bfloat16
AF = mybir.ActivationFunctionType
ALU = mybir.AluOpType
AX = mybir.AxisListType

D = 1024
DFF = 2816
NF = DFF // 128
SEQ = 2048
CTX = 256
T = SEQ + CTX
NT = T // 128
EPS = 1e-6
NCORES = 8

COMPUTE = ("tensor", "vector", "scalar", "gpsimd")
ENGINES = ("tensor", "vector", "scalar", "gpsimd", "sync")
N_DMA_SEMS = 24


class Buf:
    _n = 0

    def __init__(self, t, nslots=1, name=None):
        self.t = t
        self.nslots = nslots
        Buf._n += 1
        self.id = Buf._n
        self.name = name

    def __getitem__(self, idx):
        return self.t[idx]


class Sync:
    def __init__(self, nc, es):
        self.nc = nc
        self.sems = {e: es.enter_context(nc.semaphore("s_" + e)) for e in COMPUTE}
        self.dsems = [es.enter_context(nc.semaphore("d_%d" % i)) for i in range(N_DMA_SEMS)]
        self.pos = {e: 0 for e in COMPUTE}
        self.n_dma = 0
        self.dma_uses = [0] * N_DMA_SEMS


class Prog:
    def __init__(self, nc, sync, name="ph"):
        self.nc = nc
        self.sy = sync
        self.name = name
        self.es = ExitStack()
        self.ops = []
        self.state = {}
        self.dma_last = [None] * N_DMA_SEMS
        self.start_pos = dict(sync.pos)
        self.start_dma = [16 * u for u in sync.dma_uses]

    def sbuf(self, name, shape, dtype, nslots=1):
        t = self.es.enter_context(self.nc.sbuf_tensor(self.name + "_" + name, list(shape), dtype))
        return Buf(t, nslots, name)

    def psum(self, name, shape, dtype=F32, nslots=1):
        t = self.es.enter_context(self.nc.psum_tensor(self.name + "_" + name, list(shape), dtype))
        return Buf(t, nslots, name)

    def _expand(self, keys):
        out = []
        for k in keys:
            if isinstance(k, Buf):
                k = (k, None)
            b, s = k
            if s is None:
                out.extend((b.id, i) for i in range(b.nslots))
            elif isinstance(s, (list, tuple, range)):
                for i in s:
                    assert 0 <= i < b.nslots, (b.name, i, b.nslots)
                    out.append((b.id, i))
            else:
                assert 0 <= s < b.nslots, (b.name, s, b.nslots)
                out.append((b.id, s))
        return out

    def _deps(self, idx, reads, writes):
        deps = set()
        rk = self._expand(reads)
        wk = self._expand(writes)
        for k in rk:
            st = self.state.setdefault(k, [None, []])
            if st[0] is not None:
                deps.add(st[0])
        for k in wk:
            st = self.state.setdefault(k, [None, []])
            if st[0] is not None:
                deps.add(st[0])
            deps.update(st[1])
        for k in rk:
            self.state[k][1].append(idx)
        for k in wk:
            self.state[k] = [idx, []]
        deps.discard(idx)
        return deps

    def op(self, eng, fn, reads=(), writes=()):
        idx = len(self.ops)
        deps = self._deps(idx, reads, writes)
        self.ops.append(dict(eng=eng, fn=fn, deps=deps, dma=False, pos=self.sy.pos[eng]))
        self.sy.pos[eng] += 1
        return idx

    def dma(self, out, in_, reads=(), writes=(), q="sync"):
        idx = len(self.ops)
        deps = self._deps(idx, reads, writes)
        sy = self.sy
        s = sy.n_dma % N_DMA_SEMS
        sy.n_dma += 1
        prev = self.dma_last[s]
        sy.dma_uses[s] += 1
        self.dma_last[s] = idx
        self.ops.append(dict(eng=q, fn=lambda e: e.dma_start(out=out, in_=in_), deps=deps, dma=True, sem=s,
                             target=16 * sy.dma_uses[s], prev=prev))
        return idx

    def mm(self, out, lhsT, rhs, start, stop, r, w):
        return self.op("tensor", lambda e: e.matmul(out, lhsT=lhsT, rhs=rhs, start=start, stop=stop), r, w)

    def tr(self, out, in_, ident, r, w):
        return self.op("tensor", lambda e: e.transpose(out, in_, ident), r, w)

    def act(self, out, in_, func, r, w, bias=None, scale=None, accum_out=None):
        kw = {}
        if bias is not None:
            kw["bias"] = bias
        if scale is not None:
            kw["scale"] = scale
        if accum_out is not None:
            kw["accum_out"] = accum_out
        return self.op("scalar", lambda e: e.activation(out=out, in_=in_, func=func, **kw), r, w)

    def tt(self, out, in0, in1, op, r, w, eng="vector"):
        return self.op(eng, lambda e: e.tensor_tensor(out=out, in0=in0, in1=in1, op=op), r, w)

    def ts(self, out, in0, s1, op0, r, w, s2=None, op1=None, eng="vector"):
        if op1 is None:
            return self.op(eng, lambda e: e.tensor_scalar(out=out, in0=in0, scalar1=s1, scalar2=None, op0=op0), r, w)
        return self.op(eng, lambda e: e.tensor_scalar(out=out, in0=in0, scalar1=s1, scalar2=s2, op0=op0, op1=op1), r, w)

    def stt(self, out, in0, scalar, in1, op0, op1, r, w):
        return self.op("vector", lambda e: e.scalar_tensor_tensor(out=out, in0=in0, scalar=scalar, in1=in1,
                                                                  op0=op0, op1=op1), r, w)

    def copy(self, out, in_, r, w, eng="vector"):
        return self.op(eng, lambda e: e.tensor_copy(out=out, in_=in_), r, w)

    def memset(self, ap, val, w, eng="vector"):
        return self.op(eng, lambda e: e.memset(ap, val), (), w)

    def emit(self):
        nc = self.nc
        sy = self.sy
        ops = self.ops
        per_eng = {e: [] for e in ENGINES}
        for i, o in enumerate(ops):
            per_eng[o["eng"]].append(i)
        end_pos = dict(sy.pos)
        end_dma = [16 * u for u in sy.dma_uses]
        start_pos = self.start_pos
        start_dma = self.start_dma

        def replay(ename):
            def body(eng):
                waited = dict(start_pos)
                dwaited = list(start_dma)
                for i in per_eng[ename]:
                    o = ops[i]
                    need = {}
                    dneed = {}
                    deps = set(o["deps"])
                    if o["dma"] and o["prev"] is not None:
                        deps.add(o["prev"])
                    for j in deps:
                        d = ops[j]
                        if d["dma"]:
                            dneed[d["sem"]] = max(dneed.get(d["sem"], 0), d["target"])
                        else:
                            if d["eng"] == "tensor" and ename == "tensor" and not o["dma"]:
                                continue
                            need[d["eng"]] = max(need.get(d["eng"], 0), d["pos"] + 1)
                    for e, v in need.items():
                        if v > waited[e]:
                            eng.wait_ge(sy.sems[e], v)
                            waited[e] = v
                    for s, v in dneed.items():
                        if v > dwaited[s]:
                            eng.wait_ge(sy.dsems[s], v)
                            dwaited[s] = v
                    ins = o["fn"](eng)
                    if o["dma"]:
                        ins.then_inc(sy.dsems[o["sem"]], 16)
                    else:
                        ins.then_inc(sy.sems[ename], 1)
                if ename == "sync":
                    for s in range(N_DMA_SEMS):
                        if end_dma[s] > start_dma[s]:
                            eng.wait_ge(sy.dsems[s], end_dma[s])
                    for e in COMPUTE:
                        if end_pos[e] > start_pos[e]:
                            eng.wait_ge(sy.sems[e], end_pos[e])
            return body

        with nc.Block() as block:
            for e in ENGINES:
                if per_eng[e] or e == "sync":
                    getattr(block, e)(replay(e))
        self.es.close()


def segs(t0, n):
    out = []
    if t0 < CTX:
        m = min(n, CTX - t0)
        out.append((t0, m, 1))
        if n > m:
            out.append((CTX, n - m, 0))
    else:
        out.append((t0, n, 0))
    return out


def tiles_of(t0, n):
    return list(range(t0 // 128, (t0 + n + 127) // 128))


class K:
    pass


class Rot:
    def __init__(self, mk, n):
        self.t = [mk(i) for i in range(n)]
        self.i = 0

    def next(self):
        t = self.t[self.i % len(self.t)]
        self.i += 1
        return t


GROUPS5 = [(0, 256), (256, 512), (768, 512), (1280, 512), (1792, 512)]
LATG = GROUPS5[1:]
FULL = [[(0, 384), (384, 384), (768, 384)], [(1152, 384), (1536, 384), (1920, 384)]]
LAT = [[(256, 512), (768, 512)], [(1280, 512), (1792, 512)]]
C_ID, C_BD64, C_ROT64, C_UI, C_LI, C_SL, C_SU, C_ROT96 = range(8)
NCST = 8


def build(stage=99, dbg=(), layers=(0, 1), parts=None, skip_ffn1=False):
    nc = bass.Bass("TRN2", target_bir_lowering=False)
    dt = nc.dram_tensor

    def inp(name, shape, dtype=F32):
        return dt(name, list(shape), dtype, kind="ExternalInput").ap()

    def scratch(name, shape, dtype, nslots=1):
        return Buf(dt("scr_" + name, list(shape), dtype, kind="Internal").ap(), nslots, name)

    x_d = inp("x", [SEQ, D])
    ctx_d = inp("ctx", [CTX, D])
    ccol_d = inp("ccol", [128, 16])
    adaw_d = inp("ada_w", [2, D, 9 * D])
    adab_d = inp("ada_b_col", [128, 2, 72])
    ng_d = inp("norm_g_col", [128, 2, 3, 8])
    wg_d = inp("ffn_w_gate", [2, 2, D, DFF])
    wu_d = inp("ffn_w_up", [2, 2, D, DFF])
    wd_d = inp("ffn_w_down", [2, 2, DFF, D])
    cst_d = inp("cst", [128, NCST, 128])
    bdsel_d = inp("bdsel", [128, 2])
    odw_d = inp("od_w_in", [D, 2336])
    odwo_d = inp("od_w_out", [D, D])
    odcol_d = inp("od_cols", [128, 11])
    swcs_d = inp("swa_cs", [128, 2, SEQ])
    gw2b_d = inp("gla_w2b", [17, 2, 256])
    evw_d = inp("ev_w_in", [D, 2752])
    evwo_d = inp("ev_w_out", [D, D])
    evqu_d = inp("ev_w_q_up", [384, 768])
    evkv_d = inp("ev_w_kv_up", [256, 1024])
    evcol_d = inp("ev_cols", [128, 8])
    mlcs_d = inp("mla_cs", [96, 2, SEQ])
    lvm_d = inp("lvm", [128, 14, 128])
    gcv_d = inp("gdn_conv_col", [128, 12, 5])
    grow_d = inp("gdn_rows", [128, 544])
    out_d = dt("out", [SEQ, D], F32, kind="ExternalOutput").ap()
    dbg_d = {}
    for name, shape in dbg:
        dbg_d[name] = dt("dbg_" + name, list(shape), F32, kind="ExternalOutput").ap()

    xn_s = scratch("xn", [128, 8, T], BF16)
    mix_s = scratch("mix", [128, 8, T], BF16)
    swq_s = scratch("swq", [128, 4, SEQ], BF16)
    swk_s = scratch("swk", [128, T], BF16)
    swv_s = scratch("swv", [128, NT * 256], BF16)
    swr_s = scratch("swr", [128, NT * 2], F32)
    glq_s = [scratch("glq%d" % d, [128, 2, T], BF16) for d in range(2)]
    glk_s = [scratch("glk%d" % d, [128, 2, T], BF16) for d in range(2)]
    glkd_s = [scratch("glkd%d" % d, [T, 256], BF16) for d in range(2)]
    glet_s = [scratch("glet%d" % d, [128, 2, NT], F32) for d in range(2)]
    glv_s = scratch("glv", [T, 512], BF16)
    glr_s = scratch("glr", [128, 4, SEQ], BF16)

    mqf_s = scratch("mqf", [96, 8, T], BF16)
    mkf_s = scratch("mkf", [96, 8, T], BF16)
    mv_s = scratch("mv", [T, 512], BF16)
    mrk_s = scratch("mrk", [128, NT * 8], F32)
    gq_s = scratch("gq", [512, T], F32)
    gk_s = scratch("gk", [512, T], F32)
    gktm_s = scratch("gktm", [T, 512], F32)
    gvtm_s = scratch("gvtm", [T, 512], F32)
    gg_s = scratch("gg", [128, NT * 16], F32)
    gb_s = scratch("gb", [128, NT * 16], F32)
    ggate_s = scratch("ggate", [T, 512], BF16)
    og_s = [scratch("og%d" % d, [T, 512], F32) for d in range(2)]
    xt_s = scratch("xtpark", [128, 4 * T], F32)

    es = ExitStack()
    sy = Sync(nc, es)

    def persist(name, shape, dtype, nslots=1):
        return Buf(es.enter_context(nc.sbuf_tensor(name, list(shape), dtype)), nslots, name)

    XT = persist("XT", [128, 8, T], F32, NT)
    cst = persist("cst_s", [128, NCST, 128], F32)
    ident = cst[:, C_ID, :]
    ones_f = persist("ones_f", [128, 128], F32)
    ones_b = persist("ones_b", [128, 128], BF16)
    cstb = persist("cstb", [128, NCST, 128], BF16)
    modT = persist("modT", [128, 2, 72, 2], F32)
    AG = persist("AG", [128, 2, 3, 8, 2], F32)
    GH = persist("GH", [128, 2, 3, 8, 2], F32)
    ng = persist("ng", [128, 2, 3, 8], F32)

    def xk(t0, n):
        return [(XT, i) for i in tiles_of(t0, n)]

    P = Prog(nc, sy, "p0")
    P.dma(cst[:], cst_d, writes=[cst])
    P.dma(ng[:], ng_d, writes=[ng])
    P.memset(ones_f[:], 1.0, [ones_f])
    P.memset(ones_b[:], 1.0, [ones_b])
    P.copy(cstb[:], cst[:], [cst], [cstb], eng="gpsimd")
    xin = [P.sbuf("xin%d" % i, [128, D], F32) for i in range(2)]
    ptr = [P.psum("ptr%d" % i, [128, 4, 128]) for i in range(2)]
    for tt in range(NT):
        st = xin[tt % 2]
        src = ctx_d[tt * 128:(tt + 1) * 128, :] if tt < 2 else x_d[(tt - 2) * 128:(tt - 1) * 128, :]
        P.dma(st[:], src, writes=[st])
        for half in range(2):
            ps = ptr[half]
            for c4 in range(4):
                c = half * 4 + c4
                P.tr(ps[:, c4, :], st[:, c * 128:(c + 1) * 128], ident, [st, cst], [ps])
            if half == 0:
                P.copy(XT[:, 0:4, tt * 128:(tt + 1) * 128], ps[:], [ps], [(XT, tt)])
            else:
                P.act(XT[:, 4:8, tt * 128:(tt + 1) * 128], ps[:], AF.Copy, [ps], [(XT, tt)])

    ccol = P.sbuf("ccol", [128, 16], F32)
    sc3 = P.sbuf("sc3", [128, 8, 2], F32)
    adab = P.sbuf("adab", [128, 2, 72], F32)
    P.dma(ccol[:], ccol_d, writes=[ccol])
    P.dma(adab[:], adab_d, writes=[adab])
    P.act(sc3[:, :, 0], ccol[:, 0:8], AF.Silu, [ccol], [sc3])
    P.act(sc3[:, :, 1], ccol[:, 8:16], AF.Silu, [ccol], [sc3])
    ABLK = 512
    awst = [P.sbuf("awst%d" % i, [128, 8, ABLK], F32) for i in range(2)]
    prow = [P.psum("prow%d" % i, [2, ABLK]) for i in range(2)]
    mrow = P.sbuf("mrow", [2, 9 * D], F32)
    pmod = P.psum("pmod", [128, 72, 2])
    nb = 0
    for l in range(2):
        aw = adaw_d[l].rearrange("(kc p) n -> p kc n", p=128)
        for blk in range(9 * D // ABLK):
            st = awst[nb % 2]
            pr = prow[nb % 2]
            nb += 1
            P.dma(st[:], aw[:, :, blk * ABLK:(blk + 1) * ABLK], writes=[st])
            for kc in range(8):
                P.mm(pr[:], sc3[:, kc, :], st[:, kc, :], kc == 0, kc == 7, [st, sc3], [pr])
            P.act(mrow[:, blk * ABLK:(blk + 1) * ABLK], pr[:], AF.Copy, [pr], [mrow])
        for j in range(72):
            P.tr(pmod[:, j, :], mrow[:, j * 128:(j + 1) * 128], cst[0:2, C_ID, 0:2], [mrow, cst], [pmod])
        for s in range(2):
            P.tt(modT[:, l, :, s], pmod[:, :, s], adab[:, l, :], ALU.add, [pmod, adab], [modT])
    for l in range(2):
        for n in range(3):
            for s in range(2):
                P.stt(AG[:, l, n, :, s], modT[:, l, (3 * n + 1) * 8:(3 * n + 2) * 8, s], 1.0, ng[:, l, n, :],
                      ALU.add, ALU.mult, [modT, ng], [AG])
                P.ts(GH[:, l, n, :, s], modT[:, l, (3 * n + 2) * 8:(3 * n + 3) * 8, s], 0.5 if n != 1 else 1.0,
                     ALU.mult, [modT], [GH])
    if "modT" in dbg_d:
        P.dma(dbg_d["modT"], modT[:].rearrange("p l j s -> p (l j s)"), reads=[modT])
    P.emit()

    def norm_mod(P, W, l, n_idx, t0, n, dst_fn, dst_keys):
        sq, lnv, rs, pss, sqb = (w.next() for w in W)
        P.act(sqb[:, :, 0:n], XT[:, :, t0:t0 + n], AF.Square, xk(t0, n), [sqb])
        for c in range(8):
            P.mm(pss[:, 0:n], ones_b[:], sqb[:, c, 0:n], c == 0, c == 7, [sqb, ones_b], [pss])
        P.act(lnv[:, 0:n], pss[:, 0:n], AF.Ln, [pss], [lnv], bias=EPS, scale=1.0 / D)
        P.act(rs[:, 0:n], lnv[:, 0:n], AF.Exp, [lnv], [rs], scale=-0.5)
        P.tt(sq[:, :, 0:n], XT[:, :, t0:t0 + n], rs[:, 0:n].unsqueeze(1).broadcast_to([128, 8, n]), ALU.mult,
             xk(t0, n) + [rs, sq], [sq])
        for (s0, sn, s) in segs(t0, n):
            for c in range(8):
                P.act(dst_fn(c, s0, sn), sq[:, c, s0 - t0:s0 - t0 + sn], AF.Identity,
                      [sq], dst_keys, bias=modT[:, l, (3 * n_idx) * 8 + c, s:s + 1],
                      scale=AG[:, l, n_idx, c, s:s + 1])

    def norm_ws(P, nmax, nb=1):
        return (Rot(lambda i: P.sbuf("sq%d" % i, [128, 8, nmax], F32), nb),
                Rot(lambda i: P.sbuf("lnv%d" % i, [128, nmax], F32), nb),
                Rot(lambda i: P.sbuf("rs%d" % i, [128, nmax], F32), nb),
                Rot(lambda i: P.psum("pss%d" % i, [128, 512]), nb),
                Rot(lambda i: P.sbuf("sqb%d" % i, [128, 8, nmax], BF16), nb))

    def ffn(l, which, halves):
        n_idx = 0 if which == 0 else 2
        wg_v = wg_d[l, which].rearrange("(kc p) n -> p kc n", p=128)
        wu_v = wu_d[l, which].rearrange("(kc p) n -> p kc n", p=128)
        wd_v = wd_d[l, which].rearrange("(f p) n -> p f n", p=128)
        for hi, groups in enumerate(halves):
            h0 = groups[0][0]
            NH = sum(g[1] for g in groups)
            ng_ = len(groups)
            tag = "f%d%d%d" % (l, which, hi)
            esA = ExitStack()
            AT = Buf(esA.enter_context(nc.sbuf_tensor(tag + "AT", [128, NF, NH], BF16)), NF * ng_, "AT")
            P = Prog(nc, sy, tag + "a")
            hT = P.sbuf("hT", [128, 8, NH], BF16, ng_)
            nmax = max(g[1] for g in groups)
            W = norm_ws(P, nmax)
            for gi, (t0, n) in enumerate(groups):
                norm_mod(P, W, l, n_idx, t0, n, lambda c, s0, sn: hT[:, c, s0 - h0:s0 - h0 + sn], [(hT, gi)])
            stg = P.sbuf("stg", [128, 8, 256], F32)
            stu = P.sbuf("stu", [128, 8, 256], F32)
            wgb = [P.sbuf("wgb%d" % i, [128, 8, 256], BF16) for i in range(2)]
            wub = [P.sbuf("wub%d" % i, [128, 8, 256], BF16) for i in range(2)]
            psg = [P.psum("psg%d" % i, [128, 512]) for i in range(2)]
            psu = [P.psum("psu%d" % i, [128, 512]) for i in range(2)]
            sga = [P.sbuf("sga%d" % i, [128, nmax], F32) for i in range(2)]
            it = 0
            NB = NF // 2

            def fetch_a(fb):
                P.dma(stg[:], wg_v[:, :, fb * 256:(fb + 1) * 256], writes=[stg])
                P.dma(stu[:], wu_v[:, :, fb * 256:(fb + 1) * 256], writes=[stu])

            def cast_a(fb):
                P.act(wgb[fb % 2][:], stg[:], AF.Copy, [stg], [wgb[fb % 2]])
                P.copy(wub[fb % 2][:], stu[:], [stu], [wub[fb % 2]])

            fetch_a(0)
            cast_a(0)
            for fb in range(NB):
                pb = fb % 2
                if fb + 1 < NB:
                    fetch_a(fb + 1)
                cnt = 0
                for f2 in range(2):
                    f = fb * 2 + f2
                    for gi, (t0, n) in enumerate(groups):
                        o0 = t0 - h0
                        pg, pu, sg = psg[it % 2], psu[it % 2], sga[it % 2]
                        it += 1
                        for kc in range(8):
                            P.mm(pg[:, 0:n], wgb[pb][:, kc, f2 * 128:(f2 + 1) * 128], hT[:, kc, o0:o0 + n], kc == 0, kc == 7,
                                 [wgb[pb], (hT, gi)], [pg])
                        for kc in range(8):
                            P.mm(pu[:, 0:n], wub[pb][:, kc, f2 * 128:(f2 + 1) * 128], hT[:, kc, o0:o0 + n], kc == 0, kc == 7,
                                 [wub[pb], (hT, gi)], [pu])
                        P.act(sg[:, 0:n], pg[:, 0:n], AF.Silu, [pg], [sg])
                        P.tt(AT[:, f, o0:o0 + n], sg[:, 0:n], pu[:, 0:n], ALU.mult, [sg, pu], [(AT, f * ng_ + gi)])
                        cnt += 1
                        if cnt == ng_ and fb + 1 < NB:
                            cast_a(fb + 1)
            P.emit()
            P = Prog(nc, sy, tag + "b")
            std = P.sbuf("std", [128, NF, 256], F32)
            wdb = [P.sbuf("wdb%d" % i, [128, NF, 256], BF16) for i in range(2)]
            psy = [P.psum("psy%d" % i, [128, 512]) for i in range(2)]
            it = 0

            def fetch_b(db):
                P.dma(std[:], wd_v[:, :, db * 256:(db + 1) * 256], writes=[std])
                P.act(wdb[db % 2][:], std[:], AF.Copy, [std], [wdb[db % 2]])

            fetch_b(0)
            for db in range(4):
                pb = db % 2
                if db + 1 < 4:
                    fetch_b(db + 1)
                for d2 in range(2):
                    d = db * 2 + d2
                    for gi, (t0, n) in enumerate(groups):
                        o0 = t0 - h0
                        py = psy[it % 2]
                        it += 1
                        for f in range(NF):
                            P.mm(py[:, 0:n], wdb[pb][:, f, d2 * 128:(d2 + 1) * 128], AT[:, f, o0:o0 + n], f == 0, f == NF - 1,
                                 [wdb[pb], (AT, f * ng_ + gi)], [py])
                        for (s0, sn, s) in segs(t0, n):
                            P.stt(XT[:, d, s0:s0 + sn], py[:, s0 - t0:s0 - t0 + sn], GH[:, l, n_idx, d, s:s + 1],
                                  XT[:, d, s0:s0 + sn], ALU.mult, ALU.add, [py] + xk(s0, sn), xk(s0, sn))
            P.emit()
            esA.close()

    def dump_xt(name):
        if name in dbg_d:
            P = Prog(nc, sy, "dbg" + name)
            P.dma(dbg_d[name], XT[:].rearrange("p c t -> p (c t)"), reads=[XT])
            P.emit()

    def make_xn(l):
        P = Prog(nc, sy, "xn%d" % l)
        W = norm_ws(P, 384, 2)
        hb = Rot(lambda i: P.sbuf("hb%d" % i, [128, 8, 384], BF16), 2)
        for (t0, n) in FULL[0] + FULL[1]:
            h = hb.next()
            norm_mod(P, W, l, 1, t0, n, lambda c, s0, sn, h=h, t0=t0: h[:, c, s0 - t0:s0 - t0 + sn], [h])
            P.dma(xn_s[:, :, t0:t0 + n], h[:, :, 0:n], reads=[h], writes=[xn_s])
        P.emit()

    def load_w(P, dst, dst_ap, src_ap, stage, stage_ap):
        P.dma(stage_ap, src_ap, writes=[stage])
        P.copy(dst_ap, stage_ap, [stage], [dst], eng="gpsimd")

    def out_proj(l, wo_d, groups):
        P = Prog(nc, sy, "op%d" % l)
        wo = P.sbuf("wo", [128, 8, D], BF16)
        stwR = Rot(lambda i: P.sbuf("stw%d" % i, [128, 8, 256], F32), 2)
        wv = wo_d.rearrange("(kc p) n -> p kc n", p=128)
        for q in range(4):
            stw = stwR.next()
            P.dma(stw[:], wv[:, :, q * 256:(q + 1) * 256], writes=[stw])
            if q % 2 == 0:
                P.act(wo[:, :, q * 256:(q + 1) * 256], stw[:], AF.Copy, [stw], [wo])
            else:
                P.copy(wo[:, :, q * 256:(q + 1) * 256], stw[:], [stw], [wo])
        mb = Rot(lambda i: P.sbuf("mb%d" % i, [128, 8, 512], BF16), 2)
        pp = Rot(lambda i: P.psum("pp%d" % i, [128, 512]), 3)
        for (t0, n) in groups:
            m = mb.next()
            P.dma(m[:, :, 0:n], mix_s[:, :, t0:t0 + n], reads=[mix_s], writes=[m])
            for dc in range(8):
                ps = pp.next()
                for kc in range(8):
                    P.mm(ps[:, 0:n], wo[:, kc, dc * 128:(dc + 1) * 128], m[:, kc, 0:n], kc == 0, kc == 7, [wo, m], [ps])
                for (s0, sn, s) in segs(t0, n):
                    P.stt(XT[:, dc, s0:s0 + sn], ps[:, s0 - t0:s0 - t0 + sn], GH[:, l, 1, dc, s:s + 1],
                          XT[:, dc, s0:s0 + sn], ALU.mult, ALU.add, [ps] + xk(s0, sn), xk(s0, sn))
        P.emit()

    def swa_prep():
        P = Prog(nc, sy, "swp")
        wv_ = odw_d.rearrange("(kc p) n -> p kc n", p=128)
        odc = P.sbuf("odc", [128, 11], F32)
        P.dma(odc[:], odcol_d, writes=[odc])
        bdf = P.sbuf("bdf", [128, 2], F32)
        P.dma(bdf[:], bdsel_d, writes=[bdf])
        bds = P.sbuf("bds", [128, 2], BF16)
        P.copy(bds[:], bdf[:], [bdf], [bds])
        cs = P.sbuf("cs", [128, 2, SEQ], F32)
        P.dma(cs[:], swcs_d, writes=[cs])
        st = P.sbuf("st", [128, 8, 512], F32)
        wq = P.sbuf("wq", [128, 8, 4, 2, 64], BF16)
        wk = P.sbuf("wk", [128, 8, 128], BF16)
        wvv = P.sbuf("wvv", [128, 8, 128], BF16)
        P.dma(st[:], wv_[:, :, 1568:2080], writes=[st])
        st5 = st[:].rearrange("p k (g j d) -> p k g j d", g=2, j=4)
        for g in range(2):
            P.copy(wq[:, :, :, g, :], st5[:, :, g, :, :], [st], [wq], eng="gpsimd")
        load_w(P, wk, wk[:], wv_[:, :, 2080:2208], st, st[:, :, 0:128])
        load_w(P, wvv, wvv[:], wv_[:, :, 2208:2336], st, st[:, :, 0:128])
        xg = Rot(lambda i: P.sbuf("xg%d" % i, [128, 8, 512], BF16), 2)
        pp = Rot(lambda i: P.psum("pp%d" % i, [128, 512]), 3)
        pssR = Rot(lambda i: P.psum("pss%d" % i, [128, 512]), 2)
        prot = P.psum("prot", [128, 512])
        prk = P.psum("prk", [128, NT, 2])
        pv = P.psum("pv", [128, 128])
        sqR = Rot(lambda i: P.sbuf("sq%d" % i, [128, 512], BF16), 2)
        lnvR = Rot(lambda i: P.sbuf("lnv%d" % i, [128, 512], F32), 2)
        rsR = Rot(lambda i: P.sbuf("rs%d" % i, [128, 512], F32), 2)
        qgR = Rot(lambda i: P.sbuf("qg%d" % i, [128, 512], BF16), 2)
        t1R = Rot(lambda i: P.sbuf("t1%d" % i, [128, 512], F32), 2)
        t2R = Rot(lambda i: P.sbuf("t2%d" % i, [128, 512], F32), 2)
        qo = Rot(lambda i: P.sbuf("qo%d" % i, [128, 512], BF16), 2)
        vd = P.sbuf("vd", [128, NT, 2, 2, 64], BF16)
        rsk = P.sbuf("rsk", [128, NT * 2], F32)
        for (t0, n) in GROUPS5:
            x = xg.next()
            P.dma(x[:, :, 0:n], xn_s[:, :, t0:t0 + n], reads=[xn_s], writes=[x])
            lat = t0 >= CTX
            l0 = t0 - CTX
            if lat:
                for j in range(4):
                    ps = pp.next()
                    for kc in range(8):
                        P.mm(ps[:, 0:n], wq[:, kc, j].rearrange("p g d -> p (g d)"), x[:, kc, 0:n], kc == 0, kc == 7,
                             [wq, x], [ps])
                    sq, pss, lnv, rs, qg, t1, t2 = (sqR.next(), pssR.next(), lnvR.next(), rsR.next(), qgR.next(),
                                                    t1R.next(), t2R.next())
                    P.act(sq[:, 0:n], ps[:, 0:n], AF.Square, [ps], [sq])
                    P.mm(pss[:, 0:n], cstb[:, C_BD64, :], sq[:, 0:n], True, True, [cstb, sq], [pss])
                    P.act(lnv[:, 0:n], pss[:, 0:n], AF.Ln, [pss], [lnv], bias=EPS, scale=1.0 / 64)
                    P.act(rs[:, 0:n], lnv[:, 0:n], AF.Exp, [lnv], [rs], scale=-0.5)
                    P.act(qg[:, 0:n], ps[:, 0:n], AF.Identity, [ps, odc], [qg], scale=odc[:, 0:1])
                    P.mm(prot[:, 0:n], cstb[:, C_ROT64, :], qg[:, 0:n], True, True, [cstb, qg], [prot])
                    P.tt(t1[:, 0:n], qg[:, 0:n], cs[:, 0, l0:l0 + n], ALU.mult, [qg, cs], [t1])
                    P.tt(t2[:, 0:n], prot[:, 0:n], cs[:, 1, l0:l0 + n], ALU.mult, [prot, cs], [t2])
                    P.tt(t1[:, 0:n], t1[:, 0:n], t2[:, 0:n], ALU.add, [t1, t2], [t1])
                    q = qo.next()
                    P.tt(q[:, 0:n], t1[:, 0:n], rs[:, 0:n], ALU.mult, [t1, rs], [q])
                    P.dma(swq_s[:, j, l0:l0 + n], q[:, 0:n], reads=[q], writes=[swq_s])
            ps = pp.next()
            for kc in range(8):
                P.mm(ps[:, 0:n], wk[:, kc, :], x[:, kc, 0:n], kc == 0, kc == 7, [wk, x], [ps])
            sq, qg, t1, t2 = sqR.next(), qgR.next(), t1R.next(), t2R.next()
            P.act(sq[:, 0:n], ps[:, 0:n], AF.Square, [ps], [sq])
            for ti in range(n // 128):
                tt = t0 // 128 + ti
                P.mm(prk[:, tt, :], sq[:, ti * 128:(ti + 1) * 128], bds[:], True, True, [sq, bds], [prk])
            q = qo.next()
            if lat:
                P.act(qg[:, 0:n], ps[:, 0:n], AF.Identity, [ps, odc], [qg], scale=odc[:, 1:2])
                P.mm(prot[:, 0:n], cstb[:, C_ROT64, :], qg[:, 0:n], True, True, [cstb, qg], [prot])
                P.tt(t1[:, 0:n], qg[:, 0:n], cs[:, 0, l0:l0 + n], ALU.mult, [qg, cs], [t1])
                P.tt(t2[:, 0:n], prot[:, 0:n], cs[:, 1, l0:l0 + n], ALU.mult, [prot, cs], [t2])
                P.tt(q[:, 0:n], t1[:, 0:n], t2[:, 0:n], ALU.add, [t1, t2], [q])
            else:
                P.act(q[:, 0:n], ps[:, 0:n], AF.Identity, [ps, odc], [q], scale=odc[:, 1:2])
            P.dma(swk_s[:, t0:t0 + n], q[:, 0:n], reads=[q], writes=[swk_s])
            for ti in range(n // 128):
                tt = t0 // 128 + ti
                for kc in range(8):
                    P.mm(pv[:], x[:, kc, ti * 128:(ti + 1) * 128], wvv[:, kc, :], kc == 0, kc == 7, [x, wvv], [pv])
                pv3 = pv[:].rearrange("p (g d) -> p g d", g=2)
                P.copy(vd[:, tt, :, 0, :], pv3, [pv], [vd])
                P.act(vd[:, tt, :, 1, :], pv3, AF.Copy, [pv], [vd])
        P.act(rsk[:], prk[:].rearrange("p a b -> p (a b)"), AF.Ln, [prk], [rsk], bias=EPS, scale=1.0 / 64)
        P.act(rsk[:], rsk[:], AF.Exp, [rsk], [rsk], scale=-0.5)
        P.ts(rsk[:], rsk[:], 0.125, ALU.mult, [rsk], [rsk])
        P.dma(swr_s[:], rsk[:], reads=[rsk], writes=[swr_s])
        P.dma(swv_s[:], vd[:].rearrange("p a b c d -> p (a b c d)"), reads=[vd], writes=[swv_s])
        P.emit()

    def swa_core():
        P = Prog(nc, sy, "swc")
        q = P.sbuf("q", [128, 4, SEQ], BF16)
        k = P.sbuf("k", [128, T], BF16)
        v = P.sbuf("v", [128, NT, 2, 128], BF16)
        rsk = P.sbuf("rsk", [128, NT, 2], F32)
        odc = P.sbuf("odc", [128, 11], F32)
        esk = P.sbuf("esk", [128, 8], F32)
        mlo = P.sbuf("mlo", [128, 128], BF16)
        mhi = P.sbuf("mhi", [128, 128], BF16)
        P.dma(q[:], swq_s[:], reads=[swq_s], writes=[q])
        P.dma(k[:], swk_s[:], reads=[swk_s], writes=[k])
        P.dma(v[:].rearrange("p a b c -> p (a b c)"), swv_s[:], reads=[swv_s], writes=[v])
        P.dma(rsk[:].rearrange("p a b -> p (a b)"), swr_s[:], reads=[swr_s], writes=[rsk])
        P.dma(odc[:], odcol_d, writes=[odc])
        P.act(esk[:], odc[:, 3:11], AF.Exp, [odc], [esk])
        P.copy(mlo[:], cst[:, C_LI, :], [cst], [mlo])
        P.copy(mhi[:], cst[:, C_UI, :], [cst], [mhi])
        msw = P.sbuf("msw", [128, 4, SEQ], BF16)
        pS = Rot(lambda i: P.psum("pS%d" % i, [128, 4, 128]), 4)
        pO = Rot(lambda i: P.psum("pO%d" % i, [128, 4, 128]), 2)
        pM = Rot(lambda i: P.psum("pM%d" % i, [128, 4, 128]), 2)
        pT = Rot(lambda i: P.sbuf("pT%d" % i, [128, 4, 128], BF16), 4)
        den = P.sbuf("den", [128, 4, 128], F32)
        rden = P.sbuf("rden", [128, 4, 128], F32)
        for n in range(16):
            for g in range(2):
                kts = [(0, None), (1, None)]
                if n - 1 >= 0:
                    kts.append((n + 1, mlo))
                kts.append((n + 2, None))
                if n + 1 <= 15:
                    kts.append((n + 3, mhi))
                po = pO.next()
                pm = pM.next()
                def scores(kt, mask):
                    ps = pS.next()
                    P.mm(ps[:], k[g * 64:(g + 1) * 64, kt * 128:(kt + 1) * 128],
                         q[g * 64:(g + 1) * 64, :, n * 128:(n + 1) * 128], True, True, [k, q], [ps])
                    pt = pT.next()
                    P.act(pt[:], ps[:], AF.Exp, [ps, rsk], [pt], scale=rsk[:, kt, g:g + 1])
                    if mask is not None:
                        P.tt(pt[:], pt[:], mask[:].unsqueeze(1).broadcast_to([128, 4, 128]), ALU.mult, [pt, mask], [pt])
                    return pt

                nxt = scores(*kts[0])
                for ki, (kt, mask) in enumerate(kts):
                    pt = nxt
                    if ki + 1 < len(kts):
                        nxt = scores(*kts[ki + 1])
                    P.mm(po[:], v[:, kt, g, :], pt[:], ki == 0, ki == len(kts) - 1, [v, pt], [po])
                    P.mm(pm[:], ones_b[:], pt[:], ki == 0, ki == len(kts) - 1, [ones_b, pt], [pm])
                P.tt(den[:], pm[:], esk[:, 4 * g:4 * g + 4].unsqueeze(2).broadcast_to([128, 4, 128]), ALU.add,
                     [pm, esk], [den])
                P.act(rden[:], den[:], AF.Ln, [den], [rden])
                P.act(rden[:], rden[:], AF.Exp, [rden], [rden], scale=-1.0)
                for par in range(2):
                    sl = slice(par * 64, (par + 1) * 64)
                    P.tt(msw[sl, 2 * g:2 * g + 2, n * 128:(n + 1) * 128], po[sl, par::2, :], rden[sl, par::2, :],
                         ALU.mult, [po, rden], [msw])
        P.dma(mix_s[:, 4:8, CTX:T], msw[:], reads=[msw], writes=[mix_s])
        P.emit()

    def gla_prep():
        P = Prog(nc, sy, "glp")
        wv_ = odw_d.rearrange("(kc p) n -> p kc n", p=128)
        stR = Rot(lambda i: P.sbuf("st%d" % i, [128, 8, 256], F32), 1)
        wqk = P.sbuf("wqk", [128, 8, 512], BF16)
        wvv = P.sbuf("wvv", [128, 8, 512], BF16)
        wgz = P.sbuf("wgz", [128, 8, 32], BF16)
        wr = P.sbuf("wr", [128, 8, 512], BF16)
        for (wt_, c0_) in ((wqk, 0), (wvv, 512), (wr, 1056)):
            for hf in range(2):
                st = stR.next()
                load_w(P, wt_, wt_[:, :, hf * 256:(hf + 1) * 256], wv_[:, :, c0_ + hf * 256:c0_ + (hf + 1) * 256], st, st[:])
        st = stR.next()
        load_w(P, wgz, wgz[:], wv_[:, :, 1024:1056], st, st[:, :, 0:32])
        g2f = P.sbuf("g2f", [17, 2, 256], F32)
        g2b = P.sbuf("g2b", [17, 2, 256], BF16)
        P.dma(g2f[:], gw2b_d, writes=[g2f])
        P.copy(g2b[:], g2f[:], [g2f], [g2b])
        msk = P.sbuf("msk", [128, 512], F32)
        P.memset(msk[:], 1.0, [msk])
        for i in range(4):
            P.memset(msk[:, i * 128:i * 128 + 1], 0.0, [msk])
        gz = [P.sbuf("gz%d" % d, [17, 512], BF16) for d in range(2)]
        for d in range(2):
            P.memset(gz[d][:], 1.0, [gz[d]])
        xg = Rot(lambda i: P.sbuf("xg%d" % i, [128, 8, 512], BF16), 2)
        pp = Rot(lambda i: P.psum("pp%d" % i, [128, 512]), 4)
        ptr_ = P.psum("ptr", [128, 2, 128], BF16)
        pgzD = [P.psum("pgz%d" % i, [16, 512]) for i in range(2)]
        qkraw = P.sbuf("qkraw", [128, 4, 512], F32)
        spD = [P.sbuf("sp%d" % i, [128, 2, 512], F32) for i in range(2)]
        preD = [P.sbuf("pre%d" % i, [128, 2, 512], F32) for i in range(2)]
        bcD = [P.sbuf("bc%d" % i, [128, 2, 512], F32) for i in range(2)]
        remD = [P.sbuf("rem%d" % i, [128, 2, 512], F32) for i in range(2)]
        eeR = Rot(lambda i: P.sbuf("ee%d" % i, [128, 2, 512], F32), 2)
        etD = [P.sbuf("et%d" % i, [128, 2, 4], F32) for i in range(2)]
        ob = Rot(lambda i: P.sbuf("ob%d" % i, [128, 2, 512], BF16), 3)
        kdt = Rot(lambda i: P.sbuf("kdt%d" % i, [128, 4, 256], BF16), 2)
        vtm = Rot(lambda i: P.sbuf("vtm%d" % i, [128, 4, 512], BF16), 2)
        rg = Rot(lambda i: P.sbuf("rg%d" % i, [128, 4, 512], BF16), 1)
        idb = P.sbuf("idb", [128, 128], BF16)
        P.copy(idb[:], cst[:, C_ID, :], [cst], [idb])
        for (t0, n) in GROUPS5:
            nt = n // 128
            x = xg.next()
            P.dma(x[:, :, 0:n], xn_s[:, :, t0:t0 + n], reads=[xn_s], writes=[x])
            for cq in range(4):
                ps = pp.next()
                for kc in range(8):
                    P.mm(ps[:, 0:n], wqk[:, kc, cq * 128:(cq + 1) * 128], x[:, kc, 0:n], kc == 0, kc == 7, [wqk, x], [ps])
                P.act(qkraw[:, cq, 0:n], ps[:, 0:n], AF.Copy, [ps], [qkraw])
            vt = vtm.next()
            for ti in range(nt):
                ps = pp.next()
                for kc in range(8):
                    P.mm(ps[:], x[:, kc, ti * 128:(ti + 1) * 128], wvv[:, kc, :], kc == 0, kc == 7, [x, wvv], [ps])
                P.act(vt[:, ti, :], ps[:], AF.Copy, [ps], [vt])
            P.dma(glv_s[t0:t0 + n, :].rearrange("(a p) f -> p a f", p=128), vt[:, 0:nt, :], reads=[vt], writes=[glv_s])
            if t0 >= CTX:
                r = rg.next()
                for h in range(4):
                    ps = pp.next()
                    for kc in range(8):
                        P.mm(ps[:, 0:n], wr[:, kc, h * 128:(h + 1) * 128], x[:, kc, 0:n], kc == 0, kc == 7, [wr, x], [ps])
                    P.act(r[:, h, 0:n], ps[:, 0:n], AF.Silu, [ps], [r])
                P.dma(glr_s[:, :, t0 - CTX:t0 - CTX + n], r[:, :, 0:n], reads=[r], writes=[glr_s])
            for d in range(2):
                for kc in range(8):
                    P.mm(pgzD[d][:, 0:n], wgz[:, kc, 16 * d:16 * d + 16], x[:, kc, 0:n], kc == 0, kc == 7, [wgz, x], [pgzD[d]])
                P.act(gz[d][0:16, 0:n], pgzD[d][:, 0:n], AF.Copy, [pgzD[d]], [gz[d]])
            for c in range(2):
                for d in range(2):
                    sp, pre = spD[d], preD[d]
                    ps = pp.next()
                    P.mm(ps[:, 0:n], g2b[:, d, c * 128:(c + 1) * 128], gz[d][:, 0:n], True, True, [g2b, gz[d]], [ps])
                    P.act(sp[:, c, 0:n], ps[:, 0:n], AF.Exp, [ps], [sp], scale=-1.0)
                    P.act(sp[:, c, 0:n], sp[:, c, 0:n], AF.Ln, [sp], [sp], bias=1.0)
                    P.op("vector", lambda e, c=c, n=n, pre=pre, sp=sp: e.tensor_tensor_scan(
                        out=pre[:, c, 0:n], data0=msk[:, 0:n], data1=sp[:, c, 0:n], initial=0.0,
                        op0=ALU.mult, op1=ALU.add), [msk, sp], [pre])
            for d in range(2):
                sp, pre, bc, rem, et = spD[d], preD[d], bcD[d], remD[d], etD[d]
                pre4 = pre[:, :, 0:n].rearrange("p c (a b) -> p c a b", b=128)
                totb = pre4[:, :, :, 127:128].broadcast_to([128, 2, nt, 128])
                bc4 = bc[:, :, 0:n].rearrange("p c (a b) -> p c a b", b=128)
                rem4 = rem[:, :, 0:n].rearrange("p c (a b) -> p c a b", b=128)
                if d == 0:
                    P.copy(bc[:, :, 0:n], pre[:, :, 0:n], [pre], [bc], eng="gpsimd")
                    P.tt(rem4, totb, pre4, ALU.subtract, [pre], [rem])
                else:
                    P.tt(rem[:, :, 0:n], pre[:, :, 0:n], sp[:, :, 0:n], ALU.subtract, [pre, sp], [rem])
                    P.tt(bc4, totb, rem4, ALU.subtract, [pre, rem], [bc])
                P.act(et[:, :, 0:nt], pre4[:, :, :, 127], AF.Exp, [pre], [et], scale=-1.0 / 16)
                P.dma(glet_s[d][:, :, t0 // 128:t0 // 128 + nt], et[:, :, 0:nt], reads=[et], writes=[glet_s[d]])
            outs = []
            for d in range(2):
                bc, rem = bcD[d], remD[d]
                ee = eeR.next()
                P.act(ee[:, :, 0:n], bc[:, :, 0:n], AF.Exp, [bc], [ee], scale=-1.0 / 16)
                o = ob.next()
                P.stt(o[:, :, 0:n], qkraw[:, 0:2, 0:n], 0.125, ee[:, :, 0:n], ALU.mult, ALU.mult, [qkraw, ee], [o])
                P.dma(glq_s[d][:, :, t0:t0 + n], o[:, :, 0:n], reads=[o], writes=[glq_s[d]])
                ee = eeR.next()
                P.act(ee[:, :, 0:n], bc[:, :, 0:n], AF.Exp, [bc], [ee], scale=1.0 / 16)
                o = ob.next()
                P.tt(o[:, :, 0:n], qkraw[:, 2:4, 0:n], ee[:, :, 0:n], ALU.mult, [qkraw, ee], [o])
                P.dma(glk_s[d][:, :, t0:t0 + n], o[:, :, 0:n], reads=[o], writes=[glk_s[d]])
                ee = eeR.next()
                P.act(ee[:, :, 0:n], rem[:, :, 0:n], AF.Exp, [rem], [ee], scale=-1.0 / 16)
                o = ob.next()
                P.tt(o[:, :, 0:n], qkraw[:, 2:4, 0:n], ee[:, :, 0:n], ALU.mult, [qkraw, ee], [o])
                kd = kdt.next()
                for ti in range(nt):
                    for c in range(2):
                        P.tr(ptr_[:, c, :], o[:, c, ti * 128:(ti + 1) * 128], idb[:], [o, idb], [ptr_])
                    P.copy(kd[:, ti, :], ptr_[:].rearrange("p a b -> p (a b)"), [ptr_], [kd])
                P.dma(glkd_s[d][t0:t0 + n, :].rearrange("(a p) f -> p a f", p=128), kd[:, 0:nt, :], reads=[kd],
                      writes=[glkd_s[d]])
        P.emit()

    def gla_core():
        P = Prog(nc, sy, "glc")
        odc = P.sbuf("odc", [128, 11], F32)
        P.dma(odc[:], odcol_d, writes=[odc])
        ogla = P.sbuf("ogla", [128, 4, SEQ], F32, 16)
        for i in range(4):
            P.memset(ogla[:, i, :], 0.0, [ogla], eng="gpsimd" if i % 2 else "vector")
        S = [[P.sbuf("S%d%d" % (d, c), [128, 128], F32) for c in range(2)] for d in range(2)]
        Sb = [[P.sbuf("Sb%d%d" % (d, h), [128, 128], BF16) for h in range(4)] for d in range(2)]
        qd = Rot(lambda i: P.sbuf("qd%d" % i, [128, 2, 128], BF16), 4)
        kp = Rot(lambda i: P.sbuf("kp%d" % i, [128, 2, 128], BF16), 4)
        kd = Rot(lambda i: P.sbuf("kd%d" % i, [128, 256], BF16), 4)
        vt = Rot(lambda i: P.sbuf("vt%d" % i, [128, 512], BF16), 4)
        et = [P.sbuf("et%d" % d, [128, 2, NT], F32) for d in range(2)]
        aT = Rot(lambda i: P.sbuf("aT%d" % i, [128, 4, 128], BF16), 2)
        pA0 = Rot(lambda i: P.psum("pA0%d" % i, [128, 2, 128]), 1)
        pA1 = Rot(lambda i: P.psum("pA1%d" % i, [128, 2, 128]), 1)
        pO = Rot(lambda i: P.psum("pO%d" % i, [128, 4, 128]), 2)
        pSt = Rot(lambda i: P.psum("pSt%d" % i, [128, 128]), 3)
        orders = [list(range(NT)), [1, 0] + list(range(NT - 1, 1, -1))]
        masks = [cst[:, C_UI, :], cst[:, C_LI, :]]
        for d in range(2):
            P.dma(et[d][:], glet_s[d][:], reads=[glet_s[d]], writes=[et[d]])
            for c in range(2):
                P.memset(S[d][c][:], 0.0, [S[d][c]])
            for h in range(4):
                P.memset(Sb[d][h][:], 0.0, [Sb[d][h]])
        for s_ in range(NT):
            for d in range(2):
                tt = orders[d][s_]
                mask = masks[d]
                q_, k_, kd_, v_ = qd.next(), kp.next(), kd.next(), vt.next()
                tsl = slice(tt * 128, (tt + 1) * 128)
                P.dma(q_[:], glq_s[d][:, :, tsl], reads=[glq_s[d]], writes=[q_])
                P.dma(k_[:], glk_s[d][:, :, tsl], reads=[glk_s[d]], writes=[k_])
                P.dma(kd_[:], glkd_s[d][tsl, :], reads=[glkd_s[d]], writes=[kd_])
                P.dma(v_[:], glv_s[tsl, :], reads=[glv_s], writes=[v_])
                if tt >= 2:
                    a_ = aT.next()
                    for par, pa_ in enumerate((pA0, pA1)):
                        pa = pa_.next()
                        hs = slice(par * 64, par * 64 + 64)
                        for c in range(2):
                            P.mm(pa[:, c, :], k_[hs, c, :], q_[hs, c, :], True, True, [k_, q_], [pa])
                        P.tt(a_[:, par::2, :], pa[:], mask.unsqueeze(1).broadcast_to([128, 2, 128]), ALU.mult,
                             [pa, cst], [a_])
                    po = pO.next()
                    for h in range(4):
                        P.mm(po[:, h, :], v_[:, h * 128:(h + 1) * 128], a_[:, h, :], True, False, [v_, a_], [po])
                        P.mm(po[:, h, :], Sb[d][h][:], q_[:, h // 2, :], False, True, [Sb[d][h], q_], [po])
                    lt = tt - 2
                    osl = ogla[:, :, lt * 128:(lt + 1) * 128]
                    P.tt(osl, osl, po[:], ALU.add, [po, (ogla, lt)], [(ogla, lt)])
                for c in range(2):
                    ps = pSt.next()
                    for hh in range(2):
                        h = 2 * c + hh
                        P.mm(ps[hh * 64:(hh + 1) * 64, :], kd_[:, h * 64:(h + 1) * 64], v_[:, h * 128:(h + 1) * 128],
                             True, True, [kd_, v_], [ps])
                    P.stt(S[d][c][:], S[d][c][:], et[d][:, c, tt:tt + 1], ps[:], ALU.mult, ALU.add,
                          [S[d][c], et[d], ps], [S[d][c]])
                    for hh in range(2):
                        hs = slice(hh * 64, hh * 64 + 64)
                        P.act(Sb[d][2 * c + hh][hs, :], S[d][c][hs, :], AF.Copy, [S[d][c]], [Sb[d][2 * c + hh]])
        sqR = Rot(lambda i: P.sbuf("sq%d" % i, [128, 512], BF16), 2)
        lnvR = Rot(lambda i: P.sbuf("lnv%d" % i, [128, 512], F32), 2)
        rsR = Rot(lambda i: P.sbuf("rs%d" % i, [128, 512], F32), 2)
        tmpR = Rot(lambda i: P.sbuf("tmp%d" % i, [128, 512], F32), 2)
        pss = P.psum("pss", [128, 512])
        rg = Rot(lambda i: P.sbuf("rg%d" % i, [128, 4, 512], BF16), 2)
        mo = Rot(lambda i: P.sbuf("mo%d" % i, [128, 4, 512], BF16), 2)
        for gi in range(4):
            r = rg.next()
            m = mo.next()
            gsl = slice(gi * 512, (gi + 1) * 512)
            keys = [(ogla, i) for i in range(gi * 4, gi * 4 + 4)]
            P.dma(r[:], glr_s[:, :, gsl], reads=[glr_s], writes=[r])
            for h in range(4):
                sq, lnv, rs, tmp = sqR.next(), lnvR.next(), rsR.next(), tmpR.next()
                P.act(sq[:], ogla[:, h, gsl], AF.Square, keys, [sq])
                P.mm(pss[:], ones_b[:], sq[:], True, True, [ones_b, sq], [pss])
                P.act(lnv[:], pss[:], AF.Ln, [pss], [lnv], bias=EPS, scale=1.0 / 128)
                P.act(rs[:], lnv[:], AF.Exp, [lnv], [rs], scale=-0.5)
                P.tt(tmp[:], ogla[:, h, gsl], rs[:], ALU.mult, keys + [rs], [tmp])
                P.stt(m[:, h, :], tmp[:], odc[:, 2:3], r[:, h, :], ALU.mult, ALU.mult, [tmp, odc, r], [m])
            P.dma(mix_s[:, 0:4, CTX + gi * 512:CTX + (gi + 1) * 512], m[:], reads=[m], writes=[mix_s])
        P.emit()

    def mla_prep():
        P = Prog(nc, sy, "mlp")
        wv_ = evw_d.rearrange("(kc p) n -> p kc n", p=128)
        evc = P.sbuf("evc", [128, 8], F32)
        P.dma(evc[:], evcol_d, writes=[evc])
        cs = P.sbuf("cs", [96, 2, SEQ], F32)
        P.dma(cs[:], mlcs_d, writes=[cs])
        st = P.sbuf("st", [128, 8, 384], F32)
        wcq = P.sbuf("wcq", [128, 8, 384], BF16)
        wckv = P.sbuf("wckv", [128, 8, 256], BF16)
        wkr = P.sbuf("wkr", [128, 8, 32], BF16)
        wqu = P.sbuf("wqu", [128, 3, 768], BF16)
        wkK = P.sbuf("wkK", [128, 2, 8, 64], BF16)
        wkV = P.sbuf("wkV", [128, 2, 8, 64], BF16)
        load_w(P, wcq, wcq[:], wv_[:, :, 0:384], st, st[:])
        load_w(P, wckv, wckv[:], wv_[:, :, 384:640], st, st[:, :, 0:256])
        load_w(P, wkr, wkr[:], wv_[:, :, 640:672], st, st[:, :, 0:32])
        stf = st[:].rearrange("p a b -> p (a b)")
        qu_v = evqu_d.rearrange("(kc p) n -> p kc n", p=128)
        st_qu = stf[:, 0:3 * 768].rearrange("p (a b) -> p a b", a=3)
        load_w(P, wqu, wqu[:], qu_v, st, st_qu)
        kv_v = evkv_d.rearrange("(kc p) n -> p kc n", p=128)
        st_kv = stf[:, 0:2048].rearrange("p (a b) -> p a b", a=2)
        P.dma(st_kv, kv_v, writes=[st])
        st_kv5 = stf[:, 0:2048].rearrange("p (a h t d) -> p a h t d", a=2, h=8, t=2)
        P.copy(wkK[:], st_kv5[:, :, :, 0, :], [st], [wkK], eng="gpsimd")
        P.copy(wkV[:], st_kv5[:, :, :, 1, :], [st], [wkV], eng="gpsimd")
        xg = Rot(lambda i: P.sbuf("xg%d" % i, [128, 8, 512], BF16), 2)
        pp = Rot(lambda i: P.psum("pp%d" % i, [128, 512]), 3)
        pssR = Rot(lambda i: P.psum("pss%d" % i, [128, 512]), 2)
        protR = Rot(lambda i: P.psum("prot%d" % i, [128, 512]), 1)
        prk = P.psum("prk", [128, NT, 8])
        prkr = P.psum("prkr", [128, NT])
        raw = P.sbuf("raw", [128, 3, 512], F32)
        sq3 = P.sbuf("sq3", [128, 3, 512], BF16)
        lnvR = Rot(lambda i: P.sbuf("lnv%d" % i, [128, 512], F32), 2)
        rsR = Rot(lambda i: P.sbuf("rs%d" % i, [128, 512], F32), 2)
        cqn = P.sbuf("cqn", [128, 3, 512], BF16)
        ckvn = P.sbuf("ckvn", [128, 2, 512], BF16)
        sqR = Rot(lambda i: P.sbuf("sq%d" % i, [128, 512], BF16), 2)
        qgR = Rot(lambda i: P.sbuf("qg%d" % i, [128, 512], BF16), 2)
        t1R = Rot(lambda i: P.sbuf("t1%d" % i, [128, 512], F32), 2)
        t2R = Rot(lambda i: P.sbuf("t2%d" % i, [128, 512], F32), 2)
        qo = Rot(lambda i: P.sbuf("qo%d" % i, [128, 512], BF16), 3)
        vtm = Rot(lambda i: P.sbuf("vtm%d" % i, [128, 4, 512], BF16), 2)
        rsk = P.sbuf("rsk", [128, NT * 8], F32)
        R96 = cstb[0:96, C_ROT96, 0:96]

        def rope_out(q, src, rs, n, lat, l0, with_rs):
            if lat:
                prot, t1, t2 = protR.next(), t1R.next(), t2R.next()
                P.mm(prot[0:96, 0:n], R96, src[0:96, 0:n], True, True, [cstb, src], [prot])
                P.tt(t1[0:96, 0:n], src[0:96, 0:n], cs[:, 0, l0:l0 + n], ALU.mult, [src, cs], [t1])
                P.tt(t2[0:96, 0:n], prot[0:96, 0:n], cs[:, 1, l0:l0 + n], ALU.mult, [prot, cs], [t2])
                if with_rs:
                    P.tt(t1[0:96, 0:n], t1[0:96, 0:n], t2[0:96, 0:n], ALU.add, [t1, t2], [t1])
                    P.tt(q[0:96, 0:n], t1[0:96, 0:n], rs[0:96, 0:n], ALU.mult, [t1, rs], [q])
                else:
                    P.tt(q[0:96, 0:n], t1[0:96, 0:n], t2[0:96, 0:n], ALU.add, [t1, t2], [q])
            else:
                if with_rs:
                    P.tt(q[0:96, 0:n], src[0:96, 0:n], rs[0:96, 0:n], ALU.mult, [src, rs], [q])
                else:
                    P.copy(q[0:96, 0:n], src[0:96, 0:n], [src], [q])

        for (t0, n) in GROUPS5:
            nt = n // 128
            lat = t0 >= CTX
            l0 = t0 - CTX
            x = xg.next()
            P.dma(x[:, :, 0:n], xn_s[:, :, t0:t0 + n], reads=[xn_s], writes=[x])
            for (wt, nch, off, dstn, dim) in ((wcq, 3, 0, cqn, 384), (wckv, 2, 3, ckvn, 256)):
                for c in range(nch):
                    ps = pp.next()
                    for kc in range(8):
                        P.mm(ps[:, 0:n], wt[:, kc, c * 128:(c + 1) * 128], x[:, kc, 0:n], kc == 0, kc == 7, [wt, x], [ps])
                    P.act(raw[:, c, 0:n], ps[:, 0:n], AF.Copy, [ps], [raw])
                    P.act(sq3[:, c, 0:n], ps[:, 0:n], AF.Square, [ps], [sq3])
                pss, lnv, rs = pssR.next(), lnvR.next(), rsR.next()
                for c in range(nch):
                    P.mm(pss[:, 0:n], ones_b[:], sq3[:, c, 0:n], c == 0, c == nch - 1, [ones_b, sq3], [pss])
                P.act(lnv[:, 0:n], pss[:, 0:n], AF.Ln, [pss], [lnv], bias=EPS, scale=1.0 / dim)
                P.act(rs[:, 0:n], lnv[:, 0:n], AF.Exp, [lnv], [rs], scale=-0.5)
                for c in range(nch):
                    P.stt(dstn[:, c, 0:n], raw[:, c, 0:n], evc[:, off + c:off + c + 1], rs[:, 0:n], ALU.mult, ALU.mult,
                          [raw, evc, rs], [dstn])
            vt = vtm.next()
            for ti in range(nt):
                ps = pp.next()
                for c in range(2):
                    P.mm(ps[:], ckvn[:, c, ti * 128:(ti + 1) * 128], wkV[:, c].rearrange("p h d -> p (h d)"),
                         c == 0, c == 1, [ckvn, wkV], [ps])
                P.act(vt[:, ti, :], ps[:], AF.Copy, [ps], [vt])
            P.dma(mv_s[t0:t0 + n, :].rearrange("(a p) f -> p a f", p=128), vt[:, 0:nt, :], reads=[vt], writes=[mv_s])
            R = slice(64, 96)
            psr = pp.next()
            for kc in range(8):
                P.mm(psr[R, 0:n], wkr[:, kc, :], x[:, kc, 0:n], kc == 0, kc == 7, [wkr, x], [psr])
            sqr, krg = sqR.next(), qgR.next()
            P.act(sqr[R, 0:n], psr[R, 0:n], AF.Square, [psr], [sqr])
            for ti in range(nt):
                tt = t0 // 128 + ti
                P.mm(prkr[:, tt:tt + 1], sqr[R, ti * 128:(ti + 1) * 128], ones_b[R, 0:1], True, True, [sqr, ones_b], [prkr])
            P.act(krg[R, 0:n], psr[R, 0:n], AF.Identity, [psr, evc], [krg], scale=evc[R, 6:7])
            krf = qo.next()
            if lat:
                prot, t1, t2 = protR.next(), t1R.next(), t2R.next()
                P.mm(prot[R, 0:n], cstb[R, C_ROT96, 64:96], krg[R, 0:n], True, True, [cstb, krg], [prot])
                P.tt(t1[R, 0:n], krg[R, 0:n], cs[R, 0, l0:l0 + n], ALU.mult, [krg, cs], [t1])
                P.tt(t2[R, 0:n], prot[R, 0:n], cs[R, 1, l0:l0 + n], ALU.mult, [prot, cs], [t2])
                P.tt(krf[R, 0:n], t1[R, 0:n], t2[R, 0:n], ALU.add, [t1, t2], [krf])
            else:
                P.copy(krf[R, 0:n], krg[R, 0:n], [krg], [krf])
            for h in range(8):
                P.dma(mkf_s[R, h, t0:t0 + n], krf[R, 0:n], reads=[krf], writes=[mkf_s])
            for h in range(8):
                ps = pp.next()
                for c in range(3):
                    P.mm(ps[0:96, 0:n], wqu[:, c, h * 96:(h + 1) * 96], cqn[:, c, 0:n], c == 0, c == 2, [wqu, cqn], [ps])
                sq, pss, lnv, rs, qg = sqR.next(), pssR.next(), lnvR.next(), rsR.next(), qgR.next()
                P.act(sq[0:96, 0:n], ps[0:96, 0:n], AF.Square, [ps], [sq])
                P.mm(pss[0:96, 0:n], ones_b[0:96, 0:96], sq[0:96, 0:n], True, True, [ones_b, sq], [pss])
                P.act(lnv[0:96, 0:n], pss[0:96, 0:n], AF.Ln, [pss], [lnv], bias=EPS, scale=1.0 / 96)
                P.act(rs[0:96, 0:n], lnv[0:96, 0:n], AF.Exp, [lnv], [rs], scale=-0.5)
                P.act(qg[0:96, 0:n], ps[0:96, 0:n], AF.Identity, [ps, evc], [qg], scale=evc[0:96, 5:6])
                q = qo.next()
                rope_out(q, qg, rs, n, lat, l0, True)
                P.dma(mqf_s[:, h, t0:t0 + n], q[0:96, 0:n], reads=[q], writes=[mqf_s])
                ps = pp.next()
                for c in range(2):
                    P.mm(ps[0:64, 0:n], wkK[:, c, h, :], ckvn[:, c, 0:n], c == 0, c == 1, [wkK, ckvn], [ps])
                sq = sqR.next()
                P.act(sq[0:64, 0:n], ps[0:64, 0:n], AF.Square, [ps], [sq])
                for ti in range(nt):
                    tt = t0 // 128 + ti
                    P.mm(prk[:, tt, h:h + 1], sq[0:64, ti * 128:(ti + 1) * 128], ones_b[0:64, 0:1], True, True,
                         [sq, ones_b], [prk])
                q = qo.next()
                P.act(q[0:64, 0:n], ps[0:64, 0:n], AF.Identity, [ps, evc], [q], scale=evc[0:64, 6:7])
                P.dma(mkf_s[0:64, h, t0:t0 + n], q[0:64, 0:n], reads=[q], writes=[mkf_s])
        rk3 = P.sbuf("rk3", [128, NT, 8], F32)
        rkr = P.sbuf("rkr", [128, NT], F32)
        P.act(rkr[:], prkr[:], AF.Copy, [prkr], [rkr])
        P.tt(rk3[:], prk[:], rkr[:].unsqueeze(2).broadcast_to([128, NT, 8]), ALU.add, [prk, rkr], [rk3])
        P.act(rsk[:], rk3[:].rearrange("p a b -> p (a b)"), AF.Ln, [rk3], [rsk], bias=EPS, scale=1.0 / 96)
        P.act(rsk[:], rsk[:], AF.Exp, [rsk], [rsk], scale=-0.5)
        P.ts(rsk[:], rsk[:], float(96 ** -0.5), ALU.mult, [rsk], [rsk])
        P.dma(mrk_s[:], rsk[:], reads=[rsk], writes=[mrk_s])
        P.emit()

    def mla_core():
        P = Prog(nc, sy, "mlc")
        rsk = P.sbuf("rsk", [128, NT, 8], F32)
        P.dma(rsk[:].rearrange("p a b -> p (a b)"), mrk_s[:], reads=[mrk_s], writes=[rsk])
        kf = Rot(lambda i: P.sbuf("kf%d" % i, [96, 2, T], BF16), 2)
        qf = Rot(lambda i: P.sbuf("qf%d" % i, [96, 2, T], BF16), 2)
        vv = Rot(lambda i: P.sbuf("vv%d" % i, [128, NT, 128], BF16), 2)
        pS = Rot(lambda i: P.psum("pS%d" % i, [128, 512]), 4)
        pO = Rot(lambda i: P.psum("pO%d" % i, [128, 512]), 2)
        pM = Rot(lambda i: P.psum("pM%d" % i, [128, 512]), 2)
        pT = Rot(lambda i: P.sbuf("pT%d" % i, [128, 512], BF16), 4)
        rden = P.sbuf("rden", [128, 512], F32)
        mo = Rot(lambda i: P.sbuf("mo%d" % i, [128, 512], BF16), 2)
        for j in range(4):
            k_, q_, v_ = kf.next(), qf.next(), vv.next()
            P.dma(k_[:], mkf_s[:, 2 * j:2 * j + 2, :], reads=[mkf_s], writes=[k_])
            P.dma(q_[:], mqf_s[:, 2 * j:2 * j + 2, :], reads=[mqf_s], writes=[q_])
            P.dma(v_[:], mv_s[:, j * 128:(j + 1) * 128].rearrange("(a p) f -> p a f", p=128), reads=[mv_s], writes=[v_])
            for (t0, n) in GROUPS5:
                kts = [0, 1] if t0 < CTX else list(range(NT))
                po, pm = pO.next(), pM.next()

                def scores(kt):
                    pts = []
                    for hh in range(2):
                        ps = pS.next()
                        P.mm(ps[:, 0:n], k_[:, hh, kt * 128:(kt + 1) * 128], q_[:, hh, t0:t0 + n], True, True, [k_, q_], [ps])
                        pt = pT.next()
                        P.act(pt[:, 0:n], ps[:, 0:n], AF.Exp, [ps, rsk], [pt], scale=rsk[:, kt, 2 * j + hh:2 * j + hh + 1])
                        pts.append(pt)
                    return pts

                nxt = scores(kts[0])
                for ki, kt in enumerate(kts):
                    first, last = ki == 0, ki == len(kts) - 1
                    pts = nxt
                    if not last:
                        nxt = scores(kts[ki + 1])
                    for hh in range(2):
                        sl = slice(hh * 64, (hh + 1) * 64)
                        P.mm(po[sl, 0:n], v_[:, kt, hh * 64:(hh + 1) * 64], pts[hh][:, 0:n], first, last, [v_, pts[hh]], [po])
                    for hh in range(2):
                        sl = slice(hh * 64, (hh + 1) * 64)
                        P.mm(pm[sl, 0:n], ones_b[:, 0:64], pts[hh][:, 0:n], first, last, [ones_b, pts[hh]], [pm])
                P.act(rden[:, 0:n], pm[:, 0:n], AF.Ln, [pm], [rden])
                P.act(rden[:, 0:n], rden[:, 0:n], AF.Exp, [rden], [rden], scale=-1.0)
                m = mo.next()
                P.tt(m[:, 0:n], po[:, 0:n], rden[:, 0:n], ALU.mult, [po, rden], [m])
                P.dma(mix_s[:, j, t0:t0 + n], m[:, 0:n], reads=[m], writes=[mix_s])
        P.emit()

    def gdn_prep():
        P = Prog(nc, sy, "gdp")
        wv_ = evw_d.rearrange("(kc p) n -> p kc n", p=128)
        xn = P.sbuf("xn", [128, 8, T], BF16)
        P.dma(xn[:], xn_s[:], reads=[xn_s], writes=[xn])
        cvw = P.sbuf("cvw", [128, 12, 5], F32)
        P.dma(cvw[:], gcv_d, writes=[cvw])
        grow = P.sbuf("grow", [128, 544], F32)
        P.dma(grow[:], grow_d, writes=[grow])
        st = P.sbuf("st", [128, 8, 512], F32)
        wch = Rot(lambda i: P.sbuf("wch%d" % i, [128, 8, 128], BF16), 2)
        wab = P.sbuf("wab", [128, 8, 32], BF16)
        wgt = P.sbuf("wgt", [128, 8, 512], BF16)
        load_w(P, wab, wab[:], wv_[:, :, 2208:2240], st, st[:, :, 0:32])
        load_w(P, wgt, wgt[:], wv_[:, :, 2240:2752], st, st[:])
        pp = Rot(lambda i: P.psum("pp%d" % i, [128, 512]), 3)
        pssR = Rot(lambda i: P.psum("pss%d" % i, [128, 512]), 2)
        ptr_ = Rot(lambda i: P.psum("ptr%d" % i, [128, 4, 128]), 2)
        zcR = Rot(lambda i: P.sbuf("zc%d" % i, [128, T], F32), 2)
        accR = Rot(lambda i: P.sbuf("acc%d" % i, [128, T], F32), 1)
        sqR = Rot(lambda i: P.sbuf("sq%d" % i, [128, 512], BF16), 2)
        lnvR = Rot(lambda i: P.sbuf("lnv%d" % i, [128, 512], F32), 2)
        tmst = P.sbuf("tmst", [128, NT, 128], F32)
        gall = P.sbuf("gall", [128, NT, 16], F32)
        ball = P.sbuf("ball", [128, NT, 16], F32)
        negA = P.sbuf("negA", [128, 16], F32)
        P.act(negA[:], grow[:, 0:16], AF.Exp, [grow], [negA])
        P.ts(negA[:], negA[:], -1.0, ALU.mult, [negA], [negA])
        tg = P.sbuf("tg", [128, 16], F32)
        gt = Rot(lambda i: P.sbuf("gt%d" % i, [128, 512], BF16), 2)
        for tt in range(NT):
            tsl = slice(tt * 128, (tt + 1) * 128)
            ps = pp.next()
            for kc in range(8):
                P.mm(ps[:, 0:32], xn[:, kc, tsl], wab[:, kc, :], kc == 0, kc == 7, [xn, wab], [ps])
            P.tt(tg[:], ps[:, 0:16], grow[:, 16:32], ALU.add, [ps, grow], [tg])
            P.act(tg[:], tg[:], AF.Exp, [tg], [tg])
            P.act(tg[:], tg[:], AF.Ln, [tg], [tg], bias=1.0)
            P.tt(gall[:, tt, :], tg[:], negA[:], ALU.mult, [tg, negA], [gall])
            P.act(ball[:, tt, :], ps[:, 16:32], AF.Exp, [ps], [ball], scale=-1.0)
            P.ts(ball[:, tt, :], ball[:, tt, :], 1.0, ALU.add, [ball], [ball])
            P.op("vector", lambda e, tt=tt: e.reciprocal(out=ball[:, tt, :], in_=ball[:, tt, :]), [ball], [ball])
            ps = pp.next()
            for kc in range(8):
                P.mm(ps[:], xn[:, kc, tsl], wgt[:, kc, :], kc == 0, kc == 7, [xn, wgt], [ps])
            g_ = gt.next()
            P.act(g_[:], ps[:], AF.Silu, [ps], [g_])
            P.dma(ggate_s[tsl, :], g_[:], reads=[g_], writes=[ggate_s])
        P.dma(gg_s[:], gall[:].rearrange("p a b -> p (a b)"), reads=[gall], writes=[gg_s])
        P.dma(gb_s[:], ball[:].rearrange("p a b -> p (a b)"), reads=[ball], writes=[gb_s])
        ZW = T + 8

        def zoff(t):
            return t + 2 if t < CTX else t + 6

        zpR = Rot(lambda i: P.sbuf("zp%d" % i, [128, ZW], BF16), 2)
        for z_ in zpR.t:
            P.memset(z_[:], 0.0, [z_])
        dgR = Rot(lambda i: P.sbuf("dgw%d" % i, [128, 5, 128], BF16), 2)
        for cc in range(12):
            zc, acc = zcR.next(), accR.next()
            zp, dgw = zpR.next(), dgR.next()
            w = wch.next()
            load_w(P, w, w[:], wv_[:, :, 672 + cc * 128:672 + (cc + 1) * 128], st, st[:, :, 0:128])
            for j in range(5):
                P.ts(dgw[:, j, :], cst[:, C_ID, :], cvw[:, cc, j:j + 1], ALU.mult, [cst, cvw], [dgw], eng="gpsimd")
            for (t0, n) in GROUPS5:
                ps = pp.next()
                for kc in range(8):
                    P.mm(ps[:, 0:n], w[:, kc, :], xn[:, kc, t0:t0 + n], kc == 0, kc == 7, [w, xn], [ps])
                P.act(zp[:, zoff(t0):zoff(t0) + n], ps[:, 0:n], AF.Copy, [ps], [zp])
            for (t0, n) in GROUPS5:
                ps = pp.next()
                for j in range(5):
                    o_ = zoff(t0) + j - 2
                    P.mm(ps[:, 0:n], dgw[:, j, :], zp[:, o_:o_ + n], j == 0, j == 4, [dgw, zp], [ps])
                P.act(zc[:, t0:t0 + n], ps[:, 0:n], AF.Silu, [ps], [zc])
            if cc < 8:
                for (t0, n) in GROUPS5:
                    sq, lnv = sqR.next(), lnvR.next()
                    P.act(sq[:, 0:n], zc[:, t0:t0 + n], AF.Square, [zc], [sq])
                    pss = pssR.next()
                    P.mm(pss[:, 0:n], cstb[:, C_BD64, :], sq[:, 0:n], True, True, [cstb, sq], [pss])
                    P.act(lnv[:, 0:n], pss[:, 0:n], AF.Ln, [pss], [lnv], bias=EPS)
                    P.act(lnv[:, 0:n], lnv[:, 0:n], AF.Exp, [lnv], [lnv], scale=-0.5)
                    if cc < 4:
                        P.stt(acc[:, t0:t0 + n], zc[:, t0:t0 + n], 0.125, lnv[:, 0:n], ALU.mult, ALU.mult, [zc, lnv], [acc])
                    else:
                        P.tt(acc[:, t0:t0 + n], zc[:, t0:t0 + n], lnv[:, 0:n], ALU.mult, [zc, lnv], [acc])
                dst = gq_s if cc < 4 else gk_s
                r0 = (cc % 4) * 128
                P.dma(dst[r0:r0 + 128, :], acc[:], reads=[acc], writes=[dst])
                src = acc
            else:
                src = zc
            if cc >= 4:
                for t4 in range(0, NT, 4):
                    nn = min(4, NT - t4)
                    pt = ptr_.next()
                    for i in range(nn):
                        tt = t4 + i
                        P.tr(pt[:, i, :], src[:, tt * 128:(tt + 1) * 128], ident, [src, cst], [pt])
                    P.copy(tmst[:, t4:t4 + nn, :], pt[:, 0:nn, :], [pt], [tmst])
                dtm = gktm_s if cc < 8 else gvtm_s
                c0 = (cc % 4) * 128
                P.dma(dtm[:, c0:c0 + 128].rearrange("(a p) f -> p a f", p=128), tmst[:], reads=[tmst], writes=[dtm])
        P.emit()

    def gdn_core():
        P = Prog(nc, sy, "gdsp")
        P.dma(xt_s[:], XT[:, 4:8, :].rearrange("p c t -> p (c t)"), reads=[XT], writes=[xt_s])
        P.emit()
        P = Prog(nc, sy, "gdc")
        XTf = XT[:, 4:8, :].rearrange("p c t -> p (c t)")
        carve_off = [0]

        def carve(name, dtype):
            words = 512 if dtype == F32 else 256
            a = XTf[:, carve_off[0]:carve_off[0] + words]
            carve_off[0] += words
            if dtype != F32:
                a = a.bitcast(dtype)
            return Buf(a.rearrange("p (h i) -> p h i", h=4), 1, name)

        lvm = P.sbuf("lvm", [128, 7, 128], F32)
        nlv = P.sbuf("nlv", [128, 14, 128], BF16)
        for hf in range(2):
            P.dma(lvm[:], lvm_d[:, 7 * hf:7 * hf + 7, :], writes=[lvm])
            P.ts(nlv[:, 7 * hf:7 * hf + 7, :], lvm[:], -1.0, ALU.mult, [lvm], [nlv])
        idb = P.sbuf("idb", [128, 128], BF16)
        P.copy(idb[:], cst[:, C_ID, :], [cst], [idb])
        gall = P.sbuf("gall", [128, NT, 16], F32)
        ball = P.sbuf("ball", [128, NT, 16], F32)
        P.dma(gall[:].rearrange("p a b -> p (a b)"), gg_s[:], reads=[gg_s], writes=[gall])
        P.dma(ball[:].rearrange("p a b -> p (a b)"), gb_s[:], reads=[gb_s], writes=[ball])
        PS = Rot(lambda i: P.psum("ps%d" % i, [128, 512]), 6)
        PB = Rot(lambda i: P.psum("pb%d" % i, [128, 4, 128], BF16), 2)

        def b4(ap):
            return ap.unsqueeze(1).broadcast_to([128, 4, 128])

        def bl(ap, w):
            return ap.unsqueeze(2).broadcast_to([128, ap.shape[1], w])

        Ibc = b4(idb[:])
        D_ = []
        for d in range(2):
            n_ = lambda s, d=d: "%s_%d" % (s, d)
            o = K()
            o.S = P.sbuf(n_("S"), [128, 8, 64], F32)
            o.q = P.sbuf(n_("q"), [64, 8, 128], F32)
            o.k = P.sbuf(n_("k"), [64, 8, 128], F32)
            o.kt = P.sbuf(n_("kt"), [128, 8, 64], F32)
            o.vt = P.sbuf(n_("vt"), [128, 8, 64], F32)
            o.eg = P.sbuf(n_("eg"), [128, 24], F32)
            o.cfk = P.sbuf(n_("cfk"), [128, 8], F32)
            o.qkd = P.sbuf(n_("qkd"), [128, 8, 128], BF16)
            o.U = P.sbuf(n_("U"), [128, 8, 64], F32)
            o.wT = P.sbuf(n_("wT"), [64, 8, 128], BF16)
            o.dg = P.sbuf(n_("dg"), [128, 4, 128], F32)
            o.qdT = P.sbuf(n_("qdT"), [128, 8, 128], BF16)
            o.kdec = P.sbuf(n_("kdec"), [128, 8, 64], BF16)
            o.vnew = P.sbuf(n_("vnew"), [128, 8, 64], BF16)
            o.Sb = P.sbuf(n_("Sb"), [128, 8, 64], BF16)
            o.ost = P.sbuf(n_("ost"), [128, 512], F32)
            P.memset(o.qdT[:], 0.0, [o.qdT])
            P.memset(o.S[:], 0.0, [o.S])
            P.memset(o.Sb[:], 0.0, [o.Sb])
            o.order = list(range(NT)) if d == 0 else [1, 0] + list(range(NT - 1, 1, -1))
            o.Uinc = cst[:, C_UI, :] if d == 0 else cst[:, C_LI, :]
            o.Bst = cst[:, C_SL, :] if d == 0 else cst[:, C_SU, :]
            o.mX = (lambda lv: nlv[:, lv, :]) if d == 0 else (lambda lv: nlv[:, 7 + lv, :])
            o.mXT = (lambda lv: nlv[:, 7 + lv, :]) if d == 0 else (lambda lv: nlv[:, lv, :])
            o.d = d
            D_.append(o)
        CH = []
        for o in D_:
            for hg in range(2):
                c = K()
                c.o, c.hg, c.h0, c.hsl = o, hg, 4 * hg, slice(4 * hg, 4 * hg + 4)
                nm = lambda s, o=o, hg=hg: "%s_%d%d" % (s, o.d, hg)
                if hg == 0:
                    mk = lambda s, dt_: P.sbuf(nm(s), [128, 4, 128], dt_)
                else:
                    mk = lambda s, dt_: carve(nm(s), dt_)
                c.GU, c.eD, c.eDT = mk("GU", F32), mk("eD", F32), mk("eDT", F32)
                c.A, c.AT, c.Ym, c.Ymt = mk("A", BF16), mk("AT", BF16), mk("Ym", BF16), mk("Ymt", BF16)
                c.X = [mk("X%d" % i, BF16) for i in range(2)]
                c.XT = [mk("XT%d" % i, BF16) for i in range(2)]
                c.Rv = P.sbuf(nm("Rv"), [128, 4, 64], BF16)
                c.Rk = P.sbuf(nm("Rk"), [128, 4, 64], BF16)
                CH.append(c)

        def v4(ps):
            return ps[:].rearrange("p (h i) -> p h i", h=4)

        for s in range(NT):
            for o in D_:
                d = o.d
                tt = o.order[s]
                o.tt = tt
                tsl = slice(tt * 128, (tt + 1) * 128)
                P.dma(o.q[:], gq_s[:, tsl].rearrange("(h d) t -> d h t", d=64), reads=[gq_s], writes=[o.q])
                P.dma(o.k[:], gk_s[:, tsl].rearrange("(h d) t -> d h t", d=64), reads=[gk_s], writes=[o.k])
                P.dma(o.kt[:].rearrange("p h d -> p (h d)"), gktm_s[tsl, :], reads=[gktm_s], writes=[o.kt])
                P.dma(o.vt[:].rearrange("p h d -> p (h d)"), gvtm_s[tsl, :], reads=[gvtm_s], writes=[o.vt])
                gd = gall[:, tt, d * 8:(d + 1) * 8]
                pg = PS.next()
                P.mm(pg[:, 0:8], o.Uinc, gd, True, True, [cst, gall], [pg])
                P.mm(pg[:, 8:16], o.Bst, gd, True, True, [cst, gall], [pg])
                P.mm(pg[:, 16:24], ones_f[:], gd, True, True, [ones_f, gall], [pg])
                P.act(o.eg[:], pg[:, 0:24], AF.Exp, [pg], [o.eg])
                P.tt(o.cfk[:], ball[:, tt, d * 8:(d + 1) * 8], o.eg[:, 0:8], ALU.mult, [ball, o.eg], [o.cfk])
            for c in CH:
                o, h0 = c.o, c.h0
                d, tt = o.d, o.tt
                P.tt(c.GU[:], b4(o.Uinc), bl(gall[:, tt, d * 8 + h0:d * 8 + h0 + 4], 128), ALU.mult, [cst, gall], [c.GU],
                     eng="gpsimd")
            for c in CH:
                o = c.o
                c.pD, c.pDT = PS.next(), PS.next()
                for h in range(4):
                    P.mm(v4(c.pD)[:, h, :], c.GU[:, h, :], o.Bst, True, True, [c.GU, cst], [c.pD])
                P.mm(c.pDT[:], o.Bst, c.GU[:].rearrange("p h i -> p (h i)"), True, True, [c.GU, cst], [c.pDT])
                P.act(c.eD[:], v4(c.pD), AF.Exp, [c.pD], [c.eD])
                P.act(c.eDT[:], v4(c.pDT), AF.Exp, [c.pDT], [c.eDT])
            for c in CH:
                o, h0 = c.o, c.h0
                d, tt = o.d, o.tt
                pKK = PS.next()
                for h in range(4):
                    P.mm(v4(pKK)[:, h, :], o.k[:, h0 + h, :], o.k[:, h0 + h, :], True, True, [o.k], [pKK])
                P.tt(c.eD[:], v4(pKK), c.eD[:], ALU.mult, [pKK, c.eD], [c.eD])
                P.tt(c.eD[:], c.eD[:], b4(o.Bst), ALU.mult, [c.eD, cst], [c.eD], eng="gpsimd")
                P.tt(c.A[:], c.eD[:], bl(ball[:, tt, d * 8 + h0:d * 8 + h0 + 4], 128), ALU.mult, [c.eD, ball], [c.A],
                     eng="gpsimd")
            for c in CH:
                o, h0 = c.o, c.h0
                pQK = PS.next()
                for h in range(4):
                    P.mm(v4(pQK)[:, h, :], o.k[:, h0 + h, :], o.q[:, h0 + h, :], True, True, [o.k, o.q], [pQK])
                P.tt(c.eDT[:], v4(pQK), c.eDT[:], ALU.mult, [pQK, c.eDT], [c.eDT])
                P.tt(o.qkd[:, c.hsl, :], c.eDT[:], b4(o.Uinc), ALU.mult, [c.eDT, cst], [o.qkd], eng="gpsimd")
            for c in CH:
                pT_ = PB.next()
                for h in range(4):
                    P.tr(pT_[:, h, :], c.A[:, h, :], idb[:], [c.A, idb], [pT_])
                P.act(c.AT[:], pT_[:], AF.Copy, [pT_], [c.AT])
            for c in CH:
                o = c.o
                P.tt(c.Ym[:], c.A[:], b4(o.mX(0)), ALU.mult, [c.A, nlv], [c.Ym], eng="gpsimd")
                P.tt(c.X[0][:], c.Ym[:], Ibc, ALU.add, [c.Ym, idb], [c.X[0]], eng="gpsimd")
                P.tt(c.Ymt[:], c.AT[:], b4(o.mXT(0)), ALU.mult, [c.AT, nlv], [c.Ymt])
                P.tt(c.XT[0][:], c.Ymt[:], Ibc, ALU.add, [c.Ymt, idb], [c.XT[0]])
            cur = 0
            for lv in range(1, 7):
                last = lv == 6
                for c in CH:
                    o = c.o
                    if not last:
                        c.pY = PS.next()
                        for h in range(4):
                            P.mm(v4(c.pY)[:, h, :], c.AT[:, h, :], c.X[cur][:, h, :], True, True, [c.AT, c.X[cur]], [c.pY])
                        P.tt(c.Ym[:], v4(c.pY), b4(o.mX(lv)), ALU.mult, [c.pY, nlv], [c.Ym])
                    c.pY2 = PS.next()
                    for h in range(4):
                        P.mm(v4(c.pY2)[:, h, :], c.A[:, h, :], c.XT[cur][:, h, :], True, True, [c.A, c.XT[cur]], [c.pY2])
                    P.tt(c.Ymt[:], v4(c.pY2), b4(o.mXT(lv)), ALU.mult, [c.pY2, nlv], [c.Ymt])
                for c in CH:
                    if not last:
                        pZ = PS.next()
                        P.mm(pZ[:], idb[:], c.X[cur][:].rearrange("p h i -> p (h i)"), True, False, [idb, c.X[cur]], [pZ])
                        for h in range(4):
                            P.mm(v4(pZ)[:, h, :], c.XT[cur][:, h, :], c.Ym[:, h, :], False, h == 3, [c.XT[cur], c.Ym], [pZ])
                        P.act(c.X[1 - cur][:], v4(pZ), AF.Copy, [pZ], [c.X[1 - cur]])
                    pZ = PS.next()
                    P.mm(pZ[:], idb[:], c.XT[cur][:].rearrange("p h i -> p (h i)"), True, False, [idb, c.XT[cur]], [pZ])
                    for h in range(4):
                        P.mm(v4(pZ)[:, h, :], c.X[cur][:, h, :], c.Ymt[:, h, :], False, h == 3, [c.X[cur], c.Ymt], [pZ])
                    P.act(c.XT[1 - cur][:], v4(pZ), AF.Copy, [pZ], [c.XT[1 - cur]])
                cur = 1 - cur
            for c in CH:
                o, h0, hsl = c.o, c.h0, c.hsl
                d, tt = o.d, o.tt
                XTf_ = c.XT[cur]
                P.tt(c.Rv[:], o.vt[:, hsl, :], bl(ball[:, tt, d * 8 + h0:d * 8 + h0 + 4], 64), ALU.mult, [o.vt, ball], [c.Rv],
                     eng="gpsimd")
                P.tt(c.Rk[:], o.kt[:, hsl, :], bl(o.cfk[:, hsl], 64), ALU.mult, [o.kt, o.cfk], [c.Rk], eng="gpsimd")
                pU, pW = PS.next(), PS.next()
                pU3 = pU[:, 0:256].rearrange("p (h i) -> p h i", h=4)
                pW3 = pW[0:64, :].rearrange("p (h i) -> p h i", h=4)
                for h in range(4):
                    P.mm(pU3[:, h, :], XTf_[:, h, :], c.Rv[:, h, :], True, True, [XTf_, c.Rv], [pU])
                for h in range(4):
                    P.mm(pW3[:, h, :], c.Rk[:, h, :], XTf_[:, h, :], True, True, [XTf_, c.Rk], [pW])
                P.act(o.U[:, hsl, :], pU3, AF.Copy, [pU], [o.U])
                P.act(o.wT[:, hsl, :], pW3, AF.Copy, [pW], [o.wT])
            for o in D_:
                for half in range(2):
                    P.tt(o.dg[:], b4(cst[:, C_ID, :]), bl(o.eg[:, 4 * half:4 * half + 4], 128), ALU.mult, [cst, o.eg], [o.dg],
                         eng="gpsimd")
                    pE = PS.next()
                    P.mm(pE[0:64, :], ones_f[:, 0:64], o.dg[:].rearrange("p h i -> p (h i)"), True, True, [ones_f, o.dg], [pE])
                    P.tt(o.qdT[0:64, 4 * half:4 * half + 4, :], o.q[:, 4 * half:4 * half + 4, :],
                         pE[0:64, :].rearrange("p (h i) -> p h i", h=4), ALU.mult, [o.q, pE], [o.qdT])
                P.tt(o.kdec[:], o.kt[:], bl(o.eg[:, 8:16], 64), ALU.mult, [o.kt, o.eg], [o.kdec], eng="gpsimd")
            for o in D_:
                pWS = PS.next()
                pWS3 = pWS[:].rearrange("p (h e) -> p h e", h=8)
                for h in range(8):
                    P.mm(pWS3[:, h, :], o.wT[:, h, :], o.Sb[0:64, h, :], True, True, [o.wT, o.Sb], [pWS])
                P.tt(o.vnew[:], o.U[:], pWS3, ALU.subtract, [o.U, pWS], [o.vnew])
            for o in D_:
                tt = o.tt
                tsl = slice(tt * 128, (tt + 1) * 128)
                pOO = PS.next()
                pOO3 = pOO[:].rearrange("p (h e) -> p h e", h=8)
                for h in range(8):
                    P.mm(pOO3[:, h, :], o.qdT[:, h, :], o.Sb[:, h, :], True, False, [o.qdT, o.Sb], [pOO])
                    P.mm(pOO3[:, h, :], o.qkd[:, h, :], o.vnew[:, h, :], False, True, [o.qkd, o.vnew], [pOO])
                P.act(o.ost[:], pOO[:], AF.Copy, [pOO], [o.ost])
                P.dma(og_s[o.d][tsl, :], o.ost[:], reads=[o.ost], writes=[og_s[o.d]])
                pSn = PS.next()
                pSn3 = pSn[0:64, :].rearrange("p (h e) -> p h e", h=8)
                for h in range(8):
                    P.mm(pSn3[:, h, :], o.kdec[:, h, :], o.vnew[:, h, :], True, True, [o.kdec, o.vnew], [pSn])
                P.tt(o.S[0:64], o.S[0:64], o.eg[0:64, 16:24].unsqueeze(2).broadcast_to([64, 8, 64]), ALU.mult, [o.S, o.eg], [o.S])
                P.tt(o.S[0:64], o.S[0:64], pSn3, ALU.add, [o.S, pSn], [o.S])
                P.act(o.Sb[0:64], o.S[0:64], AF.Copy, [o.S], [o.Sb])
        P.emit()
        P = Prog(nc, sy, "gdf")
        grow = P.sbuf("grow", [128, 544], F32)
        P.dma(grow[:], grow_d, writes=[grow])
        sqoR = Rot(lambda i: P.sbuf("sqo%d" % i, [128, 8, 64], F32), 4)
        ssR = Rot(lambda i: P.sbuf("ss%d" % i, [128, 8], F32), 4)
        toR = Rot(lambda i: P.sbuf("to%d" % i, [128, 8, 64], F32), 4)
        tobR = Rot(lambda i: P.sbuf("tob%d" % i, [128, 512], BF16), 4)
        gtt = Rot(lambda i: P.sbuf("gtt%d" % i, [128, 512], BF16), 4)
        oa = Rot(lambda i: P.sbuf("oa%d" % i, [128, 512], F32), 4)
        ob_ = Rot(lambda i: P.sbuf("ob%d" % i, [128, 512], F32), 4)
        idb = P.sbuf("idb", [128, 128], BF16)
        P.copy(idb[:], cst[:, C_ID, :], [cst], [idb])
        mixo = Rot(lambda i: P.sbuf("mixo%d" % i, [128, 4, 128], BF16), 4)
        ptbR = Rot(lambda i: P.psum("ptb%d" % i, [128, 4, 128], BF16), 4)
        for tt in range(NT):
            sqo, ss, to, tob, ptb = sqoR.next(), ssR.next(), toR.next(), tobR.next(), ptbR.next()
            tsl = slice(tt * 128, (tt + 1) * 128)
            g_, a_, b_ = gtt.next(), oa.next(), ob_.next()
            P.dma(g_[:], ggate_s[tsl, :], reads=[ggate_s], writes=[g_])
            P.dma(a_[:], og_s[0][tsl, :], reads=[og_s[0]], writes=[a_])
            P.dma(b_[:], og_s[1][tsl, :], reads=[og_s[1]], writes=[b_])
            P.tt(a_[:], a_[:], b_[:], ALU.add, [a_, b_], [a_], eng="gpsimd")
            o3 = a_[:].rearrange("p (h e) -> p h e", h=8)
            P.act(sqo[:], o3, AF.Square, [a_], [sqo])
            P.op("vector", lambda e, ss=ss, sqo=sqo: e.tensor_reduce(out=ss[:], in_=sqo[:], axis=AX.X, op=ALU.add), [sqo], [ss])
            P.act(ss[:], ss[:], AF.Ln, [ss], [ss], bias=EPS, scale=1.0 / 64)
            P.act(ss[:], ss[:], AF.Exp, [ss], [ss], scale=-0.5)
            P.tt(to[:], o3, ss[:].unsqueeze(2).broadcast_to([128, 8, 64]), ALU.mult, [a_, ss], [to])
            P.tt(to[:].rearrange("p h e -> p (h e)"), to[:].rearrange("p h e -> p (h e)"), grow[:, 32:544], ALU.mult,
                 [to, grow], [to], eng="gpsimd")
            P.tt(tob[:], to[:].rearrange("p h e -> p (h e)"), g_[:], ALU.mult, [to, g_], [tob])
            for c in range(4):
                P.tr(ptb[:, c, :], tob[:, c * 128:(c + 1) * 128], idb[:], [tob, idb], [ptb])
            mo_ = mixo.next()
            P.act(mo_[:], ptb[:], AF.Copy, [ptb], [mo_])
            P.dma(mix_s[:, 4:8, tsl], mo_[:], reads=[mo_], writes=[mix_s])
        P.dma(XT[:, 4:8, :].rearrange("p c t -> p (c t)"), xt_s[:], reads=[xt_s], writes=[XT])
        P.emit()


    for l in layers:
        if stage >= 1 and not skip_ffn1:
            ffn(l, 0, FULL)
        dump_xt("xt_ffn1_%d" % l)
        if stage >= 2:
            make_xn(l)
            if l == 0:
                if parts is None or "mlp" in parts:
                    mla_prep()
                if parts is None or "mlc" in parts:
                    mla_core()
                if parts is None or "gdp" in parts:
                    gdn_prep()
                if parts is None or "gdc" in parts:
                    gdn_core()
                if parts is None or "op" in parts:
                    out_proj(l, evwo_d, GROUPS5)
            if l == 1:
                if parts is None or "swp" in parts:
                    swa_prep()
                if parts is None or "swc" in parts:
                    swa_core()
                if parts is None or "glp" in parts:
                    gla_prep()
                if parts is None or "glc" in parts:
                    gla_core()
                if parts is None or "op" in parts:
                    out_proj(l, odwo_d, LATG)
        if "mix" in dbg_d:
            P = Prog(nc, sy, "dbgmix%d" % l)
            mb_ = Rot(lambda i: P.sbuf("mb%d" % i, [128, 8, 128], BF16), 2)
            mf_ = Rot(lambda i: P.sbuf("mf%d" % i, [128, 8, 128], F32), 2)
            dv = dbg_d["mix"].rearrange("p (c t) -> p c t", c=8)
            for tt in range(NT):
                a_, f_ = mb_.next(), mf_.next()
                P.dma(a_[:], mix_s[:, :, tt * 128:(tt + 1) * 128], reads=[mix_s], writes=[a_])
                P.act(f_[:], a_[:], AF.Copy, [a_], [f_])
                P.dma(dv[:, :, tt * 128:(tt + 1) * 128], f_[:], reads=[f_])
            P.emit()
        dump_xt("xt_mix_%d" % l)
        if stage >= 3:
            ffn(l, 1, FULL if l == 0 else LAT)

    P = Prog(nc, sy, "pout")
    ost = [P.sbuf("ost%d" % i, [128, D], F32) for i in range(2)]
    pot = [P.psum("pot%d" % i, [128, 4, 128]) for i in range(2)]
    for tt in range(2, NT):
        o = ost[tt % 2]
        for half in range(2):
            ps = pot[half]
            for c4 in range(4):
                c = half * 4 + c4
                P.tr(ps[:, c4, :], XT[:, c, tt * 128:(tt + 1) * 128], ident, [(XT, tt), cst], [ps])
            if half == 0:
                P.copy(o[:, 0:512], ps[:].rearrange("p a b -> p (a b)"), [ps], [o])
            else:
                P.act(o[:, 512:1024], ps[:].rearrange("p a b -> p (a b)"), AF.Copy, [ps], [o])
        P.dma(out_d[(tt - 2) * 128:(tt - 1) * 128, :], o[:], reads=[o])
    P.emit()
    es.close()
    return nc


def rope_tables(rot_dim):
    f = np.float32
    t = np.arange(SEQ)
    row = (t // 64).astype(f)
    col = (t % 64).astype(f)
    nf = rot_dim // 4
    inv = (f(10000.0) ** (-np.arange(nf, dtype=f) / f(nf))).astype(f)
    ang = np.concatenate([row[:, None] * inv, col[:, None] * inv], axis=-1).astype(f)
    return np.cos(ang).astype(f), np.sin(ang).astype(f)


def host_consts():
    f = np.float32
    idx = np.arange(128)
    ident = np.eye(128)
    bd64 = (idx[:, None] // 64 == idx[None, :] // 64)
    Rm = np.zeros((64, 64))
    for i in range(32):
        Rm[i, i + 32] = -1.0
        Rm[i + 32, i] = 1.0
    rot = np.zeros((128, 128))
    rot[0:64, 0:64] = Rm.T
    rot[64:128, 64:128] = Rm.T
    UI = idx[:, None] <= idx[None, :]
    LI = idx[:, None] >= idx[None, :]
    SL = idx[:, None] > idx[None, :]
    SU = idx[:, None] < idx[None, :]
    rot96 = np.zeros((128, 128))
    for i in range(16):
        rot96[80 + i, 64 + i] = -1.0
        rot96[64 + i, 80 + i] = 1.0
    cst = np.stack([ident, bd64, rot, UI, LI, SL, SU, rot96], axis=1).astype(f)
    lv = []
    for k in range(7):
        bsz = 1 << k
        same = (idx[:, None] // (2 * bsz)) == (idx[None, :] // (2 * bsz))
        m = same & ((idx[:, None] % (2 * bsz)) >= bsz) & ((idx[None, :] % (2 * bsz)) < bsz)
        lv.append(m)
    lvm = np.stack(lv + [m.T for m in lv], axis=1).astype(f)
    c8, s8 = rope_tables(32)
    mcs = np.zeros((96, 2, SEQ), f)
    mcs[0:64, 0, :] = 1.0
    for dd in range(32):
        mcs[64 + dd, 0, :] = c8[:, dd % 16]
        mcs[64 + dd, 1, :] = s8[:, dd % 16]
    bdsel = np.stack([(idx // 64 == 0), (idx // 64 == 1)], axis=1).astype(f)
    c, s = rope_tables(64)
    d = idx % 64
    swcs = np.stack([c[:, d % 32].T, s[:, d % 32].T], axis=1).astype(f)
    return {"cst": np.ascontiguousarray(cst), "bdsel": np.ascontiguousarray(bdsel), "swa_cs": np.ascontiguousarray(swcs),
            "lvm": np.ascontiguousarray(lvm), "mla_cs": np.ascontiguousarray(mcs)}


_CONSTS = None


def host_inputs(inputs, b):
    global _CONSTS
    f = np.float32
    if _CONSTS is None:
        _CONSTS = host_consts()
    c = np.asarray(inputs["c"][b], f)
    cc = np.asarray(inputs["c_ctx"], f)
    ccol = np.concatenate([c.reshape(8, 128).T, cc.reshape(8, 128).T], axis=1)
    idx = np.arange(128)
    odc = np.zeros((128, 11), f)
    odc[:, 0] = np.asarray(inputs["od_swa_q_norm"], f)[0][idx % 64]
    odc[:, 1] = np.asarray(inputs["od_swa_k_norm"], f)[0][idx % 64]
    odc[:, 2] = np.asarray(inputs["od_gla_out_norm"], f)[0]
    odc[:, 3:11] = np.asarray(inputs["od_swa_sink"], f)[0][None, :]
    w2 = np.asarray(inputs["od_gla_gate_w2"], f)[0]
    gb = np.asarray(inputs["od_gla_gate_b"], f)[0]
    gw2b = np.concatenate([w2, gb[:, None, :]], axis=1).transpose(1, 0, 2)
    evc = np.zeros((128, 8), f)
    evc[:, 0:3] = np.asarray(inputs["ev_q_a_norm"], f)[0].reshape(3, 128).T
    evc[:, 3:5] = np.asarray(inputs["ev_kv_a_norm"], f)[0].reshape(2, 128).T
    evc[0:96, 5] = np.asarray(inputs["ev_mla_q_norm"], f)[0]
    evc[0:96, 6] = np.asarray(inputs["ev_mla_k_norm"], f)[0]
    gcv = np.asarray(inputs["ev_gdn_conv"], f)[0].reshape(5, 12, 128).transpose(2, 1, 0)
    grow = np.zeros((128, 544), f)
    grow[:, 0:16] = np.asarray(inputs["ev_gdn_a_log"], f)[0].reshape(16)[None, :]
    grow[:, 16:32] = np.asarray(inputs["ev_gdn_dt_bias"], f)[0].reshape(16)[None, :]
    grow[:, 32:544] = np.tile(np.asarray(inputs["ev_gdn_out_norm"], f)[0], 8)[None, :]
    m = {
        "ev_w_in": np.ascontiguousarray(np.asarray(inputs["ev_w_in"], f)[0]),
        "ev_w_out": np.ascontiguousarray(np.asarray(inputs["ev_w_out"], f)[0]),
        "ev_w_q_up": np.ascontiguousarray(np.asarray(inputs["ev_w_q_up"], f)[0]),
        "ev_w_kv_up": np.ascontiguousarray(np.asarray(inputs["ev_w_kv_up"], f)[0]),
        "ev_cols": evc,
        "gdn_conv_col": np.ascontiguousarray(gcv),
        "gdn_rows": grow,
        "x": np.ascontiguousarray(inputs["x"][b], dtype=f),
        "ctx": np.ascontiguousarray(inputs["ctx"][b], dtype=f),
        "ccol": np.ascontiguousarray(ccol),
        "ada_w": np.ascontiguousarray(inputs["ada_w"], dtype=f),
        "ada_b_col": np.ascontiguousarray(np.asarray(inputs["ada_b"], f).reshape(2, 72, 128).transpose(2, 0, 1)),
        "norm_g_col": np.ascontiguousarray(np.asarray(inputs["norm_g"], f).reshape(2, 3, 8, 128).transpose(3, 0, 1, 2)),
        "ffn_w_gate": np.ascontiguousarray(inputs["ffn_w_gate"], dtype=f),
        "ffn_w_up": np.ascontiguousarray(inputs["ffn_w_up"], dtype=f),
        "ffn_w_down": np.ascontiguousarray(inputs["ffn_w_down"], dtype=f),
        "od_w_in": np.ascontiguousarray(np.asarray(inputs["od_w_in"], f)[0]),
        "od_w_out": np.ascontiguousarray(np.asarray(inputs["od_w_out"], f)[0]),
        "od_cols": odc,
        "gla_w2b": np.ascontiguousarray(gw2b),
    }
    m.update(_CONSTS)
    return m


def kernel(**inputs):
    nc = build()
    in_maps = [host_inputs(inputs, b) for b in range(NCORES)]
    res = run_bass_kernel_spmd(nc, in_maps, core_ids=list(range(NCORES)))
    return np.stack([np.asarray(r["out"], np.float32) for r in res.results], axis=0)
```
